# Optimizing a Trainium2 kernel written in Bass

```python
import jax, jax.numpy as jnp
from jax import lax
import numpy as np

D_MODEL = 2048
BATCH = 2
SEQ = 4096
DEPTH = 1
DEC_BATCH = 8
DEC_SEQ = 32
PAST_LEN = 2048

CHUNK = 64
HEAD_DIM = 64
D_SB = D_MODEL // 2
D_RWKV = D_MODEL - D_SB
H_SB = D_SB // HEAD_DIM
H_RWKV = D_RWKV // HEAD_DIM
DECAY_LORA = 64
AAA_LORA = 64
GATE_LORA = 128
RWKV_COLS = 3 * D_RWKV + DECAY_LORA + AAA_LORA + GATE_LORA
IN_COLS = 3 * D_SB + RWKV_COLS
D_FF = ((8 * D_MODEL // 3 + 255) // 256) * 256
FFN_CONV = 3
Q_BLOCK = 128
RMS_EPS = 1e-6
LNX_EPS = 1e-5 * HEAD_DIM

kernel_name = 'hymba_stickbreak_rwkv7_convffn_step'


def rms_norm(x, g):
    xf = x.astype(jnp.float32)
    y = xf * lax.rsqrt(jnp.mean(xf * xf, axis=-1, keepdims=True) + RMS_EPS)
    return (y * g.astype(jnp.float32)).astype(x.dtype)


def sb_block(q, k, v, q_pos, k_pos):
    z = jnp.einsum('bhqd,bhkd->bhqk', q.astype(jnp.float32), k.astype(jnp.float32)) * (HEAD_DIM ** -0.5)
    mask = k_pos[None, :] < q_pos[:, None]
    log_1m = jnp.where(mask, jax.nn.log_sigmoid(-z), 0.0)
    after = lax.cumsum(log_1m, axis=3, reverse=True) - log_1m
    attn = jnp.where(mask, jnp.exp(jax.nn.log_sigmoid(z) + after), 0.0)
    return jnp.einsum('bhqk,bhkd->bhqd', attn, v.astype(jnp.float32))


def stick_breaking(q, k_all, v_all, q_start):
    B, H, T, _ = q.shape
    k_pos = jnp.arange(k_all.shape[2])
    q_pos = q_start + jnp.arange(T)
    if T <= Q_BLOCK:
        return sb_block(q, k_all, v_all, q_pos, k_pos)
    nb = T // Q_BLOCK
    qb = q.reshape(B, H, nb, Q_BLOCK, HEAD_DIM).transpose(2, 0, 1, 3, 4)
    pb = q_pos.reshape(nb, Q_BLOCK)
    out = lax.map(lambda a: sb_block(a[0], k_all, v_all, a[1], k_pos), (qb, pb))
    return out.transpose(1, 2, 0, 3, 4).reshape(B, H, T, HEAD_DIM)


def rwkv7_mix(xp, shift_prev, S0, p):
    B, T, _ = xp.shape
    prev = jnp.concatenate([shift_prev.astype(xp.dtype), xp[:, :-1]], axis=1)
    xs = xp + p['mu'] * (prev - xp)
    new_shift = xp[:, -1:]
    c = 3 * D_RWKV
    r, k, v, wd, ad, gd = jnp.split(xs, [D_RWKV, 2 * D_RWKV, c, c + DECAY_LORA, c + DECAY_LORA + AAA_LORA], axis=-1)
    w = -jax.nn.softplus(-(p['w0'] + jnp.tanh(wd) @ p['w2'])) - 0.5
    decay = jnp.exp(-jnp.exp(w.astype(jnp.float32)))
    a = jax.nn.sigmoid(p['a0'] + ad @ p['a2'])
    g = jax.nn.sigmoid(gd) @ p['g2']
    heads = lambda t: t.astype(jnp.float32).reshape(B, T, H_RWKV, HEAD_DIM)
    kk = heads(k * p['k_k'])
    kk = kk / jnp.maximum(jnp.linalg.norm(kk, axis=-1, keepdims=True), 1e-12)
    k = k * (1.0 + (a - 1.0) * p['k_a'])
    r_h, k_h, v_h, w_h, a_h = heads(r), heads(k), heads(v), heads(decay), heads(a)

    def step(S, inp):
        r_t, w_t, k_t, v_t, kk_t, a_t = inp
        sa = jnp.einsum('bhvk,bhk->bhv', S, -kk_t)
        S = S * w_t[:, :, None, :] + sa[..., None] * (kk_t * a_t)[:, :, None, :] + v_t[..., None] * k_t[:, :, None, :]
        return S, jnp.einsum('bhvk,bhk->bhv', S, r_t)

    seq = tuple(jnp.moveaxis(t, 1, 0) for t in (r_h, w_h, k_h, v_h, kk, a_h))
    S_T, y = lax.scan(step, S0.astype(jnp.float32), seq)
    y = jnp.moveaxis(y, 0, 1)
    mean = jnp.mean(y, axis=-1, keepdims=True)
    var = jnp.mean(jnp.square(y - mean), axis=-1, keepdims=True)
    y = ((y - mean) * lax.rsqrt(var + LNX_EPS)).reshape(B, T, D_RWKV) * p['lnx_w'] + p['lnx_b']
    bonus = jnp.sum(r_h * k_h * p['r_k'], axis=-1, keepdims=True) * v_h
    y = (y + bonus.reshape(B, T, D_RWKV)) * g
    return y.astype(xp.dtype), S_T, new_shift


def conv_ffn(xn, conv_prev, p):
    T = xn.shape[1]
    u = xn @ p['w_up']
    gt = xn @ p['w_gate']
    gpad = jnp.concatenate([conv_prev.astype(gt.dtype), gt], axis=1)
    gc = p['conv_b'] + sum(p['conv_w'][i] * gpad[:, i:i + T] for i in range(FFN_CONV))
    h = jax.nn.silu(gc) * u
    return h @ p['w_down'], gpad[:, -(FFN_CONV - 1):]


def layer(x, past_k, past_v, S0, shift0, conv0, p):
    B, T, _ = x.shape
    P = past_k.shape[2]
    xn = rms_norm(x, p['norm1_g'])
    proj = xn @ p['w_in']
    q, k, v, xr = jnp.split(proj, [D_SB, 2 * D_SB, 3 * D_SB], axis=-1)
    heads = lambda t: t.reshape(B, T, H_SB, HEAD_DIM).transpose(0, 2, 1, 3)
    q = rms_norm(heads(q), p['q_norm_g'])
    k = rms_norm(heads(k), p['k_norm_g'])
    v = heads(v)
    k_all = jnp.concatenate([past_k.astype(k.dtype), k], axis=2)
    v_all = jnp.concatenate([past_v.astype(v.dtype), v], axis=2)
    o = stick_breaking(q, k_all, v_all, P)
    o = rms_norm(o, p['sb_out_g'][:, None, :])
    o = o.transpose(0, 2, 1, 3).reshape(B, T, D_SB).astype(x.dtype)
    y_r, S_T, new_shift = rwkv7_mix(xr, shift0, S0, p)
    x = x + jnp.concatenate([o, y_r.astype(x.dtype)], axis=-1) @ p['w_out']
    f, new_conv = conv_ffn(rms_norm(x, p['norm2_g']), conv0, p)
    x = x + f.astype(x.dtype)
    return x, k, v, S_T, new_shift, new_conv


def setup_inputs(seed: int = 0) -> dict:
    key = jax.random.key(seed)
    ks = jax.random.split(key, 32)
    f32 = jnp.float32
    L = DEPTH

    def nrm(k, shape, scale):
        return jax.random.normal(k, shape, f32) * scale

    decay_base = -6.0 + 5.0 * jnp.arange(D_RWKV, dtype=f32) / (D_RWKV - 1)
    return {
        'x_prompt': nrm(ks[0], (BATCH, SEQ, D_MODEL), 1.0),
        'x_sample': nrm(ks[1], (DEC_BATCH, DEC_SEQ, D_MODEL), 1.0),
        'cache_sb_k': nrm(ks[2], (L, DEC_BATCH, H_SB, PAST_LEN, HEAD_DIM), 1.0),
        'cache_sb_v': nrm(ks[3], (L, DEC_BATCH, H_SB, PAST_LEN, HEAD_DIM), 1.0),
        'state_rwkv': nrm(ks[4], (L, DEC_BATCH, H_RWKV, HEAD_DIM, HEAD_DIM), 0.3),
        'state_rwkv_shift': nrm(ks[5], (L, DEC_BATCH, 1, RWKV_COLS), 1.0),
        'state_ffn_conv': nrm(ks[6], (L, DEC_BATCH, FFN_CONV - 1, D_FF), 1.0),
        'norm1_g': 1.0 + nrm(ks[7], (L, D_MODEL), 0.02),
        'w_in': nrm(ks[8], (L, D_MODEL, IN_COLS), D_MODEL ** -0.5),
        'q_norm_g': 1.0 + nrm(ks[9], (L, HEAD_DIM), 0.02),
        'k_norm_g': 1.0 + nrm(ks[10], (L, HEAD_DIM), 0.02),
        'sb_out_g': 1.0 + nrm(ks[11], (L, H_SB, HEAD_DIM), 0.02),
        'mu_shift': jax.random.uniform(ks[12], (L, RWKV_COLS), f32),
        'w0': decay_base + nrm(ks[13], (L, D_RWKV), 0.1),
        'w2': nrm(ks[14], (L, DECAY_LORA, D_RWKV), 0.1 * DECAY_LORA ** -0.5),
        'a0': nrm(ks[15], (L, D_RWKV), 0.1),
        'a2': nrm(ks[16], (L, AAA_LORA, D_RWKV), AAA_LORA ** -0.5),
        'g2': nrm(ks[17], (L, GATE_LORA, D_RWKV), GATE_LORA ** -0.5),
        'k_k': 0.85 + nrm(ks[18], (L, D_RWKV), 0.02),
        'k_a': 1.0 + nrm(ks[19], (L, D_RWKV), 0.02),
        'r_k': nrm(ks[20], (L, H_RWKV, HEAD_DIM), 0.1),
        'lnx_w': 1.0 + nrm(ks[21], (L, D_RWKV), 0.02),
        'lnx_b': nrm(ks[22], (L, D_RWKV), 0.01),
        'w_out': nrm(ks[23], (L, D_SB + D_RWKV, D_MODEL), (D_SB + D_RWKV) ** -0.5),
        'norm2_g': 1.0 + nrm(ks[24], (L, D_MODEL), 0.02),
        'w_ffn_up': nrm(ks[25], (L, D_MODEL, D_FF), D_MODEL ** -0.5),
        'w_ffn_gate': nrm(ks[26], (L, D_MODEL, D_FF), D_MODEL ** -0.5),
        'ffn_conv_w': nrm(ks[27], (L, FFN_CONV, D_FF), FFN_CONV ** -0.5),
        'ffn_conv_b': nrm(ks[28], (L, D_FF), 0.01),
        'w_ffn_down': nrm(ks[29], (L, D_FF, D_MODEL), D_FF ** -0.5),
    }


def reference(x_prompt, x_sample, cache_sb_k, cache_sb_v, state_rwkv, state_rwkv_shift, state_ffn_conv,
              norm1_g, w_in, q_norm_g, k_norm_g, sb_out_g, mu_shift, w0, w2, a0, a2, g2, k_k, k_a, r_k,
              lnx_w, lnx_b, w_out, norm2_g, w_ffn_up, w_ffn_gate, ffn_conv_w, ffn_conv_b, w_ffn_down):
    assert x_sample.shape[1] <= CHUNK
    yp, ys = x_prompt, x_sample
    b = x_prompt.shape[0]
    outs_p, outs_s = [], []
    for l in range(DEPTH):
        p = {'norm1_g': norm1_g[l], 'w_in': w_in[l], 'q_norm_g': q_norm_g[l], 'k_norm_g': k_norm_g[l],
             'sb_out_g': sb_out_g[l], 'mu': mu_shift[l], 'w0': w0[l], 'w2': w2[l], 'a0': a0[l], 'a2': a2[l],
             'g2': g2[l], 'k_k': k_k[l], 'k_a': k_a[l], 'r_k': r_k[l], 'lnx_w': lnx_w[l], 'lnx_b': lnx_b[l],
             'w_out': w_out[l], 'norm2_g': norm2_g[l], 'w_up': w_ffn_up[l], 'w_gate': w_ffn_gate[l],
             'conv_w': ffn_conv_w[l], 'conv_b': ffn_conv_b[l], 'w_down': w_ffn_down[l]}
        yp, kp, vp, sp, shp, cp = layer(
            yp,
            jnp.zeros((b, H_SB, 0, HEAD_DIM), yp.dtype),
            jnp.zeros((b, H_SB, 0, HEAD_DIM), yp.dtype),
            jnp.zeros((b, H_RWKV, HEAD_DIM, HEAD_DIM), jnp.float32),
            jnp.zeros((b, 1, RWKV_COLS), yp.dtype),
            jnp.zeros((b, FFN_CONV - 1, D_FF), yp.dtype),
            p)
        outs_p.append((kp, vp, sp, shp, cp))
        ys, ksm, vsm, ssm, shs, cs = layer(
            ys, cache_sb_k[l], cache_sb_v[l], state_rwkv[l], state_rwkv_shift[l], state_ffn_conv[l], p)
        outs_s.append((ksm, vsm, ssm, shs, cs))
    k_p, v_p, s_p, sh_p, c_p = (jnp.stack(t) for t in zip(*outs_p))
    k_s, v_s, s_s, sh_s, c_s = (jnp.stack(t) for t in zip(*outs_s))
    return (yp, ys, k_p, v_p, s_p, sh_p, c_p, k_s, v_s, s_s, sh_s, c_s)
```

```python
import numpy as np
from contextlib import ExitStack
import concourse.bass as bass
import concourse.mybir as mybir
from concourse.bass_utils import run_bass_kernel_spmd
import ml_dtypes

F32 = mybir.dt.float32
BF16 = mybir.dt.bfloat16
I32 = mybir.dt.int32
AF = mybir.ActivationFunctionType
ALU = mybir.AluOpType
AX = mybir.AxisListType

D = 2048
T_P = 4096
NS = 8
T_S = 32
PAST = 2048
DFF = 5632
NFC = DFF // 128
RMS_EPS = 1e-6
LNX_EPS = 1e-5 * 64
ENGS = ("pe", "act", "dve", "pool", "sp")


class Sched:
    def __init__(self, nc, n_dma_sems=(("sp", 20), ("pool", 10), ("act", 2))):
        self.nc = nc
        self.ops = []
        self.last_w = {}
        self.readers = {}
        self.n_dma_sems = dict(n_dma_sems)
        self.dnext = {e: 0 for e in self.n_dma_sems}
        self.dlast = {e: [None] * n for e, n in self.n_dma_sems.items()}
        self.elast = {e: None for e in ENGS}

    def op(self, eng, fn, reads=(), writes=(), dma=False):
        i = len(self.ops)
        raw = set()
        oth = set()
        for k in reads:
            if k in self.last_w:
                raw.add(self.last_w[k])
        for k in writes:
            if k in self.last_w:
                oth.add(self.last_w[k])
            for r in self.readers.get(k, ()):
                oth.add(r)
        o = dict(eng=eng, fn=fn, raw=raw, oth=oth - raw, dma=dma, sig=None, slot=None, prev_on_sem=None)
        if dma:
            k = self.dnext[eng]
            self.dnext[eng] = (k + 1) % self.n_dma_sems[eng]
            o["slot"] = k
            o["prev_on_sem"] = self.dlast[eng][k]
            self.dlast[eng][k] = i
        self.ops.append(o)
        self.elast[eng] = i
        for k in reads:
            self.readers.setdefault(k, []).append(i)
        for k in writes:
            self.last_w[k] = i
            self.readers[k] = []
        return i

    def barrier(self):
        deps = set(v for v in self.elast.values() if v is not None)
        for e, l in self.dlast.items():
            deps |= set(v for v in l if v is not None)
        for e in ENGS:
            i = len(self.ops)
            self.ops.append(dict(eng=e, fn=None, raw=set(deps), oth=set(), dma=False, sig=None, slot=None,
                                 prev_on_sem=None))
        self.last_w = {}
        self.readers = {}

    def _needs_wait(self, o, d, is_raw):
        od = self.ops[d]
        if od["dma"] or o["dma"] or od["eng"] != o["eng"]:
            return True
        if o["fn"] is None:
            return True
        return is_raw and o["eng"] != "pe"

    def emit(self, ctx):
        import os
        nmax = int(os.environ.get("P1_NOPS", "0"))
        if nmax:
            self.ops = self.ops[:nmax]
            for e, l in self.dlast.items():
                for k in range(len(l)):
                    cands = [i for i, o in enumerate(self.ops) if o["dma"] and o["eng"] == e and o["slot"] == k]
                    l[k] = cands[-1] if cands else None
        nc = self.nc
        ops = self.ops
        need = [False] * len(ops)
        for i, o in enumerate(ops):
            for d in o["raw"]:
                if self._needs_wait(o, d, True):
                    need[d] = True
            for d in o["oth"]:
                if self._needs_wait(o, d, False):
                    need[d] = True
        esem = {e: ctx.enter_context(nc.semaphore("s_" + e)) for e in ENGS}
        dsem = {e: [ctx.enter_context(nc.semaphore("d_%s%d" % (e, k))) for k in range(n)]
                for e, n in self.n_dma_sems.items()}
        ecount = {e: 0 for e in ENGS}
        dcount = {e: [0] * n for e, n in self.n_dma_sems.items()}
        for i, o in enumerate(ops):
            e = o["eng"]
            if o["fn"] is None:
                continue
            if o["dma"]:
                k = o["slot"]
                dcount[e][k] += 16
                o["sig"] = (dsem[e][k], dcount[e][k], ("d", e, k))
            elif need[i]:
                ecount[e] += 1
                o["sig"] = (esem[e], ecount[e], ("e", e))
        streams = {e: [] for e in ENGS}
        for i, o in enumerate(ops):
            streams[o["eng"]].append(i)
        engobj = dict(pe="tensor", act="scalar", dve="vector", pool="gpsimd", sp="sync")
        dlast = self.dlast
        with nc.Block() as block:
            def make(e):
                def body(eng):
                    waited = {}

                    def wait_for(d):
                        if ops[d]["sig"] is None:
                            return
                        sem, val, key = ops[d]["sig"]
                        if waited.get(key, 0) >= val:
                            return
                        waited[key] = val
                        eng.wait_ge(sem, val)

                    for i in streams[e]:
                        o = ops[i]
                        if o["dma"] and o["prev_on_sem"] is not None:
                            wait_for(o["prev_on_sem"])
                        for d in sorted(o["raw"]):
                            if self._needs_wait(o, d, True):
                                wait_for(d)
                        for d in sorted(o["oth"]):
                            if self._needs_wait(o, d, False):
                                wait_for(d)
                        if o["fn"] is None:
                            continue
                        ins = o["fn"](eng)
                        if o["sig"] is not None:
                            sem, val, key = o["sig"]
                            ins.then_inc(sem, 16 if o["dma"] else 1)
                    for k, d in enumerate(dlast.get(e, [])):
                        if d is not None:
                            wait_for(d)
                return body
            for e in ENGS:
                if streams[e]:
                    getattr(block, engobj[e])(make(e))


class B:
    def __init__(self, nc, S):
        self.nc = nc
        self.S = S
        self.rr = 0

    def dma(self, q, out, in_, r=(), w=(), slow=False):
        if slow:
            self.S.op(q, lambda e: e.dma_start(out=out, in_=in_, allow_slow_non_contiguous=True), r, w, dma=True)
        else:
            self.S.op(q, lambda e: e.dma_start(out=out, in_=in_), r, w, dma=True)

    def mm(self, out, lhsT, rhs, r, w, start=True, stop=True, skip=False):
        if skip:
            self.S.op("pe", lambda e: e.matmul(out, lhsT=lhsT, rhs=rhs, start=start, stop=stop, skip_group_check=True),
                      r, w)
        else:
            self.S.op("pe", lambda e: e.matmul(out, lhsT=lhsT, rhs=rhs, start=start, stop=stop), r, w)

    def tr(self, out, in_, ident, r, w):
        self.S.op("pe", lambda e: e.transpose(out, in_, ident), r, w)

    def act(self, out, in_, func, r, w, bias=None, scale=None, accum=None):
        kw = {}
        if bias is not None:
            kw["bias"] = bias
        if scale is not None:
            kw["scale"] = scale
        if accum is not None:
            kw["accum_out"] = accum
        self.S.op("act", lambda e: e.activation(out=out, in_=in_, func=func, **kw), r, w)

    def tt(self, eng, out, in0, in1, op, r, w):
        self.S.op(eng, lambda e: e.tensor_tensor(out=out, in0=in0, in1=in1, op=op), r, w)

    def ts(self, eng, out, in0, s1, op0, r, w, s2=None, op1=None):
        if s2 is None:
            self.S.op(eng, lambda e: e.tensor_scalar(out=out, in0=in0, scalar1=s1, scalar2=None, op0=op0), r, w)
        else:
            self.S.op(eng, lambda e: e.tensor_scalar(out=out, in0=in0, scalar1=s1, scalar2=s2, op0=op0, op1=op1), r, w)

    def stt(self, eng, out, in0, scalar, in1, op0, op1, r, w):
        self.S.op(eng, lambda e: e.scalar_tensor_tensor(out=out, in0=in0, scalar=scalar, in1=in1, op0=op0, op1=op1),
                  r, w)

    def cp(self, eng, out, in_, r, w):
        if eng == "act":
            self.S.op("act", lambda e: e.activation(out=out, in_=in_, func=AF.Copy), r, w)
        else:
            self.S.op(eng, lambda e: e.tensor_copy(out=out, in_=in_), r, w)

    def memset(self, eng, out, val, r, w):
        self.S.op(eng, lambda e: e.memset(out, val), r, w)

    def reduce(self, eng, out, in_, r, w):
        self.S.op(eng, lambda e: e.tensor_reduce(out=out, in_=in_, axis=AX.X, op=ALU.add), r, w)

    def scan(self, out, d0, d1, r, w):
        self.S.op("dve", lambda e: e.tensor_tensor_scan(out=out, data0=d0, data1=d1, initial=0.0, op0=ALU.mult,
                                                        op1=ALU.add), r, w)

    def rstd(self, out, in_, scale, eps, r, w):
        self.act(out, in_, AF.Ln, r, w, bias=eps, scale=scale)
        self.act(out, out, AF.Exp, w, w, scale=-0.5)


CB = dict(ident=0, trineg=128, ones=256, bdones=384, bdmean=512, maskp=640, masks=640 + 2048)
CB_N = 640 + 2048 + 256
CF = dict(maskA64=0, maskL64=128, id64=192, maskA32=256, maskL32=320, id32=352, seg64=384, seg32=896)
CF_N = 896 + 256


def make_consts():
    cb = np.zeros((128, CB_N), np.float32)
    i = np.arange(128)
    cb[:, 0:128] = np.eye(128)
    cb[:, 128:256] = -(i[:, None] >= i[None, :]).astype(np.float32)
    cb[:, 256:384] = 1.0
    blk = (i[:, None] // 64 == i[None, :] // 64).astype(np.float32)
    cb[:, 384:512] = blk
    cb[:, 512:640] = blk / 64.0
    q = np.arange(512)
    for d in range(4):
        cb[:, 640 + 512 * d: 640 + 512 * (d + 1)] = ((128 * d + i[:, None]) < q[None, :]).astype(np.float32)
    q2 = np.arange(256)
    cb[0:32, 640 + 2048:] = (i[0:32, None] < (q2[None, :] % 32)).astype(np.float32)
    cf = np.zeros((128, CF_N), np.float32)
    for C, ka, kl, ki in ((64, "maskA64", "maskL64", "id64"), (32, "maskA32", "maskL32", "id32")):
        s = np.arange(C)
        for r0 in (0, 64):
            cf[r0:r0 + C, CF[ka]:CF[ka] + C] = (s[:, None] < s[None, :])
            cf[r0:r0 + C, CF[ka] + C:CF[ka] + 2 * C] = (s[:, None] <= s[None, :])
            cf[r0:r0 + C, CF[kl]:CF[kl] + C] = (s[None, :] < s[:, None])
            cf[r0:r0 + C, CF[ki]:CF[ki] + C] = np.eye(C)
    cf[:, CF["seg64"]:CF["seg64"] + 512] = (np.arange(512) % 64 != 0)[None, :]
    cf[:, CF["seg32"]:CF["seg32"] + 256] = (np.arange(256) % 32 != 0)[None, :]
    return cb, cf


def pp_layout(nc2):
    nch = 3 * nc2 + 2
    L = {}
    o = 0
    L["mu"] = o; o += nch
    for k in ("w0", "a0", "kk", "ka", "rk", "lw", "lb", "sbg"):
        L[k] = o; o += nc2
    L["n"] = o
    return L


class Cfg:
    def __init__(self, name, nh, ntile, C, nseg):
        self.name = name
        self.nh = nh
        self.HC = nh * 64
        self.nc2 = nh // 2
        self.NT = ntile
        self.nsub = ntile // 128
        self.C = C
        self.nch = ntile // C
        self.nseg = nseg
        self.seglen = ntile // nseg
        self.nrch = 3 * self.nc2 + 2
        self.nlev = {64: 5, 32: 4}[C]
        self.L = pp_layout(self.nc2)


CFG_P = Cfg("p", 4, 512, 64, 1)
CFG_S = Cfg("s", 2, 256, 32, 8)
ARENA_BYTES = 200 * 1024


class Arena:
    def __init__(self, tile):
        self.t = tile
        self.off = 0

    def mark(self):
        return self.off

    def reset(self, m):
        self.off = m

    def alloc(self, shape, dtype):
        free = 1
        for s in shape[1:]:
            free *= s
        ncols = free * (2 if dtype == F32 else 1)
        ncols = (ncols + 15) // 16 * 16
        assert (self.off + ncols) * 2 <= ARENA_BYTES, ("arena overflow", self.off * 2, ncols * 2)
        ap = self.t[:, self.off:self.off + ncols]
        self.off += ncols
        if dtype == F32:
            ap = ap.bitcast(F32)
        ap = ap[:, 0:free]
        if len(shape) == 3:
            ap = ap.rearrange("p (a b) -> p a b", a=shape[1])
        elif len(shape) == 4:
            ap = ap.rearrange("p (a b c) -> p a b c", a=shape[1], b=shape[2])
        if shape[0] < 128:
            ap = ap[0:shape[0]]
        return ap


def build_phase1(stop=None):
    import os
    stop = stop if stop is not None else float(os.environ.get('P1_STOP', '99'))
    nc = bass.Bass("TRN2", target_bir_lowering=False)
    dt = lambda name, shape, ty, kind: nc.dram_tensor(name, shape, ty, kind=kind).ap()
    IN, OUT = "ExternalInput", "ExternalOutput"
    xp = dt("xp", [T_P, D], F32, IN)
    xs = dt("xs", [NS * T_S, D], F32, IN)
    w1p_sb = dt("w1p_sb", [D, 768], F32, IN)
    w1p_rw = dt("w1p_rw", [D, 1024], F32, IN)
    w1s = dt("w1s", [D, 1024], F32, IN)
    kcT = dt("kcT", [64, 2, NS, PAST], F32, IN)
    vc = dt("vc", [NS, PAST, 128], F32, IN)
    st0 = dt("st0", [128, NS, 64], F32, IN)
    sh0 = dt("sh0", [128, CFG_S.nrch, NS], F32, IN)
    g1 = dt("g1", [1, D], F32, IN)
    qkg_p = dt("qkg_p", [1, 512], F32, IN)
    qkg_s = dt("qkg_s", [1, 256], F32, IN)
    pp_p = dt("pp_p", [128, CFG_P.L["n"]], F32, IN)
    pp_s = dt("pp_s", [128, CFG_S.L["n"]], F32, IN)
    wa2_p = dt("wa2_p", [128, 256], F32, IN)
    wa2_s = dt("wa2_s", [128, 128], F32, IN)
    g2_p = dt("g2_p", [128, 256], F32, IN)
    g2_s = dt("g2_s", [128, 128], F32, IN)
    cbd = dt("cbd", [128, CB_N], F32, IN)
    cfd = dt("cfd", [128, CF_N], F32, IN)
    mix_p = dt("mix_p", [512, T_P], BF16, OUT)
    mix_s = dt("mix_s", [256, NS * T_S], BF16, OUT)
    k_p = dt("k_p", [4, T_P, 64], F32, OUT)
    v_p = dt("v_p", [4, T_P, 64], F32, OUT)
    s_p = dt("s_p", [2, 128, 64], F32, OUT)
    sh_p = dt("sh_p", [128, CFG_P.nrch], F32, OUT)
    k_s = dt("k_s", [NS * T_S, 128], F32, OUT)
    v_s = dt("v_s", [NS * T_S, 128], F32, OUT)
    s_s = dt("s_s", [128, NS, 64], F32, OUT)
    sh_s = dt("sh_s", [128, CFG_S.nrch, NS], F32, OUT)
    xnT_d = nc.dram_tensor("xnT_d", [8, 128, 16 * 512], BF16).ap()

    with ExitStack() as ctx:
        S = Sched(nc)
        b = B(nc, S)
        arena_t = ctx.enter_context(nc.sbuf_tensor("arena", [128, ARENA_BYTES // 2], BF16))
        A = Arena(arena_t)
        PS = [ctx.enter_context(nc.psum_tensor("ps%d" % i, [128, 512], F32)) for i in range(8)]

        cb = A.alloc([128, CB_N], BF16)
        cf = A.alloc([128, CF_N], F32)
        g1b = A.alloc([128, D], F32)
        xt = [A.alloc([128, D], F32) for _ in range(2)]
        xn = A.alloc([128, D], BF16)
        st = A.alloc([128, 32], F32)
        xnT = A.alloc([128, 16, 512], BF16)
        pp = A.alloc([128, 32], F32)
        omka = A.alloc([128, 2], F32)
        xrs_raw = A.alloc([128, 5, 256], F32)
        b.dma("pool", cb, cbd, [], ["cb"])
        b.dma("sp", cf, cfd, [], ["cf"])
        b.dma("sp", g1b, g1.to_broadcast([128, D]), [], ["g1b"])
        ident = cb[:, 0:128]
        trineg = cb[:, 128:256]
        ones = cb[:, 256:384]
        bdones = cb[:, 384:512]
        bdmean = cb[:, 512:640]
        base_mark = A.mark()
        if stop == 0:
            S.emit(ctx)
            return nc

        def bfview(ps):
            return ps[:].bitcast(BF16)

        def stage_norm_T(nsub, xrows):
            for m in range(nsub):
                i2 = m % 2
                kx = "xt%d" % i2
                b.dma("sp", xt[i2], xrows[m * 128:(m + 1) * 128, :], [], [kx])
                b.act(xn, xt[i2], AF.Square, [kx], ["xn", "ss"], accum=st[:, 0:1])
                b.rstd(st[:, 1:2], st[:, 0:1], 1.0 / D, RMS_EPS, ["ss"], ["rs"])
                b.stt("dve", xn, xt[i2], st[:, 1:2], g1b, ALU.mult, ALU.mult, [kx, "rs", "g1b"], ["xn"])
                for g in range(4):
                    bank = PS[g % 2]
                    bk = "ps%d" % (g % 2)
                    pb = bfview(bank)
                    for c in range(4):
                        b.tr(pb[:, c * 128:(c + 1) * 128], xn[:, (4 * g + c) * 128:(4 * g + c + 1) * 128], ident,
                             ["xn", "cb"], [bk])
                    b.cp(("act", "dve")[g % 2], xnT[:, 4 * g:4 * g + 4, m * 128:(m + 1) * 128],
                         pb[:, 0:512].rearrange("p (c t) -> p c t", c=4), [bk], ["xnT"])

        def load_params(cfg, sfx, pp_d, omka_needed=True):
            L = cfg.L
            b.dma("sp", pp[:, 0:L["n"]], pp_d, [], ["pp"])
            b.ts("dve", omka[:, 0:cfg.nc2], pp[:, L["ka"]:L["ka"] + cfg.nc2], -1.0, ALU.mult, ["pp"], ["omka"],
                 s2=1.0, op1=ALU.add)

        def sb_alloc():
            t = {}
            t["sqk"] = A.alloc([128, 512], F32)
            t["qkt"] = A.alloc([128, 512], F32)
            t["qn"] = A.alloc([128, 256], BF16)
            t["knf"] = [A.alloc([128, 256], F32) for _ in range(2)]
            t["knb"] = A.alloc([128, 256], BF16)
            t["vf"] = [A.alloc([128, 256], F32) for _ in range(2)]
            t["qkg"] = A.alloc([128, 512], F32)
            t["qT"] = A.alloc([64, 4, 512], BF16)
            t["ef"] = [A.alloc([128, 512], F32) for _ in range(2)]
            t["Lb"] = [A.alloc([128, 512], BF16) for _ in range(2)]
            t["attn"] = [A.alloc([128, 512], BF16) for _ in range(2)]
            t["Cc"] = A.alloc([128, 512], F32)
            t["of"] = A.alloc([128, 512], F32)
            t["osq"] = A.alloc([128, 512], BF16)
            t["rso"] = A.alloc([128, 512], F32)
            t["onb"] = A.alloc([128, 512], BF16)
            return t

        def stage_sb_tok(cfg, t, W1, m, kT_dst, V_dst, k_out, v_out):
            HC, nh = cfg.HC, cfg.nh
            tl = slice(m * 128, (m + 1) * 128)
            i2 = m % 2
            GA, GB, R0 = PS[0], PS[1], PS[7]
            for kc in range(16):
                b.mm(GA[:, 0:2 * HC], xnT[:, kc, tl], W1[:, kc, 0:2 * HC], ["xnT", "W1_%d" % kc], ["ps0"], start=kc == 0,
                     stop=kc == 15)
            for kc in range(16):
                b.mm(GB[:, 0:HC], xnT[:, kc, tl], W1[:, kc, 2 * HC:3 * HC], ["xnT", "W1_%d" % kc], ["ps1"], start=kc == 0,
                     stop=kc == 15)
            b.act(t["sqk"][:, 0:2 * HC], GA[:, 0:2 * HC], AF.Square, ["ps0"], ["sqk"])
            b.reduce("dve", st[:, 4:4 + 2 * nh], t["sqk"][:, 0:2 * HC].rearrange("p (h d) -> p h d", d=64), ["sqk"],
                     ["ssqk"])
            b.rstd(st[:, 4:4 + 2 * nh], st[:, 4:4 + 2 * nh], 1.0 / 64, RMS_EPS, ["ssqk"], ["ssqk"])
            b.tt("dve", t["qkt"][:, 0:2 * HC].rearrange("p (h d) -> p h d", d=64),
                 GA[:, 0:2 * HC].rearrange("p (h d) -> p h d", d=64),
                 st[:, 4:4 + 2 * nh].unsqueeze(2).to_broadcast([128, 2 * nh, 64]), ALU.mult, ["ps0", "ssqk"], ["qkt"])
            b.tt("pool", t["qn"][:, 0:HC], t["qkt"][:, 0:HC], t["qkg"][:, 0:HC], ALU.mult, ["qkt", "qkg"], ["qn"])
            b.tt("dve", t["knf"][i2][:, 0:HC], t["qkt"][:, HC:2 * HC], t["qkg"][:, HC:2 * HC], ALU.mult,
                 ["qkt", "qkg"], ["knf%d" % i2])
            b.cp("pool", t["knb"][:, 0:HC], t["knf"][i2][:, 0:HC], ["knf%d" % i2], ["knb"])
            b.dma("sp", k_out, t["knf"][i2][:, 0:HC] if cfg is CFG_S else
                  t["knf"][i2][:, 0:HC].rearrange("p (h d) -> p h d", d=64), ["knf%d" % i2], [])
            pb = bfview(R0)
            for h in range(nh):
                b.tr(pb[0:64, h * 128:(h + 1) * 128], t["qn"][:, h * 64:(h + 1) * 64], ident, ["qn", "cb"], ["ps7"])
                b.tr(pb[0:64, (nh + h) * 128:(nh + h + 1) * 128], t["knb"][:, h * 64:(h + 1) * 64], ident,
                     ["knb", "cb"], ["ps7"])
            b.cp("act", t["qT"][:, 0:nh, tl], pb[0:64, 0:nh * 128].rearrange("p (c t) -> p c t", t=128),
                 ["ps7"], ["qT"])
            b.cp("act", kT_dst, pb[0:64, nh * 128:2 * nh * 128].rearrange("p (c t) -> p c t", t=128), ["ps7"], ["kT"])
            b.cp("act", t["vf"][i2][:, 0:HC], GB[:, 0:HC], ["ps1"], ["vf%d" % i2])
            if V_dst is not None:
                b.cp("pool", V_dst, t["vf"][i2][:, 0:HC], ["vf%d" % i2], ["Vres"])
            b.dma("sp", v_out, t["vf"][i2][:, 0:HC] if cfg is CFG_S else
                  t["vf"][i2][:, 0:HC].rearrange("p (h d) -> p h d", d=64), ["vf%d" % i2],
                  ["v_out"] if cfg is CFG_S else [])

        def attn_block(t, step, nsteps, kr, ncol, zmms, avmms, mask, zkeys, vkeys):
            i2 = step % 2
            Z = PS[2 + i2]
            zk = "ps%d" % (2 + i2)
            Ab = PS[4]
            Bb = PS[5]
            first, last = step == 0, step == nsteps - 1
            ef, Lb, attn, Cc = t["ef"][i2], t["Lb"][i2], t["attn"][i2], t["Cc"]
            for (l, r, c0, n) in zmms:
                b.mm(Z[0:kr, c0:c0 + n], l, r, zkeys, [zk])
            b.act(ef[:, 0:ncol], Z[:, 0:ncol], AF.Exp, [zk], ["ef%d" % i2])
            b.act(Lb[:, 0:ncol], ef[:, 0:ncol], AF.Ln, ["ef%d" % i2], ["Lb%d" % i2], bias=1.0)
            if mask is not None:
                b.tt("dve", Lb[:, 0:ncol], Lb[:, 0:ncol], mask, ALU.mult, ["Lb%d" % i2, "cb"], ["Lb%d" % i2])
            multi = len(zmms) > 1
            for zi, (l, r, c0, n) in enumerate(zmms):
                b.mm(Ab[0:kr, c0:c0 + n], l, r, zkeys, ["ps4"], start=zi == 0, stop=False, skip=multi)
            b.mm(Ab[0:kr, 0:ncol], trineg[0:kr, 0:kr], Lb[0:kr, 0:ncol], ["cb", "Lb%d" % i2], ["ps4"], start=False,
                 stop=True, skip=multi)
            if not last:
                b.mm(Bb[:, 0:ncol], ones[0:kr, :], Lb[0:kr, 0:ncol], ["cb", "Lb%d" % i2], ["ps5"])
            if first:
                b.act(attn[:, 0:ncol], Ab[:, 0:ncol], AF.Exp, ["ps4"], ["attn%d" % i2])
            else:
                b.tt("dve", ef[:, 0:ncol], Ab[:, 0:ncol], Cc[:, 0:ncol], ALU.subtract, ["ps4", "Cc"],
                     ["ef%d" % i2])
                b.act(attn[:, 0:ncol], ef[:, 0:ncol], AF.Exp, ["ef%d" % i2], ["attn%d" % i2])
            if mask is not None:
                b.tt("dve", attn[:, 0:ncol], attn[:, 0:ncol], mask, ALU.mult, ["attn%d" % i2, "cb"],
                     ["attn%d" % i2])
            for ai, (lv, c0, n, o_ap, st_) in enumerate(avmms):
                b.mm(o_ap, lv, attn[0:kr, c0:c0 + n], vkeys + ["attn%d" % i2], ["ps6"], start=first and st_, stop=last,
                     skip=multi)
            if not last:
                if first:
                    b.cp("dve", Cc[:, 0:ncol], Bb[:, 0:ncol], ["ps5"], ["Cc"])
                else:
                    b.tt("dve", Cc[:, 0:ncol], Cc[:, 0:ncol], Bb[:, 0:ncol], ALU.add, ["ps5", "Cc"], ["Cc"])

        def attn_finish(t, ncols, sbg_col, out_dram):
            O0 = PS[6]
            GA = PS[0]
            b.cp("act", t["of"][:, 0:ncols], O0[:, 0:ncols], ["ps6"], ["of"])
            b.act(t["osq"][:, 0:ncols], t["of"][:, 0:ncols], AF.Square, ["of"], ["osq"])
            b.mm(GA[:, 0:ncols], bdones, t["osq"][:, 0:ncols], ["cb", "osq"], ["ps0"])
            b.rstd(t["rso"][:, 0:ncols], GA[:, 0:ncols], 1.0 / 64, RMS_EPS, ["ps0"], ["rso"])
            b.stt("dve", t["onb"][:, 0:ncols], t["of"][:, 0:ncols], sbg_col, t["rso"][:, 0:ncols], ALU.mult, ALU.mult,
                  ["of", "rso", "pp"], ["onb"])
            b.dma("sp", out_dram, t["onb"][:, 0:ncols], ["onb"], [])

        def load_qkg(t, cfg, qkg_d):
            HC = cfg.HC
            b.dma("sp", t["qkg"][:, 0:2 * HC], qkg_d.to_broadcast([128, 2 * HC]), [], ["qkg"])
            b.ts("dve", t["qkg"][:, 0:HC], t["qkg"][:, 0:HC], 0.125, ALU.mult, ["qkg"], ["qkg"])

        FN = ("b0", "b1", "b2", "b3", "b4", "b5", "b6", "b7", "b8", "b9")

        def rw_alloc():
            t = {}
            t["xrj"] = [A.alloc([128, 520], F32) for _ in range(2)]
            t["hal"] = A.alloc([128, 8, 8], F32)
            t["hal2"] = A.alloc([128, 8, 8], F32)
            t["xsf"] = A.alloc([128, 8, 512], F32)
            t["lora_in"] = A.alloc([128, 512], BF16)
            t["sgb"] = A.alloc([128, 512], BF16)
            t["wa2"] = A.alloc([128, 256], BF16)
            t["g2t"] = A.alloc([128, 256], BF16)
            for n in FN:
                t[n] = A.alloc([128, 512], F32)
            t["AR"] = A.alloc([128, 8, 2, 64], BF16)
            for n in ("Btb", "Ktb", "Vtb", "sqb", "yob"):
                t[n] = A.alloc([128, 512], BF16)
            for n in ("Btm", "Ktm", "Vtm"):
                t[n] = A.alloc([128, 8, 64], BF16)
            t["MTb"] = A.alloc([128, 8, 128], BF16)
            t["MTk"] = A.alloc([128, 8, 128], BF16)
            for n in ("Uj", "Lj", "Xj"):
                t[n] = [A.alloc([128, 8, 64], BF16) for _ in range(2)]
            t["Wb"] = A.alloc([128, 64], BF16)
            t["Ub"] = A.alloc([128, 64], BF16)
            t["STf"] = [A.alloc([128, 64], F32) for _ in range(8)]
            t["STb"] = [A.alloc([128, 64], BF16) for _ in range(8)]
            return t

        def stage_rwkv(cfg, t, get_raw, state_of_chunk, mix_dst):
            NT, C, nch, nc2, L = cfg.NT, cfg.C, cfg.nch, cfg.nc2, cfg.L
            nseg, sl = cfg.nseg, cfg.seglen
            xsf = t["xsf"]
            GA, GB, R0 = PS[0], PS[1], PS[7]
            for j in range(cfg.nrch):
                raw, rk = get_raw(j)
                xj = t["xrj"][j % 2][:, 0:nseg * (sl + 1)].rearrange("p (s t) -> p s t", s=nseg)
                kj = "xrj%d" % (j % 2)
                b.cp("act", xj[:, :, 1:sl + 1], raw.rearrange("p (s t) -> p s t", s=nseg), [rk], [kj])
                b.cp("pool", xj[:, :, 0:1], t["hal"][:, j, 0:nseg].unsqueeze(2), ["hal"], [kj])
                b.tt("dve", t["b9"][:, 0:NT].rearrange("p (s t) -> p s t", s=nseg), xj[:, :, 0:sl], xj[:, :, 1:sl + 1],
                     ALU.subtract, [kj], ["b9"])
                b.stt("dve", xsf[:, j, 0:NT].rearrange("p (s t) -> p s t", s=nseg),
                      t["b9"][:, 0:NT].rearrange("p (s t) -> p s t", s=nseg), pp[:, L["mu"] + j:L["mu"] + j + 1],
                      xj[:, :, 1:sl + 1], ALU.mult, ALU.add, ["b9", kj, "pp"], ["xs%d" % j])
                b.cp("pool", t["hal2"][:, j, 0:nseg].unsqueeze(2), xj[:, :, sl:sl + 1], [kj], ["hal2"])
                if cfg is CFG_P:
                    b.cp("pool", t["hal"][:, j, 0:1], t["hal2"][:, j, 0:1], ["hal2"], ["hal"])
            jw, jg = 3 * nc2, 3 * nc2 + 1
            b.act(t["lora_in"][0:64, 0:NT], xsf[0:64, jw, 0:NT], AF.Tanh, ["xs%d" % jw], ["lora_in"])
            b.cp("pool", t["lora_in"][64:128, 0:NT], xsf[64:128, jw, 0:NT], ["xs%d" % jw], ["lora_in"])
            b.act(t["sgb"][:, 0:NT], xsf[:, jg, 0:NT], AF.Sigmoid, ["xs%d" % jg], ["sgb"])
            seg = cf[:, CF["seg%d" % C]:CF["seg%d" % C] + NT]
            maskA = cf[0:C, CF["maskA%d" % C]:CF["maskA%d" % C] + 2 * C]
            maskL = cf[0:C, CF["maskL%d" % C]:CF["maskL%d" % C] + C]
            idC = cf[0:C, CF["id%d" % C]:CF["id%d" % C] + C]
            AR, Btb, Ktb, Vtb, sqb = t["AR"], t["Btb"], t["Ktb"], t["Vtb"], t["sqb"]
            for c2 in range(nc2):
                jr, jk, jv = c2, nc2 + c2, 2 * nc2 + c2
                cs = slice(c2 * 128, (c2 + 1) * 128)
                col = lambda name: pp[:, L[name] + c2:L[name] + c2 + 1]
                f = {n: t[n][:, 0:NT] for n in FN}
                xs_r, xs_k, xs_v = xsf[:, jr, 0:NT], xsf[:, jk, 0:NT], xsf[:, jv, 0:NT]
                kr_, kk_, kv_ = "xs%d" % jr, "xs%d" % jk, "xs%d" % jv
                P1, P2, P3 = GA[:, 0:NT], GB[:, 0:NT], R0[:, 0:NT]
                sgu, a_, g_, lw, cl, clm, eP, eM, ePx, tmp = (f["b0"], f["b1"], f["b2"], f["b3"], f["b4"], f["b5"],
                                                              f["b6"], f["b7"], f["b8"], f["b9"])
                b.mm(P1, t["wa2"][0:64, cs], t["lora_in"][0:64, 0:NT], ["wa2", "lora_in"], ["ps0"])
                b.mm(P2, t["wa2"][64:128, cs], t["lora_in"][64:128, 0:NT], ["wa2", "lora_in"], ["ps1"])
                b.mm(P3, t["g2t"][:, cs], t["sgb"][:, 0:NT], ["g2t", "sgb"], ["ps7"])
                b.act(sgu, P1, AF.Sigmoid, ["ps0", "pp"], ["b0"], bias=col("w0"))
                b.act(a_, P2, AF.Sigmoid, ["ps1", "pp"], ["b1"], bias=col("a0"))
                b.cp("act", g_, P3, ["ps7"], ["b2"])
                b.ts("pool", lw, sgu, -0.6065306597126334, ALU.mult, ["b0"], ["b3"])
                b.scan(cl, seg, lw, ["cf", "b3"], ["b4"])
                b.tt("pool", clm, cl, lw, ALU.subtract, ["b4", "b3"], ["b5"])
                b.act(eP, cl, AF.Exp, ["b4"], ["b6"])
                b.act(eM, cl, AF.Exp, ["b4"], ["b7"], scale=-1.0)
                b.act(ePx, clm, AF.Exp, ["b5"], ["b8"])
                kkr = f["b3"]
                b.ts("dve", kkr, xs_k, col("kk"), ALU.mult, [kk_, "pp"], ["b3"])
                b.act(sqb[:, 0:NT], kkr, AF.Square, ["b3"], ["sqb"])
                b.mm(P1, bdones, sqb[:, 0:NT], ["cb", "sqb"], ["ps0"])
                b.ts("dve", tmp, P1, 1e-24, ALU.max, ["ps0"], ["b9"])
                b.rstd(tmp, tmp, 1.0, 0.0, ["b9"], ["b9"])
                kk = f["b4"]
                b.tt("dve", kk, kkr, tmp, ALU.mult, ["b3", "b9"], ["b4"])
                tt_ = f["b5"]
                b.ts("dve", tt_, a_, col("ka"), ALU.mult, ["b1", "pp", "omka"], ["b5"], s2=omka[:, c2:c2 + 1],
                     op1=ALU.add)
                kp = f["b5"]
                b.tt("dve", kp, xs_k, tt_, ALU.mult, [kk_, "b5"], ["b5"])
                ARv = AR[:, 0:nch, :, 0:C]
                b.stt("dve", ARv[:, :, 0, :], kk.rearrange("p (c t) -> p c t", t=C), -1.0,
                      ePx.rearrange("p (c t) -> p c t", t=C), ALU.mult, ALU.mult, ["b4", "b8"], ["AR"])
                b.tt("pool", ARv[:, :, 1, :], xs_r.rearrange("p (c t) -> p c t", t=C),
                     eP.rearrange("p (c t) -> p c t", t=C), ALU.mult, [kr_, "b6"], ["AR"])
                b.tt("dve", tmp, kk, a_, ALU.mult, ["b4", "b1"], ["b9"])
                b.tt("dve", Btb[:, 0:NT], tmp, eM, ALU.mult, ["b9", "b7"], ["Btb"])
                b.tt("pool", Ktb[:, 0:NT], kp, eM, ALU.mult, ["b5", "b7"], ["Ktb"])
                b.cp("pool", Vtb[:, 0:NT], xs_v, [kv_], ["Vtb"])
                b.tt("dve", tmp, xs_r, kp, ALU.mult, [kr_, "b5"], ["b9"])
                b.ts("dve", sqb[:, 0:NT], tmp, col("rk"), ALU.mult, ["b9", "pp"], ["sqb"])
                b.mm(P2, bdones, sqb[:, 0:NT], ["cb", "sqb"], ["ps1"])
                bonus = f["b8"]
                b.tt("dve", bonus, P2, xs_v, ALU.mult, ["ps1", kv_], ["b8"])
                pbR = bfview(R0)
                PR = slice(0, 128)
                npr = 128
                HO = [slice(hh * 64, hh * 64 + C) for hh in range(2)]
                HR = [slice(hh * 64, hh * 64 + 64) for hh in range(2)]
                for (src, dst, sk, dk, ev) in ((Btb, t["Btm"], "Btb", "Btm", "act"), (Ktb, t["Ktm"], "Ktb", "Ktm", "dve"),
                                               (Vtb, t["Vtm"], "Vtb", "Vtm", "act")):
                    for hh in range(2):
                        for ch in range(nch):
                            b.tr(pbR[HO[hh], ch * 64:(ch + 1) * 64], src[HR[hh], ch * C:(ch + 1) * C],
                                 ident[HR[hh], HR[hh]], [sk, "cb"], ["ps7"])
                    b.cp(ev, dst[PR, 0:nch, :], pbR[PR, 0:nch * 64].rearrange("p (c f) -> p c f", f=64), ["ps7"], [dk])
                MTb, MTk, Uj, Lj, Xj = t["MTb"], t["MTk"], t["Uj"], t["Lj"], t["Xj"]
                mA = cf[PR, CF["maskA%d" % C]:CF["maskA%d" % C] + 2 * C]
                mL = cf[PR, CF["maskL%d" % C]:CF["maskL%d" % C] + C]
                mI = cf[PR, CF["id%d" % C]:CF["id%d" % C] + C]
                for g0 in range(0, nch, 4):
                    for hh in range(2):
                        for ch in range(g0, g0 + 4):
                            q4 = ch - g0
                            arf = AR[HR[hh], ch, :, 0:C]
                            b.mm(GA[HO[hh], q4 * 2 * C:(q4 + 1) * 2 * C].rearrange("p (a t) -> p a t", a=2),
                                 Btb[HR[hh], ch * C:(ch + 1) * C], arf, ["Btb", "AR"], ["ps0"])
                            b.mm(GB[HO[hh], q4 * 2 * C:(q4 + 1) * 2 * C].rearrange("p (a t) -> p a t", a=2),
                                 Ktb[HR[hh], ch * C:(ch + 1) * C], arf, ["Ktb", "AR"], ["ps1"])
                    b.tt("dve", MTb[PR, g0:g0 + 4, 0:2 * C], GA[PR, 0:8 * C].rearrange("p (q t) -> p q t", q=4),
                         mA.unsqueeze(1).to_broadcast([npr, 4, 2 * C]), ALU.mult, ["ps0", "cf"], ["MTb"])
                    b.tt("dve", MTk[PR, g0:g0 + 4, 0:2 * C], GB[PR, 0:8 * C].rearrange("p (q t) -> p q t", q=4),
                         mA.unsqueeze(1).to_broadcast([npr, 4, 2 * C]), ALU.mult, ["ps1", "cf"], ["MTk"])
                for hh in range(2):
                    for ch in range(nch):
                        b.mm(R0[HO[hh], ch * C:(ch + 1) * C], AR[HR[hh], ch, 0, 0:C], Btb[HR[hh], ch * C:(ch + 1) * C],
                             ["AR", "Btb"], ["ps7"])
                b.tt("dve", Lj[0][PR, 0:nch, 0:C], R0[PR, 0:nch * C].rearrange("p (q t) -> p q t", t=C),
                     mL.unsqueeze(1).to_broadcast([npr, nch, C]), ALU.mult, ["ps7", "cf"], ["Lj0"])
                b.cp("pool", Uj[0][PR, 0:nch, 0:C], MTb[PR, 0:nch, 0:C], ["MTb"], ["Uj0"])
                b.tt("dve", Xj[0][PR, 0:nch, 0:C], MTb[PR, 0:nch, 0:C], mI.unsqueeze(1).to_broadcast([npr, nch, C]),
                     ALU.add, ["MTb", "cf"], ["Xj0"])
                for lv in range(1, cfg.nlev + 1):
                    pi, ci = (lv - 1) % 2, lv % 2
                    Up, Lp, Un, Ln_, Xp, Xn = Uj[pi], Lj[pi], Uj[ci], Lj[ci], Xj[pi], Xj[ci]
                    bl, bu, bx = ((0, 1, 7), (2, 3, 4))[lv % 2]
                    pl, pu, px = PS[bl], PS[bu], PS[bx]
                    plk, puk, pxk = "ps%d" % bl, "ps%d" % bu, "ps%d" % bx
                    for hh in range(2):
                        for ch in range(nch):
                            b.mm(pl[HO[hh], ch * C:(ch + 1) * C], Up[HO[hh], ch, 0:C], Lp[HO[hh], ch, 0:C],
                                 ["Uj%d" % pi, "Lj%d" % pi], [plk])
                    b.cp("act", Ln_[PR, 0:nch, 0:C], pl[PR, 0:nch * C].rearrange("p (q t) -> p q t", t=C), [plk],
                         ["Lj%d" % ci])
                    if lv < cfg.nlev:
                        for hh in range(2):
                            for ch in range(nch):
                                b.mm(pu[HO[hh], ch * C:(ch + 1) * C], Lp[HO[hh], ch, 0:C], Up[HO[hh], ch, 0:C],
                                     ["Uj%d" % pi, "Lj%d" % pi], [puk])
                        b.cp("dve", Un[PR, 0:nch, 0:C], pu[PR, 0:nch * C].rearrange("p (q t) -> p q t", t=C), [puk],
                             ["Uj%d" % ci])
                    for hh in range(2):
                        for ch in range(nch):
                            b.mm(px[HO[hh], ch * C:(ch + 1) * C], Ln_[HO[hh], ch, 0:C], Xp[HO[hh], ch, 0:C],
                                 ["Lj%d" % ci, "Xj%d" % pi], [pxk])
                    b.tt("dve", Xn[PR, 0:nch, 0:C], px[PR, 0:nch * C].rearrange("p (q t) -> p q t", t=C),
                         Xp[PR, 0:nch, 0:C], ALU.add, [pxk, "Xj%d" % pi], ["Xj%d" % ci])
                Xf = Xj[cfg.nlev % 2]
                xk = "Xj%d" % (cfg.nlev % 2)
                PY = PS[6]
                Wb, Ub, Btm, Ktm, Vtm = t["Wb"], t["Ub"], t["Btm"], t["Ktm"], t["Vtm"]
                for ch in range(nch):
                    si = state_of_chunk(c2, ch)
                    sf, sb_, skf, skb = t["STf"][si], t["STb"][si], "STf%d" % si, "STb%d" % si
                    for hh in range(2):
                        b.mm(GA[HO[hh], 0:64], AR[HR[hh], ch, 0, 0:C], sb_[HR[hh], :], ["AR", skb], ["ps0"], start=True,
                             stop=False)
                        b.mm(GA[HO[hh], 0:64], MTk[HO[hh], ch, 0:C], Vtm[HO[hh], ch, :], ["MTk", "Vtm"], ["ps0"],
                             start=False, stop=True)
                    b.cp("act", Wb[PR, :], GA[PR, 0:64], ["ps0"], ["Wb"])
                    for hh in range(2):
                        b.mm(GB[HO[hh], 0:64], Xf[HO[hh], ch, 0:C], Wb[HO[hh], :], [xk, "Wb"], ["ps1"])
                    b.cp("dve", Ub[PR, :], GB[PR, 0:64], ["ps1"], ["Ub"])
                    for hh in range(2):
                        yo = PY[HR[hh], ch * C:(ch + 1) * C]
                        b.mm(yo, sb_[HR[hh], :], AR[HR[hh], ch, 1, 0:C], [skb, "AR"], ["ps6"], start=True, stop=False)
                        b.mm(yo, Ub[HO[hh], :], MTb[HO[hh], ch, C:2 * C], ["Ub", "MTb"], ["ps6"], start=False, stop=False)
                        b.mm(yo, Vtm[HO[hh], ch, :], MTk[HO[hh], ch, C:2 * C], ["Vtm", "MTk"], ["ps6"], start=False,
                             stop=True)
                        so = PS[5][HR[hh], 0:64]
                        b.mm(so, Btm[HO[hh], ch, :], Ub[HO[hh], :], ["Btm", "Ub"], ["ps5"], start=True, stop=False)
                        b.mm(so, Ktm[HO[hh], ch, :], Vtm[HO[hh], ch, :], ["Ktm", "Vtm"], ["ps5"], start=False, stop=True)
                    b.tt("dve", sf, PS[5][:, 0:64], sf, ALU.add, ["ps5", skf], [skf])
                    b.ts("dve", sf, sf, t["b6"][:, ch * C + C - 1:ch * C + C], ALU.mult, [skf, "b6"], [skf])
                    b.cp("pool", sb_, sf, [skf], [skb])
                y_, yc, rs = f["b0"], f["b1"], f["b7"]
                b.cp("act", y_, PY[:, 0:NT], ["ps6"], ["b0"])
                b.cp("pool", sqb[:, 0:NT], y_, ["b0"], ["sqb"])
                b.mm(P1, bdmean, sqb[:, 0:NT], ["cb", "sqb"], ["ps0"])
                b.tt("dve", yc, y_, P1, ALU.subtract, ["b0", "ps0"], ["b1"])
                b.act(sqb[:, 0:NT], yc, AF.Square, ["b1"], ["sqb"])
                b.mm(P2, bdmean, sqb[:, 0:NT], ["cb", "sqb"], ["ps1"])
                b.rstd(rs, P2, 1.0, LNX_EPS, ["ps1"], ["b7"])
                b.tt("dve", yc, yc, rs, ALU.mult, ["b1", "b7"], ["b1"])
                b.ts("dve", yc, yc, col("lw"), ALU.mult, ["b1", "pp"], ["b1"], s2=col("lb"), op1=ALU.add)
                b.tt("pool", yc, yc, bonus, ALU.add, ["b1", "b8"], ["b1"])
                b.tt("dve", t["yob"][:, 0:NT], yc, g_, ALU.mult, ["b1", "b2"], ["yob"])
                b.dma("sp", mix_dst(c2), t["yob"][:, 0:NT], ["yob"], [])

        cfg = CFG_S
        load_params(cfg, "s", pp_s)
        W1 = A.alloc([128, 16, 1024], BF16)
        for kc in range(16):
            b.dma("pool", W1[:, kc, :], w1s[kc * 128:(kc + 1) * 128, :], [], ["W1_%d" % kc])
        t = sb_alloc()
        load_qkg(t, cfg, qkg_s)
        KcT = A.alloc([64, 2, 4, PAST], BF16)
        Vc = A.alloc([128, 4, 16, 128], BF16)
        kTs = A.alloc([64, 2, 256], BF16)
        vnew = A.alloc([32, NS, 128], BF16)
        vsb = A.alloc([128, 2, 128], BF16)
        stage_norm_T(2, xs)
        for m in range(2):
            stage_sb_tok(cfg, t, W1, m, kTs[:, :, m * 128:(m + 1) * 128], vsb[:, m, :], k_s[m * 128:(m + 1) * 128, :],
                         v_s[m * 128:(m + 1) * 128, :])
        for j in range(cfg.nrch):
            bank = PS[j % 2]
            bk = "ps%d" % (j % 2)
            for kc in range(16):
                b.mm(bank[:, 0:256], W1[:, kc, 384 + j * 128:384 + (j + 1) * 128], xnT[:, kc, 0:256], ["W1_%d" % kc, "xnT"], [bk],
                     start=kc == 0, stop=kc == 15)
            b.cp("act", xrs_raw[:, j, :], bank[:, 0:256], [bk], ["xrs_raw"])
        if stop == 1:
            S.emit(ctx)
            return nc
        for bb in range(NS):
            bank = PS[bb // 4]
            b.mm(bank[0:32, (bb % 4) * 128:(bb % 4 + 1) * 128], ident[:, (bb % 4) * 32:(bb % 4 + 1) * 32], vsb[:, bb // 4, :],
                 ["cb", "Vres"], ["ps%d" % (bb // 4)])
        for g in range(2):
            b.cp(("act", "dve")[g], vnew[0:32, 4 * g:4 * g + 4, :], PS[g][0:32, :].rearrange("p (b f) -> p b f", b=4),
                 ["ps%d" % g], ["vnew"])
        masks = cb[:, CB["masks"]:CB["masks"] + 256]
        if stop == 1.2:
            S.emit(ctx)
            return nc
        for grp in range(2):
            for b4 in range(4):
                bb = grp * 4 + b4
                b.dma("pool", KcT[:, :, b4, :], kcT[:, :, bb, :], [], ["KcT%d" % b4])
                b.dma("pool", Vc[:, b4, :, :], vc[bb].rearrange("(kb p) f -> p kb f", p=128), [], ["Vc%d" % b4])
            if stop == 1.4:
                S.emit(ctx)
                return nc
            nsteps = 17
            for step in range(nsteps):
                if (stop == 1.6 and step == 1) or (stop == 1.8 and step == 2):
                    S.emit(ctx)
                    return nc
                kb = 16 - step
                pairs = [(hh, b4) for hh in range(2) for b4 in range(4)]
                if kb == 16:
                    zm = [(kTs[:, hh, (grp * 4 + b4) * 32:(grp * 4 + b4 + 1) * 32],
                           t["qT"][:, hh, (grp * 4 + b4) * 32:(grp * 4 + b4 + 1) * 32],
                           hh * 128 + b4 * 32, 32) for hh, b4 in pairs]
                    av = [(vnew[0:32, grp * 4 + b4, hh * 64:hh * 64 + 64], hh * 128 + b4 * 32, 32,
                           PS[6][hh * 64:hh * 64 + 64, (grp * 4 + b4) * 32:(grp * 4 + b4 + 1) * 32], b4 == 0) for hh, b4 in pairs]
                    attn_block(t, step, nsteps, 32, 256, zm, av, masks, ["kT", "qT"], ["vnew"])
                else:
                    zm = [(KcT[:, hh, b4, kb * 128:(kb + 1) * 128],
                           t["qT"][:, hh, (grp * 4 + b4) * 32:(grp * 4 + b4 + 1) * 32],
                           hh * 128 + b4 * 32, 32) for hh, b4 in pairs]
                    av = [(Vc[:, b4, kb, hh * 64:hh * 64 + 64], hh * 128 + b4 * 32, 32,
                           PS[6][hh * 64:hh * 64 + 64, (grp * 4 + b4) * 32:(grp * 4 + b4 + 1) * 32], b4 == 0) for hh, b4 in pairs]
                    attn_block(t, step, nsteps, 128, 256, zm, av, None, ["KcT%d" % q for q in range(4)] + ["qT"], ["Vc%d" % q for q in range(4)])
        attn_finish(t, 256, pp[:, cfg.L["sbg"]:cfg.L["sbg"] + 1], mix_s[0:128, :])
        if stop == 2:
            S.emit(ctx)
            return nc
        S.barrier()

        A.reset(base_mark)
        t = rw_alloc()
        b.dma("pool", t["wa2"][:, 0:128], wa2_s, [], ["wa2"])
        b.dma("pool", t["g2t"][:, 0:128], g2_s, [], ["g2t"])
        b.dma("sp", t["hal"][:, 0:5, :], sh0, [], ["hal"])
        for bb in range(NS):
            b.dma("sp", t["STf"][bb], st0[:, bb, :], [], ["STf%d" % bb])
            b.cp("pool", t["STb"][bb], t["STf"][bb], ["STf%d" % bb], ["STb%d" % bb])
        stage_rwkv(cfg, t, lambda j: (xrs_raw[:, j, :], "xrs_raw"), lambda c2, ch: ch, lambda c2: mix_s[128:256, :])
        b.dma("sp", sh_s, t["hal2"][:, 0:5, :], ["hal2"], [])
        for bb in range(NS):
            b.dma("sp", s_s[:, bb, :], t["STf"][bb], ["STf%d" % bb], [])
        if stop == 3:
            S.emit(ctx)
            return nc
        S.barrier()

        cfg = CFG_P
        A.reset(base_mark)
        load_params(cfg, "p", pp_p)
        W1 = A.alloc([128, 16, 768], BF16)
        for kc in range(16):
            b.dma("pool", W1[:, kc, :], w1p_sb[kc * 128:(kc + 1) * 128, :], [], ["W1_%d" % kc])
        t = sb_alloc()
        load_qkg(t, cfg, qkg_p)
        kTr = A.alloc([64, 4, T_P], BF16)
        Vr = A.alloc([128, 32, 256], BF16)
        for s in range(8):
            stage_norm_T(4, xp[s * 512:(s + 1) * 512, :])
            b.dma("sp", xnT_d[s], xnT.rearrange("p a b -> p (a b)"), ["xnT"], ["xnT_d%d" % s])
            for m in range(4):
                t0 = s * 512 + m * 128
                stage_sb_tok(cfg, t, W1, m, kTr[:, :, t0:t0 + 128], Vr[:, t0 // 128, :],
                             k_p[:, t0:t0 + 128, :].rearrange("h t d -> t h d"),
                             v_p[:, t0:t0 + 128, :].rearrange("h t d -> t h d"))
            for c2 in range(2):
                for hh in range(2):
                    h = 2 * c2 + hh
                    hr = slice(hh * 64, hh * 64 + 64)
                    nsteps = 4 * s + 4
                    for step in range(nsteps):
                        kb = 4 * s + 3 - step
                        di = kb - 4 * s
                        zm = [(kTr[:, h, kb * 128:(kb + 1) * 128], t["qT"][:, h, 0:512], 0, 512)]
                        av = [(Vr[:, kb, h * 64:(h + 1) * 64], 0, 512, PS[6][hr, 0:512], True)]
                        mk = cb[:, CB["maskp"] + 512 * di:CB["maskp"] + 512 * (di + 1)] if di >= 0 else None
                        attn_block(t, step, nsteps, 128, 512, zm, av, mk, ["kT", "qT"], ["Vres"])
                attn_finish(t, 512, pp[:, cfg.L["sbg"] + c2:cfg.L["sbg"] + c2 + 1],
                            mix_p[c2 * 128:(c2 + 1) * 128, s * 512:(s + 1) * 512])
        if stop == 4:
            S.emit(ctx)
            return nc
        S.barrier()

        A.reset(base_mark)
        W1 = A.alloc([128, 16, 1024], BF16)
        for kc in range(16):
            b.dma("pool", W1[:, kc, :], w1p_rw[kc * 128:(kc + 1) * 128, :], [], ["W1_%d" % kc])
        t = rw_alloc()
        b.dma("pool", t["wa2"], wa2_p, [], ["wa2"])
        b.dma("pool", t["g2t"], g2_p, [], ["g2t"])
        b.memset("pool", t["hal"], 0.0, [], ["hal"])
        for c2 in range(2):
            b.memset("pool", t["STf"][c2], 0.0, [], ["STf%d" % c2])
            b.memset("pool", t["STb"][c2], 0.0, [], ["STb%d" % c2])
        for s in range(8):
            b.dma("sp", xnT.rearrange("p a b -> p (a b)"), xnT_d[s], [], ["xnT"])

            def get_raw(j):
                bank = PS[j % 2]
                bk = "ps%d" % (j % 2)
                for kc in range(16):
                    b.mm(bank[:, 0:512], W1[:, kc, j * 128:(j + 1) * 128], xnT[:, kc, 0:512], ["W1_%d" % kc, "xnT"], [bk],
                         start=kc == 0, stop=kc == 15)
                return bank[:, 0:512], bk
            stage_rwkv(cfg, t, get_raw, lambda c2, ch: c2,
                       lambda c2: mix_p[256 + c2 * 128:256 + (c2 + 1) * 128, s * 512:(s + 1) * 512])
        b.dma("sp", sh_p, t["hal2"][:, :, 0], ["hal2"], [], slow=True)
        for c2 in range(2):
            b.dma("sp", s_p[c2], t["STf"][c2], ["STf%d" % c2], [])
        S.emit(ctx)
    return nc


NTK = 1058
NTO = 1056


def build_phase2():
    nc = bass.Bass("TRN2", target_bir_lowering=False)
    dt = lambda name, shape, ty, kind: nc.dram_tensor(name, shape, ty, kind=kind).ap()
    IN, OUT = "ExternalInput", "ExternalOutput"
    x2in = dt("x2in", [NTK, D], F32, IN)
    cat = dt("cat", [D, NTK], BF16, IN)
    wout = dt("wout", [D, D], F32, IN)
    wup = dt("wup", [D, DFF], F32, IN)
    wgate = dt("wgate", [D, DFF], F32, IN)
    wdown = dt("wdown", [DFF, D], F32, IN)
    g2n = dt("g2n", [1, D], F32, IN)
    ppf = dt("ppf", [128, 4 * NFC], F32, IN)
    conv0 = dt("conv0", [128, NFC, 2], F32, IN)
    hscale = dt("hscale", [128, 1], F32, IN)
    identd = dt("identd", [128, 128], F32, IN)
    y = dt("y", [NTO, D], F32, OUT)
    convp = dt("convp", [128, NFC, 2], F32, OUT)
    convs = dt("convs", [128, NFC, 2], F32, OUT)
    x2d = nc.dram_tensor("x2d", [NTK, D], F32).ap()

    with ExitStack() as ctx:
        S = Sched(nc)
        b = B(nc, S)
        arena_t = ctx.enter_context(nc.sbuf_tensor("arena", [128, ARENA_BYTES // 2], BF16))
        A = Arena(arena_t)
        PS = [ctx.enter_context(nc.psum_tensor("ps%d" % i, [128, 512], F32)) for i in range(8)]
        ident = A.alloc([128, 128], BF16)
        pf = A.alloc([128, 4 * NFC], F32)
        c0t = A.alloc([128, NFC, 2], F32)
        hs = A.alloc([128, 1], F32)
        cstp = A.alloc([128, NFC, 2], F32)
        csts = A.alloc([128, NFC, 2], F32)
        st = A.alloc([128, 8], F32)
        xn2T = A.alloc([128, 16, NTK], BF16)
        b.dma("pool", ident, identd, [], ["ident"])
        b.dma("sp", pf, ppf, [], ["pf"])
        b.dma("sp", c0t, conv0, [], ["c0t"])
        b.dma("sp", hs, hscale, [], ["hs"])
        m_persist = A.mark()

        g2b = A.alloc([128, D], F32)
        catT = A.alloc([128, 16, NTK], BF16)
        Wo = A.alloc([128, 16, D], BF16)
        xt = [A.alloc([128, D], F32) for _ in range(2)]
        x2t = [A.alloc([128, D], F32) for _ in range(2)]
        xn = A.alloc([128, D], BF16)
        b.dma("sp", g2b, g2n.to_broadcast([128, D]), [], ["g2b"])
        for kc in range(16):
            b.dma("sp", catT[:, kc, :], cat[kc * 128:(kc + 1) * 128, :], [], ["catT%d" % kc])
            b.dma("pool", Wo[:, kc, :], wout[kc * 128:(kc + 1) * 128, :], [], ["Wo%d" % kc])
        subt = [(m * 128, 128) for m in range(8)] + [(NTK - 128, 128)]
        for m, (r0, nr) in enumerate(subt):
            i2 = m % 2
            kx, k2 = "xt%d" % i2, "x2t%d" % i2
            b.dma("sp", xt[i2][0:nr], x2in[r0:r0 + nr, :], [], [kx])
            for nb in range(4):
                bank = PS[i2 * 4 + nb]
                bk = "ps%d" % (i2 * 4 + nb)
                for kc in range(16):
                    b.mm(bank[0:nr, :], catT[:, kc, r0:r0 + nr], Wo[:, kc, nb * 512:(nb + 1) * 512],
                         ["catT%d" % kc, "Wo%d" % kc], [bk], start=kc == 0, stop=kc == 15)
                b.tt("dve", x2t[i2][0:nr, nb * 512:(nb + 1) * 512], bank[0:nr, :], xt[i2][0:nr, nb * 512:(nb + 1) * 512],
                     ALU.add, [bk, kx], [k2])
            b.dma("sp", x2d[r0:r0 + nr, :], x2t[i2][0:nr], [k2], ["x2d"])
            b.act(xn[0:nr], x2t[i2][0:nr], AF.Square, [k2], ["xn", "ss"], accum=st[0:nr, 0:1])
            b.rstd(st[0:nr, 1:2], st[0:nr, 0:1], 1.0 / D, RMS_EPS, ["ss"], ["rs"])
            b.stt("dve", xn[0:nr], x2t[i2][0:nr], st[0:nr, 1:2], g2b[0:nr], ALU.mult, ALU.mult, [k2, "rs", "g2b"],
                  ["xn"])
            for g in range(4):
                bank = PS[i2 * 4 + g]
                bk = "ps%d" % (i2 * 4 + g)
                pb = bank[:].bitcast(BF16)
                for c in range(4):
                    b.tr(pb[:, c * 128:c * 128 + nr], xn[0:nr, (4 * g + c) * 128:(4 * g + c + 1) * 128],
                         ident[0:nr, 0:nr], ["xn", "ident"], [bk])
                b.cp(("act", "dve")[g % 2], xn2T[:, 4 * g:4 * g + 4, r0:r0 + nr],
                     pb[:, 0:512].rearrange("p (c t) -> p c t", c=4)[:, :, 0:nr], [bk], ["xn2T"])
        S.barrier()

        A.reset(m_persist)
        hT = A.alloc([128, NFC, NTO], BF16)
        m_hT = A.mark()
        Wu = [A.alloc([128, 16, 256], BF16) for _ in range(2)]
        Wg = [A.alloc([128, 16, 256], BF16) for _ in range(2)]
        gtp = [A.alloc([128, NTK + 2], F32) for _ in range(2)]
        ub = [A.alloc([128, NTK], F32) for _ in range(2)]
        acc = [A.alloc([128, NTK], F32) for _ in range(2)]
        groups = [(0, 353), (353, 353), (706, 352)]
        for blk in range(NFC // 2):
            w2i = blk % 2
            ku, kg = "Wu%d" % w2i, "Wg%d" % w2i
            b.dma("pool", Wu[w2i], wup[:, blk * 256:(blk + 1) * 256].rearrange("(kc p) n -> p kc n", p=128), [], [ku])
            b.dma("pool", Wg[w2i], wgate[:, blk * 256:(blk + 1) * 256].rearrange("(kc p) n -> p kc n", p=128), [], [kg])
            for fi in range(2):
                fc = 2 * blk + fi
                f2 = fc % 2
                kgt, kub, kac = "gtp%d" % f2, "ub%d" % f2, "acc%d" % f2
                G, U, AC = gtp[f2], ub[f2], acc[f2]
                for tg, (c0, n) in enumerate(groups):
                    pi = (fc * 3 + tg) % 4
                    UB, GBk = PS[2 * pi], PS[2 * pi + 1]
                    uk, gk = "ps%d" % (2 * pi), "ps%d" % (2 * pi + 1)
                    for kc in range(16):
                        b.mm(UB[:, 0:n], Wu[w2i][:, kc, fi * 128:(fi + 1) * 128], xn2T[:, kc, c0:c0 + n], [ku, "xn2T"],
                             [uk], start=kc == 0, stop=kc == 15)
                    for kc in range(16):
                        b.mm(GBk[:, 0:n], Wg[w2i][:, kc, fi * 128:(fi + 1) * 128], xn2T[:, kc, c0:c0 + n], [kg, "xn2T"],
                             [gk], start=kc == 0, stop=kc == 15)
                    b.cp("dve", U[:, c0:c0 + n], UB[:, 0:n], [uk], [kub])
                    if tg < 2:
                        b.cp("act", G[:, c0:c0 + n], GBk[:, 0:n], [gk], [kgt])
                    else:
                        b.cp("act", G[:, 706:1026], GBk[:, 0:320], [gk], [kgt])
                        b.cp("act", G[:, 1028:1060], GBk[:, 320:352], [gk], [kgt])
                b.ts("pool", G[:, 0:2], G[:, 0:2], hs[:, 0:1], ALU.mult, [kgt, "hs"], [kgt])
                b.cp("pool", G[:, 1026:1028], c0t[:, fc, :], [kgt, "c0t"], [kgt])
                b.cp("pool", cstp[:, fc, :], G[:, 1024:1026], [kgt], ["cstp"])
                b.cp("pool", csts[:, fc, :], G[:, 1058:1060], [kgt], ["csts"])
                wcol = lambda i: pf[:, i * NFC + fc:i * NFC + fc + 1]
                b.ts("dve", AC[:, 0:NTK], G[:, 2:NTK + 2], wcol(2), ALU.mult, [kgt, "pf"], [kac], s2=wcol(3), op1=ALU.add)
                b.stt("dve", AC[:, 0:NTK], G[:, 1:NTK + 1], wcol(1), AC[:, 0:NTK], ALU.mult, ALU.add, [kgt, "pf", kac],
                      [kac])
                b.stt("dve", AC[:, 0:NTK], G[:, 0:NTK], wcol(0), AC[:, 0:NTK], ALU.mult, ALU.add, [kgt, "pf", kac],
                      [kac])
                b.act(AC[:, 0:NTK], AC[:, 0:NTK], AF.Silu, [kac], [kac])
                b.tt("dve", hT[:, fc, 0:1024], AC[:, 0:1024], U[:, 2:1026], ALU.mult, [kac, kub], ["hT%d" % fc])
                b.tt("pool", hT[:, fc, 1024:1056], AC[:, 1026:1058], U[:, 1026:1058], ALU.mult, [kac, kub],
                     ["hT%d" % fc])
        b.dma("sp", convp, cstp, ["cstp"], [])
        b.dma("sp", convs, csts, ["csts"], [])
        S.barrier()

        A.reset(m_hT)
        Wd = [A.alloc([128, NFC, 256], BF16) for _ in range(2)]
        x2s = [A.alloc([128, 256], F32) for _ in range(2)]
        yt = [A.alloc([128, 256], F32) for _ in range(2)]
        subo = [(m * 128, 128) for m in range(8)] + [(NTO - 128, 128)]
        cnt = 0
        for nb in range(8):
            w2i = nb % 2
            kd = "Wd%d" % w2i
            b.dma("pool", Wd[w2i], wdown[:, nb * 256:(nb + 1) * 256].rearrange("(fc p) n -> p fc n", p=128), [], [kd])
            for m, (r0, nr) in enumerate(subo):
                i2 = cnt % 2
                bank = PS[cnt % 8]
                bk = "ps%d" % (cnt % 8)
                cnt += 1
                b.dma("sp", x2s[i2][0:nr], x2d[2 + r0:2 + r0 + nr, nb * 256:(nb + 1) * 256], [], ["x2s%d" % i2])
                for fc in range(NFC):
                    b.mm(bank[0:nr, 0:256], hT[:, fc, r0:r0 + nr], Wd[w2i][:, fc, :], [kd], [bk], start=fc == 0,
                         stop=fc == NFC - 1)
                b.tt("dve", yt[i2][0:nr], bank[0:nr, 0:256], x2s[i2][0:nr], ALU.add, [bk, "x2s%d" % i2], ["yt%d" % i2])
                b.dma("sp", y[r0:r0 + nr, nb * 256:(nb + 1) * 256], yt[i2][0:nr], ["yt%d" % i2], [])
        S.emit(ctx)
    return nc


_CACHE = {}


def _progs():
    if "p1" not in _CACHE:
        _CACHE["p1"] = build_phase1()
        _CACHE["p2"] = build_phase2()
    return _CACHE["p1"], _CACHE["p2"]


def _pp(cfg, base, mu_cols, inp):
    L = cfg.L
    nc2 = cfg.nc2
    pp = np.zeros((128, L["n"]), np.float32)
    mu = inp["mu_shift"][0]
    for j, c0 in enumerate(mu_cols):
        pp[:, L["mu"] + j] = mu[c0:c0 + 128]
    vecs = dict(w0=inp["w0"][0], a0=inp["a0"][0], kk=inp["k_k"][0], ka=inp["k_a"][0], rk=inp["r_k"][0].reshape(-1),
                lw=inp["lnx_w"][0], lb=inp["lnx_b"][0], sbg=inp["sb_out_g"][0].reshape(-1))
    for k, v in vecs.items():
        for c2 in range(nc2):
            pp[:, L[k] + c2] = v[base + c2 * 128:base + (c2 + 1) * 128]
    return pp


def _phase1(inp):
    p1, p2 = _progs()
    cb, cf = make_consts()
    w_in = inp["w_in"][0]
    RW = 3072
    in1 = []
    for c in range(8):
        bq, j = divmod(c, 4)
        pb, sbase = 256 * j, 128 * c
        d = {}
        d["xp"] = np.ascontiguousarray(inp["x_prompt"][bq])
        d["xs"] = np.ascontiguousarray(inp["x_sample"].reshape(NS * T_S, D))
        d["w1p_sb"] = np.ascontiguousarray(np.concatenate([w_in[:, o + pb:o + pb + 256] for o in (0, 1024, 2048)], 1))
        rw_cols_p = [RW + pb, RW + pb + 128, RW + 1024 + pb, RW + 1024 + pb + 128, RW + 2048 + pb, RW + 2048 + pb + 128,
                     RW + 3072, RW + 3200]
        d["w1p_rw"] = np.ascontiguousarray(np.concatenate([w_in[:, o:o + 128] for o in rw_cols_p], 1))
        rw_cols_s = [RW + sbase, RW + 1024 + sbase, RW + 2048 + sbase, RW + 3072, RW + 3200]
        d["w1s"] = np.ascontiguousarray(np.concatenate([w_in[:, o + sbase:o + sbase + 128] for o in (0, 1024, 2048)] +
                                                       [w_in[:, o:o + 128] for o in rw_cols_s], 1))
        kc = inp["cache_sb_k"][0][:, 2 * c:2 * c + 2]
        d["kcT"] = np.ascontiguousarray(kc.transpose(3, 1, 0, 2))
        vcc = inp["cache_sb_v"][0][:, 2 * c:2 * c + 2]
        d["vc"] = np.ascontiguousarray(vcc.transpose(0, 2, 1, 3).reshape(NS, PAST, 128))
        s0 = inp["state_rwkv"][0][:, 2 * c:2 * c + 2]
        d["st0"] = np.ascontiguousarray(s0.transpose(1, 3, 0, 2).reshape(128, NS, 64))
        sh = inp["state_rwkv_shift"][0][:, 0, :]
        d["sh0"] = np.ascontiguousarray(np.stack([sh[:, o - RW:o - RW + 128] for o in rw_cols_s], 1).transpose(2, 1, 0))
        d["g1"] = np.ascontiguousarray(inp["norm1_g"][0][None])
        qg, kg = inp["q_norm_g"][0], inp["k_norm_g"][0]
        d["qkg_p"] = np.concatenate([np.tile(qg, 4), np.tile(kg, 4)])[None].astype(np.float32)
        d["qkg_s"] = np.concatenate([np.tile(qg, 2), np.tile(kg, 2)])[None].astype(np.float32)
        d["pp_p"] = _pp(CFG_P, pb, [o - RW for o in rw_cols_p], inp)
        d["pp_s"] = _pp(CFG_S, sbase, [o - RW for o in rw_cols_s], inp)
        d["wa2_p"] = np.ascontiguousarray(np.concatenate([inp["w2"][0][:, pb:pb + 256], inp["a2"][0][:, pb:pb + 256]], 0))
        d["wa2_s"] = np.ascontiguousarray(np.concatenate([inp["w2"][0][:, sbase:sbase + 128],
                                                          inp["a2"][0][:, sbase:sbase + 128]], 0))
        d["g2_p"] = np.ascontiguousarray(inp["g2"][0][:, pb:pb + 256])
        d["g2_s"] = np.ascontiguousarray(inp["g2"][0][:, sbase:sbase + 128])
        d["cbd"] = cb
        d["cfd"] = cf
        in1.append(d)
    r1 = run_bass_kernel_spmd(p1, in1, core_ids=list(range(8))).results
    _CACHE["r1"] = r1

    f32 = np.float32
    k_prompt = np.zeros((1, 2, 16, T_P, 64), f32)
    v_prompt = np.zeros((1, 2, 16, T_P, 64), f32)
    rwkv_prompt = np.zeros((1, 2, 16, 64, 64), f32)
    shift_prompt = np.zeros((1, 2, 1, 3328), f32)
    k_sample = np.zeros((1, NS, 16, T_S, 64), f32)
    v_sample = np.zeros((1, NS, 16, T_S, 64), f32)
    rwkv_sample = np.zeros((1, NS, 16, 64, 64), f32)
    shift_sample = np.zeros((1, NS, 1, 3328), f32)
    cat_p = [np.zeros((D, T_P), ml_dtypes.bfloat16) for _ in range(2)]
    cat_s = np.zeros((D, NS * T_S), ml_dtypes.bfloat16)
    for c in range(8):
        bq, j = divmod(c, 4)
        r = r1[c]
        k_prompt[0, bq, 4 * j:4 * j + 4] = r["k_p"]
        v_prompt[0, bq, 4 * j:4 * j + 4] = r["v_p"]
        rwkv_prompt[0, bq, 4 * j:4 * j + 4] = r["s_p"].reshape(2, 2, 64, 64).transpose(0, 1, 3, 2).reshape(4, 64, 64)
        shp = r["sh_p"]
        pb = 256 * j
        for jj, o in enumerate([pb, pb + 128, 1024 + pb, 1024 + pb + 128, 2048 + pb, 2048 + pb + 128, 3072, 3200]):
            shift_prompt[0, bq, 0, o:o + 128] = shp[:, jj]
        k_sample[0, :, 2 * c:2 * c + 2] = r["k_s"].reshape(NS, T_S, 2, 64).transpose(0, 2, 1, 3)
        v_sample[0, :, 2 * c:2 * c + 2] = r["v_s"].reshape(NS, T_S, 2, 64).transpose(0, 2, 1, 3)
        rwkv_sample[0, :, 2 * c:2 * c + 2] = r["s_s"].reshape(2, 64, NS, 64).transpose(2, 0, 3, 1)
        shs = r["sh_s"]
        sbase = 128 * c
        for jj, o in enumerate([sbase, 1024 + sbase, 2048 + sbase, 3072, 3200]):
            shift_sample[0, :, 0, o:o + 128] = shs[:, jj, :].T
        cat_p[bq][256 * j:256 * j + 256] = r["mix_p"][0:256]
        cat_p[bq][1024 + 256 * j:1024 + 256 * j + 256] = r["mix_p"][256:512]
        cat_s[128 * c:128 * c + 128] = r["mix_s"][0:128]
        cat_s[1024 + 128 * c:1024 + 128 * c + 128] = r["mix_s"][128:256]

    outs1 = (k_prompt, v_prompt, rwkv_prompt, shift_prompt, k_sample, v_sample, rwkv_sample, shift_sample)
    return outs1, cat_p, cat_s


def _phase2(inp, cat_p, cat_s):
    p1, p2 = _progs()
    f32 = np.float32
    cw, cbias = inp["ffn_conv_w"][0], inp["ffn_conv_b"][0]
    ppf = np.concatenate([cw[i].reshape(NFC, 128).T for i in range(3)] + [cbias.reshape(NFC, 128).T], 1).astype(f32)
    in2 = []
    for c in range(8):
        bq, j = divmod(c, 4)
        d = {}
        x2 = np.zeros((NTK, D), f32)
        ct = np.zeros((D, NTK), ml_dtypes.bfloat16)
        t0 = 1024 * j
        if j > 0:
            x2[0:2] = inp["x_prompt"][bq, t0 - 2:t0]
            ct[:, 0:2] = cat_p[bq][:, t0 - 2:t0]
        x2[2:1026] = inp["x_prompt"][bq, t0:t0 + 1024]
        ct[:, 2:1026] = cat_p[bq][:, t0:t0 + 1024]
        x2[1026:] = inp["x_sample"][c]
        ct[:, 1026:] = cat_s[:, 32 * c:32 * c + 32]
        d["x2in"] = x2
        d["cat"] = ct
        d["wout"] = np.ascontiguousarray(inp["w_out"][0])
        d["wup"] = np.ascontiguousarray(inp["w_ffn_up"][0])
        d["wgate"] = np.ascontiguousarray(inp["w_ffn_gate"][0])
        d["wdown"] = np.ascontiguousarray(inp["w_ffn_down"][0])
        d["g2n"] = np.ascontiguousarray(inp["norm2_g"][0][None])
        d["ppf"] = ppf
        d["conv0"] = np.ascontiguousarray(inp["state_ffn_conv"][0][c].reshape(2, NFC, 128).transpose(2, 1, 0))
        d["hscale"] = np.full((128, 1), 0.0 if j == 0 else 1.0, f32)
        d["identd"] = np.eye(128, dtype=f32)
        in2.append(d)
    r2 = run_bass_kernel_spmd(p2, in2, core_ids=list(range(8))).results
    y_prompt = np.zeros((2, T_P, D), f32)
    y_sample = np.zeros((NS, T_S, D), f32)
    conv_prompt = np.zeros((1, 2, 2, DFF), f32)
    conv_sample = np.zeros((1, NS, 2, DFF), f32)
    for c in range(8):
        bq, j = divmod(c, 4)
        r = r2[c]
        y_prompt[bq, 1024 * j:1024 * j + 1024] = r["y"][0:1024]
        y_sample[c] = r["y"][1024:1056]
        if j == 3:
            conv_prompt[0, bq] = r["convp"].transpose(2, 1, 0).reshape(2, DFF)
        conv_sample[0, c] = r["convs"].transpose(2, 1, 0).reshape(2, DFF)
    return y_prompt, y_sample, conv_prompt, conv_sample


def kernel(**inp):
    inp = {k: np.asarray(v) for k, v in inp.items()}
    (k_prompt, v_prompt, rwkv_prompt, shift_prompt, k_sample, v_sample, rwkv_sample, shift_sample), cat_p, cat_s = \
        _phase1(inp)
    y_prompt, y_sample, conv_prompt, conv_sample = _phase2(inp, cat_p, cat_s)
    return (y_prompt, y_sample, k_prompt, v_prompt, rwkv_prompt, shift_prompt, conv_prompt,
            k_sample, v_sample, rwkv_sample, shift_sample, conv_sample)
```

```python
import numpy as np
from contextlib import ExitStack
import concourse.bass as bass
import concourse.mybir as mybir
from concourse.bass_utils import run_bass_kernel_spmd
import ml_dtypes

F32 = mybir.dt.float32
BF16 = mybir.dt.bfloat16
I32 = mybir.dt.int32
AF = mybir.ActivationFunctionType
ALU = mybir.AluOpType
AX = mybir.AxisListType

D = 2048
T_P = 4096
NS = 8
T_S = 32
PAST = 2048
DFF = 5632
NFC = DFF // 128
RMS_EPS = 1e-6
LNX_EPS = 1e-5 * 64
ENGS = ("pe", "act", "dve", "pool", "sp")


class Sched:
    def __init__(self, nc, n_dma_sems=(("sp", 20), ("pool", 10), ("act", 2))):
        self.nc = nc
        self.ops = []
        self.last_w = {}
        self.readers = {}
        self.n_dma_sems = dict(n_dma_sems)
        self.dnext = {e: 0 for e in self.n_dma_sems}
        self.dlast = {e: [None] * n for e, n in self.n_dma_sems.items()}
        self.elast = {e: None for e in ENGS}

    def op(self, eng, fn, reads=(), writes=(), dma=False):
        i = len(self.ops)
        raw = set()
        oth = set()
        for k in reads:
            if k in self.last_w:
                raw.add(self.last_w[k])
        for k in writes:
            if k in self.last_w:
                oth.add(self.last_w[k])
            for r in self.readers.get(k, ()):
                oth.add(r)
        o = dict(eng=eng, fn=fn, raw=raw, oth=oth - raw, dma=dma, sig=None, slot=None, prev_on_sem=None)
        if dma:
            k = self.dnext[eng]
            self.dnext[eng] = (k + 1) % self.n_dma_sems[eng]
            o["slot"] = k
            o["prev_on_sem"] = self.dlast[eng][k]
            self.dlast[eng][k] = i
        self.ops.append(o)
        self.elast[eng] = i
        for k in reads:
            self.readers.setdefault(k, []).append(i)
        for k in writes:
            self.last_w[k] = i
            self.readers[k] = []
        return i

    def barrier(self):
        deps = set(v for v in self.elast.values() if v is not None)
        for e, l in self.dlast.items():
            deps |= set(v for v in l if v is not None)
        for e in ENGS:
            i = len(self.ops)
            self.ops.append(dict(eng=e, fn=None, raw=set(deps), oth=set(), dma=False, sig=None, slot=None,
                                 prev_on_sem=None))
        self.last_w = {}
        self.readers = {}

    def _needs_wait(self, o, d, is_raw):
        od = self.ops[d]
        if od["dma"] or o["dma"] or od["eng"] != o["eng"]:
            return True
        if o["fn"] is None:
            return True
        return is_raw and o["eng"] != "pe"

    def emit(self, ctx):
        import os
        nmax = int(os.environ.get("P1_NOPS", "0"))
        if nmax:
            self.ops = self.ops[:nmax]
            for e, l in self.dlast.items():
                for k in range(len(l)):
                    cands = [i for i, o in enumerate(self.ops) if o["dma"] and o["eng"] == e and o["slot"] == k]
                    l[k] = cands[-1] if cands else None
        nc = self.nc
        ops = self.ops
        need = [False] * len(ops)
        for i, o in enumerate(ops):
            for d in o["raw"]:
                if self._needs_wait(o, d, True):
                    need[d] = True
            for d in o["oth"]:
                if self._needs_wait(o, d, False):
                    need[d] = True
        esem = {e: ctx.enter_context(nc.semaphore("s_" + e)) for e in ENGS}
        dsem = {e: [ctx.enter_context(nc.semaphore("d_%s%d" % (e, k))) for k in range(n)]
                for e, n in self.n_dma_sems.items()}
        ecount = {e: 0 for e in ENGS}
        dcount = {e: [0] * n for e, n in self.n_dma_sems.items()}
        for i, o in enumerate(ops):
            e = o["eng"]
            if o["fn"] is None:
                continue
            if o["dma"]:
                k = o["slot"]
                dcount[e][k] += 16
                o["sig"] = (dsem[e][k], dcount[e][k], ("d", e, k))
            elif need[i]:
                ecount[e] += 1
                o["sig"] = (esem[e], ecount[e], ("e", e))
        streams = {e: [] for e in ENGS}
        for i, o in enumerate(ops):
            streams[o["eng"]].append(i)
        engobj = dict(pe="tensor", act="scalar", dve="vector", pool="gpsimd", sp="sync")
        dlast = self.dlast
        with nc.Block() as block:
            def make(e):
                def body(eng):
                    waited = {}

                    def wait_for(d):
                        if ops[d]["sig"] is None:
                            return
                        sem, val, key = ops[d]["sig"]
                        if waited.get(key, 0) >= val:
                            return
                        waited[key] = val
                        eng.wait_ge(sem, val)

                    for i in streams[e]:
                        o = ops[i]
                        if o["dma"] and o["prev_on_sem"] is not None:
                            wait_for(o["prev_on_sem"])
                        for d in sorted(o["raw"]):
                            if self._needs_wait(o, d, True):
                                wait_for(d)
                        for d in sorted(o["oth"]):
                            if self._needs_wait(o, d, False):
                                wait_for(d)
                        if o["fn"] is None:
                            continue
                        ins = o["fn"](eng)
                        if o["sig"] is not None:
                            sem, val, key = o["sig"]
                            ins.then_inc(sem, 16 if o["dma"] else 1)
                    for k, d in enumerate(dlast.get(e, [])):
                        if d is not None:
                            wait_for(d)
                return body
            for e in ENGS:
                if streams[e]:
                    getattr(block, engobj[e])(make(e))


class B:
    def __init__(self, nc, S):
        self.nc = nc
        self.S = S
        self.rr = 0

    def dma(self, q, out, in_, r=(), w=(), slow=False):
        if slow:
            self.S.op(q, lambda e: e.dma_start(out=out, in_=in_, allow_slow_non_contiguous=True), r, w, dma=True)
        else:
            self.S.op(q, lambda e: e.dma_start(out=out, in_=in_), r, w, dma=True)

    def mm(self, out, lhsT, rhs, r, w, start=True, stop=True, skip=False):
        if skip:
            self.S.op("pe", lambda e: e.matmul(out, lhsT=lhsT, rhs=rhs, start=start, stop=stop, skip_group_check=True),
                      r, w)
        else:
            self.S.op("pe", lambda e: e.matmul(out, lhsT=lhsT, rhs=rhs, start=start, stop=stop), r, w)

    def tr(self, out, in_, ident, r, w):
        self.S.op("pe", lambda e: e.transpose(out, in_, ident), r, w)

    def act(self, out, in_, func, r, w, bias=None, scale=None, accum=None):
        kw = {}
        if bias is not None:
            kw["bias"] = bias
        if scale is not None:
            kw["scale"] = scale
        if accum is not None:
            kw["accum_out"] = accum
        self.S.op("act", lambda e: e.activation(out=out, in_=in_, func=func, **kw), r, w)

    def tt(self, eng, out, in0, in1, op, r, w):
        self.S.op(eng, lambda e: e.tensor_tensor(out=out, in0=in0, in1=in1, op=op), r, w)

    def ts(self, eng, out, in0, s1, op0, r, w, s2=None, op1=None):
        if s2 is None:
            self.S.op(eng, lambda e: e.tensor_scalar(out=out, in0=in0, scalar1=s1, scalar2=None, op0=op0), r, w)
        else:
            self.S.op(eng, lambda e: e.tensor_scalar(out=out, in0=in0, scalar1=s1, scalar2=s2, op0=op0, op1=op1), r, w)

    def stt(self, eng, out, in0, scalar, in1, op0, op1, r, w):
        self.S.op(eng, lambda e: e.scalar_tensor_tensor(out=out, in0=in0, scalar=scalar, in1=in1, op0=op0, op1=op1),
                  r, w)

    def cp(self, eng, out, in_, r, w):
        if eng == "act":
            self.S.op("act", lambda e: e.activation(out=out, in_=in_, func=AF.Copy), r, w)
        else:
            self.S.op(eng, lambda e: e.tensor_copy(out=out, in_=in_), r, w)

    def memset(self, eng, out, val, r, w):
        self.S.op(eng, lambda e: e.memset(out, val), r, w)

    def reduce(self, eng, out, in_, r, w):
        self.S.op(eng, lambda e: e.tensor_reduce(out=out, in_=in_, axis=AX.X, op=ALU.add), r, w)

    def scan(self, out, d0, d1, r, w):
        self.S.op("dve", lambda e: e.tensor_tensor_scan(out=out, data0=d0, data1=d1, initial=0.0, op0=ALU.mult,
                                                        op1=ALU.add), r, w)

    def rstd(self, out, in_, scale, eps, r, w):
        self.act(out, in_, AF.Ln, r, w, bias=eps, scale=scale)
        self.act(out, out, AF.Exp, w, w, scale=-0.5)


CB = dict(ident=0, trineg=128, ones=256, bdones=384, bdmean=512, maskp=640, masks=640 + 2048)
CB_N = 640 + 2048 + 256
CF = dict(maskA64=0, maskL64=128, id64=192, maskA32=256, maskL32=320, id32=352, seg64=384, seg32=896)
CF_N = 896 + 256


def make_consts():
    cb = np.zeros((128, CB_N), np.float32)
    i = np.arange(128)
    cb[:, 0:128] = np.eye(128)
    cb[:, 128:256] = -(i[:, None] >= i[None, :]).astype(np.float32)
    cb[:, 256:384] = 1.0
    blk = (i[:, None] // 64 == i[None, :] // 64).astype(np.float32)
    cb[:, 384:512] = blk
    cb[:, 512:640] = blk / 64.0
    q = np.arange(512)
    for d in range(4):
        cb[:, 640 + 512 * d: 640 + 512 * (d + 1)] = ((128 * d + i[:, None]) < q[None, :]).astype(np.float32)
    q2 = np.arange(256)
    cb[0:32, 640 + 2048:] = (i[0:32, None] < (q2[None, :] % 32)).astype(np.float32)
    cf = np.zeros((128, CF_N), np.float32)
    for C, ka, kl, ki in ((64, "maskA64", "maskL64", "id64"), (32, "maskA32", "maskL32", "id32")):
        s = np.arange(C)
        for r0 in (0, 64):
            cf[r0:r0 + C, CF[ka]:CF[ka] + C] = (s[:, None] < s[None, :])
            cf[r0:r0 + C, CF[ka] + C:CF[ka] + 2 * C] = (s[:, None] <= s[None, :])
            cf[r0:r0 + C, CF[kl]:CF[kl] + C] = (s[None, :] < s[:, None])
            cf[r0:r0 + C, CF[ki]:CF[ki] + C] = np.eye(C)
    cf[:, CF["seg64"]:CF["seg64"] + 512] = (np.arange(512) % 64 != 0)[None, :]
    cf[:, CF["seg32"]:CF["seg32"] + 256] = (np.arange(256) % 32 != 0)[None, :]
    return cb, cf


def pp_layout(nc2):
    nch = 3 * nc2 + 2
    L = {}
    o = 0
    L["mu"] = o; o += nch
    for k in ("w0", "a0", "kk", "ka", "rk", "lw", "lb", "sbg"):
        L[k] = o; o += nc2
    L["n"] = o
    return L


class Cfg:
    def __init__(self, name, nh, ntile, C, nseg):
        self.name = name
        self.nh = nh
        self.HC = nh * 64
        self.nc2 = nh // 2
        self.NT = ntile
        self.nsub = ntile // 128
        self.C = C
        self.nch = ntile // C
        self.nseg = nseg
        self.seglen = ntile // nseg
        self.nrch = 3 * self.nc2 + 2
        self.nlev = {64: 5, 32: 4}[C]
        self.L = pp_layout(self.nc2)


CFG_P = Cfg("p", 4, 512, 64, 1)
CFG_S = Cfg("s", 2, 256, 32, 8)
ARENA_BYTES = 200 * 1024


class Arena:
    def __init__(self, tile):
        self.t = tile
        self.off = 0

    def mark(self):
        return self.off

    def reset(self, m):
        self.off = m

    def alloc(self, shape, dtype):
        free = 1
        for s in shape[1:]:
            free *= s
        ncols = free * (2 if dtype == F32 else 1)
        ncols = (ncols + 15) // 16 * 16
        assert (self.off + ncols) * 2 <= ARENA_BYTES, ("arena overflow", self.off * 2, ncols * 2)
        ap = self.t[:, self.off:self.off + ncols]
        self.off += ncols
        if dtype == F32:
            ap = ap.bitcast(F32)
        ap = ap[:, 0:free]
        if len(shape) == 3:
            ap = ap.rearrange("p (a b) -> p a b", a=shape[1])
        elif len(shape) == 4:
            ap = ap.rearrange("p (a b c) -> p a b c", a=shape[1], b=shape[2])
        if shape[0] < 128:
            ap = ap[0:shape[0]]
        return ap


def build_phase1(stop=None):
    import os
    stop = stop if stop is not None else float(os.environ.get('P1_STOP', '99'))
    nc = bass.Bass("TRN2", target_bir_lowering=False)
    dt = lambda name, shape, ty, kind: nc.dram_tensor(name, shape, ty, kind=kind).ap()
    IN, OUT = "ExternalInput", "ExternalOutput"
    xp = dt("xp", [T_P, D], F32, IN)
    xs = dt("xs", [NS * T_S, D], F32, IN)
    w1p_sb = dt("w1p_sb", [D, 768], F32, IN)
    w1p_rw = dt("w1p_rw", [D, 1024], F32, IN)
    w1s = dt("w1s", [D, 1024], F32, IN)
    kcT = dt("kcT", [64, 2, NS, PAST], F32, IN)
    vc = dt("vc", [NS, PAST, 128], F32, IN)
    st0 = dt("st0", [128, NS, 64], F32, IN)
    sh0 = dt("sh0", [128, CFG_S.nrch, NS], F32, IN)
    g1 = dt("g1", [1, D], F32, IN)
    qkg_p = dt("qkg_p", [1, 512], F32, IN)
    qkg_s = dt("qkg_s", [1, 256], F32, IN)
    pp_p = dt("pp_p", [128, CFG_P.L["n"]], F32, IN)
    pp_s = dt("pp_s", [128, CFG_S.L["n"]], F32, IN)
    wa2_p = dt("wa2_p", [128, 256], F32, IN)
    wa2_s = dt("wa2_s", [128, 128], F32, IN)
    g2_p = dt("g2_p", [128, 256], F32, IN)
    g2_s = dt("g2_s", [128, 128], F32, IN)
    cbd = dt("cbd", [128, CB_N], F32, IN)
    cfd = dt("cfd", [128, CF_N], F32, IN)
    mix_p = dt("mix_p", [512, T_P], BF16, OUT)
    mix_s = dt("mix_s", [256, NS * T_S], BF16, OUT)
    k_p = dt("k_p", [4, T_P, 64], F32, OUT)
    v_p = dt("v_p", [4, T_P, 64], F32, OUT)
    s_p = dt("s_p", [2, 128, 64], F32, OUT)
    sh_p = dt("sh_p", [128, CFG_P.nrch], F32, OUT)
    k_s = dt("k_s", [NS * T_S, 128], F32, OUT)
    v_s = dt("v_s", [NS * T_S, 128], F32, OUT)
    s_s = dt("s_s", [128, NS, 64], F32, OUT)
    sh_s = dt("sh_s", [128, CFG_S.nrch, NS], F32, OUT)
    xnT_d = nc.dram_tensor("xnT_d", [8, 128, 16 * 512], BF16).ap()

    with ExitStack() as ctx:
        S = Sched(nc)
        b = B(nc, S)
        arena_t = ctx.enter_context(nc.sbuf_tensor("arena", [128, ARENA_BYTES // 2], BF16))
        A = Arena(arena_t)
        PS = [ctx.enter_context(nc.psum_tensor("ps%d" % i, [128, 512], F32)) for i in range(8)]

        cb = A.alloc([128, CB_N], BF16)
        cf = A.alloc([128, CF_N], F32)
        g1b = A.alloc([128, D], F32)
        xt = [A.alloc([128, D], F32) for _ in range(2)]
        xn = A.alloc([128, D], BF16)
        st = A.alloc([128, 32], F32)
        xnT = A.alloc([128, 16, 512], BF16)
        pp = A.alloc([128, 32], F32)
        omka = A.alloc([128, 2], F32)
        xrs_raw = A.alloc([128, 5, 256], F32)
        b.dma("pool", cb, cbd, [], ["cb"])
        b.dma("sp", cf, cfd, [], ["cf"])
        b.dma("sp", g1b, g1.to_broadcast([128, D]), [], ["g1b"])
        ident = cb[:, 0:128]
        trineg = cb[:, 128:256]
        ones = cb[:, 256:384]
        bdones = cb[:, 384:512]
        bdmean = cb[:, 512:640]
        base_mark = A.mark()
        if stop == 0:
            S.emit(ctx)
            return nc

        def bfview(ps):
            return ps[:].bitcast(BF16)

        def stage_norm_T(nsub, xrows):
            for m in range(nsub):
                i2 = m % 2
                kx = "xt%d" % i2
                b.dma("sp", xt[i2], xrows[m * 128:(m + 1) * 128, :], [], [kx])
                b.act(xn, xt[i2], AF.Square, [kx], ["xn", "ss"], accum=st[:, 0:1])
                b.rstd(st[:, 1:2], st[:, 0:1], 1.0 / D, RMS_EPS, ["ss"], ["rs"])
                b.stt("dve", xn, xt[i2], st[:, 1:2], g1b, ALU.mult, ALU.mult, [kx, "rs", "g1b"], ["xn"])
                for g in range(4):
                    bank = PS[g % 2]
                    bk = "ps%d" % (g % 2)
                    pb = bfview(bank)
                    for c in range(4):
                        b.tr(pb[:, c * 128:(c + 1) * 128], xn[:, (4 * g + c) * 128:(4 * g + c + 1) * 128], ident,
                             ["xn", "cb"], [bk])
                    b.cp(("act", "dve")[g % 2], xnT[:, 4 * g:4 * g + 4, m * 128:(m + 1) * 128],
                         pb[:, 0:512].rearrange("p (c t) -> p c t", c=4), [bk], ["xnT"])

        def load_params(cfg, sfx, pp_d, omka_needed=True):
            L = cfg.L
            b.dma("sp", pp[:, 0:L["n"]], pp_d, [], ["pp"])
            b.ts("dve", omka[:, 0:cfg.nc2], pp[:, L["ka"]:L["ka"] + cfg.nc2], -1.0, ALU.mult, ["pp"], ["omka"],
                 s2=1.0, op1=ALU.add)

        def sb_alloc():
            t = {}
            t["sqk"] = A.alloc([128, 512], F32)
            t["qkt"] = A.alloc([128, 512], F32)
            t["qn"] = A.alloc([128, 256], BF16)
            t["knf"] = [A.alloc([128, 256], F32) for _ in range(2)]
            t["knb"] = A.alloc([128, 256], BF16)
            t["vf"] = [A.alloc([128, 256], F32) for _ in range(2)]
            t["qkg"] = A.alloc([128, 512], F32)
            t["qT"] = A.alloc([64, 4, 512], BF16)
            t["ef"] = [A.alloc([128, 512], F32) for _ in range(2)]
            t["Lb"] = [A.alloc([128, 512], BF16) for _ in range(2)]
            t["attn"] = [A.alloc([128, 512], BF16) for _ in range(2)]
            t["Cc"] = A.alloc([128, 512], F32)
            t["of"] = A.alloc([128, 512], F32)
            t["osq"] = A.alloc([128, 512], BF16)
            t["rso"] = A.alloc([128, 512], F32)
            t["onb"] = A.alloc([128, 512], BF16)
            return t

        def stage_sb_tok(cfg, t, W1, m, kT_dst, V_dst, k_out, v_out):
            HC, nh = cfg.HC, cfg.nh
            tl = slice(m * 128, (m + 1) * 128)
            i2 = m % 2
            GA, GB, R0 = PS[0], PS[1], PS[7]
            for kc in range(16):
                b.mm(GA[:, 0:2 * HC], xnT[:, kc, tl], W1[:, kc, 0:2 * HC], ["xnT", "W1_%d" % kc], ["ps0"], start=kc == 0,
                     stop=kc == 15)
            for kc in range(16):
                b.mm(GB[:, 0:HC], xnT[:, kc, tl], W1[:, kc, 2 * HC:3 * HC], ["xnT", "W1_%d" % kc], ["ps1"], start=kc == 0,
                     stop=kc == 15)
            b.act(t["sqk"][:, 0:2 * HC], GA[:, 0:2 * HC], AF.Square, ["ps0"], ["sqk"])
            b.reduce("dve", st[:, 4:4 + 2 * nh], t["sqk"][:, 0:2 * HC].rearrange("p (h d) -> p h d", d=64), ["sqk"],
                     ["ssqk"])
            b.rstd(st[:, 4:4 + 2 * nh], st[:, 4:4 + 2 * nh], 1.0 / 64, RMS_EPS, ["ssqk"], ["ssqk"])
            b.tt("dve", t["qkt"][:, 0:2 * HC].rearrange("p (h d) -> p h d", d=64),
                 GA[:, 0:2 * HC].rearrange("p (h d) -> p h d", d=64),
                 st[:, 4:4 + 2 * nh].unsqueeze(2).to_broadcast([128, 2 * nh, 64]), ALU.mult, ["ps0", "ssqk"], ["qkt"])
            b.tt("pool", t["qn"][:, 0:HC], t["qkt"][:, 0:HC], t["qkg"][:, 0:HC], ALU.mult, ["qkt", "qkg"], ["qn"])
            b.tt("dve", t["knf"][i2][:, 0:HC], t["qkt"][:, HC:2 * HC], t["qkg"][:, HC:2 * HC], ALU.mult,
                 ["qkt", "qkg"], ["knf%d" % i2])
            b.cp("pool", t["knb"][:, 0:HC], t["knf"][i2][:, 0:HC], ["knf%d" % i2], ["knb"])
            b.dma("sp", k_out, t["knf"][i2][:, 0:HC] if cfg is CFG_S else
                  t["knf"][i2][:, 0:HC].rearrange("p (h d) -> p h d", d=64), ["knf%d" % i2], [])
            pb = bfview(R0)
            for h in range(nh):
                b.tr(pb[0:64, h * 128:(h + 1) * 128], t["qn"][:, h * 64:(h + 1) * 64], ident, ["qn", "cb"], ["ps7"])
                b.tr(pb[0:64, (nh + h) * 128:(nh + h + 1) * 128], t["knb"][:, h * 64:(h + 1) * 64], ident,
                     ["knb", "cb"], ["ps7"])
            b.cp("act", t["qT"][:, 0:nh, tl], pb[0:64, 0:nh * 128].rearrange("p (c t) -> p c t", t=128),
                 ["ps7"], ["qT"])
            b.cp("act", kT_dst, pb[0:64, nh * 128:2 * nh * 128].rearrange("p (c t) -> p c t", t=128), ["ps7"], ["kT"])
            b.cp("act", t["vf"][i2][:, 0:HC], GB[:, 0:HC], ["ps1"], ["vf%d" % i2])
            if V_dst is not None:
                b.cp("pool", V_dst, t["vf"][i2][:, 0:HC], ["vf%d" % i2], ["Vres"])
            b.dma("sp", v_out, t["vf"][i2][:, 0:HC] if cfg is CFG_S else
                  t["vf"][i2][:, 0:HC].rearrange("p (h d) -> p h d", d=64), ["vf%d" % i2],
                  ["v_out"] if cfg is CFG_S else [])

        def attn_A(t, it):
            i2, kr, ncol = it["i2"], it["kr"], it["ncol"]
            Z = PS[2 + i2]
            zk = "ps%d" % (2 + i2)
            ef, Lb = t["ef"][i2], t["Lb"][i2]
            for (l, r, c0, n) in it["zm"]:
                b.mm(Z[0:kr, c0:c0 + n], l, r, it["zkeys"], [zk])
            b.act(ef[:, 0:ncol], Z[:, 0:ncol], AF.Exp, [zk], ["ef%d" % i2])
            b.act(Lb[:, 0:ncol], ef[:, 0:ncol], AF.Ln, ["ef%d" % i2], ["Lb%d" % i2], bias=1.0)
            if it["mask"] is not None:
                b.tt("dve", Lb[:, 0:ncol], Lb[:, 0:ncol], it["mask"], ALU.mult, ["Lb%d" % i2, "cb"], ["Lb%d" % i2])

        def attn_B(t, it):
            i2, kr, ncol = it["i2"], it["kr"], it["ncol"]
            first, last = it["step"] == 0, it["step"] == it["nsteps"] - 1
            Ab, Bb = PS[4], PS[5]
            ef, Lb, attn, Cc = t["ef"][i2], t["Lb"][i2], t["attn"][i2], t["Cc"]
            zmms = it["zm"]
            multi = len(zmms) > 1
            for zi, (l, r, c0, n) in enumerate(zmms):
                b.mm(Ab[0:kr, c0:c0 + n], l, r, it["zkeys"], ["ps4"], start=zi == 0, stop=False, skip=multi)
            b.mm(Ab[0:kr, 0:ncol], trineg[0:kr, 0:kr], Lb[0:kr, 0:ncol], ["cb", "Lb%d" % i2], ["ps4"], start=False,
                 stop=True, skip=multi)
            if not last:
                b.mm(Bb[:, 0:ncol], ones[0:kr, :], Lb[0:kr, 0:ncol], ["cb", "Lb%d" % i2], ["ps5"])
            if first:
                b.act(attn[:, 0:ncol], Ab[:, 0:ncol], AF.Exp, ["ps4"], ["attn%d" % i2])
            else:
                b.tt("dve", ef[:, 0:ncol], Ab[:, 0:ncol], Cc[:, 0:ncol], ALU.subtract, ["ps4", "Cc"],
                     ["ef%d" % i2])
                b.act(attn[:, 0:ncol], ef[:, 0:ncol], AF.Exp, ["ef%d" % i2], ["attn%d" % i2])
            if it["mask"] is not None:
                b.tt("dve", attn[:, 0:ncol], attn[:, 0:ncol], it["mask"], ALU.mult, ["attn%d" % i2, "cb"],
                     ["attn%d" % i2])
            for ai, (lv, c0, n, o_ap, st_) in enumerate(it["av"]):
                b.mm(o_ap, lv, attn[0:kr, c0:c0 + n], it["vkeys"] + ["attn%d" % i2], [it["okey"]], start=first and st_,
                     stop=last, skip=multi)
            if not last:
                if first:
                    b.cp("dve", Cc[:, 0:ncol], Bb[:, 0:ncol], ["ps5"], ["Cc"])
                else:
                    b.tt("dve", Cc[:, 0:ncol], Cc[:, 0:ncol], Bb[:, 0:ncol], ALU.add, ["ps5", "Cc"], ["Cc"])

        def attn_run(t, items, finish_after):
            for i, it in enumerate(items):
                it["i2"] = i % 2
            attn_A(t, items[0])
            for i in range(len(items)):
                if i + 1 < len(items):
                    attn_A(t, items[i + 1])
                attn_B(t, items[i])
                if i in finish_after:
                    finish_after[i]()

        def attn_finish(t, ncols, sbg_col, out_dram, ob=6):
            O0 = PS[ob]
            GA = PS[0]
            b.cp("act", t["of"][:, 0:ncols], O0[:, 0:ncols], ["ps%d" % ob], ["of"])
            b.act(t["osq"][:, 0:ncols], t["of"][:, 0:ncols], AF.Square, ["of"], ["osq"])
            b.mm(GA[:, 0:ncols], bdones, t["osq"][:, 0:ncols], ["cb", "osq"], ["ps0"])
            b.rstd(t["rso"][:, 0:ncols], GA[:, 0:ncols], 1.0 / 64, RMS_EPS, ["ps0"], ["rso"])
            b.stt("dve", t["onb"][:, 0:ncols], t["of"][:, 0:ncols], sbg_col, t["rso"][:, 0:ncols], ALU.mult, ALU.mult,
                  ["of", "rso", "pp"], ["onb"])
            b.dma("sp", out_dram, t["onb"][:, 0:ncols], ["onb"], [])

        def load_qkg(t, cfg, qkg_d):
            HC = cfg.HC
            b.dma("sp", t["qkg"][:, 0:2 * HC], qkg_d.to_broadcast([128, 2 * HC]), [], ["qkg"])
            b.ts("dve", t["qkg"][:, 0:HC], t["qkg"][:, 0:HC], 0.125, ALU.mult, ["qkg"], ["qkg"])

        FN = ("b0", "b1", "b2", "b3", "b4", "b5", "b6", "b7", "b8", "b9")

        def rw_alloc():
            t = {}
            t["xrj"] = [A.alloc([128, 520], F32) for _ in range(2)]
            t["hal"] = A.alloc([128, 8, 8], F32)
            t["hal2"] = A.alloc([128, 8, 8], F32)
            t["xsf"] = A.alloc([128, 8, 512], F32)
            t["lora_in"] = A.alloc([128, 512], BF16)
            t["sgb"] = A.alloc([128, 512], BF16)
            t["wa2"] = A.alloc([128, 256], BF16)
            t["g2t"] = A.alloc([128, 256], BF16)
            for n in FN:
                t[n] = A.alloc([128, 512], F32)
            t["AR"] = A.alloc([128, 8, 2, 64], BF16)
            for n in ("Btb", "Ktb", "Vtb", "sqb", "yob"):
                t[n] = A.alloc([128, 512], BF16)
            for n in ("Btm", "Ktm", "Vtm"):
                t[n] = A.alloc([128, 8, 64], BF16)
            t["MTb"] = A.alloc([128, 8, 128], BF16)
            t["MTk"] = A.alloc([128, 8, 128], BF16)
            for n in ("Uj", "Lj", "Xj"):
                t[n] = [A.alloc([128, 8, 64], BF16) for _ in range(2)]
            t["Wb"] = A.alloc([128, 64], BF16)
            t["Ub"] = A.alloc([128, 64], BF16)
            t["STf"] = [A.alloc([128, 64], F32) for _ in range(8)]
            t["STb"] = [A.alloc([128, 64], BF16) for _ in range(8)]
            return t

        def stage_rwkv(cfg, t, get_raw, state_of_chunk, mix_dst):
            NT, C, nch, nc2, L = cfg.NT, cfg.C, cfg.nch, cfg.nc2, cfg.L
            nseg, sl = cfg.nseg, cfg.seglen
            xsf = t["xsf"]
            GA, GB, R0 = PS[0], PS[1], PS[7]
            for j in range(cfg.nrch):
                raw, rk = get_raw(j)
                xj = t["xrj"][j % 2][:, 0:nseg * (sl + 1)].rearrange("p (s t) -> p s t", s=nseg)
                kj = "xrj%d" % (j % 2)
                b.cp("act", xj[:, :, 1:sl + 1], raw.rearrange("p (s t) -> p s t", s=nseg), [rk], [kj])
                b.cp("pool", xj[:, :, 0:1], t["hal"][:, j, 0:nseg].unsqueeze(2), ["hal"], [kj])
                b.tt("dve", t["b9"][:, 0:NT].rearrange("p (s t) -> p s t", s=nseg), xj[:, :, 0:sl], xj[:, :, 1:sl + 1],
                     ALU.subtract, [kj], ["b9"])
                b.stt("dve", xsf[:, j, 0:NT].rearrange("p (s t) -> p s t", s=nseg),
                      t["b9"][:, 0:NT].rearrange("p (s t) -> p s t", s=nseg), pp[:, L["mu"] + j:L["mu"] + j + 1],
                      xj[:, :, 1:sl + 1], ALU.mult, ALU.add, ["b9", kj, "pp"], ["xs%d" % j])
                b.cp("pool", t["hal2"][:, j, 0:nseg].unsqueeze(2), xj[:, :, sl:sl + 1], [kj], ["hal2"])
                if cfg is CFG_P:
                    b.cp("pool", t["hal"][:, j, 0:1], t["hal2"][:, j, 0:1], ["hal2"], ["hal"])
            jw, jg = 3 * nc2, 3 * nc2 + 1
            b.act(t["lora_in"][0:64, 0:NT], xsf[0:64, jw, 0:NT], AF.Tanh, ["xs%d" % jw], ["lora_in"])
            b.cp("pool", t["lora_in"][64:128, 0:NT], xsf[64:128, jw, 0:NT], ["xs%d" % jw], ["lora_in"])
            b.act(t["sgb"][:, 0:NT], xsf[:, jg, 0:NT], AF.Sigmoid, ["xs%d" % jg], ["sgb"])
            seg = cf[:, CF["seg%d" % C]:CF["seg%d" % C] + NT]
            maskA = cf[0:C, CF["maskA%d" % C]:CF["maskA%d" % C] + 2 * C]
            maskL = cf[0:C, CF["maskL%d" % C]:CF["maskL%d" % C] + C]
            idC = cf[0:C, CF["id%d" % C]:CF["id%d" % C] + C]
            AR, Btb, Ktb, Vtb, sqb = t["AR"], t["Btb"], t["Ktb"], t["Vtb"], t["sqb"]
            for c2 in range(nc2):
                jr, jk, jv = c2, nc2 + c2, 2 * nc2 + c2
                cs = slice(c2 * 128, (c2 + 1) * 128)
                col = lambda name: pp[:, L[name] + c2:L[name] + c2 + 1]
                f = {n: t[n][:, 0:NT] for n in FN}
                xs_r, xs_k, xs_v = xsf[:, jr, 0:NT], xsf[:, jk, 0:NT], xsf[:, jv, 0:NT]
                kr_, kk_, kv_ = "xs%d" % jr, "xs%d" % jk, "xs%d" % jv
                P1, P2, P3 = GA[:, 0:NT], GB[:, 0:NT], R0[:, 0:NT]
                sgu, a_, g_, lw, cl, clm, eP, eM, ePx, tmp = (f["b0"], f["b1"], f["b2"], f["b3"], f["b4"], f["b5"],
                                                              f["b6"], f["b7"], f["b8"], f["b9"])
                b.mm(P1, t["wa2"][0:64, cs], t["lora_in"][0:64, 0:NT], ["wa2", "lora_in"], ["ps0"])
                b.mm(P2, t["wa2"][64:128, cs], t["lora_in"][64:128, 0:NT], ["wa2", "lora_in"], ["ps1"])
                b.mm(P3, t["g2t"][:, cs], t["sgb"][:, 0:NT], ["g2t", "sgb"], ["ps7"])
                b.act(sgu, P1, AF.Sigmoid, ["ps0", "pp"], ["b0"], bias=col("w0"))
                b.act(a_, P2, AF.Sigmoid, ["ps1", "pp"], ["b1"], bias=col("a0"))
                b.cp("act", g_, P3, ["ps7"], ["b2"])
                b.ts("pool", lw, sgu, -0.6065306597126334, ALU.mult, ["b0"], ["b3"])
                b.scan(cl, seg, lw, ["cf", "b3"], ["b4"])
                b.tt("pool", clm, cl, lw, ALU.subtract, ["b4", "b3"], ["b5"])
                b.act(eP, cl, AF.Exp, ["b4"], ["b6"])
                b.act(eM, cl, AF.Exp, ["b4"], ["b7"], scale=-1.0)
                b.act(ePx, clm, AF.Exp, ["b5"], ["b8"])
                kkr = f["b3"]
                b.ts("dve", kkr, xs_k, col("kk"), ALU.mult, [kk_, "pp"], ["b3"])
                b.act(sqb[:, 0:NT], kkr, AF.Square, ["b3"], ["sqb"])
                b.mm(P1, bdones, sqb[:, 0:NT], ["cb", "sqb"], ["ps0"])
                b.ts("dve", tmp, P1, 1e-24, ALU.max, ["ps0"], ["b9"])
                b.rstd(tmp, tmp, 1.0, 0.0, ["b9"], ["b9"])
                kk = f["b4"]
                b.tt("dve", kk, kkr, tmp, ALU.mult, ["b3", "b9"], ["b4"])
                tt_ = f["b5"]
                b.ts("dve", tt_, a_, col("ka"), ALU.mult, ["b1", "pp", "omka"], ["b5"], s2=omka[:, c2:c2 + 1],
                     op1=ALU.add)
                kp = f["b5"]
                b.tt("dve", kp, xs_k, tt_, ALU.mult, [kk_, "b5"], ["b5"])
                ARv = AR[:, 0:nch, :, 0:C]
                b.stt("dve", ARv[:, :, 0, :], kk.rearrange("p (c t) -> p c t", t=C), -1.0,
                      ePx.rearrange("p (c t) -> p c t", t=C), ALU.mult, ALU.mult, ["b4", "b8"], ["AR"])
                b.tt("pool", ARv[:, :, 1, :], xs_r.rearrange("p (c t) -> p c t", t=C),
                     eP.rearrange("p (c t) -> p c t", t=C), ALU.mult, [kr_, "b6"], ["AR"])
                b.tt("dve", tmp, kk, a_, ALU.mult, ["b4", "b1"], ["b9"])
                b.tt("dve", Btb[:, 0:NT], tmp, eM, ALU.mult, ["b9", "b7"], ["Btb"])
                b.tt("pool", Ktb[:, 0:NT], kp, eM, ALU.mult, ["b5", "b7"], ["Ktb"])
                b.cp("pool", Vtb[:, 0:NT], xs_v, [kv_], ["Vtb"])
                b.tt("dve", tmp, xs_r, kp, ALU.mult, [kr_, "b5"], ["b9"])
                b.ts("dve", sqb[:, 0:NT], tmp, col("rk"), ALU.mult, ["b9", "pp"], ["sqb"])
                b.mm(P2, bdones, sqb[:, 0:NT], ["cb", "sqb"], ["ps1"])
                bonus = f["b8"]
                b.tt("dve", bonus, P2, xs_v, ALU.mult, ["ps1", kv_], ["b8"])
                pbR = bfview(R0)
                PR = slice(0, 128)
                npr = 128
                HO = [slice(hh * 64, hh * 64 + C) for hh in range(2)]
                HR = [slice(hh * 64, hh * 64 + 64) for hh in range(2)]
                for (src, dst, sk, dk, ev) in ((Btb, t["Btm"], "Btb", "Btm", "act"), (Ktb, t["Ktm"], "Ktb", "Ktm", "dve"),
                                               (Vtb, t["Vtm"], "Vtb", "Vtm", "act")):
                    for hh in range(2):
                        for ch in range(nch):
                            b.tr(pbR[HO[hh], ch * 64:(ch + 1) * 64], src[HR[hh], ch * C:(ch + 1) * C],
                                 ident[HR[hh], HR[hh]], [sk, "cb"], ["ps7"])
                    b.cp(ev, dst[PR, 0:nch, :], pbR[PR, 0:nch * 64].rearrange("p (c f) -> p c f", f=64), ["ps7"], [dk])
                MTb, MTk, Uj, Lj, Xj = t["MTb"], t["MTk"], t["Uj"], t["Lj"], t["Xj"]
                mA = cf[PR, CF["maskA%d" % C]:CF["maskA%d" % C] + 2 * C]
                mL = cf[PR, CF["maskL%d" % C]:CF["maskL%d" % C] + C]
                mI = cf[PR, CF["id%d" % C]:CF["id%d" % C] + C]
                for g0 in range(0, nch, 4):
                    for hh in range(2):
                        for ch in range(g0, g0 + 4):
                            q4 = ch - g0
                            arf = AR[HR[hh], ch, :, 0:C]
                            b.mm(GA[HO[hh], q4 * 2 * C:(q4 + 1) * 2 * C].rearrange("p (a t) -> p a t", a=2),
                                 Btb[HR[hh], ch * C:(ch + 1) * C], arf, ["Btb", "AR"], ["ps0"])
                            b.mm(GB[HO[hh], q4 * 2 * C:(q4 + 1) * 2 * C].rearrange("p (a t) -> p a t", a=2),
                                 Ktb[HR[hh], ch * C:(ch + 1) * C], arf, ["Ktb", "AR"], ["ps1"])
                    b.tt("dve", MTb[PR, g0:g0 + 4, 0:2 * C], GA[PR, 0:8 * C].rearrange("p (q t) -> p q t", q=4),
                         mA.unsqueeze(1).to_broadcast([npr, 4, 2 * C]), ALU.mult, ["ps0", "cf"], ["MTb"])
                    b.tt("dve", MTk[PR, g0:g0 + 4, 0:2 * C], GB[PR, 0:8 * C].rearrange("p (q t) -> p q t", q=4),
                         mA.unsqueeze(1).to_broadcast([npr, 4, 2 * C]), ALU.mult, ["ps1", "cf"], ["MTk"])
                for hh in range(2):
                    for ch in range(nch):
                        b.mm(R0[HO[hh], ch * C:(ch + 1) * C], AR[HR[hh], ch, 0, 0:C], Btb[HR[hh], ch * C:(ch + 1) * C],
                             ["AR", "Btb"], ["ps7"])
                b.tt("dve", Lj[0][PR, 0:nch, 0:C], R0[PR, 0:nch * C].rearrange("p (q t) -> p q t", t=C),
                     mL.unsqueeze(1).to_broadcast([npr, nch, C]), ALU.mult, ["ps7", "cf"], ["Lj0"])
                b.cp("pool", Uj[0][PR, 0:nch, 0:C], MTb[PR, 0:nch, 0:C], ["MTb"], ["Uj0"])
                b.tt("dve", Xj[0][PR, 0:nch, 0:C], MTb[PR, 0:nch, 0:C], mI.unsqueeze(1).to_broadcast([npr, nch, C]),
                     ALU.add, ["MTb", "cf"], ["Xj0"])
                for lv in range(1, cfg.nlev + 1):
                    pi, ci = (lv - 1) % 2, lv % 2
                    Up, Lp, Un, Ln_, Xp, Xn = Uj[pi], Lj[pi], Uj[ci], Lj[ci], Xj[pi], Xj[ci]
                    bl, bu, bx = ((0, 1, 7), (2, 3, 4))[lv % 2]
                    pl, pu, px = PS[bl], PS[bu], PS[bx]
                    plk, puk, pxk = "ps%d" % bl, "ps%d" % bu, "ps%d" % bx
                    for hh in range(2):
                        for ch in range(nch):
                            b.mm(pl[HO[hh], ch * C:(ch + 1) * C], Up[HO[hh], ch, 0:C], Lp[HO[hh], ch, 0:C],
                                 ["Uj%d" % pi, "Lj%d" % pi], [plk])
                    b.cp("act", Ln_[PR, 0:nch, 0:C], pl[PR, 0:nch * C].rearrange("p (q t) -> p q t", t=C), [plk],
                         ["Lj%d" % ci])
                    if lv < cfg.nlev:
                        for hh in range(2):
                            for ch in range(nch):
                                b.mm(pu[HO[hh], ch * C:(ch + 1) * C], Lp[HO[hh], ch, 0:C], Up[HO[hh], ch, 0:C],
                                     ["Uj%d" % pi, "Lj%d" % pi], [puk])
                        b.cp("dve", Un[PR, 0:nch, 0:C], pu[PR, 0:nch * C].rearrange("p (q t) -> p q t", t=C), [puk],
                             ["Uj%d" % ci])
                    for hh in range(2):
                        for ch in range(nch):
                            b.mm(px[HO[hh], ch * C:(ch + 1) * C], Ln_[HO[hh], ch, 0:C], Xp[HO[hh], ch, 0:C],
                                 ["Lj%d" % ci, "Xj%d" % pi], [pxk])
                    b.tt("dve", Xn[PR, 0:nch, 0:C], px[PR, 0:nch * C].rearrange("p (q t) -> p q t", t=C),
                         Xp[PR, 0:nch, 0:C], ALU.add, [pxk, "Xj%d" % pi], ["Xj%d" % ci])
                Xf = Xj[cfg.nlev % 2]
                xk = "Xj%d" % (cfg.nlev % 2)
                PY = PS[6]
                Wb, Ub, Btm, Ktm, Vtm = t["Wb"], t["Ub"], t["Btm"], t["Ktm"], t["Vtm"]
                for ch in range(nch):
                    si = state_of_chunk(c2, ch)
                    sf, sb_, skf, skb = t["STf"][si], t["STb"][si], "STf%d" % si, "STb%d" % si
                    for hh in range(2):
                        b.mm(GA[HO[hh], 0:64], AR[HR[hh], ch, 0, 0:C], sb_[HR[hh], :], ["AR", skb], ["ps0"], start=True,
                             stop=False)
                        b.mm(GA[HO[hh], 0:64], MTk[HO[hh], ch, 0:C], Vtm[HO[hh], ch, :], ["MTk", "Vtm"], ["ps0"],
                             start=False, stop=True)
                    b.cp("act", Wb[PR, :], GA[PR, 0:64], ["ps0"], ["Wb"])
                    for hh in range(2):
                        b.mm(GB[HO[hh], 0:64], Xf[HO[hh], ch, 0:C], Wb[HO[hh], :], [xk, "Wb"], ["ps1"])
                    b.cp("dve", Ub[PR, :], GB[PR, 0:64], ["ps1"], ["Ub"])
                    for hh in range(2):
                        yo = PY[HR[hh], ch * C:(ch + 1) * C]
                        b.mm(yo, sb_[HR[hh], :], AR[HR[hh], ch, 1, 0:C], [skb, "AR"], ["ps6"], start=True, stop=False)
                        b.mm(yo, Ub[HO[hh], :], MTb[HO[hh], ch, C:2 * C], ["Ub", "MTb"], ["ps6"], start=False, stop=False)
                        b.mm(yo, Vtm[HO[hh], ch, :], MTk[HO[hh], ch, C:2 * C], ["Vtm", "MTk"], ["ps6"], start=False,
                             stop=True)
                        so = PS[5][HR[hh], 0:64]
                        b.mm(so, Btm[HO[hh], ch, :], Ub[HO[hh], :], ["Btm", "Ub"], ["ps5"], start=True, stop=False)
                        b.mm(so, Ktm[HO[hh], ch, :], Vtm[HO[hh], ch, :], ["Ktm", "Vtm"], ["ps5"], start=False, stop=True)
                    b.tt("dve", sf, PS[5][:, 0:64], sf, ALU.add, ["ps5", skf], [skf])
                    b.ts("dve", sf, sf, t["b6"][:, ch * C + C - 1:ch * C + C], ALU.mult, [skf, "b6"], [skf])
                    b.cp("pool", sb_, sf, [skf], [skb])
                y_, yc, rs = f["b0"], f["b1"], f["b7"]
                b.cp("act", y_, PY[:, 0:NT], ["ps6"], ["b0"])
                b.cp("pool", sqb[:, 0:NT], y_, ["b0"], ["sqb"])
                b.mm(P1, bdmean, sqb[:, 0:NT], ["cb", "sqb"], ["ps0"])
                b.tt("dve", yc, y_, P1, ALU.subtract, ["b0", "ps0"], ["b1"])
                b.act(sqb[:, 0:NT], yc, AF.Square, ["b1"], ["sqb"])
                b.mm(P2, bdmean, sqb[:, 0:NT], ["cb", "sqb"], ["ps1"])
                b.rstd(rs, P2, 1.0, LNX_EPS, ["ps1"], ["b7"])
                b.tt("dve", yc, yc, rs, ALU.mult, ["b1", "b7"], ["b1"])
                b.ts("dve", yc, yc, col("lw"), ALU.mult, ["b1", "pp"], ["b1"], s2=col("lb"), op1=ALU.add)
                b.tt("pool", yc, yc, bonus, ALU.add, ["b1", "b8"], ["b1"])
                b.tt("dve", t["yob"][:, 0:NT], yc, g_, ALU.mult, ["b1", "b2"], ["yob"])
                b.dma("sp", mix_dst(c2), t["yob"][:, 0:NT], ["yob"], [])

        cfg = CFG_S
        load_params(cfg, "s", pp_s)
        W1 = A.alloc([128, 16, 1024], BF16)
        for kc in range(16):
            b.dma("pool", W1[:, kc, :], w1s[kc * 128:(kc + 1) * 128, :], [], ["W1_%d" % kc])
        t = sb_alloc()
        load_qkg(t, cfg, qkg_s)
        KcT = A.alloc([64, 2, 4, PAST], BF16)
        Vc = A.alloc([128, 4, 16, 128], BF16)
        kTs = A.alloc([64, 2, 256], BF16)
        vnew = A.alloc([32, NS, 128], BF16)
        vsb = A.alloc([128, 2, 128], BF16)
        stage_norm_T(2, xs)
        for m in range(2):
            stage_sb_tok(cfg, t, W1, m, kTs[:, :, m * 128:(m + 1) * 128], vsb[:, m, :], k_s[m * 128:(m + 1) * 128, :],
                         v_s[m * 128:(m + 1) * 128, :])
        for j in range(cfg.nrch):
            bank = PS[j % 2]
            bk = "ps%d" % (j % 2)
            for kc in range(16):
                b.mm(bank[:, 0:256], W1[:, kc, 384 + j * 128:384 + (j + 1) * 128], xnT[:, kc, 0:256], ["W1_%d" % kc, "xnT"], [bk],
                     start=kc == 0, stop=kc == 15)
            b.cp("act", xrs_raw[:, j, :], bank[:, 0:256], [bk], ["xrs_raw"])
        if stop == 1:
            S.emit(ctx)
            return nc
        for bb in range(NS):
            bank = PS[bb // 4]
            b.mm(bank[0:32, (bb % 4) * 128:(bb % 4 + 1) * 128], ident[:, (bb % 4) * 32:(bb % 4 + 1) * 32], vsb[:, bb // 4, :],
                 ["cb", "Vres"], ["ps%d" % (bb // 4)])
        for g in range(2):
            b.cp(("act", "dve")[g], vnew[0:32, 4 * g:4 * g + 4, :], PS[g][0:32, :].rearrange("p (b f) -> p b f", b=4),
                 ["ps%d" % g], ["vnew"])
        masks = cb[:, CB["masks"]:CB["masks"] + 256]
        if stop == 1.2:
            S.emit(ctx)
            return nc
        for grp in range(2):
            for b4 in range(4):
                bb = grp * 4 + b4
                b.dma("pool", KcT[:, :, b4, :], kcT[:, :, bb, :], [], ["KcT%d" % b4])
                b.dma("pool", Vc[:, b4, :, :], vc[bb].rearrange("(kb p) f -> p kb f", p=128), [], ["Vc%d" % b4])
            if stop == 1.4:
                S.emit(ctx)
                return nc
            nsteps = 17
            items = []
            for step in range(nsteps):
                kb = 16 - step
                pairs = [(hh, b4) for hh in range(2) for b4 in range(4)]
                cols = lambda b4: slice((grp * 4 + b4) * 32, (grp * 4 + b4 + 1) * 32)
                if kb == 16:
                    zm = [(kTs[:, hh, cols(b4)], t["qT"][:, hh, cols(b4)], hh * 128 + b4 * 32, 32) for hh, b4 in pairs]
                    av = [(vnew[0:32, grp * 4 + b4, hh * 64:hh * 64 + 64], hh * 128 + b4 * 32, 32,
                           PS[6][hh * 64:hh * 64 + 64, cols(b4)], b4 == 0) for hh, b4 in pairs]
                    items.append(dict(step=step, nsteps=nsteps, kr=32, ncol=256, zm=zm, av=av, mask=masks,
                                      zkeys=["kT", "qT"], vkeys=["vnew"], okey="ps6"))
                else:
                    zm = [(KcT[:, hh, b4, kb * 128:(kb + 1) * 128], t["qT"][:, hh, cols(b4)], hh * 128 + b4 * 32, 32)
                          for hh, b4 in pairs]
                    av = [(Vc[:, b4, kb, hh * 64:hh * 64 + 64], hh * 128 + b4 * 32, 32,
                           PS[6][hh * 64:hh * 64 + 64, cols(b4)], b4 == 0) for hh, b4 in pairs]
                    items.append(dict(step=step, nsteps=nsteps, kr=128, ncol=256, zm=zm, av=av, mask=None,
                                      zkeys=["KcT%d" % q for q in range(4)] + ["qT"],
                                      vkeys=["Vc%d" % q for q in range(4)], okey="ps6"))
            attn_run(t, items, {})
        attn_finish(t, 256, pp[:, cfg.L["sbg"]:cfg.L["sbg"] + 1], mix_s[0:128, :])
        if stop == 2:
            S.emit(ctx)
            return nc
        S.barrier()

        A.reset(base_mark)
        t = rw_alloc()
        b.dma("pool", t["wa2"][:, 0:128], wa2_s, [], ["wa2"])
        b.dma("pool", t["g2t"][:, 0:128], g2_s, [], ["g2t"])
        b.dma("sp", t["hal"][:, 0:5, :], sh0, [], ["hal"])
        for bb in range(NS):
            b.dma("sp", t["STf"][bb], st0[:, bb, :], [], ["STf%d" % bb])
            b.cp("pool", t["STb"][bb], t["STf"][bb], ["STf%d" % bb], ["STb%d" % bb])
        stage_rwkv(cfg, t, lambda j: (xrs_raw[:, j, :], "xrs_raw"), lambda c2, ch: ch, lambda c2: mix_s[128:256, :])
        b.dma("sp", sh_s, t["hal2"][:, 0:5, :], ["hal2"], [])
        for bb in range(NS):
            b.dma("sp", s_s[:, bb, :], t["STf"][bb], ["STf%d" % bb], [])
        if stop == 3:
            S.emit(ctx)
            return nc
        S.barrier()

        cfg = CFG_P
        A.reset(base_mark)
        load_params(cfg, "p", pp_p)
        W1 = A.alloc([128, 16, 768], BF16)
        for kc in range(16):
            b.dma("pool", W1[:, kc, :], w1p_sb[kc * 128:(kc + 1) * 128, :], [], ["W1_%d" % kc])
        t = sb_alloc()
        load_qkg(t, cfg, qkg_p)
        kTr = A.alloc([64, 4, T_P], BF16)
        Vr = A.alloc([128, 32, 256], BF16)
        for s in range(8):
            stage_norm_T(4, xp[s * 512:(s + 1) * 512, :])
            b.dma("sp", xnT_d[s], xnT.rearrange("p a b -> p (a b)"), ["xnT"], ["xnT_d%d" % s])
            for m in range(4):
                t0 = s * 512 + m * 128
                stage_sb_tok(cfg, t, W1, m, kTr[:, :, t0:t0 + 128], Vr[:, t0 // 128, :],
                             k_p[:, t0:t0 + 128, :].rearrange("h t d -> t h d"),
                             v_p[:, t0:t0 + 128, :].rearrange("h t d -> t h d"))
            items = []
            fin = {}
            for c2 in range(2):
                ob = 6 + (c2 % 2)
                for hh in range(2):
                    h = 2 * c2 + hh
                    hr = slice(hh * 64, hh * 64 + 64)
                    nsteps = 4 * s + 4
                    for step in range(nsteps):
                        kb = 4 * s + 3 - step
                        di = kb - 4 * s
                        zm = [(kTr[:, h, kb * 128:(kb + 1) * 128], t["qT"][:, h, 0:512], 0, 512)]
                        av = [(Vr[:, kb, h * 64:(h + 1) * 64], 0, 512, PS[ob][hr, 0:512], True)]
                        mk = cb[:, CB["maskp"] + 512 * di:CB["maskp"] + 512 * (di + 1)] if di >= 0 else None
                        items.append(dict(step=step, nsteps=nsteps, kr=128, ncol=512, zm=zm, av=av, mask=mk,
                                          zkeys=["kT", "qT"], vkeys=["Vres"], okey="ps%d" % ob))
                fin[len(items) - 1] = (lambda c2=c2, ob=ob, s=s: attn_finish(
                    t, 512, pp[:, cfg.L["sbg"] + c2:cfg.L["sbg"] + c2 + 1],
                    mix_p[c2 * 128:(c2 + 1) * 128, s * 512:(s + 1) * 512], ob))
            attn_run(t, items, fin)
        if stop == 4:
            S.emit(ctx)
            return nc
        S.barrier()

        A.reset(base_mark)
        W1 = A.alloc([128, 16, 1024], BF16)
        for kc in range(16):
            b.dma("pool", W1[:, kc, :], w1p_rw[kc * 128:(kc + 1) * 128, :], [], ["W1_%d" % kc])
        t = rw_alloc()
        b.dma("pool", t["wa2"], wa2_p, [], ["wa2"])
        b.dma("pool", t["g2t"], g2_p, [], ["g2t"])
        b.memset("pool", t["hal"], 0.0, [], ["hal"])
        for c2 in range(2):
            b.memset("pool", t["STf"][c2], 0.0, [], ["STf%d" % c2])
            b.memset("pool", t["STb"][c2], 0.0, [], ["STb%d" % c2])
        for s in range(8):
            b.dma("sp", xnT.rearrange("p a b -> p (a b)"), xnT_d[s], [], ["xnT"])

            def get_raw(j):
                bank = PS[j % 2]
                bk = "ps%d" % (j % 2)
                for kc in range(16):
                    b.mm(bank[:, 0:512], W1[:, kc, j * 128:(j + 1) * 128], xnT[:, kc, 0:512], ["W1_%d" % kc, "xnT"], [bk],
                         start=kc == 0, stop=kc == 15)
                return bank[:, 0:512], bk
            stage_rwkv(cfg, t, get_raw, lambda c2, ch: c2,
                       lambda c2: mix_p[256 + c2 * 128:256 + (c2 + 1) * 128, s * 512:(s + 1) * 512])
        b.dma("sp", sh_p, t["hal2"][:, :, 0], ["hal2"], [], slow=True)
        for c2 in range(2):
            b.dma("sp", s_p[c2], t["STf"][c2], ["STf%d" % c2], [])
        S.emit(ctx)
    return nc


NTK = 1058
NTO = 1056


def build_phase2():
    nc = bass.Bass("TRN2", target_bir_lowering=False)
    dt = lambda name, shape, ty, kind: nc.dram_tensor(name, shape, ty, kind=kind).ap()
    IN, OUT = "ExternalInput", "ExternalOutput"
    x2in = dt("x2in", [NTK, D], F32, IN)
    cat = dt("cat", [D, NTK], BF16, IN)
    wout = dt("wout", [D, D], F32, IN)
    wup = dt("wup", [D, DFF], F32, IN)
    wgate = dt("wgate", [D, DFF], F32, IN)
    wdown = dt("wdown", [DFF, D], F32, IN)
    g2n = dt("g2n", [1, D], F32, IN)
    ppf = dt("ppf", [128, 4 * NFC], F32, IN)
    conv0 = dt("conv0", [128, NFC, 2], F32, IN)
    hscale = dt("hscale", [128, 1], F32, IN)
    identd = dt("identd", [128, 128], F32, IN)
    y = dt("y", [NTO, D], F32, OUT)
    convp = dt("convp", [128, NFC, 2], F32, OUT)
    convs = dt("convs", [128, NFC, 2], F32, OUT)
    x2d = nc.dram_tensor("x2d", [NTK, D], F32).ap()

    with ExitStack() as ctx:
        S = Sched(nc)
        b = B(nc, S)
        arena_t = ctx.enter_context(nc.sbuf_tensor("arena", [128, ARENA_BYTES // 2], BF16))
        A = Arena(arena_t)
        PS = [ctx.enter_context(nc.psum_tensor("ps%d" % i, [128, 512], F32)) for i in range(8)]
        ident = A.alloc([128, 128], BF16)
        pf = A.alloc([128, 4 * NFC], F32)
        c0t = A.alloc([128, NFC, 2], F32)
        hs = A.alloc([128, 1], F32)
        cstp = A.alloc([128, NFC, 2], F32)
        csts = A.alloc([128, NFC, 2], F32)
        st = A.alloc([128, 8], F32)
        xn2T = A.alloc([128, 16, NTK], BF16)
        b.dma("pool", ident, identd, [], ["ident"])
        b.dma("sp", pf, ppf, [], ["pf"])
        b.dma("sp", c0t, conv0, [], ["c0t"])
        b.dma("sp", hs, hscale, [], ["hs"])
        m_persist = A.mark()

        g2b = A.alloc([128, D], F32)
        catT = A.alloc([128, 16, NTK], BF16)
        Wo = A.alloc([128, 16, D], BF16)
        xt = [A.alloc([128, D], F32) for _ in range(2)]
        x2t = [A.alloc([128, D], F32) for _ in range(2)]
        xn = A.alloc([128, D], BF16)
        b.dma("sp", g2b, g2n.to_broadcast([128, D]), [], ["g2b"])
        for kc in range(16):
            b.dma("sp", catT[:, kc, :], cat[kc * 128:(kc + 1) * 128, :], [], ["catT%d" % kc])
            b.dma("pool", Wo[:, kc, :], wout[kc * 128:(kc + 1) * 128, :], [], ["Wo%d" % kc])
        subt = [(m * 128, 128) for m in range(8)] + [(NTK - 128, 128)]
        for m, (r0, nr) in enumerate(subt):
            i2 = m % 2
            kx, k2 = "xt%d" % i2, "x2t%d" % i2
            b.dma("sp", xt[i2][0:nr], x2in[r0:r0 + nr, :], [], [kx])
            for nb in range(4):
                bank = PS[i2 * 4 + nb]
                bk = "ps%d" % (i2 * 4 + nb)
                for kc in range(16):
                    b.mm(bank[0:nr, :], catT[:, kc, r0:r0 + nr], Wo[:, kc, nb * 512:(nb + 1) * 512],
                         ["catT%d" % kc, "Wo%d" % kc], [bk], start=kc == 0, stop=kc == 15)
                b.tt("dve", x2t[i2][0:nr, nb * 512:(nb + 1) * 512], bank[0:nr, :], xt[i2][0:nr, nb * 512:(nb + 1) * 512],
                     ALU.add, [bk, kx], [k2])
            b.dma("sp", x2d[r0:r0 + nr, :], x2t[i2][0:nr], [k2], ["x2d"])
            b.act(xn[0:nr], x2t[i2][0:nr], AF.Square, [k2], ["xn", "ss"], accum=st[0:nr, 0:1])
            b.rstd(st[0:nr, 1:2], st[0:nr, 0:1], 1.0 / D, RMS_EPS, ["ss"], ["rs"])
            b.stt("dve", xn[0:nr], x2t[i2][0:nr], st[0:nr, 1:2], g2b[0:nr], ALU.mult, ALU.mult, [k2, "rs", "g2b"],
                  ["xn"])
            for g in range(4):
                bank = PS[i2 * 4 + g]
                bk = "ps%d" % (i2 * 4 + g)
                pb = bank[:].bitcast(BF16)
                for c in range(4):
                    b.tr(pb[:, c * 128:c * 128 + nr], xn[0:nr, (4 * g + c) * 128:(4 * g + c + 1) * 128],
                         ident[0:nr, 0:nr], ["xn", "ident"], [bk])
                b.cp(("act", "dve")[g % 2], xn2T[:, 4 * g:4 * g + 4, r0:r0 + nr],
                     pb[:, 0:512].rearrange("p (c t) -> p c t", c=4)[:, :, 0:nr], [bk], ["xn2T"])
        S.barrier()

        A.reset(m_persist)
        hT = A.alloc([128, NFC, NTO], BF16)
        m_hT = A.mark()
        Wu = [A.alloc([128, 16, 256], BF16) for _ in range(2)]
        Wg = [A.alloc([128, 16, 256], BF16) for _ in range(2)]
        gtp = [A.alloc([128, NTK + 2], F32) for _ in range(2)]
        ub = [A.alloc([128, NTK], F32) for _ in range(2)]
        acc = [A.alloc([128, NTK], F32) for _ in range(2)]
        groups = [(0, 353), (353, 353), (706, 352)]
        for blk in range(NFC // 2):
            w2i = blk % 2
            ku, kg = "Wu%d" % w2i, "Wg%d" % w2i
            b.dma("pool", Wu[w2i], wup[:, blk * 256:(blk + 1) * 256].rearrange("(kc p) n -> p kc n", p=128), [], [ku])
            b.dma("pool", Wg[w2i], wgate[:, blk * 256:(blk + 1) * 256].rearrange("(kc p) n -> p kc n", p=128), [], [kg])
            for fi in range(2):
                fc = 2 * blk + fi
                f2 = fc % 2
                kgt, kub, kac = "gtp%d" % f2, "ub%d" % f2, "acc%d" % f2
                G, U, AC = gtp[f2], ub[f2], acc[f2]
                for tg, (c0, n) in enumerate(groups):
                    pi = (fc * 3 + tg) % 4
                    UB, GBk = PS[2 * pi], PS[2 * pi + 1]
                    uk, gk = "ps%d" % (2 * pi), "ps%d" % (2 * pi + 1)
                    for kc in range(16):
                        b.mm(UB[:, 0:n], Wu[w2i][:, kc, fi * 128:(fi + 1) * 128], xn2T[:, kc, c0:c0 + n], [ku, "xn2T"],
                             [uk], start=kc == 0, stop=kc == 15)
                    for kc in range(16):
                        b.mm(GBk[:, 0:n], Wg[w2i][:, kc, fi * 128:(fi + 1) * 128], xn2T[:, kc, c0:c0 + n], [kg, "xn2T"],
                             [gk], start=kc == 0, stop=kc == 15)
                    b.cp("dve", U[:, c0:c0 + n], UB[:, 0:n], [uk], [kub])
                    if tg < 2:
                        b.cp("act", G[:, c0:c0 + n], GBk[:, 0:n], [gk], [kgt])
                    else:
                        b.cp("act", G[:, 706:1026], GBk[:, 0:320], [gk], [kgt])
                        b.cp("act", G[:, 1028:1060], GBk[:, 320:352], [gk], [kgt])
                b.ts("pool", G[:, 0:2], G[:, 0:2], hs[:, 0:1], ALU.mult, [kgt, "hs"], [kgt])
                b.cp("pool", G[:, 1026:1028], c0t[:, fc, :], [kgt, "c0t"], [kgt])
                b.cp("pool", cstp[:, fc, :], G[:, 1024:1026], [kgt], ["cstp"])
                b.cp("pool", csts[:, fc, :], G[:, 1058:1060], [kgt], ["csts"])
                wcol = lambda i: pf[:, i * NFC + fc:i * NFC + fc + 1]
                b.ts("dve", AC[:, 0:NTK], G[:, 2:NTK + 2], wcol(2), ALU.mult, [kgt, "pf"], [kac], s2=wcol(3), op1=ALU.add)
                b.stt("dve", AC[:, 0:NTK], G[:, 1:NTK + 1], wcol(1), AC[:, 0:NTK], ALU.mult, ALU.add, [kgt, "pf", kac],
                      [kac])
                b.stt("dve", AC[:, 0:NTK], G[:, 0:NTK], wcol(0), AC[:, 0:NTK], ALU.mult, ALU.add, [kgt, "pf", kac],
                      [kac])
                b.act(AC[:, 0:NTK], AC[:, 0:NTK], AF.Silu, [kac], [kac])
                b.tt("dve", hT[:, fc, 0:1024], AC[:, 0:1024], U[:, 2:1026], ALU.mult, [kac, kub], ["hT%d" % fc])
                b.tt("pool", hT[:, fc, 1024:1056], AC[:, 1026:1058], U[:, 1026:1058], ALU.mult, [kac, kub],
                     ["hT%d" % fc])
        b.dma("sp", convp, cstp, ["cstp"], [])
        b.dma("sp", convs, csts, ["csts"], [])
        S.barrier()

        A.reset(m_hT)
        Wd = [A.alloc([128, NFC, 256], BF16) for _ in range(2)]
        x2s = [A.alloc([128, 256], F32) for _ in range(2)]
        yt = [A.alloc([128, 256], F32) for _ in range(2)]
        subo = [(m * 128, 128) for m in range(8)] + [(NTO - 128, 128)]
        cnt = 0
        for nb in range(8):
            w2i = nb % 2
            kd = "Wd%d" % w2i
            b.dma("pool", Wd[w2i], wdown[:, nb * 256:(nb + 1) * 256].rearrange("(fc p) n -> p fc n", p=128), [], [kd])
            for m, (r0, nr) in enumerate(subo):
                i2 = cnt % 2
                bank = PS[cnt % 8]
                bk = "ps%d" % (cnt % 8)
                cnt += 1
                b.dma("sp", x2s[i2][0:nr], x2d[2 + r0:2 + r0 + nr, nb * 256:(nb + 1) * 256], [], ["x2s%d" % i2])
                for fc in range(NFC):
                    b.mm(bank[0:nr, 0:256], hT[:, fc, r0:r0 + nr], Wd[w2i][:, fc, :], [kd], [bk], start=fc == 0,
                         stop=fc == NFC - 1)
                b.tt("dve", yt[i2][0:nr], bank[0:nr, 0:256], x2s[i2][0:nr], ALU.add, [bk, "x2s%d" % i2], ["yt%d" % i2])
                b.dma("sp", y[r0:r0 + nr, nb * 256:(nb + 1) * 256], yt[i2][0:nr], ["yt%d" % i2], [])
        S.emit(ctx)
    return nc


_CACHE = {}


def _progs():
    if "p1" not in _CACHE:
        _CACHE["p1"] = build_phase1()
        _CACHE["p2"] = build_phase2()
    return _CACHE["p1"], _CACHE["p2"]


def _pp(cfg, base, mu_cols, inp):
    L = cfg.L
    nc2 = cfg.nc2
    pp = np.zeros((128, L["n"]), np.float32)
    mu = inp["mu_shift"][0]
    for j, c0 in enumerate(mu_cols):
        pp[:, L["mu"] + j] = mu[c0:c0 + 128]
    vecs = dict(w0=inp["w0"][0], a0=inp["a0"][0], kk=inp["k_k"][0], ka=inp["k_a"][0], rk=inp["r_k"][0].reshape(-1),
                lw=inp["lnx_w"][0], lb=inp["lnx_b"][0], sbg=inp["sb_out_g"][0].reshape(-1))
    for k, v in vecs.items():
        for c2 in range(nc2):
            pp[:, L[k] + c2] = v[base + c2 * 128:base + (c2 + 1) * 128]
    return pp


def _phase1(inp):
    p1, p2 = _progs()
    cb, cf = make_consts()
    w_in = inp["w_in"][0]
    RW = 3072
    in1 = []
    for c in range(8):
        bq, j = divmod(c, 4)
        pb, sbase = 256 * j, 128 * c
        d = {}
        d["xp"] = np.ascontiguousarray(inp["x_prompt"][bq])
        d["xs"] = np.ascontiguousarray(inp["x_sample"].reshape(NS * T_S, D))
        d["w1p_sb"] = np.ascontiguousarray(np.concatenate([w_in[:, o + pb:o + pb + 256] for o in (0, 1024, 2048)], 1))
        rw_cols_p = [RW + pb, RW + pb + 128, RW + 1024 + pb, RW + 1024 + pb + 128, RW + 2048 + pb, RW + 2048 + pb + 128,
                     RW + 3072, RW + 3200]
        d["w1p_rw"] = np.ascontiguousarray(np.concatenate([w_in[:, o:o + 128] for o in rw_cols_p], 1))
        rw_cols_s = [RW + sbase, RW + 1024 + sbase, RW + 2048 + sbase, RW + 3072, RW + 3200]
        d["w1s"] = np.ascontiguousarray(np.concatenate([w_in[:, o + sbase:o + sbase + 128] for o in (0, 1024, 2048)] +
                                                       [w_in[:, o:o + 128] for o in rw_cols_s], 1))
        kc = inp["cache_sb_k"][0][:, 2 * c:2 * c + 2]
        d["kcT"] = np.ascontiguousarray(kc.transpose(3, 1, 0, 2))
        vcc = inp["cache_sb_v"][0][:, 2 * c:2 * c + 2]
        d["vc"] = np.ascontiguousarray(vcc.transpose(0, 2, 1, 3).reshape(NS, PAST, 128))
        s0 = inp["state_rwkv"][0][:, 2 * c:2 * c + 2]
        d["st0"] = np.ascontiguousarray(s0.transpose(1, 3, 0, 2).reshape(128, NS, 64))
        sh = inp["state_rwkv_shift"][0][:, 0, :]
        d["sh0"] = np.ascontiguousarray(np.stack([sh[:, o - RW:o - RW + 128] for o in rw_cols_s], 1).transpose(2, 1, 0))
        d["g1"] = np.ascontiguousarray(inp["norm1_g"][0][None])
        qg, kg = inp["q_norm_g"][0], inp["k_norm_g"][0]
        d["qkg_p"] = np.concatenate([np.tile(qg, 4), np.tile(kg, 4)])[None].astype(np.float32)
        d["qkg_s"] = np.concatenate([np.tile(qg, 2), np.tile(kg, 2)])[None].astype(np.float32)
        d["pp_p"] = _pp(CFG_P, pb, [o - RW for o in rw_cols_p], inp)
        d["pp_s"] = _pp(CFG_S, sbase, [o - RW for o in rw_cols_s], inp)
        d["wa2_p"] = np.ascontiguousarray(np.concatenate([inp["w2"][0][:, pb:pb + 256], inp["a2"][0][:, pb:pb + 256]], 0))
        d["wa2_s"] = np.ascontiguousarray(np.concatenate([inp["w2"][0][:, sbase:sbase + 128],
                                                          inp["a2"][0][:, sbase:sbase + 128]], 0))
        d["g2_p"] = np.ascontiguousarray(inp["g2"][0][:, pb:pb + 256])
        d["g2_s"] = np.ascontiguousarray(inp["g2"][0][:, sbase:sbase + 128])
        d["cbd"] = cb
        d["cfd"] = cf
        in1.append(d)
    r1 = run_bass_kernel_spmd(p1, in1, core_ids=list(range(8))).results
    _CACHE["r1"] = r1

    f32 = np.float32
    k_prompt = np.zeros((1, 2, 16, T_P, 64), f32)
    v_prompt = np.zeros((1, 2, 16, T_P, 64), f32)
    rwkv_prompt = np.zeros((1, 2, 16, 64, 64), f32)
    shift_prompt = np.zeros((1, 2, 1, 3328), f32)
    k_sample = np.zeros((1, NS, 16, T_S, 64), f32)
    v_sample = np.zeros((1, NS, 16, T_S, 64), f32)
    rwkv_sample = np.zeros((1, NS, 16, 64, 64), f32)
    shift_sample = np.zeros((1, NS, 1, 3328), f32)
    cat_p = [np.zeros((D, T_P), ml_dtypes.bfloat16) for _ in range(2)]
    cat_s = np.zeros((D, NS * T_S), ml_dtypes.bfloat16)
    for c in range(8):
        bq, j = divmod(c, 4)
        r = r1[c]
        k_prompt[0, bq, 4 * j:4 * j + 4] = r["k_p"]
        v_prompt[0, bq, 4 * j:4 * j + 4] = r["v_p"]
        rwkv_prompt[0, bq, 4 * j:4 * j + 4] = r["s_p"].reshape(2, 2, 64, 64).transpose(0, 1, 3, 2).reshape(4, 64, 64)
        shp = r["sh_p"]
        pb = 256 * j
        for jj, o in enumerate([pb, pb + 128, 1024 + pb, 1024 + pb + 128, 2048 + pb, 2048 + pb + 128, 3072, 3200]):
            shift_prompt[0, bq, 0, o:o + 128] = shp[:, jj]
        k_sample[0, :, 2 * c:2 * c + 2] = r["k_s"].reshape(NS, T_S, 2, 64).transpose(0, 2, 1, 3)
        v_sample[0, :, 2 * c:2 * c + 2] = r["v_s"].reshape(NS, T_S, 2, 64).transpose(0, 2, 1, 3)
        rwkv_sample[0, :, 2 * c:2 * c + 2] = r["s_s"].reshape(2, 64, NS, 64).transpose(2, 0, 3, 1)
        shs = r["sh_s"]
        sbase = 128 * c
        for jj, o in enumerate([sbase, 1024 + sbase, 2048 + sbase, 3072, 3200]):
            shift_sample[0, :, 0, o:o + 128] = shs[:, jj, :].T
        cat_p[bq][256 * j:256 * j + 256] = r["mix_p"][0:256]
        cat_p[bq][1024 + 256 * j:1024 + 256 * j + 256] = r["mix_p"][256:512]
        cat_s[128 * c:128 * c + 128] = r["mix_s"][0:128]
        cat_s[1024 + 128 * c:1024 + 128 * c + 128] = r["mix_s"][128:256]

    outs1 = (k_prompt, v_prompt, rwkv_prompt, shift_prompt, k_sample, v_sample, rwkv_sample, shift_sample)
    return outs1, cat_p, cat_s


def _phase2(inp, cat_p, cat_s):
    p1, p2 = _progs()
    f32 = np.float32
    cw, cbias = inp["ffn_conv_w"][0], inp["ffn_conv_b"][0]
    ppf = np.concatenate([cw[i].reshape(NFC, 128).T for i in range(3)] + [cbias.reshape(NFC, 128).T], 1).astype(f32)
    in2 = []
    for c in range(8):
        bq, j = divmod(c, 4)
        d = {}
        x2 = np.zeros((NTK, D), f32)
        ct = np.zeros((D, NTK), ml_dtypes.bfloat16)
        t0 = 1024 * j
        if j > 0:
            x2[0:2] = inp["x_prompt"][bq, t0 - 2:t0]
            ct[:, 0:2] = cat_p[bq][:, t0 - 2:t0]
        x2[2:1026] = inp["x_prompt"][bq, t0:t0 + 1024]
        ct[:, 2:1026] = cat_p[bq][:, t0:t0 + 1024]
        x2[1026:] = inp["x_sample"][c]
        ct[:, 1026:] = cat_s[:, 32 * c:32 * c + 32]
        d["x2in"] = x2
        d["cat"] = ct
        d["wout"] = np.ascontiguousarray(inp["w_out"][0])
        d["wup"] = np.ascontiguousarray(inp["w_ffn_up"][0])
        d["wgate"] = np.ascontiguousarray(inp["w_ffn_gate"][0])
        d["wdown"] = np.ascontiguousarray(inp["w_ffn_down"][0])
        d["g2n"] = np.ascontiguousarray(inp["norm2_g"][0][None])
        d["ppf"] = ppf
        d["conv0"] = np.ascontiguousarray(inp["state_ffn_conv"][0][c].reshape(2, NFC, 128).transpose(2, 1, 0))
        d["hscale"] = np.full((128, 1), 0.0 if j == 0 else 1.0, f32)
        d["identd"] = np.eye(128, dtype=f32)
        in2.append(d)
    r2 = run_bass_kernel_spmd(p2, in2, core_ids=list(range(8))).results
    y_prompt = np.zeros((2, T_P, D), f32)
    y_sample = np.zeros((NS, T_S, D), f32)
    conv_prompt = np.zeros((1, 2, 2, DFF), f32)
    conv_sample = np.zeros((1, NS, 2, DFF), f32)
    for c in range(8):
        bq, j = divmod(c, 4)
        r = r2[c]
        y_prompt[bq, 1024 * j:1024 * j + 1024] = r["y"][0:1024]
        y_sample[c] = r["y"][1024:1056]
        if j == 3:
            conv_prompt[0, bq] = r["convp"].transpose(2, 1, 0).reshape(2, DFF)
        conv_sample[0, c] = r["convs"].transpose(2, 1, 0).reshape(2, DFF)
    return y_prompt, y_sample, conv_prompt, conv_sample


def kernel(**inp):
    inp = {k: np.asarray(v) for k, v in inp.items()}
    (k_prompt, v_prompt, rwkv_prompt, shift_prompt, k_sample, v_sample, rwkv_sample, shift_sample), cat_p, cat_s = \
        _phase1(inp)
    y_prompt, y_sample, conv_prompt, conv_sample = _phase2(inp, cat_p, cat_s)
    return (y_prompt, y_sample, k_prompt, v_prompt, rwkv_prompt, shift_prompt, conv_prompt,
            k_sample, v_sample, rwkv_sample, shift_sample, conv_sample)
```

```python
import numpy as np
from contextlib import ExitStack
import concourse.bass as bass
import concourse.mybir as mybir
from concourse.bass_utils import run_bass_kernel_spmd
import ml_dtypes

F32 = mybir.dt.float32
BF16 = mybir.dt.bfloat16
I32 = mybir.dt.int32
AF = mybir.ActivationFunctionType
ALU = mybir.AluOpType
AX = mybir.AxisListType

D = 2048
T_P = 4096
NS = 8
T_S = 32
PAST = 2048
DFF = 5632
NFC = DFF // 128
RMS_EPS = 1e-6
LNX_EPS = 1e-5 * 64
ENGS = ("pe", "act", "dve", "pool", "sp")


class Sched:
    def __init__(self, nc, n_dma_sems=(("sp", 20), ("pool", 10), ("act", 2))):
        self.nc = nc
        self.ops = []
        self.last_w = {}
        self.readers = {}
        self.n_dma_sems = dict(n_dma_sems)
        self.dnext = {e: 0 for e in self.n_dma_sems}
        self.dlast = {e: [None] * n for e, n in self.n_dma_sems.items()}
        self.elast = {e: None for e in ENGS}

    def op(self, eng, fn, reads=(), writes=(), dma=False):
        i = len(self.ops)
        raw = set()
        oth = set()
        for k in reads:
            if k in self.last_w:
                raw.add(self.last_w[k])
        for k in writes:
            if k in self.last_w:
                oth.add(self.last_w[k])
            for r in self.readers.get(k, ()):
                oth.add(r)
        o = dict(eng=eng, fn=fn, raw=raw, oth=oth - raw, dma=dma, sig=None, slot=None, prev_on_sem=None)
        if dma:
            k = self.dnext[eng]
            self.dnext[eng] = (k + 1) % self.n_dma_sems[eng]
            o["slot"] = k
            o["prev_on_sem"] = self.dlast[eng][k]
            self.dlast[eng][k] = i
        self.ops.append(o)
        self.elast[eng] = i
        for k in reads:
            self.readers.setdefault(k, []).append(i)
        for k in writes:
            self.last_w[k] = i
            self.readers[k] = []
        return i

    def barrier(self):
        deps = set(v for v in self.elast.values() if v is not None)
        for e, l in self.dlast.items():
            deps |= set(v for v in l if v is not None)
        for e in ENGS:
            i = len(self.ops)
            self.ops.append(dict(eng=e, fn=None, raw=set(deps), oth=set(), dma=False, sig=None, slot=None,
                                 prev_on_sem=None))
        self.last_w = {}
        self.readers = {}

    def _needs_wait(self, o, d, is_raw):
        od = self.ops[d]
        if od["dma"] or o["dma"] or od["eng"] != o["eng"]:
            return True
        if o["fn"] is None:
            return True
        return is_raw and o["eng"] != "pe"

    def emit(self, ctx):
        import os
        nmax = int(os.environ.get("P1_NOPS", "0"))
        if nmax:
            self.ops = self.ops[:nmax]
            for e, l in self.dlast.items():
                for k in range(len(l)):
                    cands = [i for i, o in enumerate(self.ops) if o["dma"] and o["eng"] == e and o["slot"] == k]
                    l[k] = cands[-1] if cands else None
        nc = self.nc
        ops = self.ops
        need = [False] * len(ops)
        for i, o in enumerate(ops):
            for d in o["raw"]:
                if self._needs_wait(o, d, True):
                    need[d] = True
            for d in o["oth"]:
                if self._needs_wait(o, d, False):
                    need[d] = True
        esem = {e: ctx.enter_context(nc.semaphore("s_" + e)) for e in ENGS}
        dsem = {e: [ctx.enter_context(nc.semaphore("d_%s%d" % (e, k))) for k in range(n)]
                for e, n in self.n_dma_sems.items()}
        ecount = {e: 0 for e in ENGS}
        dcount = {e: [0] * n for e, n in self.n_dma_sems.items()}
        for i, o in enumerate(ops):
            e = o["eng"]
            if o["fn"] is None:
                continue
            if o["dma"]:
                k = o["slot"]
                dcount[e][k] += 16
                o["sig"] = (dsem[e][k], dcount[e][k], ("d", e, k))
            elif need[i]:
                ecount[e] += 1
                o["sig"] = (esem[e], ecount[e], ("e", e))
        streams = {e: [] for e in ENGS}
        for i, o in enumerate(ops):
            streams[o["eng"]].append(i)
        engobj = dict(pe="tensor", act="scalar", dve="vector", pool="gpsimd", sp="sync")
        dlast = self.dlast
        with nc.Block() as block:
            def make(e):
                def body(eng):
                    waited = {}

                    def wait_for(d):
                        if ops[d]["sig"] is None:
                            return
                        sem, val, key = ops[d]["sig"]
                        if waited.get(key, 0) >= val:
                            return
                        waited[key] = val
                        eng.wait_ge(sem, val)

                    for i in streams[e]:
                        o = ops[i]
                        if o["dma"] and o["prev_on_sem"] is not None:
                            wait_for(o["prev_on_sem"])
                        for d in sorted(o["raw"]):
                            if self._needs_wait(o, d, True):
                                wait_for(d)
                        for d in sorted(o["oth"]):
                            if self._needs_wait(o, d, False):
                                wait_for(d)
                        if o["fn"] is None:
                            continue
                        ins = o["fn"](eng)
                        if o["sig"] is not None:
                            sem, val, key = o["sig"]
                            ins.then_inc(sem, 16 if o["dma"] else 1)
                    for k, d in enumerate(dlast.get(e, [])):
                        if d is not None:
                            wait_for(d)
                return body
            for e in ENGS:
                if streams[e]:
                    getattr(block, engobj[e])(make(e))


class B:
    def __init__(self, nc, S):
        self.nc = nc
        self.S = S
        self.rr = 0

    def dma(self, q, out, in_, r=(), w=(), slow=False):
        if slow:
            self.S.op(q, lambda e: e.dma_start(out=out, in_=in_, allow_slow_non_contiguous=True), r, w, dma=True)
        else:
            self.S.op(q, lambda e: e.dma_start(out=out, in_=in_), r, w, dma=True)

    def mm(self, out, lhsT, rhs, r, w, start=True, stop=True, skip=False):
        if skip:
            self.S.op("pe", lambda e: e.matmul(out, lhsT=lhsT, rhs=rhs, start=start, stop=stop, skip_group_check=True),
                      r, w)
        else:
            self.S.op("pe", lambda e: e.matmul(out, lhsT=lhsT, rhs=rhs, start=start, stop=stop), r, w)

    def tr(self, out, in_, ident, r, w):
        self.S.op("pe", lambda e: e.transpose(out, in_, ident), r, w)

    def act(self, out, in_, func, r, w, bias=None, scale=None, accum=None):
        kw = {}
        if bias is not None:
            kw["bias"] = bias
        if scale is not None:
            kw["scale"] = scale
        if accum is not None:
            kw["accum_out"] = accum
        self.S.op("act", lambda e: e.activation(out=out, in_=in_, func=func, **kw), r, w)

    def tt(self, eng, out, in0, in1, op, r, w):
        self.S.op(eng, lambda e: e.tensor_tensor(out=out, in0=in0, in1=in1, op=op), r, w)

    def ts(self, eng, out, in0, s1, op0, r, w, s2=None, op1=None):
        if s2 is None:
            self.S.op(eng, lambda e: e.tensor_scalar(out=out, in0=in0, scalar1=s1, scalar2=None, op0=op0), r, w)
        else:
            self.S.op(eng, lambda e: e.tensor_scalar(out=out, in0=in0, scalar1=s1, scalar2=s2, op0=op0, op1=op1), r, w)

    def stt(self, eng, out, in0, scalar, in1, op0, op1, r, w):
        self.S.op(eng, lambda e: e.scalar_tensor_tensor(out=out, in0=in0, scalar=scalar, in1=in1, op0=op0, op1=op1),
                  r, w)

    def cp(self, eng, out, in_, r, w):
        if eng == "act":
            self.S.op("act", lambda e: e.activation(out=out, in_=in_, func=AF.Copy), r, w)
        else:
            self.S.op(eng, lambda e: e.tensor_copy(out=out, in_=in_), r, w)

    def memset(self, eng, out, val, r, w):
        self.S.op(eng, lambda e: e.memset(out, val), r, w)

    def reduce(self, eng, out, in_, r, w):
        self.S.op(eng, lambda e: e.tensor_reduce(out=out, in_=in_, axis=AX.X, op=ALU.add), r, w)

    def scan(self, out, d0, d1, r, w):
        self.S.op("dve", lambda e: e.tensor_tensor_scan(out=out, data0=d0, data1=d1, initial=0.0, op0=ALU.mult,
                                                        op1=ALU.add), r, w)

    def rstd(self, out, in_, scale, eps, r, w):
        self.act(out, in_, AF.Ln, r, w, bias=eps, scale=scale)
        self.act(out, out, AF.Exp, w, w, scale=-0.5)


CB = dict(ident=0, trineg=128, ones=256, bdones=384, bdmean=512, maskp=640, masks=640 + 2048)
CB_N = 640 + 2048 + 256
CF = dict(maskA64=0, maskL64=128, id64=192, maskA32=256, maskL32=320, id32=352, seg64=384, seg32=896)
CF_N = 896 + 256


def make_consts():
    cb = np.zeros((128, CB_N), np.float32)
    i = np.arange(128)
    cb[:, 0:128] = np.eye(128)
    cb[:, 128:256] = -(i[:, None] >= i[None, :]).astype(np.float32)
    cb[:, 256:384] = 1.0
    blk = (i[:, None] // 64 == i[None, :] // 64).astype(np.float32)
    cb[:, 384:512] = blk
    cb[:, 512:640] = blk / 64.0
    q = np.arange(512)
    for d in range(4):
        cb[:, 640 + 512 * d: 640 + 512 * (d + 1)] = ((128 * d + i[:, None]) < q[None, :]).astype(np.float32)
    q2 = np.arange(256)
    cb[0:32, 640 + 2048:] = (i[0:32, None] < (q2[None, :] % 32)).astype(np.float32)
    cf = np.zeros((128, CF_N), np.float32)
    for C, ka, kl, ki in ((64, "maskA64", "maskL64", "id64"), (32, "maskA32", "maskL32", "id32")):
        s = np.arange(C)
        for r0 in (0, 64):
            cf[r0:r0 + C, CF[ka]:CF[ka] + C] = (s[:, None] < s[None, :])
            cf[r0:r0 + C, CF[ka] + C:CF[ka] + 2 * C] = (s[:, None] <= s[None, :])
            cf[r0:r0 + C, CF[kl]:CF[kl] + C] = (s[None, :] < s[:, None])
            cf[r0:r0 + C, CF[ki]:CF[ki] + C] = np.eye(C)
    cf[:, CF["seg64"]:CF["seg64"] + 512] = (np.arange(512) % 64 != 0)[None, :]
    cf[:, CF["seg32"]:CF["seg32"] + 256] = (np.arange(256) % 32 != 0)[None, :]
    return cb, cf


def pp_layout(nc2):
    nch = 3 * nc2 + 2
    L = {}
    o = 0
    L["mu"] = o; o += nch
    for k in ("w0", "a0", "kk", "ka", "rk", "lw", "lb", "sbg"):
        L[k] = o; o += nc2
    L["n"] = o
    return L


class Cfg:
    def __init__(self, name, nh, ntile, C, nseg):
        self.name = name
        self.nh = nh
        self.HC = nh * 64
        self.nc2 = nh // 2
        self.NT = ntile
        self.nsub = ntile // 128
        self.C = C
        self.nch = ntile // C
        self.nseg = nseg
        self.seglen = ntile // nseg
        self.nrch = 3 * self.nc2 + 2
        self.nlev = {64: 5, 32: 4}[C]
        self.L = pp_layout(self.nc2)


CFG_P = Cfg("p", 4, 512, 64, 1)
CFG_S = Cfg("s", 2, 256, 32, 8)
ARENA_BYTES = 200 * 1024


class Arena:
    def __init__(self, tile):
        self.t = tile
        self.off = 0

    def mark(self):
        return self.off

    def reset(self, m):
        self.off = m

    def alloc(self, shape, dtype):
        free = 1
        for s in shape[1:]:
            free *= s
        ncols = free * (2 if dtype == F32 else 1)
        ncols = (ncols + 15) // 16 * 16
        assert (self.off + ncols) * 2 <= ARENA_BYTES, ("arena overflow", self.off * 2, ncols * 2)
        ap = self.t[:, self.off:self.off + ncols]
        self.off += ncols
        if dtype == F32:
            ap = ap.bitcast(F32)
        ap = ap[:, 0:free]
        if len(shape) == 3:
            ap = ap.rearrange("p (a b) -> p a b", a=shape[1])
        elif len(shape) == 4:
            ap = ap.rearrange("p (a b c) -> p a b c", a=shape[1], b=shape[2])
        if shape[0] < 128:
            ap = ap[0:shape[0]]
        return ap


def build_phase1(stop=None):
    import os
    stop = stop if stop is not None else float(os.environ.get('P1_STOP', '99'))
    nc = bass.Bass("TRN2", target_bir_lowering=False)
    dt = lambda name, shape, ty, kind: nc.dram_tensor(name, shape, ty, kind=kind).ap()
    IN, OUT = "ExternalInput", "ExternalOutput"
    xp = dt("xp", [T_P, D], F32, IN)
    xs = dt("xs", [NS * T_S, D], F32, IN)
    w1p_sb = dt("w1p_sb", [D, 768], F32, IN)
    w1p_rw = dt("w1p_rw", [D, 1024], F32, IN)
    w1s = dt("w1s", [D, 1024], F32, IN)
    kcT = dt("kcT", [64, 2, NS, PAST], F32, IN)
    vc = dt("vc", [NS, PAST, 128], F32, IN)
    st0 = dt("st0", [128, NS, 64], F32, IN)
    sh0 = dt("sh0", [128, CFG_S.nrch, NS], F32, IN)
    g1 = dt("g1", [1, D], F32, IN)
    qkg_p = dt("qkg_p", [1, 512], F32, IN)
    qkg_s = dt("qkg_s", [1, 256], F32, IN)
    pp_p = dt("pp_p", [128, CFG_P.L["n"]], F32, IN)
    pp_s = dt("pp_s", [128, CFG_S.L["n"]], F32, IN)
    wa2_p = dt("wa2_p", [128, 256], F32, IN)
    wa2_s = dt("wa2_s", [128, 128], F32, IN)
    g2_p = dt("g2_p", [128, 256], F32, IN)
    g2_s = dt("g2_s", [128, 128], F32, IN)
    cbd = dt("cbd", [128, CB_N], F32, IN)
    cfd = dt("cfd", [128, CF_N], F32, IN)
    mix_p = dt("mix_p", [512, T_P], BF16, OUT)
    mix_s = dt("mix_s", [256, NS * T_S], BF16, OUT)
    k_p = dt("k_p", [4, T_P, 64], F32, OUT)
    v_p = dt("v_p", [4, T_P, 64], F32, OUT)
    s_p = dt("s_p", [2, 128, 64], F32, OUT)
    sh_p = dt("sh_p", [128, CFG_P.nrch], F32, OUT)
    k_s = dt("k_s", [NS * T_S, 128], F32, OUT)
    v_s = dt("v_s", [NS * T_S, 128], F32, OUT)
    s_s = dt("s_s", [128, NS, 64], F32, OUT)
    sh_s = dt("sh_s", [128, CFG_S.nrch, NS], F32, OUT)
    xnT_d = nc.dram_tensor("xnT_d", [8, 128, 16 * 512], BF16).ap()

    with ExitStack() as ctx:
        S = Sched(nc)
        b = B(nc, S)
        arena_t = ctx.enter_context(nc.sbuf_tensor("arena", [128, ARENA_BYTES // 2], BF16))
        A = Arena(arena_t)
        PS = [ctx.enter_context(nc.psum_tensor("ps%d" % i, [128, 512], F32)) for i in range(8)]

        cb = A.alloc([128, CB_N], BF16)
        cf = A.alloc([128, CF_N], F32)
        g1b = A.alloc([128, D], F32)
        xt = [A.alloc([128, D], F32) for _ in range(2)]
        xn = A.alloc([128, D], BF16)
        st = A.alloc([128, 32], F32)
        xnT = A.alloc([128, 16, 512], BF16)
        pp = A.alloc([128, 32], F32)
        omka = A.alloc([128, 2], F32)
        xrs_raw = A.alloc([128, 5, 256], F32)
        b.dma("pool", cb, cbd, [], ["cb"])
        b.dma("sp", cf, cfd, [], ["cf"])
        b.dma("sp", g1b, g1.to_broadcast([128, D]), [], ["g1b"])
        ident = cb[:, 0:128]
        trineg = cb[:, 128:256]
        ones = cb[:, 256:384]
        bdones = cb[:, 384:512]
        bdmean = cb[:, 512:640]
        base_mark = A.mark()
        if stop == 0:
            S.emit(ctx)
            return nc

        def bfview(ps):
            return ps[:].bitcast(BF16)

        def stage_norm_T(nsub, xrows):
            for m in range(nsub):
                i2 = m % 2
                kx = "xt%d" % i2
                b.dma("sp", xt[i2], xrows[m * 128:(m + 1) * 128, :], [], [kx])
                b.act(xn, xt[i2], AF.Square, [kx], ["xn", "ss"], accum=st[:, 0:1])
                b.rstd(st[:, 1:2], st[:, 0:1], 1.0 / D, RMS_EPS, ["ss"], ["rs"])
                b.stt("dve", xn, xt[i2], st[:, 1:2], g1b, ALU.mult, ALU.mult, [kx, "rs", "g1b"], ["xn"])
                for g in range(4):
                    bank = PS[g % 2]
                    bk = "ps%d" % (g % 2)
                    pb = bfview(bank)
                    for c in range(4):
                        b.tr(pb[:, c * 128:(c + 1) * 128], xn[:, (4 * g + c) * 128:(4 * g + c + 1) * 128], ident,
                             ["xn", "cb"], [bk])
                    b.cp(("act", "dve")[g % 2], xnT[:, 4 * g:4 * g + 4, m * 128:(m + 1) * 128],
                         pb[:, 0:512].rearrange("p (c t) -> p c t", c=4), [bk], ["xnT"])

        def load_params(cfg, sfx, pp_d, omka_needed=True):
            L = cfg.L
            b.dma("sp", pp[:, 0:L["n"]], pp_d, [], ["pp"])
            b.ts("dve", omka[:, 0:cfg.nc2], pp[:, L["ka"]:L["ka"] + cfg.nc2], -1.0, ALU.mult, ["pp"], ["omka"],
                 s2=1.0, op1=ALU.add)

        def sb_alloc():
            t = {}
            t["sqk"] = A.alloc([128, 512], F32)
            t["qkt"] = A.alloc([128, 512], F32)
            t["qn"] = A.alloc([128, 256], BF16)
            t["knf"] = [A.alloc([128, 256], F32) for _ in range(2)]
            t["knb"] = A.alloc([128, 256], BF16)
            t["vf"] = [A.alloc([128, 256], F32) for _ in range(2)]
            t["qkg"] = A.alloc([128, 512], F32)
            t["qT"] = A.alloc([64, 4, 512], BF16)
            t["ef"] = [A.alloc([128, 512], F32) for _ in range(2)]
            t["Lb"] = [A.alloc([128, 512], BF16) for _ in range(2)]
            t["attn"] = [A.alloc([128, 512], BF16) for _ in range(2)]
            t["Cc"] = A.alloc([128, 512], F32)
            t["of"] = A.alloc([128, 512], F32)
            t["osq"] = A.alloc([128, 512], BF16)
            t["rso"] = A.alloc([128, 512], F32)
            t["onb"] = A.alloc([128, 512], BF16)
            return t

        def stage_sb_tok(cfg, t, W1, m, kT_dst, V_dst, k_out, v_out):
            HC, nh = cfg.HC, cfg.nh
            tl = slice(m * 128, (m + 1) * 128)
            i2 = m % 2
            GA, GB, R0 = PS[0], PS[1], PS[7]
            for kc in range(16):
                b.mm(GA[:, 0:2 * HC], xnT[:, kc, tl], W1[:, kc, 0:2 * HC], ["xnT", "W1_%d" % kc], ["ps0"], start=kc == 0,
                     stop=kc == 15)
            for kc in range(16):
                b.mm(GB[:, 0:HC], xnT[:, kc, tl], W1[:, kc, 2 * HC:3 * HC], ["xnT", "W1_%d" % kc], ["ps1"], start=kc == 0,
                     stop=kc == 15)
            b.act(t["sqk"][:, 0:2 * HC], GA[:, 0:2 * HC], AF.Square, ["ps0"], ["sqk"])
            b.reduce("dve", st[:, 4:4 + 2 * nh], t["sqk"][:, 0:2 * HC].rearrange("p (h d) -> p h d", d=64), ["sqk"],
                     ["ssqk"])
            b.rstd(st[:, 4:4 + 2 * nh], st[:, 4:4 + 2 * nh], 1.0 / 64, RMS_EPS, ["ssqk"], ["ssqk"])
            b.tt("dve", t["qkt"][:, 0:2 * HC].rearrange("p (h d) -> p h d", d=64),
                 GA[:, 0:2 * HC].rearrange("p (h d) -> p h d", d=64),
                 st[:, 4:4 + 2 * nh].unsqueeze(2).to_broadcast([128, 2 * nh, 64]), ALU.mult, ["ps0", "ssqk"], ["qkt"])
            b.tt("pool", t["qn"][:, 0:HC], t["qkt"][:, 0:HC], t["qkg"][:, 0:HC], ALU.mult, ["qkt", "qkg"], ["qn"])
            b.tt("dve", t["knf"][i2][:, 0:HC], t["qkt"][:, HC:2 * HC], t["qkg"][:, HC:2 * HC], ALU.mult,
                 ["qkt", "qkg"], ["knf%d" % i2])
            b.cp("pool", t["knb"][:, 0:HC], t["knf"][i2][:, 0:HC], ["knf%d" % i2], ["knb"])
            b.dma("sp", k_out, t["knf"][i2][:, 0:HC] if cfg is CFG_S else
                  t["knf"][i2][:, 0:HC].rearrange("p (h d) -> p h d", d=64), ["knf%d" % i2], [])
            pb = bfview(R0)
            for h in range(nh):
                b.tr(pb[0:64, h * 128:(h + 1) * 128], t["qn"][:, h * 64:(h + 1) * 64], ident, ["qn", "cb"], ["ps7"])
                b.tr(pb[0:64, (nh + h) * 128:(nh + h + 1) * 128], t["knb"][:, h * 64:(h + 1) * 64], ident,
                     ["knb", "cb"], ["ps7"])
            b.cp("act", t["qT"][:, 0:nh, tl], pb[0:64, 0:nh * 128].rearrange("p (c t) -> p c t", t=128),
                 ["ps7"], ["qT"])
            b.cp("act", kT_dst, pb[0:64, nh * 128:2 * nh * 128].rearrange("p (c t) -> p c t", t=128), ["ps7"], ["kT"])
            b.cp("act", t["vf"][i2][:, 0:HC], GB[:, 0:HC], ["ps1"], ["vf%d" % i2])
            if V_dst is not None:
                b.cp("pool", V_dst, t["vf"][i2][:, 0:HC], ["vf%d" % i2], ["Vres"])
            b.dma("sp", v_out, t["vf"][i2][:, 0:HC] if cfg is CFG_S else
                  t["vf"][i2][:, 0:HC].rearrange("p (h d) -> p h d", d=64), ["vf%d" % i2],
                  ["v_out"] if cfg is CFG_S else [])

        def attn_A(t, it):
            i2, kr, ncol = it["i2"], it["kr"], it["ncol"]
            Z = PS[2 + i2]
            zk = "ps%d" % (2 + i2)
            ef, Lb = t["ef"][i2], t["Lb"][i2]
            for (l, r, c0, n) in it["zm"]:
                b.mm(Z[0:kr, c0:c0 + n], l, r, it["zkeys"], [zk])
            b.act(ef[:, 0:ncol], Z[:, 0:ncol], AF.Exp, [zk], ["ef%d" % i2])
            b.act(Lb[:, 0:ncol], ef[:, 0:ncol], AF.Ln, ["ef%d" % i2], ["Lb%d" % i2], bias=1.0)
            if it["mask"] is not None:
                b.tt("dve", Lb[:, 0:ncol], Lb[:, 0:ncol], it["mask"], ALU.mult, ["Lb%d" % i2, "cb"], ["Lb%d" % i2])

        def attn_B(t, it):
            i2, kr, ncol = it["i2"], it["kr"], it["ncol"]
            first, last = it["step"] == 0, it["step"] == it["nsteps"] - 1
            ab = (4, 1)[i2]
            ak = "ps%d" % ab
            Ab, Bb = PS[ab], PS[5]
            ef, Lb, attn, Cc = t["ef"][i2], t["Lb"][i2], t["attn"][i2], t["Cc"]
            zmms = it["zm"]
            multi = len(zmms) > 1
            for zi, (l, r, c0, n) in enumerate(zmms):
                b.mm(Ab[0:kr, c0:c0 + n], l, r, it["zkeys"], [ak], start=zi == 0, stop=False, skip=multi)
            b.mm(Ab[0:kr, 0:ncol], trineg[0:kr, 0:kr], Lb[0:kr, 0:ncol], ["cb", "Lb%d" % i2], [ak], start=False,
                 stop=True, skip=multi)
            if not last:
                b.mm(Bb[:, 0:ncol], ones[0:kr, :], Lb[0:kr, 0:ncol], ["cb", "Lb%d" % i2], ["ps5"])
            if first:
                b.act(attn[:, 0:ncol], Ab[:, 0:ncol], AF.Exp, [ak], ["attn%d" % i2])
            else:
                b.tt("dve", ef[:, 0:ncol], Ab[:, 0:ncol], Cc[:, 0:ncol], ALU.subtract, [ak, "Cc"],
                     ["ef%d" % i2])
                b.act(attn[:, 0:ncol], ef[:, 0:ncol], AF.Exp, ["ef%d" % i2], ["attn%d" % i2])
            if it["mask"] is not None:
                b.tt("dve", attn[:, 0:ncol], attn[:, 0:ncol], it["mask"], ALU.mult, ["attn%d" % i2, "cb"],
                     ["attn%d" % i2])
            if not last:
                if first:
                    b.cp("dve", Cc[:, 0:ncol], Bb[:, 0:ncol], ["ps5"], ["Cc"])
                else:
                    b.tt("dve", Cc[:, 0:ncol], Cc[:, 0:ncol], Bb[:, 0:ncol], ALU.add, ["ps5", "Cc"], ["Cc"])

        def attn_C(t, it):
            i2, kr = it["i2"], it["kr"]
            first, last = it["step"] == 0, it["step"] == it["nsteps"] - 1
            multi = len(it["zm"]) > 1
            attn = t["attn"][i2]
            for ai, (lv, c0, n, o_ap, st_) in enumerate(it["av"]):
                b.mm(o_ap, lv, attn[0:kr, c0:c0 + n], it["vkeys"] + ["attn%d" % i2], [it["okey"]], start=first and st_,
                     stop=last, skip=multi)

        def attn_run(t, items, finish_after):
            n = len(items)
            for i, it in enumerate(items):
                it["i2"] = i % 2
            attn_A(t, items[0])
            if n > 1:
                attn_A(t, items[1])
            attn_B(t, items[0])
            for i in range(n):
                if i + 2 < n:
                    attn_A(t, items[i + 2])
                if i + 1 < n:
                    attn_B(t, items[i + 1])
                attn_C(t, items[i])
                if i in finish_after:
                    finish_after[i]()

        def attn_finish(t, ncols, sbg_col, out_dram, ob=6):
            O0 = PS[ob]
            GA = PS[0]
            b.cp("act", t["of"][:, 0:ncols], O0[:, 0:ncols], ["ps%d" % ob], ["of"])
            b.act(t["osq"][:, 0:ncols], t["of"][:, 0:ncols], AF.Square, ["of"], ["osq"])
            b.mm(GA[:, 0:ncols], bdones, t["osq"][:, 0:ncols], ["cb", "osq"], ["ps0"])
            b.rstd(t["rso"][:, 0:ncols], GA[:, 0:ncols], 1.0 / 64, RMS_EPS, ["ps0"], ["rso"])
            b.stt("dve", t["onb"][:, 0:ncols], t["of"][:, 0:ncols], sbg_col, t["rso"][:, 0:ncols], ALU.mult, ALU.mult,
                  ["of", "rso", "pp"], ["onb"])
            b.dma("sp", out_dram, t["onb"][:, 0:ncols], ["onb"], [])

        def load_qkg(t, cfg, qkg_d):
            HC = cfg.HC
            b.dma("sp", t["qkg"][:, 0:2 * HC], qkg_d.to_broadcast([128, 2 * HC]), [], ["qkg"])
            b.ts("dve", t["qkg"][:, 0:HC], t["qkg"][:, 0:HC], 0.125, ALU.mult, ["qkg"], ["qkg"])

        FN = ("b0", "b1", "b2", "b3", "b4", "b5", "b6", "b7", "b8", "b9")

        def rw_alloc():
            t = {}
            t["xrj"] = [A.alloc([128, 520], F32) for _ in range(2)]
            t["hal"] = A.alloc([128, 8, 8], F32)
            t["hal2"] = A.alloc([128, 8, 8], F32)
            t["xsf"] = A.alloc([128, 8, 512], F32)
            t["lora_in"] = A.alloc([128, 512], BF16)
            t["sgb"] = A.alloc([128, 512], BF16)
            t["wa2"] = A.alloc([128, 256], BF16)
            t["g2t"] = A.alloc([128, 256], BF16)
            for n in FN:
                t[n] = A.alloc([128, 512], F32)
            t["AR"] = A.alloc([128, 8, 2, 64], BF16)
            for n in ("Btb", "Ktb", "Vtb", "sqb", "yob"):
                t[n] = A.alloc([128, 512], BF16)
            for n in ("Btm", "Ktm", "Vtm"):
                t[n] = A.alloc([128, 8, 64], BF16)
            t["MTb"] = A.alloc([128, 8, 128], BF16)
            t["MTk"] = A.alloc([128, 8, 128], BF16)
            for n in ("Uj", "Lj", "Xj"):
                t[n] = [A.alloc([128, 8, 64], BF16) for _ in range(2)]
            t["Wb"] = A.alloc([128, 64], BF16)
            t["Ub"] = A.alloc([128, 64], BF16)
            t["STf"] = [A.alloc([128, 64], F32) for _ in range(8)]
            t["STb"] = [A.alloc([128, 64], BF16) for _ in range(8)]
            return t

        def stage_rwkv(cfg, t, get_raw, state_of_chunk, mix_dst):
            NT, C, nch, nc2, L = cfg.NT, cfg.C, cfg.nch, cfg.nc2, cfg.L
            nseg, sl = cfg.nseg, cfg.seglen
            xsf = t["xsf"]
            GA, GB, R0 = PS[0], PS[1], PS[7]
            for j in range(cfg.nrch):
                raw, rk = get_raw(j)
                xj = t["xrj"][j % 2][:, 0:nseg * (sl + 1)].rearrange("p (s t) -> p s t", s=nseg)
                kj = "xrj%d" % (j % 2)
                b.cp("act", xj[:, :, 1:sl + 1], raw.rearrange("p (s t) -> p s t", s=nseg), [rk], [kj])
                b.cp("pool", xj[:, :, 0:1], t["hal"][:, j, 0:nseg].unsqueeze(2), ["hal"], [kj])
                b.tt("dve", t["b9"][:, 0:NT].rearrange("p (s t) -> p s t", s=nseg), xj[:, :, 0:sl], xj[:, :, 1:sl + 1],
                     ALU.subtract, [kj], ["b9"])
                b.stt("dve", xsf[:, j, 0:NT].rearrange("p (s t) -> p s t", s=nseg),
                      t["b9"][:, 0:NT].rearrange("p (s t) -> p s t", s=nseg), pp[:, L["mu"] + j:L["mu"] + j + 1],
                      xj[:, :, 1:sl + 1], ALU.mult, ALU.add, ["b9", kj, "pp"], ["xs%d" % j])
                b.cp("pool", t["hal2"][:, j, 0:nseg].unsqueeze(2), xj[:, :, sl:sl + 1], [kj], ["hal2"])
                if cfg is CFG_P:
                    b.cp("pool", t["hal"][:, j, 0:1], t["hal2"][:, j, 0:1], ["hal2"], ["hal"])
            jw, jg = 3 * nc2, 3 * nc2 + 1
            b.act(t["lora_in"][0:64, 0:NT], xsf[0:64, jw, 0:NT], AF.Tanh, ["xs%d" % jw], ["lora_in"])
            b.cp("pool", t["lora_in"][64:128, 0:NT], xsf[64:128, jw, 0:NT], ["xs%d" % jw], ["lora_in"])
            b.act(t["sgb"][:, 0:NT], xsf[:, jg, 0:NT], AF.Sigmoid, ["xs%d" % jg], ["sgb"])
            seg = cf[:, CF["seg%d" % C]:CF["seg%d" % C] + NT]
            maskA = cf[0:C, CF["maskA%d" % C]:CF["maskA%d" % C] + 2 * C]
            maskL = cf[0:C, CF["maskL%d" % C]:CF["maskL%d" % C] + C]
            idC = cf[0:C, CF["id%d" % C]:CF["id%d" % C] + C]
            AR, Btb, Ktb, Vtb, sqb = t["AR"], t["Btb"], t["Ktb"], t["Vtb"], t["sqb"]
            for c2 in range(nc2):
                jr, jk, jv = c2, nc2 + c2, 2 * nc2 + c2
                cs = slice(c2 * 128, (c2 + 1) * 128)
                col = lambda name: pp[:, L[name] + c2:L[name] + c2 + 1]
                f = {n: t[n][:, 0:NT] for n in FN}
                xs_r, xs_k, xs_v = xsf[:, jr, 0:NT], xsf[:, jk, 0:NT], xsf[:, jv, 0:NT]
                kr_, kk_, kv_ = "xs%d" % jr, "xs%d" % jk, "xs%d" % jv
                P1, P2, P3 = GA[:, 0:NT], GB[:, 0:NT], R0[:, 0:NT]
                sgu, a_, g_, lw, cl, clm, eP, eM, ePx, tmp = (f["b0"], f["b1"], f["b2"], f["b3"], f["b4"], f["b5"],
                                                              f["b6"], f["b7"], f["b8"], f["b9"])
                b.mm(P1, t["wa2"][0:64, cs], t["lora_in"][0:64, 0:NT], ["wa2", "lora_in"], ["ps0"])
                b.mm(P2, t["wa2"][64:128, cs], t["lora_in"][64:128, 0:NT], ["wa2", "lora_in"], ["ps1"])
                b.mm(P3, t["g2t"][:, cs], t["sgb"][:, 0:NT], ["g2t", "sgb"], ["ps7"])
                b.act(sgu, P1, AF.Sigmoid, ["ps0", "pp"], ["b0"], bias=col("w0"))
                b.act(a_, P2, AF.Sigmoid, ["ps1", "pp"], ["b1"], bias=col("a0"))
                b.cp("act", g_, P3, ["ps7"], ["b2"])
                b.ts("pool", lw, sgu, -0.6065306597126334, ALU.mult, ["b0"], ["b3"])
                b.scan(cl, seg, lw, ["cf", "b3"], ["b4"])
                b.tt("pool", clm, cl, lw, ALU.subtract, ["b4", "b3"], ["b5"])
                b.act(eP, cl, AF.Exp, ["b4"], ["b6"])
                b.act(eM, cl, AF.Exp, ["b4"], ["b7"], scale=-1.0)
                b.act(ePx, clm, AF.Exp, ["b5"], ["b8"])
                kkr = f["b3"]
                b.ts("dve", kkr, xs_k, col("kk"), ALU.mult, [kk_, "pp"], ["b3"])
                b.act(sqb[:, 0:NT], kkr, AF.Square, ["b3"], ["sqb"])
                b.mm(P1, bdones, sqb[:, 0:NT], ["cb", "sqb"], ["ps0"])
                b.ts("dve", tmp, P1, 1e-24, ALU.max, ["ps0"], ["b9"])
                b.rstd(tmp, tmp, 1.0, 0.0, ["b9"], ["b9"])
                kk = f["b4"]
                b.tt("dve", kk, kkr, tmp, ALU.mult, ["b3", "b9"], ["b4"])
                tt_ = f["b5"]
                b.ts("dve", tt_, a_, col("ka"), ALU.mult, ["b1", "pp", "omka"], ["b5"], s2=omka[:, c2:c2 + 1],
                     op1=ALU.add)
                kp = f["b5"]
                b.tt("dve", kp, xs_k, tt_, ALU.mult, [kk_, "b5"], ["b5"])
                ARv = AR[:, 0:nch, :, 0:C]
                b.stt("dve", ARv[:, :, 0, :], kk.rearrange("p (c t) -> p c t", t=C), -1.0,
                      ePx.rearrange("p (c t) -> p c t", t=C), ALU.mult, ALU.mult, ["b4", "b8"], ["AR"])
                b.tt("pool", ARv[:, :, 1, :], xs_r.rearrange("p (c t) -> p c t", t=C),
                     eP.rearrange("p (c t) -> p c t", t=C), ALU.mult, [kr_, "b6"], ["AR"])
                b.tt("dve", tmp, kk, a_, ALU.mult, ["b4", "b1"], ["b9"])
                b.tt("dve", Btb[:, 0:NT], tmp, eM, ALU.mult, ["b9", "b7"], ["Btb"])
                b.tt("pool", Ktb[:, 0:NT], kp, eM, ALU.mult, ["b5", "b7"], ["Ktb"])
                b.cp("pool", Vtb[:, 0:NT], xs_v, [kv_], ["Vtb"])
                b.tt("dve", tmp, xs_r, kp, ALU.mult, [kr_, "b5"], ["b9"])
                b.ts("dve", sqb[:, 0:NT], tmp, col("rk"), ALU.mult, ["b9", "pp"], ["sqb"])
                b.mm(P2, bdones, sqb[:, 0:NT], ["cb", "sqb"], ["ps1"])
                bonus = f["b8"]
                b.tt("dve", bonus, P2, xs_v, ALU.mult, ["ps1", kv_], ["b8"])
                pbR = bfview(R0)
                PR = slice(0, 128)
                npr = 128
                HO = [slice(hh * 64, hh * 64 + C) for hh in range(2)]
                HR = [slice(hh * 64, hh * 64 + 64) for hh in range(2)]
                for (src, dst, sk, dk, ev) in ((Btb, t["Btm"], "Btb", "Btm", "act"), (Ktb, t["Ktm"], "Ktb", "Ktm", "dve"),
                                               (Vtb, t["Vtm"], "Vtb", "Vtm", "act")):
                    for hh in range(2):
                        for ch in range(nch):
                            b.tr(pbR[HO[hh], ch * 64:(ch + 1) * 64], src[HR[hh], ch * C:(ch + 1) * C],
                                 ident[HR[hh], HR[hh]], [sk, "cb"], ["ps7"])
                    b.cp(ev, dst[PR, 0:nch, :], pbR[PR, 0:nch * 64].rearrange("p (c f) -> p c f", f=64), ["ps7"], [dk])
                MTb, MTk, Uj, Lj, Xj = t["MTb"], t["MTk"], t["Uj"], t["Lj"], t["Xj"]
                mA = cf[PR, CF["maskA%d" % C]:CF["maskA%d" % C] + 2 * C]
                mL = cf[PR, CF["maskL%d" % C]:CF["maskL%d" % C] + C]
                mI = cf[PR, CF["id%d" % C]:CF["id%d" % C] + C]
                for g0 in range(0, nch, 4):
                    for hh in range(2):
                        for ch in range(g0, g0 + 4):
                            q4 = ch - g0
                            arf = AR[HR[hh], ch, :, 0:C]
                            b.mm(GA[HO[hh], q4 * 2 * C:(q4 + 1) * 2 * C].rearrange("p (a t) -> p a t", a=2),
                                 Btb[HR[hh], ch * C:(ch + 1) * C], arf, ["Btb", "AR"], ["ps0"])
                            b.mm(GB[HO[hh], q4 * 2 * C:(q4 + 1) * 2 * C].rearrange("p (a t) -> p a t", a=2),
                                 Ktb[HR[hh], ch * C:(ch + 1) * C], arf, ["Ktb", "AR"], ["ps1"])
                    b.tt("dve", MTb[PR, g0:g0 + 4, 0:2 * C], GA[PR, 0:8 * C].rearrange("p (q t) -> p q t", q=4),
                         mA.unsqueeze(1).to_broadcast([npr, 4, 2 * C]), ALU.mult, ["ps0", "cf"], ["MTb"])
                    b.tt("dve", MTk[PR, g0:g0 + 4, 0:2 * C], GB[PR, 0:8 * C].rearrange("p (q t) -> p q t", q=4),
                         mA.unsqueeze(1).to_broadcast([npr, 4, 2 * C]), ALU.mult, ["ps1", "cf"], ["MTk"])
                for hh in range(2):
                    for ch in range(nch):
                        b.mm(R0[HO[hh], ch * C:(ch + 1) * C], AR[HR[hh], ch, 0, 0:C], Btb[HR[hh], ch * C:(ch + 1) * C],
                             ["AR", "Btb"], ["ps7"])
                b.tt("dve", Lj[0][PR, 0:nch, 0:C], R0[PR, 0:nch * C].rearrange("p (q t) -> p q t", t=C),
                     mL.unsqueeze(1).to_broadcast([npr, nch, C]), ALU.mult, ["ps7", "cf"], ["Lj0"])
                b.cp("pool", Uj[0][PR, 0:nch, 0:C], MTb[PR, 0:nch, 0:C], ["MTb"], ["Uj0"])
                b.tt("dve", Xj[0][PR, 0:nch, 0:C], MTb[PR, 0:nch, 0:C], mI.unsqueeze(1).to_broadcast([npr, nch, C]),
                     ALU.add, ["MTb", "cf"], ["Xj0"])
                for lv in range(1, cfg.nlev + 1):
                    pi, ci = (lv - 1) % 2, lv % 2
                    Up, Lp, Un, Ln_, Xp, Xn = Uj[pi], Lj[pi], Uj[ci], Lj[ci], Xj[pi], Xj[ci]
                    bl, bu, bx = ((0, 1, 7), (2, 3, 4))[lv % 2]
                    pl, pu, px = PS[bl], PS[bu], PS[bx]
                    plk, puk, pxk = "ps%d" % bl, "ps%d" % bu, "ps%d" % bx
                    for hh in range(2):
                        for ch in range(nch):
                            b.mm(pl[HO[hh], ch * C:(ch + 1) * C], Up[HO[hh], ch, 0:C], Lp[HO[hh], ch, 0:C],
                                 ["Uj%d" % pi, "Lj%d" % pi], [plk])
                    b.cp("act", Ln_[PR, 0:nch, 0:C], pl[PR, 0:nch * C].rearrange("p (q t) -> p q t", t=C), [plk],
                         ["Lj%d" % ci])
                    if lv < cfg.nlev:
                        for hh in range(2):
                            for ch in range(nch):
                                b.mm(pu[HO[hh], ch * C:(ch + 1) * C], Lp[HO[hh], ch, 0:C], Up[HO[hh], ch, 0:C],
                                     ["Uj%d" % pi, "Lj%d" % pi], [puk])
                        b.cp("dve", Un[PR, 0:nch, 0:C], pu[PR, 0:nch * C].rearrange("p (q t) -> p q t", t=C), [puk],
                             ["Uj%d" % ci])
                    for hh in range(2):
                        for ch in range(nch):
                            b.mm(px[HO[hh], ch * C:(ch + 1) * C], Ln_[HO[hh], ch, 0:C], Xp[HO[hh], ch, 0:C],
                                 ["Lj%d" % ci, "Xj%d" % pi], [pxk])
                    b.tt("dve", Xn[PR, 0:nch, 0:C], px[PR, 0:nch * C].rearrange("p (q t) -> p q t", t=C),
                         Xp[PR, 0:nch, 0:C], ALU.add, [pxk, "Xj%d" % pi], ["Xj%d" % ci])
                Xf = Xj[cfg.nlev % 2]
                xk = "Xj%d" % (cfg.nlev % 2)
                PY = PS[6]
                Wb, Ub, Btm, Ktm, Vtm = t["Wb"], t["Ub"], t["Btm"], t["Ktm"], t["Vtm"]
                for ch in range(nch):
                    si = state_of_chunk(c2, ch)
                    sf, sb_, skf, skb = t["STf"][si], t["STb"][si], "STf%d" % si, "STb%d" % si
                    for hh in range(2):
                        b.mm(GA[HO[hh], 0:64], AR[HR[hh], ch, 0, 0:C], sb_[HR[hh], :], ["AR", skb], ["ps0"], start=True,
                             stop=False)
                        b.mm(GA[HO[hh], 0:64], MTk[HO[hh], ch, 0:C], Vtm[HO[hh], ch, :], ["MTk", "Vtm"], ["ps0"],
                             start=False, stop=True)
                    b.cp("act", Wb[PR, :], GA[PR, 0:64], ["ps0"], ["Wb"])
                    for hh in range(2):
                        b.mm(GB[HO[hh], 0:64], Xf[HO[hh], ch, 0:C], Wb[HO[hh], :], [xk, "Wb"], ["ps1"])
                    b.cp("dve", Ub[PR, :], GB[PR, 0:64], ["ps1"], ["Ub"])
                    for hh in range(2):
                        yo = PY[HR[hh], ch * C:(ch + 1) * C]
                        b.mm(yo, sb_[HR[hh], :], AR[HR[hh], ch, 1, 0:C], [skb, "AR"], ["ps6"], start=True, stop=False)
                        b.mm(yo, Ub[HO[hh], :], MTb[HO[hh], ch, C:2 * C], ["Ub", "MTb"], ["ps6"], start=False, stop=False)
                        b.mm(yo, Vtm[HO[hh], ch, :], MTk[HO[hh], ch, C:2 * C], ["Vtm", "MTk"], ["ps6"], start=False,
                             stop=True)
                        so = PS[5][HR[hh], 0:64]
                        b.mm(so, Btm[HO[hh], ch, :], Ub[HO[hh], :], ["Btm", "Ub"], ["ps5"], start=True, stop=False)
                        b.mm(so, Ktm[HO[hh], ch, :], Vtm[HO[hh], ch, :], ["Ktm", "Vtm"], ["ps5"], start=False, stop=True)
                    b.tt("dve", sf, PS[5][:, 0:64], sf, ALU.add, ["ps5", skf], [skf])
                    b.ts("dve", sf, sf, t["b6"][:, ch * C + C - 1:ch * C + C], ALU.mult, [skf, "b6"], [skf])
                    b.cp("pool", sb_, sf, [skf], [skb])
                y_, yc, rs = f["b0"], f["b1"], f["b7"]
                b.cp("act", y_, PY[:, 0:NT], ["ps6"], ["b0"])
                b.cp("pool", sqb[:, 0:NT], y_, ["b0"], ["sqb"])
                b.mm(P1, bdmean, sqb[:, 0:NT], ["cb", "sqb"], ["ps0"])
                b.tt("dve", yc, y_, P1, ALU.subtract, ["b0", "ps0"], ["b1"])
                b.act(sqb[:, 0:NT], yc, AF.Square, ["b1"], ["sqb"])
                b.mm(P2, bdmean, sqb[:, 0:NT], ["cb", "sqb"], ["ps1"])
                b.rstd(rs, P2, 1.0, LNX_EPS, ["ps1"], ["b7"])
                b.tt("dve", yc, yc, rs, ALU.mult, ["b1", "b7"], ["b1"])
                b.ts("dve", yc, yc, col("lw"), ALU.mult, ["b1", "pp"], ["b1"], s2=col("lb"), op1=ALU.add)
                b.tt("pool", yc, yc, bonus, ALU.add, ["b1", "b8"], ["b1"])
                b.tt("dve", t["yob"][:, 0:NT], yc, g_, ALU.mult, ["b1", "b2"], ["yob"])
                b.dma("sp", mix_dst(c2), t["yob"][:, 0:NT], ["yob"], [])

        cfg = CFG_S
        load_params(cfg, "s", pp_s)
        W1 = A.alloc([128, 16, 1024], BF16)
        for kc in range(16):
            b.dma("pool", W1[:, kc, :], w1s[kc * 128:(kc + 1) * 128, :], [], ["W1_%d" % kc])
        t = sb_alloc()
        load_qkg(t, cfg, qkg_s)
        KcT = A.alloc([64, 2, 4, PAST], BF16)
        Vc = A.alloc([128, 4, 16, 128], BF16)
        kTs = A.alloc([64, 2, 256], BF16)
        vnew = A.alloc([32, NS, 128], BF16)
        vsb = A.alloc([128, 2, 128], BF16)
        stage_norm_T(2, xs)
        for m in range(2):
            stage_sb_tok(cfg, t, W1, m, kTs[:, :, m * 128:(m + 1) * 128], vsb[:, m, :], k_s[m * 128:(m + 1) * 128, :],
                         v_s[m * 128:(m + 1) * 128, :])
        for j in range(cfg.nrch):
            bank = PS[j % 2]
            bk = "ps%d" % (j % 2)
            for kc in range(16):
                b.mm(bank[:, 0:256], W1[:, kc, 384 + j * 128:384 + (j + 1) * 128], xnT[:, kc, 0:256], ["W1_%d" % kc, "xnT"], [bk],
                     start=kc == 0, stop=kc == 15)
            b.cp("act", xrs_raw[:, j, :], bank[:, 0:256], [bk], ["xrs_raw"])
        if stop == 1:
            S.emit(ctx)
            return nc
        for bb in range(NS):
            bank = PS[bb // 4]
            b.mm(bank[0:32, (bb % 4) * 128:(bb % 4 + 1) * 128], ident[:, (bb % 4) * 32:(bb % 4 + 1) * 32], vsb[:, bb // 4, :],
                 ["cb", "Vres"], ["ps%d" % (bb // 4)])
        for g in range(2):
            b.cp(("act", "dve")[g], vnew[0:32, 4 * g:4 * g + 4, :], PS[g][0:32, :].rearrange("p (b f) -> p b f", b=4),
                 ["ps%d" % g], ["vnew"])
        masks = cb[:, CB["masks"]:CB["masks"] + 256]
        if stop == 1.2:
            S.emit(ctx)
            return nc
        for grp in range(2):
            for b4 in range(4):
                bb = grp * 4 + b4
                b.dma("pool", KcT[:, :, b4, :], kcT[:, :, bb, :], [], ["KcT%d" % b4])
                b.dma("pool", Vc[:, b4, :, :], vc[bb].rearrange("(kb p) f -> p kb f", p=128), [], ["Vc%d" % b4])
            if stop == 1.4:
                S.emit(ctx)
                return nc
            nsteps = 17
            items = []
            for step in range(nsteps):
                kb = 16 - step
                pairs = [(hh, b4) for hh in range(2) for b4 in range(4)]
                cols = lambda b4: slice((grp * 4 + b4) * 32, (grp * 4 + b4 + 1) * 32)
                if kb == 16:
                    zm = [(kTs[:, hh, cols(b4)], t["qT"][:, hh, cols(b4)], hh * 128 + b4 * 32, 32) for hh, b4 in pairs]
                    av = [(vnew[0:32, grp * 4 + b4, hh * 64:hh * 64 + 64], hh * 128 + b4 * 32, 32,
                           PS[6][hh * 64:hh * 64 + 64, cols(b4)], b4 == 0) for hh, b4 in pairs]
                    items.append(dict(step=step, nsteps=nsteps, kr=32, ncol=256, zm=zm, av=av, mask=masks,
                                      zkeys=["kT", "qT"], vkeys=["vnew"], okey="ps6"))
                else:
                    zm = [(KcT[:, hh, b4, kb * 128:(kb + 1) * 128], t["qT"][:, hh, cols(b4)], hh * 128 + b4 * 32, 32)
                          for hh, b4 in pairs]
                    av = [(Vc[:, b4, kb, hh * 64:hh * 64 + 64], hh * 128 + b4 * 32, 32,
                           PS[6][hh * 64:hh * 64 + 64, cols(b4)], b4 == 0) for hh, b4 in pairs]
                    items.append(dict(step=step, nsteps=nsteps, kr=128, ncol=256, zm=zm, av=av, mask=None,
                                      zkeys=["KcT%d" % q for q in range(4)] + ["qT"],
                                      vkeys=["Vc%d" % q for q in range(4)], okey="ps6"))
            attn_run(t, items, {})
        attn_finish(t, 256, pp[:, cfg.L["sbg"]:cfg.L["sbg"] + 1], mix_s[0:128, :])
        if stop == 2:
            S.emit(ctx)
            return nc
        S.barrier()

        A.reset(base_mark)
        t = rw_alloc()
        b.dma("pool", t["wa2"][:, 0:128], wa2_s, [], ["wa2"])
        b.dma("pool", t["g2t"][:, 0:128], g2_s, [], ["g2t"])
        b.dma("sp", t["hal"][:, 0:5, :], sh0, [], ["hal"])
        for bb in range(NS):
            b.dma("sp", t["STf"][bb], st0[:, bb, :], [], ["STf%d" % bb])
            b.cp("pool", t["STb"][bb], t["STf"][bb], ["STf%d" % bb], ["STb%d" % bb])
        stage_rwkv(cfg, t, lambda j: (xrs_raw[:, j, :], "xrs_raw"), lambda c2, ch: ch, lambda c2: mix_s[128:256, :])
        b.dma("sp", sh_s, t["hal2"][:, 0:5, :], ["hal2"], [])
        for bb in range(NS):
            b.dma("sp", s_s[:, bb, :], t["STf"][bb], ["STf%d" % bb], [])
        if stop == 3:
            S.emit(ctx)
            return nc
        S.barrier()

        cfg = CFG_P
        A.reset(base_mark)
        load_params(cfg, "p", pp_p)
        W1 = A.alloc([128, 16, 768], BF16)
        for kc in range(16):
            b.dma("pool", W1[:, kc, :], w1p_sb[kc * 128:(kc + 1) * 128, :], [], ["W1_%d" % kc])
        t = sb_alloc()
        load_qkg(t, cfg, qkg_p)
        kTr = A.alloc([64, 4, T_P], BF16)
        Vr = A.alloc([128, 32, 256], BF16)
        for s in range(8):
            stage_norm_T(4, xp[s * 512:(s + 1) * 512, :])
            b.dma("sp", xnT_d[s], xnT.rearrange("p a b -> p (a b)"), ["xnT"], ["xnT_d%d" % s])
            for m in range(4):
                t0 = s * 512 + m * 128
                stage_sb_tok(cfg, t, W1, m, kTr[:, :, t0:t0 + 128], Vr[:, t0 // 128, :],
                             k_p[:, t0:t0 + 128, :].rearrange("h t d -> t h d"),
                             v_p[:, t0:t0 + 128, :].rearrange("h t d -> t h d"))
            items = []
            fin = {}
            for c2 in range(2):
                ob = 6 + (c2 % 2)
                for hh in range(2):
                    h = 2 * c2 + hh
                    hr = slice(hh * 64, hh * 64 + 64)
                    nsteps = 4 * s + 4
                    for step in range(nsteps):
                        kb = 4 * s + 3 - step
                        di = kb - 4 * s
                        zm = [(kTr[:, h, kb * 128:(kb + 1) * 128], t["qT"][:, h, 0:512], 0, 512)]
                        av = [(Vr[:, kb, h * 64:(h + 1) * 64], 0, 512, PS[ob][hr, 0:512], True)]
                        mk = cb[:, CB["maskp"] + 512 * di:CB["maskp"] + 512 * (di + 1)] if di >= 0 else None
                        items.append(dict(step=step, nsteps=nsteps, kr=128, ncol=512, zm=zm, av=av, mask=mk,
                                          zkeys=["kT", "qT"], vkeys=["Vres"], okey="ps%d" % ob))
                fin[len(items) - 1] = (lambda c2=c2, ob=ob, s=s: attn_finish(
                    t, 512, pp[:, cfg.L["sbg"] + c2:cfg.L["sbg"] + c2 + 1],
                    mix_p[c2 * 128:(c2 + 1) * 128, s * 512:(s + 1) * 512], ob))
            attn_run(t, items, fin)
        if stop == 4:
            S.emit(ctx)
            return nc
        S.barrier()

        A.reset(base_mark)
        W1 = A.alloc([128, 16, 1024], BF16)
        for kc in range(16):
            b.dma("pool", W1[:, kc, :], w1p_rw[kc * 128:(kc + 1) * 128, :], [], ["W1_%d" % kc])
        t = rw_alloc()
        b.dma("pool", t["wa2"], wa2_p, [], ["wa2"])
        b.dma("pool", t["g2t"], g2_p, [], ["g2t"])
        b.memset("pool", t["hal"], 0.0, [], ["hal"])
        for c2 in range(2):
            b.memset("pool", t["STf"][c2], 0.0, [], ["STf%d" % c2])
            b.memset("pool", t["STb"][c2], 0.0, [], ["STb%d" % c2])
        for s in range(8):
            b.dma("sp", xnT.rearrange("p a b -> p (a b)"), xnT_d[s], [], ["xnT"])

            def get_raw(j):
                bank = PS[j % 2]
                bk = "ps%d" % (j % 2)
                for kc in range(16):
                    b.mm(bank[:, 0:512], W1[:, kc, j * 128:(j + 1) * 128], xnT[:, kc, 0:512], ["W1_%d" % kc, "xnT"], [bk],
                         start=kc == 0, stop=kc == 15)
                return bank[:, 0:512], bk
            stage_rwkv(cfg, t, get_raw, lambda c2, ch: c2,
                       lambda c2: mix_p[256 + c2 * 128:256 + (c2 + 1) * 128, s * 512:(s + 1) * 512])
        b.dma("sp", sh_p, t["hal2"][:, :, 0], ["hal2"], [], slow=True)
        for c2 in range(2):
            b.dma("sp", s_p[c2], t["STf"][c2], ["STf%d" % c2], [])
        S.emit(ctx)
    return nc


NTK = 1058
NTO = 1056


def build_phase2():
    nc = bass.Bass("TRN2", target_bir_lowering=False)
    dt = lambda name, shape, ty, kind: nc.dram_tensor(name, shape, ty, kind=kind).ap()
    IN, OUT = "ExternalInput", "ExternalOutput"
    x2in = dt("x2in", [NTK, D], F32, IN)
    cat = dt("cat", [D, NTK], BF16, IN)
    wout = dt("wout", [D, D], F32, IN)
    wup = dt("wup", [D, DFF], F32, IN)
    wgate = dt("wgate", [D, DFF], F32, IN)
    wdown = dt("wdown", [DFF, D], F32, IN)
    g2n = dt("g2n", [1, D], F32, IN)
    ppf = dt("ppf", [128, 4 * NFC], F32, IN)
    conv0 = dt("conv0", [128, NFC, 2], F32, IN)
    hscale = dt("hscale", [128, 1], F32, IN)
    identd = dt("identd", [128, 128], F32, IN)
    y = dt("y", [NTO, D], F32, OUT)
    convp = dt("convp", [128, NFC, 2], F32, OUT)
    convs = dt("convs", [128, NFC, 2], F32, OUT)
    x2d = nc.dram_tensor("x2d", [NTK, D], F32).ap()

    with ExitStack() as ctx:
        S = Sched(nc)
        b = B(nc, S)
        arena_t = ctx.enter_context(nc.sbuf_tensor("arena", [128, ARENA_BYTES // 2], BF16))
        A = Arena(arena_t)
        PS = [ctx.enter_context(nc.psum_tensor("ps%d" % i, [128, 512], F32)) for i in range(8)]
        ident = A.alloc([128, 128], BF16)
        pf = A.alloc([128, 4 * NFC], F32)
        c0t = A.alloc([128, NFC, 2], F32)
        hs = A.alloc([128, 1], F32)
        cstp = A.alloc([128, NFC, 2], F32)
        csts = A.alloc([128, NFC, 2], F32)
        st = A.alloc([128, 8], F32)
        xn2T = A.alloc([128, 16, NTK], BF16)
        b.dma("pool", ident, identd, [], ["ident"])
        b.dma("sp", pf, ppf, [], ["pf"])
        b.dma("sp", c0t, conv0, [], ["c0t"])
        b.dma("sp", hs, hscale, [], ["hs"])
        m_persist = A.mark()

        g2b = A.alloc([128, D], F32)
        catT = A.alloc([128, 16, NTK], BF16)
        Wo = A.alloc([128, 16, D], BF16)
        xt = [A.alloc([128, D], F32) for _ in range(2)]
        x2t = [A.alloc([128, D], F32) for _ in range(2)]
        xn = A.alloc([128, D], BF16)
        b.dma("sp", g2b, g2n.to_broadcast([128, D]), [], ["g2b"])
        for kc in range(16):
            b.dma("sp", catT[:, kc, :], cat[kc * 128:(kc + 1) * 128, :], [], ["catT%d" % kc])
            b.dma("pool", Wo[:, kc, :], wout[kc * 128:(kc + 1) * 128, :], [], ["Wo%d" % kc])
        subt = [(m * 128, 128) for m in range(8)] + [(NTK - 128, 128)]
        for m, (r0, nr) in enumerate(subt):
            i2 = m % 2
            kx, k2 = "xt%d" % i2, "x2t%d" % i2
            b.dma("sp", xt[i2][0:nr], x2in[r0:r0 + nr, :], [], [kx])
            for nb in range(4):
                bank = PS[i2 * 4 + nb]
                bk = "ps%d" % (i2 * 4 + nb)
                for kc in range(16):
                    b.mm(bank[0:nr, :], catT[:, kc, r0:r0 + nr], Wo[:, kc, nb * 512:(nb + 1) * 512],
                         ["catT%d" % kc, "Wo%d" % kc], [bk], start=kc == 0, stop=kc == 15)
                b.tt("dve", x2t[i2][0:nr, nb * 512:(nb + 1) * 512], bank[0:nr, :], xt[i2][0:nr, nb * 512:(nb + 1) * 512],
                     ALU.add, [bk, kx], [k2])
            b.dma("sp", x2d[r0:r0 + nr, :], x2t[i2][0:nr], [k2], ["x2d"])
            b.act(xn[0:nr], x2t[i2][0:nr], AF.Square, [k2], ["xn", "ss"], accum=st[0:nr, 0:1])
            b.rstd(st[0:nr, 1:2], st[0:nr, 0:1], 1.0 / D, RMS_EPS, ["ss"], ["rs"])
            b.stt("dve", xn[0:nr], x2t[i2][0:nr], st[0:nr, 1:2], g2b[0:nr], ALU.mult, ALU.mult, [k2, "rs", "g2b"],
                  ["xn"])
            for g in range(4):
                bank = PS[i2 * 4 + g]
                bk = "ps%d" % (i2 * 4 + g)
                pb = bank[:].bitcast(BF16)
                for c in range(4):
                    b.tr(pb[:, c * 128:c * 128 + nr], xn[0:nr, (4 * g + c) * 128:(4 * g + c + 1) * 128],
                         ident[0:nr, 0:nr], ["xn", "ident"], [bk])
                b.cp(("act", "dve")[g % 2], xn2T[:, 4 * g:4 * g + 4, r0:r0 + nr],
                     pb[:, 0:512].rearrange("p (c t) -> p c t", c=4)[:, :, 0:nr], [bk], ["xn2T"])
        S.barrier()

        A.reset(m_persist)
        hT = A.alloc([128, NFC, NTO], BF16)
        m_hT = A.mark()
        Wu = [A.alloc([128, 16, 256], BF16) for _ in range(2)]
        Wg = [A.alloc([128, 16, 256], BF16) for _ in range(2)]
        gtp = [A.alloc([128, NTK + 2], F32) for _ in range(2)]
        ub = [A.alloc([128, NTK], F32) for _ in range(2)]
        acc = [A.alloc([128, NTK], F32) for _ in range(2)]
        groups = [(0, 353), (353, 353), (706, 352)]
        for blk in range(NFC // 2):
            w2i = blk % 2
            ku, kg = "Wu%d" % w2i, "Wg%d" % w2i
            b.dma("pool", Wu[w2i], wup[:, blk * 256:(blk + 1) * 256].rearrange("(kc p) n -> p kc n", p=128), [], [ku])
            b.dma("pool", Wg[w2i], wgate[:, blk * 256:(blk + 1) * 256].rearrange("(kc p) n -> p kc n", p=128), [], [kg])
            for fi in range(2):
                fc = 2 * blk + fi
                f2 = fc % 2
                kgt, kub, kac = "gtp%d" % f2, "ub%d" % f2, "acc%d" % f2
                G, U, AC = gtp[f2], ub[f2], acc[f2]
                for tg, (c0, n) in enumerate(groups):
                    pi = (fc * 3 + tg) % 4
                    UB, GBk = PS[2 * pi], PS[2 * pi + 1]
                    uk, gk = "ps%d" % (2 * pi), "ps%d" % (2 * pi + 1)
                    for kc in range(16):
                        b.mm(UB[:, 0:n], Wu[w2i][:, kc, fi * 128:(fi + 1) * 128], xn2T[:, kc, c0:c0 + n], [ku, "xn2T"],
                             [uk], start=kc == 0, stop=kc == 15)
                    for kc in range(16):
                        b.mm(GBk[:, 0:n], Wg[w2i][:, kc, fi * 128:(fi + 1) * 128], xn2T[:, kc, c0:c0 + n], [kg, "xn2T"],
                             [gk], start=kc == 0, stop=kc == 15)
                    b.cp("dve", U[:, c0:c0 + n], UB[:, 0:n], [uk], [kub])
                    if tg < 2:
                        b.cp("act", G[:, c0:c0 + n], GBk[:, 0:n], [gk], [kgt])
                    else:
                        b.cp("act", G[:, 706:1026], GBk[:, 0:320], [gk], [kgt])
                        b.cp("act", G[:, 1028:1060], GBk[:, 320:352], [gk], [kgt])
                b.ts("pool", G[:, 0:2], G[:, 0:2], hs[:, 0:1], ALU.mult, [kgt, "hs"], [kgt])
                b.cp("pool", G[:, 1026:1028], c0t[:, fc, :], [kgt, "c0t"], [kgt])
                b.cp("pool", cstp[:, fc, :], G[:, 1024:1026], [kgt], ["cstp"])
                b.cp("pool", csts[:, fc, :], G[:, 1058:1060], [kgt], ["csts"])
                wcol = lambda i: pf[:, i * NFC + fc:i * NFC + fc + 1]
                b.ts("dve", AC[:, 0:NTK], G[:, 2:NTK + 2], wcol(2), ALU.mult, [kgt, "pf"], [kac], s2=wcol(3), op1=ALU.add)
                b.stt("dve", AC[:, 0:NTK], G[:, 1:NTK + 1], wcol(1), AC[:, 0:NTK], ALU.mult, ALU.add, [kgt, "pf", kac],
                      [kac])
                b.stt("dve", AC[:, 0:NTK], G[:, 0:NTK], wcol(0), AC[:, 0:NTK], ALU.mult, ALU.add, [kgt, "pf", kac],
                      [kac])
                b.act(AC[:, 0:NTK], AC[:, 0:NTK], AF.Silu, [kac], [kac])
                b.tt("dve", hT[:, fc, 0:1024], AC[:, 0:1024], U[:, 2:1026], ALU.mult, [kac, kub], ["hT%d" % fc])
                b.tt("pool", hT[:, fc, 1024:1056], AC[:, 1026:1058], U[:, 1026:1058], ALU.mult, [kac, kub],
                     ["hT%d" % fc])
        b.dma("sp", convp, cstp, ["cstp"], [])
        b.dma("sp", convs, csts, ["csts"], [])
        S.barrier()

        A.reset(m_hT)
        Wd = [A.alloc([128, NFC, 256], BF16) for _ in range(2)]
        x2s = [A.alloc([128, 256], F32) for _ in range(2)]
        yt = [A.alloc([128, 256], F32) for _ in range(2)]
        subo = [(m * 128, 128) for m in range(8)] + [(NTO - 128, 128)]
        cnt = 0
        for nb in range(8):
            w2i = nb % 2
            kd = "Wd%d" % w2i
            b.dma("pool", Wd[w2i], wdown[:, nb * 256:(nb + 1) * 256].rearrange("(fc p) n -> p fc n", p=128), [], [kd])
            for m, (r0, nr) in enumerate(subo):
                i2 = cnt % 2
                bank = PS[cnt % 8]
                bk = "ps%d" % (cnt % 8)
                cnt += 1
                b.dma("sp", x2s[i2][0:nr], x2d[2 + r0:2 + r0 + nr, nb * 256:(nb + 1) * 256], [], ["x2s%d" % i2])
                for fc in range(NFC):
                    b.mm(bank[0:nr, 0:256], hT[:, fc, r0:r0 + nr], Wd[w2i][:, fc, :], [kd], [bk], start=fc == 0,
                         stop=fc == NFC - 1)
                b.tt("dve", yt[i2][0:nr], bank[0:nr, 0:256], x2s[i2][0:nr], ALU.add, [bk, "x2s%d" % i2], ["yt%d" % i2])
                b.dma("sp", y[r0:r0 + nr, nb * 256:(nb + 1) * 256], yt[i2][0:nr], ["yt%d" % i2], [])
        S.emit(ctx)
    return nc


_CACHE = {}


def _progs():
    if "p1" not in _CACHE:
        _CACHE["p1"] = build_phase1()
        _CACHE["p2"] = build_phase2()
    return _CACHE["p1"], _CACHE["p2"]


def _pp(cfg, base, mu_cols, inp):
    L = cfg.L
    nc2 = cfg.nc2
    pp = np.zeros((128, L["n"]), np.float32)
    mu = inp["mu_shift"][0]
    for j, c0 in enumerate(mu_cols):
        pp[:, L["mu"] + j] = mu[c0:c0 + 128]
    vecs = dict(w0=inp["w0"][0], a0=inp["a0"][0], kk=inp["k_k"][0], ka=inp["k_a"][0], rk=inp["r_k"][0].reshape(-1),
                lw=inp["lnx_w"][0], lb=inp["lnx_b"][0], sbg=inp["sb_out_g"][0].reshape(-1))
    for k, v in vecs.items():
        for c2 in range(nc2):
            pp[:, L[k] + c2] = v[base + c2 * 128:base + (c2 + 1) * 128]
    return pp


def _phase1(inp):
    p1, p2 = _progs()
    cb, cf = make_consts()
    w_in = inp["w_in"][0]
    RW = 3072
    in1 = []
    for c in range(8):
        bq, j = divmod(c, 4)
        pb, sbase = 256 * j, 128 * c
        d = {}
        d["xp"] = np.ascontiguousarray(inp["x_prompt"][bq])
        d["xs"] = np.ascontiguousarray(inp["x_sample"].reshape(NS * T_S, D))
        d["w1p_sb"] = np.ascontiguousarray(np.concatenate([w_in[:, o + pb:o + pb + 256] for o in (0, 1024, 2048)], 1))
        rw_cols_p = [RW + pb, RW + pb + 128, RW + 1024 + pb, RW + 1024 + pb + 128, RW + 2048 + pb, RW + 2048 + pb + 128,
                     RW + 3072, RW + 3200]
        d["w1p_rw"] = np.ascontiguousarray(np.concatenate([w_in[:, o:o + 128] for o in rw_cols_p], 1))
        rw_cols_s = [RW + sbase, RW + 1024 + sbase, RW + 2048 + sbase, RW + 3072, RW + 3200]
        d["w1s"] = np.ascontiguousarray(np.concatenate([w_in[:, o + sbase:o + sbase + 128] for o in (0, 1024, 2048)] +
                                                       [w_in[:, o:o + 128] for o in rw_cols_s], 1))
        kc = inp["cache_sb_k"][0][:, 2 * c:2 * c + 2]
        d["kcT"] = np.ascontiguousarray(kc.transpose(3, 1, 0, 2))
        vcc = inp["cache_sb_v"][0][:, 2 * c:2 * c + 2]
        d["vc"] = np.ascontiguousarray(vcc.transpose(0, 2, 1, 3).reshape(NS, PAST, 128))
        s0 = inp["state_rwkv"][0][:, 2 * c:2 * c + 2]
        d["st0"] = np.ascontiguousarray(s0.transpose(1, 3, 0, 2).reshape(128, NS, 64))
        sh = inp["state_rwkv_shift"][0][:, 0, :]
        d["sh0"] = np.ascontiguousarray(np.stack([sh[:, o - RW:o - RW + 128] for o in rw_cols_s], 1).transpose(2, 1, 0))
        d["g1"] = np.ascontiguousarray(inp["norm1_g"][0][None])
        qg, kg = inp["q_norm_g"][0], inp["k_norm_g"][0]
        d["qkg_p"] = np.concatenate([np.tile(qg, 4), np.tile(kg, 4)])[None].astype(np.float32)
        d["qkg_s"] = np.concatenate([np.tile(qg, 2), np.tile(kg, 2)])[None].astype(np.float32)
        d["pp_p"] = _pp(CFG_P, pb, [o - RW for o in rw_cols_p], inp)
        d["pp_s"] = _pp(CFG_S, sbase, [o - RW for o in rw_cols_s], inp)
        d["wa2_p"] = np.ascontiguousarray(np.concatenate([inp["w2"][0][:, pb:pb + 256], inp["a2"][0][:, pb:pb + 256]], 0))
        d["wa2_s"] = np.ascontiguousarray(np.concatenate([inp["w2"][0][:, sbase:sbase + 128],
                                                          inp["a2"][0][:, sbase:sbase + 128]], 0))
        d["g2_p"] = np.ascontiguousarray(inp["g2"][0][:, pb:pb + 256])
        d["g2_s"] = np.ascontiguousarray(inp["g2"][0][:, sbase:sbase + 128])
        d["cbd"] = cb
        d["cfd"] = cf
        in1.append(d)
    r1 = run_bass_kernel_spmd(p1, in1, core_ids=list(range(8))).results
    _CACHE["r1"] = r1

    f32 = np.float32
    k_prompt = np.zeros((1, 2, 16, T_P, 64), f32)
    v_prompt = np.zeros((1, 2, 16, T_P, 64), f32)
    rwkv_prompt = np.zeros((1, 2, 16, 64, 64), f32)
    shift_prompt = np.zeros((1, 2, 1, 3328), f32)
    k_sample = np.zeros((1, NS, 16, T_S, 64), f32)
    v_sample = np.zeros((1, NS, 16, T_S, 64), f32)
    rwkv_sample = np.zeros((1, NS, 16, 64, 64), f32)
    shift_sample = np.zeros((1, NS, 1, 3328), f32)
    cat_p = [np.zeros((D, T_P), ml_dtypes.bfloat16) for _ in range(2)]
    cat_s = np.zeros((D, NS * T_S), ml_dtypes.bfloat16)
    for c in range(8):
        bq, j = divmod(c, 4)
        r = r1[c]
        k_prompt[0, bq, 4 * j:4 * j + 4] = r["k_p"]
        v_prompt[0, bq, 4 * j:4 * j + 4] = r["v_p"]
        rwkv_prompt[0, bq, 4 * j:4 * j + 4] = r["s_p"].reshape(2, 2, 64, 64).transpose(0, 1, 3, 2).reshape(4, 64, 64)
        shp = r["sh_p"]
        pb = 256 * j
        for jj, o in enumerate([pb, pb + 128, 1024 + pb, 1024 + pb + 128, 2048 + pb, 2048 + pb + 128, 3072, 3200]):
            shift_prompt[0, bq, 0, o:o + 128] = shp[:, jj]
        k_sample[0, :, 2 * c:2 * c + 2] = r["k_s"].reshape(NS, T_S, 2, 64).transpose(0, 2, 1, 3)
        v_sample[0, :, 2 * c:2 * c + 2] = r["v_s"].reshape(NS, T_S, 2, 64).transpose(0, 2, 1, 3)
        rwkv_sample[0, :, 2 * c:2 * c + 2] = r["s_s"].reshape(2, 64, NS, 64).transpose(2, 0, 3, 1)
        shs = r["sh_s"]
        sbase = 128 * c
        for jj, o in enumerate([sbase, 1024 + sbase, 2048 + sbase, 3072, 3200]):
            shift_sample[0, :, 0, o:o + 128] = shs[:, jj, :].T
        cat_p[bq][256 * j:256 * j + 256] = r["mix_p"][0:256]
        cat_p[bq][1024 + 256 * j:1024 + 256 * j + 256] = r["mix_p"][256:512]
        cat_s[128 * c:128 * c + 128] = r["mix_s"][0:128]
        cat_s[1024 + 128 * c:1024 + 128 * c + 128] = r["mix_s"][128:256]

    outs1 = (k_prompt, v_prompt, rwkv_prompt, shift_prompt, k_sample, v_sample, rwkv_sample, shift_sample)
    return outs1, cat_p, cat_s


def _phase2(inp, cat_p, cat_s):
    p1, p2 = _progs()
    f32 = np.float32
    cw, cbias = inp["ffn_conv_w"][0], inp["ffn_conv_b"][0]
    ppf = np.concatenate([cw[i].reshape(NFC, 128).T for i in range(3)] + [cbias.reshape(NFC, 128).T], 1).astype(f32)
    in2 = []
    for c in range(8):
        bq, j = divmod(c, 4)
        d = {}
        x2 = np.zeros((NTK, D), f32)
        ct = np.zeros((D, NTK), ml_dtypes.bfloat16)
        t0 = 1024 * j
        if j > 0:
            x2[0:2] = inp["x_prompt"][bq, t0 - 2:t0]
            ct[:, 0:2] = cat_p[bq][:, t0 - 2:t0]
        x2[2:1026] = inp["x_prompt"][bq, t0:t0 + 1024]
        ct[:, 2:1026] = cat_p[bq][:, t0:t0 + 1024]
        x2[1026:] = inp["x_sample"][c]
        ct[:, 1026:] = cat_s[:, 32 * c:32 * c + 32]
        d["x2in"] = x2
        d["cat"] = ct
        d["wout"] = np.ascontiguousarray(inp["w_out"][0])
        d["wup"] = np.ascontiguousarray(inp["w_ffn_up"][0])
        d["wgate"] = np.ascontiguousarray(inp["w_ffn_gate"][0])
        d["wdown"] = np.ascontiguousarray(inp["w_ffn_down"][0])
        d["g2n"] = np.ascontiguousarray(inp["norm2_g"][0][None])
        d["ppf"] = ppf
        d["conv0"] = np.ascontiguousarray(inp["state_ffn_conv"][0][c].reshape(2, NFC, 128).transpose(2, 1, 0))
        d["hscale"] = np.full((128, 1), 0.0 if j == 0 else 1.0, f32)
        d["identd"] = np.eye(128, dtype=f32)
        in2.append(d)
    r2 = run_bass_kernel_spmd(p2, in2, core_ids=list(range(8))).results
    y_prompt = np.zeros((2, T_P, D), f32)
    y_sample = np.zeros((NS, T_S, D), f32)
    conv_prompt = np.zeros((1, 2, 2, DFF), f32)
    conv_sample = np.zeros((1, NS, 2, DFF), f32)
    for c in range(8):
        bq, j = divmod(c, 4)
        r = r2[c]
        y_prompt[bq, 1024 * j:1024 * j + 1024] = r["y"][0:1024]
        y_sample[c] = r["y"][1024:1056]
        if j == 3:
            conv_prompt[0, bq] = r["convp"].transpose(2, 1, 0).reshape(2, DFF)
        conv_sample[0, c] = r["convs"].transpose(2, 1, 0).reshape(2, DFF)
    return y_prompt, y_sample, conv_prompt, conv_sample


def kernel(**inp):
    inp = {k: np.asarray(v) for k, v in inp.items()}
    (k_prompt, v_prompt, rwkv_prompt, shift_prompt, k_sample, v_sample, rwkv_sample, shift_sample), cat_p, cat_s = \
        _phase1(inp)
    y_prompt, y_sample, conv_prompt, conv_sample = _phase2(inp, cat_p, cat_s)
    return (y_prompt, y_sample, k_prompt, v_prompt, rwkv_prompt, shift_prompt, conv_prompt,
            k_sample, v_sample, rwkv_sample, shift_sample, conv_sample)
```

```python
import numpy as np
from contextlib import ExitStack
import concourse.bass as bass
import concourse.mybir as mybir
from concourse.bass_utils import run_bass_kernel_spmd
import ml_dtypes

F32 = mybir.dt.float32
BF16 = mybir.dt.bfloat16
I32 = mybir.dt.int32
AF = mybir.ActivationFunctionType
ALU = mybir.AluOpType
AX = mybir.AxisListType

D = 2048
T_P = 4096
NS = 8
T_S = 32
PAST = 2048
DFF = 5632
NFC = DFF // 128
RMS_EPS = 1e-6
LNX_EPS = 1e-5 * 64
ENGS = ("pe", "act", "dve", "pool", "sp")


class Sched:
    def __init__(self, nc, n_dma_sems=(("sp", 20), ("pool", 10), ("act", 2))):
        self.nc = nc
        self.ops = []
        self.last_w = {}
        self.readers = {}
        self.n_dma_sems = dict(n_dma_sems)
        self.dnext = {e: 0 for e in self.n_dma_sems}
        self.dlast = {e: [None] * n for e, n in self.n_dma_sems.items()}
        self.elast = {e: None for e in ENGS}

    def op(self, eng, fn, reads=(), writes=(), dma=False):
        i = len(self.ops)
        raw = set()
        oth = set()
        for k in reads:
            if k in self.last_w:
                raw.add(self.last_w[k])
        for k in writes:
            if k in self.last_w:
                oth.add(self.last_w[k])
            for r in self.readers.get(k, ()):
                oth.add(r)
        o = dict(eng=eng, fn=fn, raw=raw, oth=oth - raw, dma=dma, sig=None, slot=None, prev_on_sem=None)
        if dma:
            k = self.dnext[eng]
            self.dnext[eng] = (k + 1) % self.n_dma_sems[eng]
            o["slot"] = k
            o["prev_on_sem"] = self.dlast[eng][k]
            self.dlast[eng][k] = i
        self.ops.append(o)
        self.elast[eng] = i
        for k in reads:
            self.readers.setdefault(k, []).append(i)
        for k in writes:
            self.last_w[k] = i
            self.readers[k] = []
        return i

    def barrier(self):
        deps = set(v for v in self.elast.values() if v is not None)
        for e, l in self.dlast.items():
            deps |= set(v for v in l if v is not None)
        for e in ENGS:
            i = len(self.ops)
            self.ops.append(dict(eng=e, fn=None, raw=set(deps), oth=set(), dma=False, sig=None, slot=None,
                                 prev_on_sem=None))
        self.last_w = {}
        self.readers = {}

    def _needs_wait(self, o, d, is_raw):
        od = self.ops[d]
        if od["dma"] or o["dma"] or od["eng"] != o["eng"]:
            return True
        if o["fn"] is None:
            return True
        return is_raw and o["eng"] != "pe"

    def emit(self, ctx):
        import os
        nmax = int(os.environ.get("P1_NOPS", "0"))
        if nmax:
            self.ops = self.ops[:nmax]
            for e, l in self.dlast.items():
                for k in range(len(l)):
                    cands = [i for i, o in enumerate(self.ops) if o["dma"] and o["eng"] == e and o["slot"] == k]
                    l[k] = cands[-1] if cands else None
        nc = self.nc
        ops = self.ops
        need = [False] * len(ops)
        for i, o in enumerate(ops):
            for d in o["raw"]:
                if self._needs_wait(o, d, True):
                    need[d] = True
            for d in o["oth"]:
                if self._needs_wait(o, d, False):
                    need[d] = True
        esem = {e: ctx.enter_context(nc.semaphore("s_" + e)) for e in ENGS}
        dsem = {e: [ctx.enter_context(nc.semaphore("d_%s%d" % (e, k))) for k in range(n)]
                for e, n in self.n_dma_sems.items()}
        ecount = {e: 0 for e in ENGS}
        dcount = {e: [0] * n for e, n in self.n_dma_sems.items()}
        for i, o in enumerate(ops):
            e = o["eng"]
            if o["fn"] is None:
                continue
            if o["dma"]:
                k = o["slot"]
                dcount[e][k] += 16
                o["sig"] = (dsem[e][k], dcount[e][k], ("d", e, k))
            elif need[i]:
                ecount[e] += 1
                o["sig"] = (esem[e], ecount[e], ("e", e))
        streams = {e: [] for e in ENGS}
        for i, o in enumerate(ops):
            streams[o["eng"]].append(i)
        engobj = dict(pe="tensor", act="scalar", dve="vector", pool="gpsimd", sp="sync")
        dlast = self.dlast
        with nc.Block() as block:
            def make(e):
                def body(eng):
                    waited = {}

                    def wait_for(d):
                        if ops[d]["sig"] is None:
                            return
                        sem, val, key = ops[d]["sig"]
                        if waited.get(key, 0) >= val:
                            return
                        waited[key] = val
                        eng.wait_ge(sem, val)

                    for i in streams[e]:
                        o = ops[i]
                        if o["dma"] and o["prev_on_sem"] is not None:
                            wait_for(o["prev_on_sem"])
                        for d in sorted(o["raw"]):
                            if self._needs_wait(o, d, True):
                                wait_for(d)
                        for d in sorted(o["oth"]):
                            if self._needs_wait(o, d, False):
                                wait_for(d)
                        if o["fn"] is None:
                            continue
                        ins = o["fn"](eng)
                        if o["sig"] is not None:
                            sem, val, key = o["sig"]
                            ins.then_inc(sem, 16 if o["dma"] else 1)
                    for k, d in enumerate(dlast.get(e, [])):
                        if d is not None:
                            wait_for(d)
                return body
            for e in ENGS:
                if streams[e]:
                    getattr(block, engobj[e])(make(e))


class B:
    def __init__(self, nc, S):
        self.nc = nc
        self.S = S
        self.rr = 0
        self.suffix = ""
        self.local = ()

    def _op(self, eng, fn, r=(), w=(), dma=False):
        return self.S.op(eng, fn, self._k(r), self._k(w), dma=dma)

    def _k(self, keys):
        if not self.suffix:
            return list(keys)
        return [k + self.suffix if k.rstrip("0123456789") in self.local else k for k in keys]

    def dma(self, q, out, in_, r=(), w=(), slow=False):
        if slow:
            self._op(q, lambda e: e.dma_start(out=out, in_=in_, allow_slow_non_contiguous=True), r, w, dma=True)
        else:
            self._op(q, lambda e: e.dma_start(out=out, in_=in_), r, w, dma=True)

    def mm(self, out, lhsT, rhs, r, w, start=True, stop=True, skip=False):
        if skip:
            self._op("pe", lambda e: e.matmul(out, lhsT=lhsT, rhs=rhs, start=start, stop=stop, skip_group_check=True),
                      r, w)
        else:
            self._op("pe", lambda e: e.matmul(out, lhsT=lhsT, rhs=rhs, start=start, stop=stop), r, w)

    def tr(self, out, in_, ident, r, w):
        self._op("pe", lambda e: e.transpose(out, in_, ident), r, w)

    def act(self, out, in_, func, r, w, bias=None, scale=None, accum=None):
        kw = {}
        if bias is not None:
            kw["bias"] = bias
        if scale is not None:
            kw["scale"] = scale
        if accum is not None:
            kw["accum_out"] = accum
        self._op("act", lambda e: e.activation(out=out, in_=in_, func=func, **kw), r, w)

    def tt(self, eng, out, in0, in1, op, r, w):
        self._op(eng, lambda e: e.tensor_tensor(out=out, in0=in0, in1=in1, op=op), r, w)

    def ts(self, eng, out, in0, s1, op0, r, w, s2=None, op1=None):
        if s2 is None:
            self._op(eng, lambda e: e.tensor_scalar(out=out, in0=in0, scalar1=s1, scalar2=None, op0=op0), r, w)
        else:
            self._op(eng, lambda e: e.tensor_scalar(out=out, in0=in0, scalar1=s1, scalar2=s2, op0=op0, op1=op1), r, w)

    def stt(self, eng, out, in0, scalar, in1, op0, op1, r, w):
        self._op(eng, lambda e: e.scalar_tensor_tensor(out=out, in0=in0, scalar=scalar, in1=in1, op0=op0, op1=op1),
                  r, w)

    def cp(self, eng, out, in_, r, w):
        if eng == "act":
            self._op("act", lambda e: e.activation(out=out, in_=in_, func=AF.Copy), r, w)
        else:
            self._op(eng, lambda e: e.tensor_copy(out=out, in_=in_), r, w)

    def memset(self, eng, out, val, r, w):
        self._op(eng, lambda e: e.memset(out, val), r, w)

    def reduce(self, eng, out, in_, r, w):
        self._op(eng, lambda e: e.tensor_reduce(out=out, in_=in_, axis=AX.X, op=ALU.add), r, w)

    def scan(self, out, d0, d1, r, w):
        self._op("dve", lambda e: e.tensor_tensor_scan(out=out, data0=d0, data1=d1, initial=0.0, op0=ALU.mult,
                                                        op1=ALU.add), r, w)

    def rstd(self, out, in_, scale, eps, r, w):
        self.act(out, in_, AF.Ln, r, w, bias=eps, scale=scale)
        self.act(out, out, AF.Exp, w, w, scale=-0.5)


CB = dict(ident=0, trineg=128, ones=256, bdones=384, bdmean=512, maskp=640, masks=640 + 2048)
CB_N = 640 + 2048 + 256
CF = dict(maskA64=0, maskL64=128, id64=192, maskA32=256, maskL32=320, id32=352, seg64=384, seg32=896)
CF_N = 896 + 256


def make_consts():
    cb = np.zeros((128, CB_N), np.float32)
    i = np.arange(128)
    cb[:, 0:128] = np.eye(128)
    cb[:, 128:256] = -(i[:, None] >= i[None, :]).astype(np.float32)
    cb[:, 256:384] = 1.0
    blk = (i[:, None] // 64 == i[None, :] // 64).astype(np.float32)
    cb[:, 384:512] = blk
    cb[:, 512:640] = blk / 64.0
    q = np.arange(512)
    for d in range(4):
        cb[:, 640 + 512 * d: 640 + 512 * (d + 1)] = ((128 * d + i[:, None]) < q[None, :]).astype(np.float32)
    q2 = np.arange(256)
    cb[0:32, 640 + 2048:] = (i[0:32, None] < (q2[None, :] % 32)).astype(np.float32)
    cf = np.zeros((128, CF_N), np.float32)
    for C, ka, kl, ki in ((64, "maskA64", "maskL64", "id64"), (32, "maskA32", "maskL32", "id32")):
        s = np.arange(C)
        for r0 in (0, 64):
            cf[r0:r0 + C, CF[ka]:CF[ka] + C] = (s[:, None] < s[None, :])
            cf[r0:r0 + C, CF[ka] + C:CF[ka] + 2 * C] = (s[:, None] <= s[None, :])
            cf[r0:r0 + C, CF[kl]:CF[kl] + C] = (s[None, :] < s[:, None])
            cf[r0:r0 + C, CF[ki]:CF[ki] + C] = np.eye(C)
    cf[:, CF["seg64"]:CF["seg64"] + 512] = (np.arange(512) % 64 != 0)[None, :]
    cf[:, CF["seg32"]:CF["seg32"] + 256] = (np.arange(256) % 32 != 0)[None, :]
    return cb, cf


def pp_layout(nc2):
    nch = 3 * nc2 + 2
    L = {}
    o = 0
    L["mu"] = o; o += nch
    for k in ("w0", "a0", "kk", "ka", "rk", "lw", "lb", "sbg"):
        L[k] = o; o += nc2
    L["n"] = o
    return L


class Cfg:
    def __init__(self, name, nh, ntile, C, nseg):
        self.name = name
        self.nh = nh
        self.HC = nh * 64
        self.nc2 = nh // 2
        self.NT = ntile
        self.nsub = ntile // 128
        self.C = C
        self.nch = ntile // C
        self.nseg = nseg
        self.seglen = ntile // nseg
        self.nrch = 3 * self.nc2 + 2
        self.nlev = {64: 5, 32: 4}[C]
        self.L = pp_layout(self.nc2)


CFG_P = Cfg("p", 4, 512, 64, 1)
CFG_S = Cfg("s", 2, 256, 32, 8)
ARENA_BYTES = 200 * 1024


class Arena:
    def __init__(self, tile):
        self.t = tile
        self.off = 0

    def mark(self):
        return self.off

    def reset(self, m):
        self.off = m

    def alloc(self, shape, dtype):
        free = 1
        for s in shape[1:]:
            free *= s
        ncols = free * (2 if dtype == F32 else 1)
        ncols = (ncols + 15) // 16 * 16
        assert (self.off + ncols) * 2 <= ARENA_BYTES, ("arena overflow", self.off * 2, ncols * 2)
        ap = self.t[:, self.off:self.off + ncols]
        self.off += ncols
        if dtype == F32:
            ap = ap.bitcast(F32)
        ap = ap[:, 0:free]
        if len(shape) == 3:
            ap = ap.rearrange("p (a b) -> p a b", a=shape[1])
        elif len(shape) == 4:
            ap = ap.rearrange("p (a b c) -> p a b c", a=shape[1], b=shape[2])
        if shape[0] < 128:
            ap = ap[0:shape[0]]
        return ap


def build_phase1(stop=None):
    import os
    stop = stop if stop is not None else float(os.environ.get('P1_STOP', '99'))
    nc = bass.Bass("TRN2", target_bir_lowering=False)
    dt = lambda name, shape, ty, kind: nc.dram_tensor(name, shape, ty, kind=kind).ap()
    IN, OUT = "ExternalInput", "ExternalOutput"
    xp = dt("xp", [T_P, D], F32, IN)
    xs = dt("xs", [NS * T_S, D], F32, IN)
    w1p_sb = dt("w1p_sb", [D, 768], F32, IN)
    w1p_rw = dt("w1p_rw", [D, 1024], F32, IN)
    w1s = dt("w1s", [D, 1024], F32, IN)
    kcT = dt("kcT", [64, 2, NS, PAST], F32, IN)
    vc = dt("vc", [NS, PAST, 128], F32, IN)
    st0 = dt("st0", [128, NS, 64], F32, IN)
    sh0 = dt("sh0", [128, CFG_S.nrch, NS], F32, IN)
    g1 = dt("g1", [1, D], F32, IN)
    qkg_p = dt("qkg_p", [1, 512], F32, IN)
    qkg_s = dt("qkg_s", [1, 256], F32, IN)
    pp_p = dt("pp_p", [128, CFG_P.L["n"]], F32, IN)
    pp_s = dt("pp_s", [128, CFG_S.L["n"]], F32, IN)
    wa2_p = dt("wa2_p", [128, 256], F32, IN)
    wa2_s = dt("wa2_s", [128, 128], F32, IN)
    g2_p = dt("g2_p", [128, 256], F32, IN)
    g2_s = dt("g2_s", [128, 128], F32, IN)
    cbd = dt("cbd", [128, CB_N], F32, IN)
    cfd = dt("cfd", [128, CF_N], F32, IN)
    mix_p = dt("mix_p", [512, T_P], BF16, OUT)
    mix_s = dt("mix_s", [256, NS * T_S], BF16, OUT)
    k_p = dt("k_p", [4, T_P, 64], F32, OUT)
    v_p = dt("v_p", [4, T_P, 64], F32, OUT)
    s_p = dt("s_p", [2, 128, 64], F32, OUT)
    sh_p = dt("sh_p", [128, CFG_P.nrch], F32, OUT)
    k_s = dt("k_s", [NS * T_S, 128], F32, OUT)
    v_s = dt("v_s", [NS * T_S, 128], F32, OUT)
    s_s = dt("s_s", [128, NS, 64], F32, OUT)
    sh_s = dt("sh_s", [128, CFG_S.nrch, NS], F32, OUT)
    xnT_d = nc.dram_tensor("xnT_d", [8, 128, 16 * 512], BF16).ap()

    with ExitStack() as ctx:
        S = Sched(nc)
        b = B(nc, S)
        arena_t = ctx.enter_context(nc.sbuf_tensor("arena", [128, ARENA_BYTES // 2], BF16))
        A = Arena(arena_t)
        PS = [ctx.enter_context(nc.psum_tensor("ps%d" % i, [128, 512], F32)) for i in range(8)]

        cb = A.alloc([128, CB_N], BF16)
        cf = A.alloc([128, CF_N], F32)
        st = A.alloc([128, 32], F32)
        xnT = A.alloc([128, 16, 512], BF16)
        pp = A.alloc([128, 32], F32)
        omka = A.alloc([128, 2], F32)
        xrs_raw = A.alloc([128, 5, 256], F32)
        rw_mark = A.mark()
        g1b = A.alloc([128, D], F32)
        xt = [A.alloc([128, D], F32) for _ in range(2)]
        xn = A.alloc([128, D], BF16)
        b.dma("pool", cb, cbd, [], ["cb"])
        b.dma("sp", cf, cfd, [], ["cf"])
        b.dma("sp", g1b, g1.to_broadcast([128, D]), [], ["g1b"])
        ident = cb[:, 0:128]
        trineg = cb[:, 128:256]
        ones = cb[:, 256:384]
        bdones = cb[:, 384:512]
        bdmean = cb[:, 512:640]
        base_mark = A.mark()
        if stop == 0:
            S.emit(ctx)
            return nc

        def bfview(ps):
            return ps[:].bitcast(BF16)

        def stage_norm_T(nsub, xrows):
            for m in range(nsub):
                i2 = m % 2
                kx = "xt%d" % i2
                b.dma("sp", xt[i2], xrows[m * 128:(m + 1) * 128, :], [], [kx])
                b.act(xn, xt[i2], AF.Square, [kx], ["xn", "ss"], accum=st[:, 0:1])
                b.rstd(st[:, 1:2], st[:, 0:1], 1.0 / D, RMS_EPS, ["ss"], ["rs"])
                b.stt("dve", xn, xt[i2], st[:, 1:2], g1b, ALU.mult, ALU.mult, [kx, "rs", "g1b"], ["xn"])
                for g in range(4):
                    bank = PS[g % 2]
                    bk = "ps%d" % (g % 2)
                    pb = bfview(bank)
                    for c in range(4):
                        b.tr(pb[:, c * 128:(c + 1) * 128], xn[:, (4 * g + c) * 128:(4 * g + c + 1) * 128], ident,
                             ["xn", "cb"], [bk])
                    b.cp(("act", "dve")[g % 2], xnT[:, 4 * g:4 * g + 4, m * 128:(m + 1) * 128],
                         pb[:, 0:512].rearrange("p (c t) -> p c t", c=4), [bk], ["xnT"])

        def load_params(cfg, sfx, pp_d, omka_needed=True):
            L = cfg.L
            b.dma("sp", pp[:, 0:L["n"]], pp_d, [], ["pp"])
            b.ts("dve", omka[:, 0:cfg.nc2], pp[:, L["ka"]:L["ka"] + cfg.nc2], -1.0, ALU.mult, ["pp"], ["omka"],
                 s2=1.0, op1=ALU.add)

        def sb_alloc():
            t = {}
            t["sqk"] = A.alloc([128, 512], F32)
            t["qkt"] = A.alloc([128, 512], F32)
            t["qn"] = A.alloc([128, 256], BF16)
            t["knf"] = [A.alloc([128, 256], F32) for _ in range(2)]
            t["knb"] = A.alloc([128, 256], BF16)
            t["vf"] = [A.alloc([128, 256], F32) for _ in range(2)]
            t["qkg"] = A.alloc([128, 512], F32)
            t["qT"] = A.alloc([64, 4, 512], BF16)
            t["ef"] = [A.alloc([128, 512], F32) for _ in range(2)]
            t["Lb"] = [A.alloc([128, 512], BF16) for _ in range(2)]
            t["attn"] = [A.alloc([128, 512], BF16) for _ in range(2)]
            t["Cc"] = A.alloc([128, 512], F32)
            t["of"] = A.alloc([128, 512], F32)
            t["osq"] = A.alloc([128, 512], BF16)
            t["rso"] = A.alloc([128, 512], F32)
            t["onb"] = A.alloc([128, 512], BF16)
            return t

        def stage_sb_tok(cfg, t, W1, m, kT_dst, V_dst, k_out, v_out):
            HC, nh = cfg.HC, cfg.nh
            tl = slice(m * 128, (m + 1) * 128)
            i2 = m % 2
            GA, GB, R0 = PS[0], PS[1], PS[7]
            for kc in range(16):
                b.mm(GA[:, 0:2 * HC], xnT[:, kc, tl], W1[:, kc, 0:2 * HC], ["xnT", "W1_%d" % kc], ["ps0"], start=kc == 0,
                     stop=kc == 15)
            for kc in range(16):
                b.mm(GB[:, 0:HC], xnT[:, kc, tl], W1[:, kc, 2 * HC:3 * HC], ["xnT", "W1_%d" % kc], ["ps1"], start=kc == 0,
                     stop=kc == 15)
            b.act(t["sqk"][:, 0:2 * HC], GA[:, 0:2 * HC], AF.Square, ["ps0"], ["sqk"])
            b.reduce("dve", st[:, 4:4 + 2 * nh], t["sqk"][:, 0:2 * HC].rearrange("p (h d) -> p h d", d=64), ["sqk"],
                     ["ssqk"])
            b.rstd(st[:, 4:4 + 2 * nh], st[:, 4:4 + 2 * nh], 1.0 / 64, RMS_EPS, ["ssqk"], ["ssqk"])
            b.tt("dve", t["qkt"][:, 0:2 * HC].rearrange("p (h d) -> p h d", d=64),
                 GA[:, 0:2 * HC].rearrange("p (h d) -> p h d", d=64),
                 st[:, 4:4 + 2 * nh].unsqueeze(2).to_broadcast([128, 2 * nh, 64]), ALU.mult, ["ps0", "ssqk"], ["qkt"])
            b.tt("pool", t["qn"][:, 0:HC], t["qkt"][:, 0:HC], t["qkg"][:, 0:HC], ALU.mult, ["qkt", "qkg"], ["qn"])
            b.tt("dve", t["knf"][i2][:, 0:HC], t["qkt"][:, HC:2 * HC], t["qkg"][:, HC:2 * HC], ALU.mult,
                 ["qkt", "qkg"], ["knf%d" % i2])
            b.cp("pool", t["knb"][:, 0:HC], t["knf"][i2][:, 0:HC], ["knf%d" % i2], ["knb"])
            b.dma("sp", k_out, t["knf"][i2][:, 0:HC] if cfg is CFG_S else
                  t["knf"][i2][:, 0:HC].rearrange("p (h d) -> p h d", d=64), ["knf%d" % i2], [])
            pb = bfview(R0)
            for h in range(nh):
                b.tr(pb[0:64, h * 128:(h + 1) * 128], t["qn"][:, h * 64:(h + 1) * 64], ident, ["qn", "cb"], ["ps7"])
                b.tr(pb[0:64, (nh + h) * 128:(nh + h + 1) * 128], t["knb"][:, h * 64:(h + 1) * 64], ident,
                     ["knb", "cb"], ["ps7"])
            b.cp("act", t["qT"][:, 0:nh, tl], pb[0:64, 0:nh * 128].rearrange("p (c t) -> p c t", t=128),
                 ["ps7"], ["qT"])
            b.cp("act", kT_dst, pb[0:64, nh * 128:2 * nh * 128].rearrange("p (c t) -> p c t", t=128), ["ps7"], ["kT"])
            b.cp("act", t["vf"][i2][:, 0:HC], GB[:, 0:HC], ["ps1"], ["vf%d" % i2])
            if V_dst is not None:
                b.cp("pool", V_dst, t["vf"][i2][:, 0:HC], ["vf%d" % i2], ["Vres"])
            b.dma("sp", v_out, t["vf"][i2][:, 0:HC] if cfg is CFG_S else
                  t["vf"][i2][:, 0:HC].rearrange("p (h d) -> p h d", d=64), ["vf%d" % i2],
                  ["v_out"] if cfg is CFG_S else [])

        def attn_A(t, it):
            i2, kr, ncol = it["i2"], it["kr"], it["ncol"]
            Z = PS[2 + i2]
            zk = "ps%d" % (2 + i2)
            ef, Lb = t["ef"][i2], t["Lb"][i2]
            for (l, r, c0, n) in it["zm"]:
                b.mm(Z[0:kr, c0:c0 + n], l, r, it["zkeys"], [zk])
            b.act(ef[:, 0:ncol], Z[:, 0:ncol], AF.Exp, [zk], ["ef%d" % i2])
            b.act(Lb[:, 0:ncol], ef[:, 0:ncol], AF.Ln, ["ef%d" % i2], ["Lb%d" % i2], bias=1.0)
            if it["mask"] is not None:
                b.tt("dve", Lb[:, 0:ncol], Lb[:, 0:ncol], it["mask"], ALU.mult, ["Lb%d" % i2, "cb"], ["Lb%d" % i2])

        def attn_B(t, it):
            i2, kr, ncol = it["i2"], it["kr"], it["ncol"]
            first, last = it["step"] == 0, it["step"] == it["nsteps"] - 1
            ab = (4, 1)[i2]
            ak = "ps%d" % ab
            Ab, Bb = PS[ab], PS[5]
            ef, Lb, attn, Cc = t["ef"][i2], t["Lb"][i2], t["attn"][i2], t["Cc"]
            zmms = it["zm"]
            multi = len(zmms) > 1
            for zi, (l, r, c0, n) in enumerate(zmms):
                b.mm(Ab[0:kr, c0:c0 + n], l, r, it["zkeys"], [ak], start=zi == 0, stop=False, skip=multi)
            b.mm(Ab[0:kr, 0:ncol], trineg[0:kr, 0:kr], Lb[0:kr, 0:ncol], ["cb", "Lb%d" % i2], [ak], start=False,
                 stop=True, skip=multi)
            if not last:
                b.mm(Bb[:, 0:ncol], ones[0:kr, :], Lb[0:kr, 0:ncol], ["cb", "Lb%d" % i2], ["ps5"])
            if first:
                b.act(attn[:, 0:ncol], Ab[:, 0:ncol], AF.Exp, [ak], ["attn%d" % i2])
            else:
                b.tt("dve", ef[:, 0:ncol], Ab[:, 0:ncol], Cc[:, 0:ncol], ALU.subtract, [ak, "Cc"],
                     ["ef%d" % i2])
                b.act(attn[:, 0:ncol], ef[:, 0:ncol], AF.Exp, ["ef%d" % i2], ["attn%d" % i2])
            if it["mask"] is not None:
                b.tt("dve", attn[:, 0:ncol], attn[:, 0:ncol], it["mask"], ALU.mult, ["attn%d" % i2, "cb"],
                     ["attn%d" % i2])
            if not last:
                if first:
                    b.cp("dve", Cc[:, 0:ncol], Bb[:, 0:ncol], ["ps5"], ["Cc"])
                else:
                    b.tt("dve", Cc[:, 0:ncol], Cc[:, 0:ncol], Bb[:, 0:ncol], ALU.add, ["ps5", "Cc"], ["Cc"])

        def attn_C(t, it):
            i2, kr = it["i2"], it["kr"]
            first, last = it["step"] == 0, it["step"] == it["nsteps"] - 1
            multi = len(it["zm"]) > 1
            attn = t["attn"][i2]
            for ai, (lv, c0, n, o_ap, st_) in enumerate(it["av"]):
                b.mm(o_ap, lv, attn[0:kr, c0:c0 + n], it["vkeys"] + ["attn%d" % i2], [it["okey"]], start=first and st_,
                     stop=last, skip=multi)

        def attn_run(t, items, finish_after):
            n = len(items)
            for i, it in enumerate(items):
                it["i2"] = i % 2
            attn_A(t, items[0])
            if n > 1:
                attn_A(t, items[1])
            attn_B(t, items[0])
            for i in range(n):
                if i + 2 < n:
                    attn_A(t, items[i + 2])
                if i + 1 < n:
                    attn_B(t, items[i + 1])
                attn_C(t, items[i])
                if i in finish_after:
                    finish_after[i]()

        def attn_finish(t, ncols, sbg_col, out_dram, ob=6):
            O0 = PS[ob]
            GA = PS[0]
            b.cp("act", t["of"][:, 0:ncols], O0[:, 0:ncols], ["ps%d" % ob], ["of"])
            b.act(t["osq"][:, 0:ncols], t["of"][:, 0:ncols], AF.Square, ["of"], ["osq"])
            b.mm(GA[:, 0:ncols], bdones, t["osq"][:, 0:ncols], ["cb", "osq"], ["ps0"])
            b.rstd(t["rso"][:, 0:ncols], GA[:, 0:ncols], 1.0 / 64, RMS_EPS, ["ps0"], ["rso"])
            b.stt("dve", t["onb"][:, 0:ncols], t["of"][:, 0:ncols], sbg_col, t["rso"][:, 0:ncols], ALU.mult, ALU.mult,
                  ["of", "rso", "pp"], ["onb"])
            b.dma("sp", out_dram, t["onb"][:, 0:ncols], ["onb"], [])

        def load_qkg(t, cfg, qkg_d):
            HC = cfg.HC
            b.dma("sp", t["qkg"][:, 0:2 * HC], qkg_d.to_broadcast([128, 2 * HC]), [], ["qkg"])
            b.ts("dve", t["qkg"][:, 0:HC], t["qkg"][:, 0:HC], 0.125, ALU.mult, ["qkg"], ["qkg"])

        FN = ("b0", "b1", "b2", "b3", "b4", "b5", "b6", "b7", "b8", "b9")

        LOCAL = ("b", "AR", "Btb", "Ktb", "Vtb", "sqb", "yob", "Btm", "Ktm", "Vtm", "MTb", "MTk", "Uj", "Lj", "Xj", "Wb",
                 "Ub")

        def rw_alloc(nsets=2):
            t = {}
            t["xrj"] = [A.alloc([128, 520], F32) for _ in range(2)]
            t["hal"] = A.alloc([128, 8, 8], F32)
            t["hal2"] = A.alloc([128, 8, 8], F32)
            t["xsf"] = A.alloc([128, 8, 512], F32)
            t["lora_in"] = A.alloc([128, 512], BF16)
            t["sgb"] = A.alloc([128, 512], BF16)
            t["wa2"] = A.alloc([128, 256], BF16)
            t["g2t"] = A.alloc([128, 256], BF16)
            t["shtmp"] = A.alloc([128, 512], F32)
            t["STf"] = [A.alloc([128, 64], F32) for _ in range(8)]
            t["STb"] = [A.alloc([128, 64], BF16) for _ in range(8)]
            t["c2"] = []
            for _ in range(nsets):
                u = {}
                for n in FN:
                    u[n] = A.alloc([128, 512], F32)
                u["AR"] = A.alloc([128, 8, 2, 64], BF16)
                for n in ("Btb", "Ktb", "Vtb", "sqb", "yob"):
                    u[n] = A.alloc([128, 512], BF16)
                for n in ("Btm", "Ktm", "Vtm"):
                    u[n] = A.alloc([128, 8, 64], BF16)
                u["MTb"] = A.alloc([128, 8, 128], BF16)
                u["MTk"] = A.alloc([128, 8, 128], BF16)
                for n in ("Uj", "Lj", "Xj"):
                    u[n] = [A.alloc([128, 8, 64], BF16) for _ in range(2)]
                u["Wb"] = A.alloc([128, 64], BF16)
                u["Ub"] = A.alloc([128, 64], BF16)
                t["c2"].append(u)
            t.update(t["c2"][0])
            return t

        def stage_rwkv(cfg, t, get_raw, state_of_chunk, mix_dst):
            NT, C, nch, nc2, L = cfg.NT, cfg.C, cfg.nch, cfg.nc2, cfg.L
            nseg, sl = cfg.nseg, cfg.seglen
            xsf = t["xsf"]
            GA, GB, R0 = PS[0], PS[1], PS[7]
            for j in range(cfg.nrch):
                raw, rk = get_raw(j)
                xj = t["xrj"][j % 2][:, 0:nseg * (sl + 1)].rearrange("p (s t) -> p s t", s=nseg)
                kj = "xrj%d" % (j % 2)
                b.cp("act", xj[:, :, 1:sl + 1], raw.rearrange("p (s t) -> p s t", s=nseg), [rk], [kj])
                b.cp("pool", xj[:, :, 0:1], t["hal"][:, j, 0:nseg].unsqueeze(2), ["hal"], [kj])
                b.tt("dve", t["shtmp"][:, 0:NT].rearrange("p (s t) -> p s t", s=nseg), xj[:, :, 0:sl], xj[:, :, 1:sl + 1],
                     ALU.subtract, [kj], ["shtmp"])
                b.stt("dve", xsf[:, j, 0:NT].rearrange("p (s t) -> p s t", s=nseg),
                      t["shtmp"][:, 0:NT].rearrange("p (s t) -> p s t", s=nseg), pp[:, L["mu"] + j:L["mu"] + j + 1],
                      xj[:, :, 1:sl + 1], ALU.mult, ALU.add, ["shtmp", kj, "pp"], ["xs%d" % j])
                b.cp("pool", t["hal2"][:, j, 0:nseg].unsqueeze(2), xj[:, :, sl:sl + 1], [kj], ["hal2"])
                if cfg is CFG_P:
                    b.cp("pool", t["hal"][:, j, 0:1], t["hal2"][:, j, 0:1], ["hal2"], ["hal"])
            jw, jg = 3 * nc2, 3 * nc2 + 1
            b.act(t["lora_in"][0:64, 0:NT], xsf[0:64, jw, 0:NT], AF.Tanh, ["xs%d" % jw], ["lora_in"])
            b.cp("pool", t["lora_in"][64:128, 0:NT], xsf[64:128, jw, 0:NT], ["xs%d" % jw], ["lora_in"])
            b.act(t["sgb"][:, 0:NT], xsf[:, jg, 0:NT], AF.Sigmoid, ["xs%d" % jg], ["sgb"])
            seg = cf[:, CF["seg%d" % C]:CF["seg%d" % C] + NT]
            maskA = cf[0:C, CF["maskA%d" % C]:CF["maskA%d" % C] + 2 * C]
            maskL = cf[0:C, CF["maskL%d" % C]:CF["maskL%d" % C] + C]
            idC = cf[0:C, CF["id%d" % C]:CF["id%d" % C] + C]
            AR, Btb, Ktb, Vtb, sqb = t["AR"], t["Btb"], t["Ktb"], t["Vtb"], t["sqb"]
            BANKS = ((0, 1, 7, 6), (2, 3, 4, 5))

            def c2body(c2, t):
                ia, ib, ir, iy = BANKS[c2]
                GA, GB, R0, PY = PS[ia], PS[ib], PS[ir], PS[iy]
                kA, kB, kR, kY = "ps%d" % ia, "ps%d" % ib, "ps%d" % ir, "ps%d" % iy
                AR, Btb, Ktb, Vtb, sqb = t["AR"], t["Btb"], t["Ktb"], t["Vtb"], t["sqb"]
                jr, jk, jv = c2, nc2 + c2, 2 * nc2 + c2
                cs = slice(c2 * 128, (c2 + 1) * 128)
                col = lambda name: pp[:, L[name] + c2:L[name] + c2 + 1]
                f = {n: t[n][:, 0:NT] for n in FN}
                xs_r, xs_k, xs_v = xsf[:, jr, 0:NT], xsf[:, jk, 0:NT], xsf[:, jv, 0:NT]
                kr_, kk_, kv_ = "xs%d" % jr, "xs%d" % jk, "xs%d" % jv
                P1, P2, P3 = GA[:, 0:NT], GB[:, 0:NT], R0[:, 0:NT]
                sgu, a_, g_, lw, cl, clm, eP, eM, ePx, tmp = (f["b0"], f["b1"], f["b2"], f["b3"], f["b4"], f["b5"],
                                                              f["b6"], f["b7"], f["b8"], f["b9"])
                b.mm(P1, t["wa2"][0:64, cs], t["lora_in"][0:64, 0:NT], ["wa2", "lora_in"], [kA])
                b.mm(P2, t["wa2"][64:128, cs], t["lora_in"][64:128, 0:NT], ["wa2", "lora_in"], [kB])
                b.mm(P3, t["g2t"][:, cs], t["sgb"][:, 0:NT], ["g2t", "sgb"], [kR])
                b.act(sgu, P1, AF.Sigmoid, [kA, "pp"], ["b0"], bias=col("w0"))
                b.act(a_, P2, AF.Sigmoid, [kB, "pp"], ["b1"], bias=col("a0"))
                b.cp("act", g_, P3, [kR], ["b2"])
                yield
                b.ts("pool", lw, sgu, -0.6065306597126334, ALU.mult, ["b0"], ["b3"])
                b.scan(cl, seg, lw, ["cf", "b3"], ["b4"])
                b.tt("pool", clm, cl, lw, ALU.subtract, ["b4", "b3"], ["b5"])
                b.act(eP, cl, AF.Exp, ["b4"], ["b6"])
                b.act(eM, cl, AF.Exp, ["b4"], ["b7"], scale=-1.0)
                b.act(ePx, clm, AF.Exp, ["b5"], ["b8"])
                yield
                kkr = f["b3"]
                b.ts("dve", kkr, xs_k, col("kk"), ALU.mult, [kk_, "pp"], ["b3"])
                b.act(sqb[:, 0:NT], kkr, AF.Square, ["b3"], ["sqb"])
                b.mm(P1, bdones, sqb[:, 0:NT], ["cb", "sqb"], [kA])
                b.ts("dve", tmp, P1, 1e-24, ALU.max, [kA], ["b9"])
                b.rstd(tmp, tmp, 1.0, 0.0, ["b9"], ["b9"])
                kk = f["b4"]
                b.tt("dve", kk, kkr, tmp, ALU.mult, ["b3", "b9"], ["b4"])
                yield
                tt_ = f["b5"]
                b.ts("dve", tt_, a_, col("ka"), ALU.mult, ["b1", "pp", "omka"], ["b5"], s2=omka[:, c2:c2 + 1],
                     op1=ALU.add)
                kp = f["b5"]
                b.tt("dve", kp, xs_k, tt_, ALU.mult, [kk_, "b5"], ["b5"])
                ARv = AR[:, 0:nch, :, 0:C]
                b.stt("dve", ARv[:, :, 0, :], kk.rearrange("p (c t) -> p c t", t=C), -1.0,
                      ePx.rearrange("p (c t) -> p c t", t=C), ALU.mult, ALU.mult, ["b4", "b8"], ["AR"])
                b.tt("pool", ARv[:, :, 1, :], xs_r.rearrange("p (c t) -> p c t", t=C),
                     eP.rearrange("p (c t) -> p c t", t=C), ALU.mult, [kr_, "b6"], ["AR"])
                b.tt("dve", tmp, kk, a_, ALU.mult, ["b4", "b1"], ["b9"])
                b.tt("dve", Btb[:, 0:NT], tmp, eM, ALU.mult, ["b9", "b7"], ["Btb"])
                b.tt("pool", Ktb[:, 0:NT], kp, eM, ALU.mult, ["b5", "b7"], ["Ktb"])
                b.cp("pool", Vtb[:, 0:NT], xs_v, [kv_], ["Vtb"])
                yield
                b.tt("dve", tmp, xs_r, kp, ALU.mult, [kr_, "b5"], ["b9"])
                b.ts("dve", sqb[:, 0:NT], tmp, col("rk"), ALU.mult, ["b9", "pp"], ["sqb"])
                b.mm(P2, bdones, sqb[:, 0:NT], ["cb", "sqb"], [kB])
                bonus = f["b8"]
                b.tt("dve", bonus, P2, xs_v, ALU.mult, [kB, kv_], ["b8"])
                yield
                pbR = bfview(R0)
                PR = slice(0, 128)
                npr = 128
                HO = [slice(hh * 64, hh * 64 + C) for hh in range(2)]
                HR = [slice(hh * 64, hh * 64 + 64) for hh in range(2)]
                for (src, dst, sk, dk, ev) in ((Btb, t["Btm"], "Btb", "Btm", "act"), (Ktb, t["Ktm"], "Ktb", "Ktm", "dve"),
                                               (Vtb, t["Vtm"], "Vtb", "Vtm", "act")):
                    for hh in range(2):
                        for ch in range(nch):
                            b.tr(pbR[HO[hh], ch * 64:(ch + 1) * 64], src[HR[hh], ch * C:(ch + 1) * C],
                                 ident[HR[hh], HR[hh]], [sk, "cb"], [kR])
                    b.cp(ev, dst[PR, 0:nch, :], pbR[PR, 0:nch * 64].rearrange("p (c f) -> p c f", f=64), [kR], [dk])
                    yield
                MTb, MTk, Uj, Lj, Xj = t["MTb"], t["MTk"], t["Uj"], t["Lj"], t["Xj"]
                mA = cf[PR, CF["maskA%d" % C]:CF["maskA%d" % C] + 2 * C]
                mL = cf[PR, CF["maskL%d" % C]:CF["maskL%d" % C] + C]
                mI = cf[PR, CF["id%d" % C]:CF["id%d" % C] + C]
                for g0 in range(0, nch, 4):
                    for hh in range(2):
                        for ch in range(g0, g0 + 4):
                            q4 = ch - g0
                            arf = AR[HR[hh], ch, :, 0:C]
                            b.mm(GA[HO[hh], q4 * 2 * C:(q4 + 1) * 2 * C].rearrange("p (a t) -> p a t", a=2),
                                 Btb[HR[hh], ch * C:(ch + 1) * C], arf, ["Btb", "AR"], [kA])
                            b.mm(GB[HO[hh], q4 * 2 * C:(q4 + 1) * 2 * C].rearrange("p (a t) -> p a t", a=2),
                                 Ktb[HR[hh], ch * C:(ch + 1) * C], arf, ["Ktb", "AR"], [kB])
                    b.tt("dve", MTb[PR, g0:g0 + 4, 0:2 * C], GA[PR, 0:8 * C].rearrange("p (q t) -> p q t", q=4),
                         mA.unsqueeze(1).to_broadcast([npr, 4, 2 * C]), ALU.mult, [kA, "cf"], ["MTb"])
                    b.tt("dve", MTk[PR, g0:g0 + 4, 0:2 * C], GB[PR, 0:8 * C].rearrange("p (q t) -> p q t", q=4),
                         mA.unsqueeze(1).to_broadcast([npr, 4, 2 * C]), ALU.mult, [kB, "cf"], ["MTk"])
                    yield
                for hh in range(2):
                    for ch in range(nch):
                        b.mm(R0[HO[hh], ch * C:(ch + 1) * C], AR[HR[hh], ch, 0, 0:C], Btb[HR[hh], ch * C:(ch + 1) * C],
                             ["AR", "Btb"], [kR])
                b.tt("dve", Lj[0][PR, 0:nch, 0:C], R0[PR, 0:nch * C].rearrange("p (q t) -> p q t", t=C),
                     mL.unsqueeze(1).to_broadcast([npr, nch, C]), ALU.mult, [kR, "cf"], ["Lj0"])
                b.cp("pool", Uj[0][PR, 0:nch, 0:C], MTb[PR, 0:nch, 0:C], ["MTb"], ["Uj0"])
                b.tt("dve", Xj[0][PR, 0:nch, 0:C], MTb[PR, 0:nch, 0:C], mI.unsqueeze(1).to_broadcast([npr, nch, C]),
                     ALU.add, ["MTb", "cf"], ["Xj0"])
                yield
                for lv in range(1, cfg.nlev + 1):
                    pi, ci = (lv - 1) % 2, lv % 2
                    Up, Lp, Un, Ln_, Xp, Xn = Uj[pi], Lj[pi], Uj[ci], Lj[ci], Xj[pi], Xj[ci]
                    pl, pu, px = GA, GB, R0
                    plk, puk, pxk = kA, kB, kR
                    for hh in range(2):
                        for ch in range(nch):
                            b.mm(pl[HO[hh], ch * C:(ch + 1) * C], Up[HO[hh], ch, 0:C], Lp[HO[hh], ch, 0:C],
                                 ["Uj%d" % pi, "Lj%d" % pi], [plk])
                    b.cp("act", Ln_[PR, 0:nch, 0:C], pl[PR, 0:nch * C].rearrange("p (q t) -> p q t", t=C), [plk],
                         ["Lj%d" % ci])
                    yield
                    if lv < cfg.nlev:
                        for hh in range(2):
                            for ch in range(nch):
                                b.mm(pu[HO[hh], ch * C:(ch + 1) * C], Lp[HO[hh], ch, 0:C], Up[HO[hh], ch, 0:C],
                                     ["Uj%d" % pi, "Lj%d" % pi], [puk])
                        b.cp("dve", Un[PR, 0:nch, 0:C], pu[PR, 0:nch * C].rearrange("p (q t) -> p q t", t=C), [puk],
                             ["Uj%d" % ci])
                    for hh in range(2):
                        for ch in range(nch):
                            b.mm(px[HO[hh], ch * C:(ch + 1) * C], Ln_[HO[hh], ch, 0:C], Xp[HO[hh], ch, 0:C],
                                 ["Lj%d" % ci, "Xj%d" % pi], [pxk])
                    b.tt("dve", Xn[PR, 0:nch, 0:C], px[PR, 0:nch * C].rearrange("p (q t) -> p q t", t=C),
                         Xp[PR, 0:nch, 0:C], ALU.add, [pxk, "Xj%d" % pi], ["Xj%d" % ci])
                    yield
                Xf = Xj[cfg.nlev % 2]
                xk = "Xj%d" % (cfg.nlev % 2)
                Wb, Ub, Btm, Ktm, Vtm = t["Wb"], t["Ub"], t["Btm"], t["Ktm"], t["Vtm"]
                for ch in range(nch):
                    si = state_of_chunk(c2, ch)
                    sf, sb_, skf, skb = t["STf"][si], t["STb"][si], "STf%d" % si, "STb%d" % si
                    for hh in range(2):
                        b.mm(GA[HO[hh], 0:64], AR[HR[hh], ch, 0, 0:C], sb_[HR[hh], :], ["AR", skb], [kA], start=True,
                             stop=False)
                        b.mm(GA[HO[hh], 0:64], MTk[HO[hh], ch, 0:C], Vtm[HO[hh], ch, :], ["MTk", "Vtm"], [kA],
                             start=False, stop=True)
                    b.cp("act", Wb[PR, :], GA[PR, 0:64], [kA], ["Wb"])
                    yield
                    for hh in range(2):
                        b.mm(GB[HO[hh], 0:64], Xf[HO[hh], ch, 0:C], Wb[HO[hh], :], [xk, "Wb"], [kB])
                    b.cp("dve", Ub[PR, :], GB[PR, 0:64], [kB], ["Ub"])
                    yield
                    for hh in range(2):
                        yo = PY[HR[hh], ch * C:(ch + 1) * C]
                        b.mm(yo, sb_[HR[hh], :], AR[HR[hh], ch, 1, 0:C], [skb, "AR"], [kY], start=True, stop=False)
                        b.mm(yo, Ub[HO[hh], :], MTb[HO[hh], ch, C:2 * C], ["Ub", "MTb"], [kY], start=False, stop=False)
                        b.mm(yo, Vtm[HO[hh], ch, :], MTk[HO[hh], ch, C:2 * C], ["Vtm", "MTk"], [kY], start=False,
                             stop=True)
                        so = R0[HR[hh], 0:64]
                        b.mm(so, Btm[HO[hh], ch, :], Ub[HO[hh], :], ["Btm", "Ub"], [kR], start=True, stop=False)
                        b.mm(so, Ktm[HO[hh], ch, :], Vtm[HO[hh], ch, :], ["Ktm", "Vtm"], [kR], start=False, stop=True)
                    b.tt("dve", sf, R0[:, 0:64], sf, ALU.add, [kR, skf], [skf])
                    b.ts("dve", sf, sf, t["b6"][:, ch * C + C - 1:ch * C + C], ALU.mult, [skf, "b6"], [skf])
                    b.cp("pool", sb_, sf, [skf], [skb])
                    yield
                y_, yc, rs = f["b0"], f["b1"], f["b7"]
                b.cp("act", y_, PY[:, 0:NT], [kY], ["b0"])
                b.cp("pool", sqb[:, 0:NT], y_, ["b0"], ["sqb"])
                b.mm(P1, bdmean, sqb[:, 0:NT], ["cb", "sqb"], [kA])
                b.tt("dve", yc, y_, P1, ALU.subtract, ["b0", kA], ["b1"])
                yield
                b.act(sqb[:, 0:NT], yc, AF.Square, ["b1"], ["sqb"])
                b.mm(P2, bdmean, sqb[:, 0:NT], ["cb", "sqb"], [kB])
                b.rstd(rs, P2, 1.0, LNX_EPS, [kB], ["b7"])
                b.tt("dve", yc, yc, rs, ALU.mult, ["b1", "b7"], ["b1"])
                b.ts("dve", yc, yc, col("lw"), ALU.mult, ["b1", "pp"], ["b1"], s2=col("lb"), op1=ALU.add)
                b.tt("pool", yc, yc, bonus, ALU.add, ["b1", "b8"], ["b1"])
                b.tt("dve", t["yob"][:, 0:NT], yc, g_, ALU.mult, ["b1", "b2"], ["yob"])
                b.dma("sp", mix_dst(c2), t["yob"][:, 0:NT], ["yob"], [])

            gens = []
            for c2 in range(nc2):
                tc = dict(t)
                tc.update(t["c2"][c2 % len(t["c2"])])
                gens.append((c2, c2body(c2, tc)))
            b.local = LOCAL
            while gens:
                for item in list(gens):
                    c2, g = item
                    b.suffix = "@%d" % c2
                    try:
                        next(g)
                    except StopIteration:
                        gens.remove(item)
            b.suffix = ""

        cfg = CFG_S
        load_params(cfg, "s", pp_s)
        W1 = A.alloc([128, 16, 1024], BF16)
        for kc in range(16):
            b.dma("pool", W1[:, kc, :], w1s[kc * 128:(kc + 1) * 128, :], [], ["W1_%d" % kc])
        t = sb_alloc()
        load_qkg(t, cfg, qkg_s)
        KcT = A.alloc([64, 2, 4, PAST], BF16)
        Vc = A.alloc([128, 4, 16, 128], BF16)
        kTs = A.alloc([64, 2, 256], BF16)
        vnew = A.alloc([32, NS, 128], BF16)
        vsb = A.alloc([128, 2, 128], BF16)
        stage_norm_T(2, xs)
        for m in range(2):
            stage_sb_tok(cfg, t, W1, m, kTs[:, :, m * 128:(m + 1) * 128], vsb[:, m, :], k_s[m * 128:(m + 1) * 128, :],
                         v_s[m * 128:(m + 1) * 128, :])
        for j in range(cfg.nrch):
            bank = PS[j % 2]
            bk = "ps%d" % (j % 2)
            for kc in range(16):
                b.mm(bank[:, 0:256], W1[:, kc, 384 + j * 128:384 + (j + 1) * 128], xnT[:, kc, 0:256], ["W1_%d" % kc, "xnT"], [bk],
                     start=kc == 0, stop=kc == 15)
            b.cp("act", xrs_raw[:, j, :], bank[:, 0:256], [bk], ["xrs_raw"])
        if stop == 1:
            S.emit(ctx)
            return nc
        for bb in range(NS):
            bank = PS[bb // 4]
            b.mm(bank[0:32, (bb % 4) * 128:(bb % 4 + 1) * 128], ident[:, (bb % 4) * 32:(bb % 4 + 1) * 32], vsb[:, bb // 4, :],
                 ["cb", "Vres"], ["ps%d" % (bb // 4)])
        for g in range(2):
            b.cp(("act", "dve")[g], vnew[0:32, 4 * g:4 * g + 4, :], PS[g][0:32, :].rearrange("p (b f) -> p b f", b=4),
                 ["ps%d" % g], ["vnew"])
        masks = cb[:, CB["masks"]:CB["masks"] + 256]
        if stop == 1.2:
            S.emit(ctx)
            return nc
        for grp in range(2):
            for b4 in range(4):
                bb = grp * 4 + b4
                b.dma("pool", KcT[:, :, b4, :], kcT[:, :, bb, :], [], ["KcT%d" % b4])
                b.dma("pool", Vc[:, b4, :, :], vc[bb].rearrange("(kb p) f -> p kb f", p=128), [], ["Vc%d" % b4])
            if stop == 1.4:
                S.emit(ctx)
                return nc
            nsteps = 17
            items = []
            for step in range(nsteps):
                kb = 16 - step
                pairs = [(hh, b4) for hh in range(2) for b4 in range(4)]
                cols = lambda b4: slice((grp * 4 + b4) * 32, (grp * 4 + b4 + 1) * 32)
                if kb == 16:
                    zm = [(kTs[:, hh, cols(b4)], t["qT"][:, hh, cols(b4)], hh * 128 + b4 * 32, 32) for hh, b4 in pairs]
                    av = [(vnew[0:32, grp * 4 + b4, hh * 64:hh * 64 + 64], hh * 128 + b4 * 32, 32,
                           PS[6][hh * 64:hh * 64 + 64, cols(b4)], b4 == 0) for hh, b4 in pairs]
                    items.append(dict(step=step, nsteps=nsteps, kr=32, ncol=256, zm=zm, av=av, mask=masks,
                                      zkeys=["kT", "qT"], vkeys=["vnew"], okey="ps6"))
                else:
                    zm = [(KcT[:, hh, b4, kb * 128:(kb + 1) * 128], t["qT"][:, hh, cols(b4)], hh * 128 + b4 * 32, 32)
                          for hh, b4 in pairs]
                    av = [(Vc[:, b4, kb, hh * 64:hh * 64 + 64], hh * 128 + b4 * 32, 32,
                           PS[6][hh * 64:hh * 64 + 64, cols(b4)], b4 == 0) for hh, b4 in pairs]
                    items.append(dict(step=step, nsteps=nsteps, kr=128, ncol=256, zm=zm, av=av, mask=None,
                                      zkeys=["KcT%d" % q for q in range(4)] + ["qT"],
                                      vkeys=["Vc%d" % q for q in range(4)], okey="ps6"))
            attn_run(t, items, {})
        attn_finish(t, 256, pp[:, cfg.L["sbg"]:cfg.L["sbg"] + 1], mix_s[0:128, :])
        if stop == 2:
            S.emit(ctx)
            return nc
        S.barrier()

        A.reset(rw_mark)
        t = rw_alloc()
        b.dma("pool", t["wa2"][:, 0:128], wa2_s, [], ["wa2"])
        b.dma("pool", t["g2t"][:, 0:128], g2_s, [], ["g2t"])
        b.dma("sp", t["hal"][:, 0:5, :], sh0, [], ["hal"])
        for bb in range(NS):
            b.dma("sp", t["STf"][bb], st0[:, bb, :], [], ["STf%d" % bb])
            b.cp("pool", t["STb"][bb], t["STf"][bb], ["STf%d" % bb], ["STb%d" % bb])
        stage_rwkv(cfg, t, lambda j: (xrs_raw[:, j, :], "xrs_raw"), lambda c2, ch: ch, lambda c2: mix_s[128:256, :])
        b.dma("sp", sh_s, t["hal2"][:, 0:5, :], ["hal2"], [])
        for bb in range(NS):
            b.dma("sp", s_s[:, bb, :], t["STf"][bb], ["STf%d" % bb], [])
        if stop == 3:
            S.emit(ctx)
            return nc
        S.barrier()

        cfg = CFG_P
        A.reset(base_mark)
        b.dma("sp", g1b, g1.to_broadcast([128, D]), [], ["g1b"])
        load_params(cfg, "p", pp_p)
        W1 = A.alloc([128, 16, 768], BF16)
        for kc in range(16):
            b.dma("pool", W1[:, kc, :], w1p_sb[kc * 128:(kc + 1) * 128, :], [], ["W1_%d" % kc])
        t = sb_alloc()
        load_qkg(t, cfg, qkg_p)
        kTr = A.alloc([64, 4, T_P], BF16)
        Vr = A.alloc([128, 32, 256], BF16)
        for s in range(8):
            stage_norm_T(4, xp[s * 512:(s + 1) * 512, :])
            b.dma("sp", xnT_d[s], xnT.rearrange("p a b -> p (a b)"), ["xnT"], ["xnT_d%d" % s])
            for m in range(4):
                t0 = s * 512 + m * 128
                stage_sb_tok(cfg, t, W1, m, kTr[:, :, t0:t0 + 128], Vr[:, t0 // 128, :],
                             k_p[:, t0:t0 + 128, :].rearrange("h t d -> t h d"),
                             v_p[:, t0:t0 + 128, :].rearrange("h t d -> t h d"))
            items = []
            fin = {}
            for c2 in range(2):
                ob = 6 + (c2 % 2)
                for hh in range(2):
                    h = 2 * c2 + hh
                    hr = slice(hh * 64, hh * 64 + 64)
                    nsteps = 4 * s + 4
                    for step in range(nsteps):
                        kb = 4 * s + 3 - step
                        di = kb - 4 * s
                        zm = [(kTr[:, h, kb * 128:(kb + 1) * 128], t["qT"][:, h, 0:512], 0, 512)]
                        av = [(Vr[:, kb, h * 64:(h + 1) * 64], 0, 512, PS[ob][hr, 0:512], True)]
                        mk = cb[:, CB["maskp"] + 512 * di:CB["maskp"] + 512 * (di + 1)] if di >= 0 else None
                        items.append(dict(step=step, nsteps=nsteps, kr=128, ncol=512, zm=zm, av=av, mask=mk,
                                          zkeys=["kT", "qT"], vkeys=["Vres"], okey="ps%d" % ob))
                fin[len(items) - 1] = (lambda c2=c2, ob=ob, s=s: attn_finish(
                    t, 512, pp[:, cfg.L["sbg"] + c2:cfg.L["sbg"] + c2 + 1],
                    mix_p[c2 * 128:(c2 + 1) * 128, s * 512:(s + 1) * 512], ob))
            attn_run(t, items, fin)
        if stop == 4:
            S.emit(ctx)
            return nc
        S.barrier()

        A.reset(rw_mark)
        W1 = A.alloc([128, 16, 1024], BF16)
        for kc in range(16):
            b.dma("pool", W1[:, kc, :], w1p_rw[kc * 128:(kc + 1) * 128, :], [], ["W1_%d" % kc])
        t = rw_alloc()
        b.dma("pool", t["wa2"], wa2_p, [], ["wa2"])
        b.dma("pool", t["g2t"], g2_p, [], ["g2t"])
        b.memset("pool", t["hal"], 0.0, [], ["hal"])
        for c2 in range(2):
            b.memset("pool", t["STf"][c2], 0.0, [], ["STf%d" % c2])
            b.memset("pool", t["STb"][c2], 0.0, [], ["STb%d" % c2])
        for s in range(8):
            b.dma("sp", xnT.rearrange("p a b -> p (a b)"), xnT_d[s], [], ["xnT"])

            def get_raw(j):
                bank = PS[j % 2]
                bk = "ps%d" % (j % 2)
                for kc in range(16):
                    b.mm(bank[:, 0:512], W1[:, kc, j * 128:(j + 1) * 128], xnT[:, kc, 0:512], ["W1_%d" % kc, "xnT"], [bk],
                         start=kc == 0, stop=kc == 15)
                return bank[:, 0:512], bk
            stage_rwkv(cfg, t, get_raw, lambda c2, ch: c2,
                       lambda c2: mix_p[256 + c2 * 128:256 + (c2 + 1) * 128, s * 512:(s + 1) * 512])
        b.dma("sp", sh_p, t["hal2"][:, :, 0], ["hal2"], [], slow=True)
        for c2 in range(2):
            b.dma("sp", s_p[c2], t["STf"][c2], ["STf%d" % c2], [])
        S.emit(ctx)
    return nc


NTK = 1058
NTO = 1056


def build_phase2():
    nc = bass.Bass("TRN2", target_bir_lowering=False)
    dt = lambda name, shape, ty, kind: nc.dram_tensor(name, shape, ty, kind=kind).ap()
    IN, OUT = "ExternalInput", "ExternalOutput"
    x2in = dt("x2in", [NTK, D], F32, IN)
    cat = dt("cat", [D, NTK], BF16, IN)
    wout = dt("wout", [D, D], F32, IN)
    wup = dt("wup", [D, DFF], F32, IN)
    wgate = dt("wgate", [D, DFF], F32, IN)
    wdown = dt("wdown", [DFF, D], F32, IN)
    g2n = dt("g2n", [1, D], F32, IN)
    ppf = dt("ppf", [128, 4 * NFC], F32, IN)
    conv0 = dt("conv0", [128, NFC, 2], F32, IN)
    hscale = dt("hscale", [128, 1], F32, IN)
    identd = dt("identd", [128, 128], F32, IN)
    y = dt("y", [NTO, D], F32, OUT)
    convp = dt("convp", [128, NFC, 2], F32, OUT)
    convs = dt("convs", [128, NFC, 2], F32, OUT)
    x2d = nc.dram_tensor("x2d", [NTK, D], F32).ap()

    with ExitStack() as ctx:
        S = Sched(nc)
        b = B(nc, S)
        arena_t = ctx.enter_context(nc.sbuf_tensor("arena", [128, ARENA_BYTES // 2], BF16))
        A = Arena(arena_t)
        PS = [ctx.enter_context(nc.psum_tensor("ps%d" % i, [128, 512], F32)) for i in range(8)]
        ident = A.alloc([128, 128], BF16)
        pf = A.alloc([128, 4 * NFC], F32)
        c0t = A.alloc([128, NFC, 2], F32)
        hs = A.alloc([128, 1], F32)
        cstp = A.alloc([128, NFC, 2], F32)
        csts = A.alloc([128, NFC, 2], F32)
        st = A.alloc([128, 8], F32)
        xn2T = A.alloc([128, 16, NTK], BF16)
        b.dma("pool", ident, identd, [], ["ident"])
        b.dma("sp", pf, ppf, [], ["pf"])
        b.dma("sp", c0t, conv0, [], ["c0t"])
        b.dma("sp", hs, hscale, [], ["hs"])
        m_persist = A.mark()

        g2b = A.alloc([128, D], F32)
        catT = A.alloc([128, 16, NTK], BF16)
        Wo = A.alloc([128, 16, D], BF16)
        xt = [A.alloc([128, D], F32) for _ in range(2)]
        x2t = [A.alloc([128, D], F32) for _ in range(2)]
        xn = A.alloc([128, D], BF16)
        b.dma("sp", g2b, g2n.to_broadcast([128, D]), [], ["g2b"])
        for kc in range(16):
            b.dma("sp", catT[:, kc, :], cat[kc * 128:(kc + 1) * 128, :], [], ["catT%d" % kc])
            b.dma("pool", Wo[:, kc, :], wout[kc * 128:(kc + 1) * 128, :], [], ["Wo%d" % kc])
        subt = [(m * 128, 128) for m in range(8)] + [(NTK - 128, 128)]
        for m, (r0, nr) in enumerate(subt):
            i2 = m % 2
            kx, k2 = "xt%d" % i2, "x2t%d" % i2
            b.dma("sp", xt[i2][0:nr], x2in[r0:r0 + nr, :], [], [kx])
            for nb in range(4):
                bank = PS[i2 * 4 + nb]
                bk = "ps%d" % (i2 * 4 + nb)
                for kc in range(16):
                    b.mm(bank[0:nr, :], catT[:, kc, r0:r0 + nr], Wo[:, kc, nb * 512:(nb + 1) * 512],
                         ["catT%d" % kc, "Wo%d" % kc], [bk], start=kc == 0, stop=kc == 15)
                b.tt("dve", x2t[i2][0:nr, nb * 512:(nb + 1) * 512], bank[0:nr, :], xt[i2][0:nr, nb * 512:(nb + 1) * 512],
                     ALU.add, [bk, kx], [k2])
            b.dma("sp", x2d[r0:r0 + nr, :], x2t[i2][0:nr], [k2], ["x2d"])
            b.act(xn[0:nr], x2t[i2][0:nr], AF.Square, [k2], ["xn", "ss"], accum=st[0:nr, 0:1])
            b.rstd(st[0:nr, 1:2], st[0:nr, 0:1], 1.0 / D, RMS_EPS, ["ss"], ["rs"])
            b.stt("dve", xn[0:nr], x2t[i2][0:nr], st[0:nr, 1:2], g2b[0:nr], ALU.mult, ALU.mult, [k2, "rs", "g2b"],
                  ["xn"])
            for g in range(4):
                bank = PS[i2 * 4 + g]
                bk = "ps%d" % (i2 * 4 + g)
                pb = bank[:].bitcast(BF16)
                for c in range(4):
                    b.tr(pb[:, c * 128:c * 128 + nr], xn[0:nr, (4 * g + c) * 128:(4 * g + c + 1) * 128],
                         ident[0:nr, 0:nr], ["xn", "ident"], [bk])
                b.cp(("act", "dve")[g % 2], xn2T[:, 4 * g:4 * g + 4, r0:r0 + nr],
                     pb[:, 0:512].rearrange("p (c t) -> p c t", c=4)[:, :, 0:nr], [bk], ["xn2T"])
        S.barrier()

        A.reset(m_persist)
        hT = A.alloc([128, NFC, NTO], BF16)
        m_hT = A.mark()
        Wu = [A.alloc([128, 16, 256], BF16) for _ in range(2)]
        Wg = [A.alloc([128, 16, 256], BF16) for _ in range(2)]
        gtp = [A.alloc([128, NTK + 2], F32) for _ in range(2)]
        ub = [A.alloc([128, NTK], F32) for _ in range(2)]
        acc = [A.alloc([128, NTK], F32) for _ in range(2)]
        groups = [(0, 353), (353, 353), (706, 352)]
        for blk in range(NFC // 2):
            w2i = blk % 2
            ku, kg = "Wu%d" % w2i, "Wg%d" % w2i
            b.dma("pool", Wu[w2i], wup[:, blk * 256:(blk + 1) * 256].rearrange("(kc p) n -> p kc n", p=128), [], [ku])
            b.dma("pool", Wg[w2i], wgate[:, blk * 256:(blk + 1) * 256].rearrange("(kc p) n -> p kc n", p=128), [], [kg])
            for fi in range(2):
                fc = 2 * blk + fi
                f2 = fc % 2
                kgt, kub, kac = "gtp%d" % f2, "ub%d" % f2, "acc%d" % f2
                G, U, AC = gtp[f2], ub[f2], acc[f2]
                for tg, (c0, n) in enumerate(groups):
                    pi = (fc * 3 + tg) % 4
                    UB, GBk = PS[2 * pi], PS[2 * pi + 1]
                    uk, gk = "ps%d" % (2 * pi), "ps%d" % (2 * pi + 1)
                    for kc in range(16):
                        b.mm(UB[:, 0:n], Wu[w2i][:, kc, fi * 128:(fi + 1) * 128], xn2T[:, kc, c0:c0 + n], [ku, "xn2T"],
                             [uk], start=kc == 0, stop=kc == 15)
                    for kc in range(16):
                        b.mm(GBk[:, 0:n], Wg[w2i][:, kc, fi * 128:(fi + 1) * 128], xn2T[:, kc, c0:c0 + n], [kg, "xn2T"],
                             [gk], start=kc == 0, stop=kc == 15)
                    b.cp("dve", U[:, c0:c0 + n], UB[:, 0:n], [uk], [kub])
                    if tg < 2:
                        b.cp("act", G[:, c0:c0 + n], GBk[:, 0:n], [gk], [kgt])
                    else:
                        b.cp("act", G[:, 706:1026], GBk[:, 0:320], [gk], [kgt])
                        b.cp("act", G[:, 1028:1060], GBk[:, 320:352], [gk], [kgt])
                b.ts("pool", G[:, 0:2], G[:, 0:2], hs[:, 0:1], ALU.mult, [kgt, "hs"], [kgt])
                b.cp("pool", G[:, 1026:1028], c0t[:, fc, :], [kgt, "c0t"], [kgt])
                b.cp("pool", cstp[:, fc, :], G[:, 1024:1026], [kgt], ["cstp"])
                b.cp("pool", csts[:, fc, :], G[:, 1058:1060], [kgt], ["csts"])
                wcol = lambda i: pf[:, i * NFC + fc:i * NFC + fc + 1]
                b.ts("dve", AC[:, 0:NTK], G[:, 2:NTK + 2], wcol(2), ALU.mult, [kgt, "pf"], [kac], s2=wcol(3), op1=ALU.add)
                b.stt("dve", AC[:, 0:NTK], G[:, 1:NTK + 1], wcol(1), AC[:, 0:NTK], ALU.mult, ALU.add, [kgt, "pf", kac],
                      [kac])
                b.stt("dve", AC[:, 0:NTK], G[:, 0:NTK], wcol(0), AC[:, 0:NTK], ALU.mult, ALU.add, [kgt, "pf", kac],
                      [kac])
                b.act(AC[:, 0:NTK], AC[:, 0:NTK], AF.Silu, [kac], [kac])
                b.tt("dve", hT[:, fc, 0:1024], AC[:, 0:1024], U[:, 2:1026], ALU.mult, [kac, kub], ["hT%d" % fc])
                b.tt("pool", hT[:, fc, 1024:1056], AC[:, 1026:1058], U[:, 1026:1058], ALU.mult, [kac, kub],
                     ["hT%d" % fc])
        b.dma("sp", convp, cstp, ["cstp"], [])
        b.dma("sp", convs, csts, ["csts"], [])
        S.barrier()

        A.reset(m_hT)
        Wd = [A.alloc([128, NFC, 256], BF16) for _ in range(2)]
        x2s = [A.alloc([128, 256], F32) for _ in range(2)]
        yt = [A.alloc([128, 256], F32) for _ in range(2)]
        subo = [(m * 128, 128) for m in range(8)] + [(NTO - 128, 128)]
        cnt = 0
        for nb in range(8):
            w2i = nb % 2
            kd = "Wd%d" % w2i
            b.dma("pool", Wd[w2i], wdown[:, nb * 256:(nb + 1) * 256].rearrange("(fc p) n -> p fc n", p=128), [], [kd])
            for m, (r0, nr) in enumerate(subo):
                i2 = cnt % 2
                bank = PS[cnt % 8]
                bk = "ps%d" % (cnt % 8)
                cnt += 1
                b.dma("sp", x2s[i2][0:nr], x2d[2 + r0:2 + r0 + nr, nb * 256:(nb + 1) * 256], [], ["x2s%d" % i2])
                for fc in range(NFC):
                    b.mm(bank[0:nr, 0:256], hT[:, fc, r0:r0 + nr], Wd[w2i][:, fc, :], [kd], [bk], start=fc == 0,
                         stop=fc == NFC - 1)
                b.tt("dve", yt[i2][0:nr], bank[0:nr, 0:256], x2s[i2][0:nr], ALU.add, [bk, "x2s%d" % i2], ["yt%d" % i2])
                b.dma("sp", y[r0:r0 + nr, nb * 256:(nb + 1) * 256], yt[i2][0:nr], ["yt%d" % i2], [])
        S.emit(ctx)
    return nc


_CACHE = {}


def _progs():
    if "p1" not in _CACHE:
        _CACHE["p1"] = build_phase1()
        _CACHE["p2"] = build_phase2()
    return _CACHE["p1"], _CACHE["p2"]


def _pp(cfg, base, mu_cols, inp):
    L = cfg.L
    nc2 = cfg.nc2
    pp = np.zeros((128, L["n"]), np.float32)
    mu = inp["mu_shift"][0]
    for j, c0 in enumerate(mu_cols):
        pp[:, L["mu"] + j] = mu[c0:c0 + 128]
    vecs = dict(w0=inp["w0"][0], a0=inp["a0"][0], kk=inp["k_k"][0], ka=inp["k_a"][0], rk=inp["r_k"][0].reshape(-1),
                lw=inp["lnx_w"][0], lb=inp["lnx_b"][0], sbg=inp["sb_out_g"][0].reshape(-1))
    for k, v in vecs.items():
        for c2 in range(nc2):
            pp[:, L[k] + c2] = v[base + c2 * 128:base + (c2 + 1) * 128]
    return pp


def _phase1(inp):
    p1, p2 = _progs()
    cb, cf = make_consts()
    w_in = inp["w_in"][0]
    RW = 3072
    in1 = []
    for c in range(8):
        bq, j = divmod(c, 4)
        pb, sbase = 256 * j, 128 * c
        d = {}
        d["xp"] = np.ascontiguousarray(inp["x_prompt"][bq])
        d["xs"] = np.ascontiguousarray(inp["x_sample"].reshape(NS * T_S, D))
        d["w1p_sb"] = np.ascontiguousarray(np.concatenate([w_in[:, o + pb:o + pb + 256] for o in (0, 1024, 2048)], 1))
        rw_cols_p = [RW + pb, RW + pb + 128, RW + 1024 + pb, RW + 1024 + pb + 128, RW + 2048 + pb, RW + 2048 + pb + 128,
                     RW + 3072, RW + 3200]
        d["w1p_rw"] = np.ascontiguousarray(np.concatenate([w_in[:, o:o + 128] for o in rw_cols_p], 1))
        rw_cols_s = [RW + sbase, RW + 1024 + sbase, RW + 2048 + sbase, RW + 3072, RW + 3200]
        d["w1s"] = np.ascontiguousarray(np.concatenate([w_in[:, o + sbase:o + sbase + 128] for o in (0, 1024, 2048)] +
                                                       [w_in[:, o:o + 128] for o in rw_cols_s], 1))
        kc = inp["cache_sb_k"][0][:, 2 * c:2 * c + 2]
        d["kcT"] = np.ascontiguousarray(kc.transpose(3, 1, 0, 2))
        vcc = inp["cache_sb_v"][0][:, 2 * c:2 * c + 2]
        d["vc"] = np.ascontiguousarray(vcc.transpose(0, 2, 1, 3).reshape(NS, PAST, 128))
        s0 = inp["state_rwkv"][0][:, 2 * c:2 * c + 2]
        d["st0"] = np.ascontiguousarray(s0.transpose(1, 3, 0, 2).reshape(128, NS, 64))
        sh = inp["state_rwkv_shift"][0][:, 0, :]
        d["sh0"] = np.ascontiguousarray(np.stack([sh[:, o - RW:o - RW + 128] for o in rw_cols_s], 1).transpose(2, 1, 0))
        d["g1"] = np.ascontiguousarray(inp["norm1_g"][0][None])
        qg, kg = inp["q_norm_g"][0], inp["k_norm_g"][0]
        d["qkg_p"] = np.concatenate([np.tile(qg, 4), np.tile(kg, 4)])[None].astype(np.float32)
        d["qkg_s"] = np.concatenate([np.tile(qg, 2), np.tile(kg, 2)])[None].astype(np.float32)
        d["pp_p"] = _pp(CFG_P, pb, [o - RW for o in rw_cols_p], inp)
        d["pp_s"] = _pp(CFG_S, sbase, [o - RW for o in rw_cols_s], inp)
        d["wa2_p"] = np.ascontiguousarray(np.concatenate([inp["w2"][0][:, pb:pb + 256], inp["a2"][0][:, pb:pb + 256]], 0))
        d["wa2_s"] = np.ascontiguousarray(np.concatenate([inp["w2"][0][:, sbase:sbase + 128],
                                                          inp["a2"][0][:, sbase:sbase + 128]], 0))
        d["g2_p"] = np.ascontiguousarray(inp["g2"][0][:, pb:pb + 256])
        d["g2_s"] = np.ascontiguousarray(inp["g2"][0][:, sbase:sbase + 128])
        d["cbd"] = cb
        d["cfd"] = cf
        in1.append(d)
    r1 = run_bass_kernel_spmd(p1, in1, core_ids=list(range(8))).results
    _CACHE["r1"] = r1

    f32 = np.float32
    k_prompt = np.zeros((1, 2, 16, T_P, 64), f32)
    v_prompt = np.zeros((1, 2, 16, T_P, 64), f32)
    rwkv_prompt = np.zeros((1, 2, 16, 64, 64), f32)
    shift_prompt = np.zeros((1, 2, 1, 3328), f32)
    k_sample = np.zeros((1, NS, 16, T_S, 64), f32)
    v_sample = np.zeros((1, NS, 16, T_S, 64), f32)
    rwkv_sample = np.zeros((1, NS, 16, 64, 64), f32)
    shift_sample = np.zeros((1, NS, 1, 3328), f32)
    cat_p = [np.zeros((D, T_P), ml_dtypes.bfloat16) for _ in range(2)]
    cat_s = np.zeros((D, NS * T_S), ml_dtypes.bfloat16)
    for c in range(8):
        bq, j = divmod(c, 4)
        r = r1[c]
        k_prompt[0, bq, 4 * j:4 * j + 4] = r["k_p"]
        v_prompt[0, bq, 4 * j:4 * j + 4] = r["v_p"]
        rwkv_prompt[0, bq, 4 * j:4 * j + 4] = r["s_p"].reshape(2, 2, 64, 64).transpose(0, 1, 3, 2).reshape(4, 64, 64)
        shp = r["sh_p"]
        pb = 256 * j
        for jj, o in enumerate([pb, pb + 128, 1024 + pb, 1024 + pb + 128, 2048 + pb, 2048 + pb + 128, 3072, 3200]):
            shift_prompt[0, bq, 0, o:o + 128] = shp[:, jj]
        k_sample[0, :, 2 * c:2 * c + 2] = r["k_s"].reshape(NS, T_S, 2, 64).transpose(0, 2, 1, 3)
        v_sample[0, :, 2 * c:2 * c + 2] = r["v_s"].reshape(NS, T_S, 2, 64).transpose(0, 2, 1, 3)
        rwkv_sample[0, :, 2 * c:2 * c + 2] = r["s_s"].reshape(2, 64, NS, 64).transpose(2, 0, 3, 1)
        shs = r["sh_s"]
        sbase = 128 * c
        for jj, o in enumerate([sbase, 1024 + sbase, 2048 + sbase, 3072, 3200]):
            shift_sample[0, :, 0, o:o + 128] = shs[:, jj, :].T
        cat_p[bq][256 * j:256 * j + 256] = r["mix_p"][0:256]
        cat_p[bq][1024 + 256 * j:1024 + 256 * j + 256] = r["mix_p"][256:512]
        cat_s[128 * c:128 * c + 128] = r["mix_s"][0:128]
        cat_s[1024 + 128 * c:1024 + 128 * c + 128] = r["mix_s"][128:256]

    outs1 = (k_prompt, v_prompt, rwkv_prompt, shift_prompt, k_sample, v_sample, rwkv_sample, shift_sample)
    return outs1, cat_p, cat_s


def _phase2(inp, cat_p, cat_s):
    p1, p2 = _progs()
    f32 = np.float32
    cw, cbias = inp["ffn_conv_w"][0], inp["ffn_conv_b"][0]
    ppf = np.concatenate([cw[i].reshape(NFC, 128).T for i in range(3)] + [cbias.reshape(NFC, 128).T], 1).astype(f32)
    in2 = []
    for c in range(8):
        bq, j = divmod(c, 4)
        d = {}
        x2 = np.zeros((NTK, D), f32)
        ct = np.zeros((D, NTK), ml_dtypes.bfloat16)
        t0 = 1024 * j
        if j > 0:
            x2[0:2] = inp["x_prompt"][bq, t0 - 2:t0]
            ct[:, 0:2] = cat_p[bq][:, t0 - 2:t0]
        x2[2:1026] = inp["x_prompt"][bq, t0:t0 + 1024]
        ct[:, 2:1026] = cat_p[bq][:, t0:t0 + 1024]
        x2[1026:] = inp["x_sample"][c]
        ct[:, 1026:] = cat_s[:, 32 * c:32 * c + 32]
        d["x2in"] = x2
        d["cat"] = ct
        d["wout"] = np.ascontiguousarray(inp["w_out"][0])
        d["wup"] = np.ascontiguousarray(inp["w_ffn_up"][0])
        d["wgate"] = np.ascontiguousarray(inp["w_ffn_gate"][0])
        d["wdown"] = np.ascontiguousarray(inp["w_ffn_down"][0])
        d["g2n"] = np.ascontiguousarray(inp["norm2_g"][0][None])
        d["ppf"] = ppf
        d["conv0"] = np.ascontiguousarray(inp["state_ffn_conv"][0][c].reshape(2, NFC, 128).transpose(2, 1, 0))
        d["hscale"] = np.full((128, 1), 0.0 if j == 0 else 1.0, f32)
        d["identd"] = np.eye(128, dtype=f32)
        in2.append(d)
    r2 = run_bass_kernel_spmd(p2, in2, core_ids=list(range(8))).results
    y_prompt = np.zeros((2, T_P, D), f32)
    y_sample = np.zeros((NS, T_S, D), f32)
    conv_prompt = np.zeros((1, 2, 2, DFF), f32)
    conv_sample = np.zeros((1, NS, 2, DFF), f32)
    for c in range(8):
        bq, j = divmod(c, 4)
        r = r2[c]
        y_prompt[bq, 1024 * j:1024 * j + 1024] = r["y"][0:1024]
        y_sample[c] = r["y"][1024:1056]
        if j == 3:
            conv_prompt[0, bq] = r["convp"].transpose(2, 1, 0).reshape(2, DFF)
        conv_sample[0, c] = r["convs"].transpose(2, 1, 0).reshape(2, DFF)
    return y_prompt, y_sample, conv_prompt, conv_sample


def kernel(**inp):
    inp = {k: np.asarray(v) for k, v in inp.items()}
    (k_prompt, v_prompt, rwkv_prompt, shift_prompt, k_sample, v_sample, rwkv_sample, shift_sample), cat_p, cat_s = \
        _phase1(inp)
    y_prompt, y_sample, conv_prompt, conv_sample = _phase2(inp, cat_p, cat_s)
    return (y_prompt, y_sample, k_prompt, v_prompt, rwkv_prompt, shift_prompt, conv_prompt,
            k_sample, v_sample, rwkv_sample, shift_sample, conv_sample)
```

```python
import numpy as np
from contextlib import ExitStack
import concourse.bass as bass
import concourse.mybir as mybir
from concourse.bass_utils import run_bass_kernel_spmd
import ml_dtypes

F32 = mybir.dt.float32
BF16 = mybir.dt.bfloat16
I32 = mybir.dt.int32
AF = mybir.ActivationFunctionType
ALU = mybir.AluOpType
AX = mybir.AxisListType

D = 2048
T_P = 4096
NS = 8
T_S = 32
PAST = 2048
DFF = 5632
NFC = DFF // 128
RMS_EPS = 1e-6
LNX_EPS = 1e-5 * 64
ENGS = ("pe", "act", "dve", "pool", "sp")


class Sched:
    def __init__(self, nc, n_dma_sems=(("sp", 20), ("pool", 10), ("act", 2))):
        self.nc = nc
        self.ops = []
        self.last_w = {}
        self.readers = {}
        self.n_dma_sems = dict(n_dma_sems)
        self.dnext = {e: 0 for e in self.n_dma_sems}
        self.dlast = {e: [None] * n for e, n in self.n_dma_sems.items()}
        self.elast = {e: None for e in ENGS}

    def op(self, eng, fn, reads=(), writes=(), dma=False):
        i = len(self.ops)
        raw = set()
        oth = set()
        for k in reads:
            if k in self.last_w:
                raw.add(self.last_w[k])
        for k in writes:
            if k in self.last_w:
                oth.add(self.last_w[k])
            for r in self.readers.get(k, ()):
                oth.add(r)
        o = dict(eng=eng, fn=fn, raw=raw, oth=oth - raw, dma=dma, sig=None, slot=None, prev_on_sem=None)
        if dma:
            k = self.dnext[eng]
            self.dnext[eng] = (k + 1) % self.n_dma_sems[eng]
            o["slot"] = k
            o["prev_on_sem"] = self.dlast[eng][k]
            self.dlast[eng][k] = i
        self.ops.append(o)
        self.elast[eng] = i
        for k in reads:
            self.readers.setdefault(k, []).append(i)
        for k in writes:
            self.last_w[k] = i
            self.readers[k] = []
        return i

    def barrier(self):
        deps = set(v for v in self.elast.values() if v is not None)
        for e, l in self.dlast.items():
            deps |= set(v for v in l if v is not None)
        for e in ENGS:
            i = len(self.ops)
            self.ops.append(dict(eng=e, fn=None, raw=set(deps), oth=set(), dma=False, sig=None, slot=None,
                                 prev_on_sem=None))
        self.last_w = {}
        self.readers = {}

    def _needs_wait(self, o, d, is_raw):
        od = self.ops[d]
        if od["dma"] or o["dma"] or od["eng"] != o["eng"]:
            return True
        if o["fn"] is None:
            return True
        return is_raw and o["eng"] != "pe"

    def emit(self, ctx):
        import os
        nmax = int(os.environ.get("P1_NOPS", "0"))
        if nmax:
            self.ops = self.ops[:nmax]
            for e, l in self.dlast.items():
                for k in range(len(l)):
                    cands = [i for i, o in enumerate(self.ops) if o["dma"] and o["eng"] == e and o["slot"] == k]
                    l[k] = cands[-1] if cands else None
        nc = self.nc
        ops = self.ops
        need = [False] * len(ops)
        for i, o in enumerate(ops):
            for d in o["raw"]:
                if self._needs_wait(o, d, True):
                    need[d] = True
            for d in o["oth"]:
                if self._needs_wait(o, d, False):
                    need[d] = True
        esem = {e: ctx.enter_context(nc.semaphore("s_" + e)) for e in ENGS}
        dsem = {e: [ctx.enter_context(nc.semaphore("d_%s%d" % (e, k))) for k in range(n)]
                for e, n in self.n_dma_sems.items()}
        ecount = {e: 0 for e in ENGS}
        dcount = {e: [0] * n for e, n in self.n_dma_sems.items()}
        for i, o in enumerate(ops):
            e = o["eng"]
            if o["fn"] is None:
                continue
            if o["dma"]:
                k = o["slot"]
                dcount[e][k] += 16
                o["sig"] = (dsem[e][k], dcount[e][k], ("d", e, k))
            elif need[i]:
                ecount[e] += 1
                o["sig"] = (esem[e], ecount[e], ("e", e))
        streams = {e: [] for e in ENGS}
        for i, o in enumerate(ops):
            streams[o["eng"]].append(i)
        engobj = dict(pe="tensor", act="scalar", dve="vector", pool="gpsimd", sp="sync")
        dlast = self.dlast
        with nc.Block() as block:
            def make(e):
                def body(eng):
                    waited = {}

                    def wait_for(d):
                        if ops[d]["sig"] is None:
                            return
                        sem, val, key = ops[d]["sig"]
                        if waited.get(key, 0) >= val:
                            return
                        waited[key] = val
                        eng.wait_ge(sem, val)

                    for i in streams[e]:
                        o = ops[i]
                        if o["dma"] and o["prev_on_sem"] is not None:
                            wait_for(o["prev_on_sem"])
                        for d in sorted(o["raw"]):
                            if self._needs_wait(o, d, True):
                                wait_for(d)
                        for d in sorted(o["oth"]):
                            if self._needs_wait(o, d, False):
                                wait_for(d)
                        if o["fn"] is None:
                            continue
                        ins = o["fn"](eng)
                        if o["sig"] is not None:
                            sem, val, key = o["sig"]
                            ins.then_inc(sem, 16 if o["dma"] else 1)
                    for k, d in enumerate(dlast.get(e, [])):
                        if d is not None:
                            wait_for(d)
                return body
            for e in ENGS:
                if streams[e]:
                    getattr(block, engobj[e])(make(e))


class B:
    def __init__(self, nc, S):
        self.nc = nc
        self.S = S
        self.rr = 0
        self.suffix = ""
        self.local = ()

    def _op(self, eng, fn, r=(), w=(), dma=False):
        return self.S.op(eng, fn, self._k(r), self._k(w), dma=dma)

    def _k(self, keys):
        if not self.suffix:
            return list(keys)
        return [k + self.suffix if k.rstrip("0123456789") in self.local else k for k in keys]

    def dma(self, q, out, in_, r=(), w=(), slow=False):
        if slow:
            self._op(q, lambda e: e.dma_start(out=out, in_=in_, allow_slow_non_contiguous=True), r, w, dma=True)
        else:
            self._op(q, lambda e: e.dma_start(out=out, in_=in_), r, w, dma=True)

    def mm(self, out, lhsT, rhs, r, w, start=True, stop=True, skip=False):
        if skip:
            self._op("pe", lambda e: e.matmul(out, lhsT=lhsT, rhs=rhs, start=start, stop=stop, skip_group_check=True),
                      r, w)
        else:
            self._op("pe", lambda e: e.matmul(out, lhsT=lhsT, rhs=rhs, start=start, stop=stop), r, w)

    def tr(self, out, in_, ident, r, w):
        self._op("pe", lambda e: e.transpose(out, in_, ident), r, w)

    def act(self, out, in_, func, r, w, bias=None, scale=None, accum=None):
        kw = {}
        if bias is not None:
            kw["bias"] = bias
        if scale is not None:
            kw["scale"] = scale
        if accum is not None:
            kw["accum_out"] = accum
        self._op("act", lambda e: e.activation(out=out, in_=in_, func=func, **kw), r, w)

    def tt(self, eng, out, in0, in1, op, r, w):
        self._op(eng, lambda e: e.tensor_tensor(out=out, in0=in0, in1=in1, op=op), r, w)

    def ts(self, eng, out, in0, s1, op0, r, w, s2=None, op1=None):
        if s2 is None:
            self._op(eng, lambda e: e.tensor_scalar(out=out, in0=in0, scalar1=s1, scalar2=None, op0=op0), r, w)
        else:
            self._op(eng, lambda e: e.tensor_scalar(out=out, in0=in0, scalar1=s1, scalar2=s2, op0=op0, op1=op1), r, w)

    def stt(self, eng, out, in0, scalar, in1, op0, op1, r, w):
        self._op(eng, lambda e: e.scalar_tensor_tensor(out=out, in0=in0, scalar=scalar, in1=in1, op0=op0, op1=op1),
                  r, w)

    def cp(self, eng, out, in_, r, w):
        if eng == "act":
            self._op("act", lambda e: e.activation(out=out, in_=in_, func=AF.Copy), r, w)
        else:
            self._op(eng, lambda e: e.tensor_copy(out=out, in_=in_), r, w)

    def memset(self, eng, out, val, r, w):
        self._op(eng, lambda e: e.memset(out, val), r, w)

    def reduce(self, eng, out, in_, r, w):
        self._op(eng, lambda e: e.tensor_reduce(out=out, in_=in_, axis=AX.X, op=ALU.add), r, w)

    def scan(self, out, d0, d1, r, w):
        self._op("dve", lambda e: e.tensor_tensor_scan(out=out, data0=d0, data1=d1, initial=0.0, op0=ALU.mult,
                                                        op1=ALU.add), r, w)

    def rstd(self, out, in_, scale, eps, r, w):
        self.act(out, in_, AF.Ln, r, w, bias=eps, scale=scale)
        self.act(out, out, AF.Exp, w, w, scale=-0.5)


CB = dict(ident=0, trineg=128, ones=256, bdones=384, bdmean=512, maskp=640, masks=640 + 2048)
CB_N = 640 + 2048 + 256
CF = dict(maskA64=0, maskL64=128, id64=192, maskA32=256, maskL32=320, id32=352, seg64=384, seg32=896)
CF_N = 896 + 256


def make_consts():
    cb = np.zeros((128, CB_N), np.float32)
    i = np.arange(128)
    cb[:, 0:128] = np.eye(128)
    cb[:, 128:256] = -(i[:, None] >= i[None, :]).astype(np.float32)
    cb[:, 256:384] = 1.0
    blk = (i[:, None] // 64 == i[None, :] // 64).astype(np.float32)
    cb[:, 384:512] = blk
    cb[:, 512:640] = blk / 64.0
    q = np.arange(512)
    for d in range(4):
        cb[:, 640 + 512 * d: 640 + 512 * (d + 1)] = ((128 * d + i[:, None]) < q[None, :]).astype(np.float32)
    q2 = np.arange(256)
    cb[0:32, 640 + 2048:] = (i[0:32, None] < (q2[None, :] % 32)).astype(np.float32)
    cf = np.zeros((128, CF_N), np.float32)
    for C, ka, kl, ki in ((64, "maskA64", "maskL64", "id64"), (32, "maskA32", "maskL32", "id32")):
        s = np.arange(C)
        for r0 in (0, 64):
            cf[r0:r0 + C, CF[ka]:CF[ka] + C] = (s[:, None] < s[None, :])
            cf[r0:r0 + C, CF[ka] + C:CF[ka] + 2 * C] = (s[:, None] <= s[None, :])
            cf[r0:r0 + C, CF[kl]:CF[kl] + C] = (s[None, :] < s[:, None])
            cf[r0:r0 + C, CF[ki]:CF[ki] + C] = np.eye(C)
    cf[:, CF["seg64"]:CF["seg64"] + 512] = (np.arange(512) % 64 != 0)[None, :]
    cf[:, CF["seg32"]:CF["seg32"] + 256] = (np.arange(256) % 32 != 0)[None, :]
    return cb, cf


def pp_layout(nc2):
    nch = 3 * nc2 + 2
    L = {}
    o = 0
    L["mu"] = o; o += nch
    for k in ("w0", "a0", "kk", "ka", "rk", "lw", "lb", "sbg"):
        L[k] = o; o += nc2
    L["n"] = o
    return L


class Cfg:
    def __init__(self, name, nh, ntile, C, nseg):
        self.name = name
        self.nh = nh
        self.HC = nh * 64
        self.nc2 = nh // 2
        self.NT = ntile
        self.nsub = ntile // 128
        self.C = C
        self.nch = ntile // C
        self.nseg = nseg
        self.seglen = ntile // nseg
        self.nrch = 3 * self.nc2 + 2
        self.nlev = {64: 5, 32: 4}[C]
        self.L = pp_layout(self.nc2)


CFG_P = Cfg("p", 4, 512, 64, 1)
CFG_S = Cfg("s", 2, 256, 32, 8)
ARENA_BYTES = 200 * 1024


class Arena:
    def __init__(self, tile):
        self.t = tile
        self.off = 0

    def mark(self):
        return self.off

    def reset(self, m):
        self.off = m

    def alloc(self, shape, dtype):
        free = 1
        for s in shape[1:]:
            free *= s
        ncols = free * (2 if dtype == F32 else 1)
        ncols = (ncols + 15) // 16 * 16
        assert (self.off + ncols) * 2 <= ARENA_BYTES, ("arena overflow", self.off * 2, ncols * 2)
        ap = self.t[:, self.off:self.off + ncols]
        self.off += ncols
        if dtype == F32:
            ap = ap.bitcast(F32)
        ap = ap[:, 0:free]
        if len(shape) == 3:
            ap = ap.rearrange("p (a b) -> p a b", a=shape[1])
        elif len(shape) == 4:
            ap = ap.rearrange("p (a b c) -> p a b c", a=shape[1], b=shape[2])
        if shape[0] < 128:
            ap = ap[0:shape[0]]
        return ap


def build_phase1(stop=None):
    import os
    stop = stop if stop is not None else float(os.environ.get('P1_STOP', '99'))
    nc = bass.Bass("TRN2", target_bir_lowering=False)
    dt = lambda name, shape, ty, kind: nc.dram_tensor(name, shape, ty, kind=kind).ap()
    IN, OUT = "ExternalInput", "ExternalOutput"
    xp = dt("xp", [T_P, D], F32, IN)
    xs = dt("xs", [NS * T_S, D], F32, IN)
    w1p_sb = dt("w1p_sb", [D, 768], F32, IN)
    w1p_rw = dt("w1p_rw", [D, 1024], F32, IN)
    w1s = dt("w1s", [D, 1024], F32, IN)
    kcT = dt("kcT", [64, 2, NS, PAST], F32, IN)
    vc = dt("vc", [NS, PAST, 128], F32, IN)
    st0 = dt("st0", [128, NS, 64], F32, IN)
    sh0 = dt("sh0", [128, CFG_S.nrch, NS], F32, IN)
    g1 = dt("g1", [1, D], F32, IN)
    qkg_p = dt("qkg_p", [1, 512], F32, IN)
    qkg_s = dt("qkg_s", [1, 256], F32, IN)
    pp_p = dt("pp_p", [128, CFG_P.L["n"]], F32, IN)
    pp_s = dt("pp_s", [128, CFG_S.L["n"]], F32, IN)
    wa2_p = dt("wa2_p", [128, 256], F32, IN)
    wa2_s = dt("wa2_s", [128, 128], F32, IN)
    g2_p = dt("g2_p", [128, 256], F32, IN)
    g2_s = dt("g2_s", [128, 128], F32, IN)
    cbd = dt("cbd", [128, CB_N], F32, IN)
    cfd = dt("cfd", [128, CF_N], F32, IN)
    mix_p = dt("mix_p", [512, T_P], BF16, OUT)
    mix_s = dt("mix_s", [256, NS * T_S], BF16, OUT)
    k_p = dt("k_p", [4, T_P, 64], F32, OUT)
    v_p = dt("v_p", [4, T_P, 64], F32, OUT)
    s_p = dt("s_p", [2, 128, 64], F32, OUT)
    sh_p = dt("sh_p", [128, CFG_P.nrch], F32, OUT)
    k_s = dt("k_s", [NS * T_S, 128], F32, OUT)
    v_s = dt("v_s", [NS * T_S, 128], F32, OUT)
    s_s = dt("s_s", [128, NS, 64], F32, OUT)
    sh_s = dt("sh_s", [128, CFG_S.nrch, NS], F32, OUT)
    xnT_d = nc.dram_tensor("xnT_d", [8, 128, 16 * 512], BF16).ap()

    with ExitStack() as ctx:
        S = Sched(nc)
        b = B(nc, S)
        arena_t = ctx.enter_context(nc.sbuf_tensor("arena", [128, ARENA_BYTES // 2], BF16))
        A = Arena(arena_t)
        PS = [ctx.enter_context(nc.psum_tensor("ps%d" % i, [128, 512], F32)) for i in range(8)]

        cb = A.alloc([128, CB_N], BF16)
        cf = A.alloc([128, CF_N], F32)
        st = A.alloc([128, 32], F32)
        xnT = A.alloc([128, 16, 512], BF16)
        pp = A.alloc([128, 32], F32)
        omka = A.alloc([128, 2], F32)
        xrs_raw = A.alloc([128, 5, 256], F32)
        rw_mark = A.mark()
        g1b = A.alloc([128, D], F32)
        xt = [A.alloc([128, D], F32) for _ in range(2)]
        xn = A.alloc([128, D], BF16)
        b.dma("pool", cb, cbd, [], ["cb"])
        b.dma("sp", cf, cfd, [], ["cf"])
        b.dma("sp", g1b, g1.to_broadcast([128, D]), [], ["g1b"])
        ident = cb[:, 0:128]
        trineg = cb[:, 128:256]
        ones = cb[:, 256:384]
        bdones = cb[:, 384:512]
        bdmean = cb[:, 512:640]
        base_mark = A.mark()
        if stop == 0:
            S.emit(ctx)
            return nc

        def bfview(ps):
            return ps[:].bitcast(BF16)

        def stage_norm_T(nsub, xrows):
            for m in range(nsub):
                i2 = m % 2
                kx = "xt%d" % i2
                b.dma("sp", xt[i2], xrows[m * 128:(m + 1) * 128, :], [], [kx])
                b.act(xn, xt[i2], AF.Square, [kx], ["xn", "ss"], accum=st[:, 0:1])
                b.rstd(st[:, 1:2], st[:, 0:1], 1.0 / D, RMS_EPS, ["ss"], ["rs"])
                b.stt("dve", xn, xt[i2], st[:, 1:2], g1b, ALU.mult, ALU.mult, [kx, "rs", "g1b"], ["xn"])
                for g in range(4):
                    bank = PS[g % 2]
                    bk = "ps%d" % (g % 2)
                    pb = bfview(bank)
                    for c in range(4):
                        b.tr(pb[:, c * 128:(c + 1) * 128], xn[:, (4 * g + c) * 128:(4 * g + c + 1) * 128], ident,
                             ["xn", "cb"], [bk])
                    b.cp(("act", "dve")[g % 2], xnT[:, 4 * g:4 * g + 4, m * 128:(m + 1) * 128],
                         pb[:, 0:512].rearrange("p (c t) -> p c t", c=4), [bk], ["xnT"])

        def load_params(cfg, sfx, pp_d, omka_needed=True):
            L = cfg.L
            b.dma("sp", pp[:, 0:L["n"]], pp_d, [], ["pp"])
            b.ts("dve", omka[:, 0:cfg.nc2], pp[:, L["ka"]:L["ka"] + cfg.nc2], -1.0, ALU.mult, ["pp"], ["omka"],
                 s2=1.0, op1=ALU.add)

        def sb_alloc():
            t = {}
            t["sqk"] = A.alloc([128, 512], F32)
            t["qkt"] = A.alloc([128, 512], F32)
            t["qn"] = A.alloc([128, 256], BF16)
            t["knf"] = [A.alloc([128, 256], F32) for _ in range(2)]
            t["knb"] = A.alloc([128, 256], BF16)
            t["vf"] = [A.alloc([128, 256], F32) for _ in range(2)]
            t["qkg"] = A.alloc([128, 512], F32)
            t["qT"] = A.alloc([64, 4, 512], BF16)
            t["ef"] = [A.alloc([128, 512], F32) for _ in range(2)]
            t["Lb"] = [A.alloc([128, 512], BF16) for _ in range(2)]
            t["attn"] = [A.alloc([128, 512], BF16) for _ in range(2)]
            t["Cc"] = A.alloc([128, 512], F32)
            t["of"] = A.alloc([128, 512], F32)
            t["osq"] = A.alloc([128, 512], BF16)
            t["rso"] = A.alloc([128, 512], F32)
            t["onb"] = A.alloc([128, 512], BF16)
            return t

        def stage_sb_tok(cfg, t, W1, m, kT_dst, V_dst, k_out, v_out):
            HC, nh = cfg.HC, cfg.nh
            tl = slice(m * 128, (m + 1) * 128)
            i2 = m % 2
            GA, GB, R0 = PS[0], PS[1], PS[7]
            for kc in range(16):
                b.mm(GA[:, 0:2 * HC], xnT[:, kc, tl], W1[:, kc, 0:2 * HC], ["xnT", "W1_%d" % kc], ["ps0"], start=kc == 0,
                     stop=kc == 15)
            for kc in range(16):
                b.mm(GB[:, 0:HC], xnT[:, kc, tl], W1[:, kc, 2 * HC:3 * HC], ["xnT", "W1_%d" % kc], ["ps1"], start=kc == 0,
                     stop=kc == 15)
            b.act(t["sqk"][:, 0:2 * HC], GA[:, 0:2 * HC], AF.Square, ["ps0"], ["sqk"])
            b.reduce("dve", st[:, 4:4 + 2 * nh], t["sqk"][:, 0:2 * HC].rearrange("p (h d) -> p h d", d=64), ["sqk"],
                     ["ssqk"])
            b.rstd(st[:, 4:4 + 2 * nh], st[:, 4:4 + 2 * nh], 1.0 / 64, RMS_EPS, ["ssqk"], ["ssqk"])
            b.tt("dve", t["qkt"][:, 0:2 * HC].rearrange("p (h d) -> p h d", d=64),
                 GA[:, 0:2 * HC].rearrange("p (h d) -> p h d", d=64),
                 st[:, 4:4 + 2 * nh].unsqueeze(2).to_broadcast([128, 2 * nh, 64]), ALU.mult, ["ps0", "ssqk"], ["qkt"])
            b.tt("pool", t["qn"][:, 0:HC], t["qkt"][:, 0:HC], t["qkg"][:, 0:HC], ALU.mult, ["qkt", "qkg"], ["qn"])
            b.tt("dve", t["knf"][i2][:, 0:HC], t["qkt"][:, HC:2 * HC], t["qkg"][:, HC:2 * HC], ALU.mult,
                 ["qkt", "qkg"], ["knf%d" % i2])
            b.cp("pool", t["knb"][:, 0:HC], t["knf"][i2][:, 0:HC], ["knf%d" % i2], ["knb"])
            b.dma("sp", k_out, t["knf"][i2][:, 0:HC] if cfg is CFG_S else
                  t["knf"][i2][:, 0:HC].rearrange("p (h d) -> p h d", d=64), ["knf%d" % i2], [])
            pb = bfview(R0)
            for h in range(nh):
                b.tr(pb[0:64, h * 128:(h + 1) * 128], t["qn"][:, h * 64:(h + 1) * 64], ident, ["qn", "cb"], ["ps7"])
                b.tr(pb[0:64, (nh + h) * 128:(nh + h + 1) * 128], t["knb"][:, h * 64:(h + 1) * 64], ident,
                     ["knb", "cb"], ["ps7"])
            b.cp("act", t["qT"][:, 0:nh, tl], pb[0:64, 0:nh * 128].rearrange("p (c t) -> p c t", t=128),
                 ["ps7"], ["qT"])
            b.cp("act", kT_dst, pb[0:64, nh * 128:2 * nh * 128].rearrange("p (c t) -> p c t", t=128), ["ps7"], ["kT"])
            b.cp("act", t["vf"][i2][:, 0:HC], GB[:, 0:HC], ["ps1"], ["vf%d" % i2])
            if V_dst is not None:
                b.cp("pool", V_dst, t["vf"][i2][:, 0:HC], ["vf%d" % i2], ["Vres"])
            b.dma("sp", v_out, t["vf"][i2][:, 0:HC] if cfg is CFG_S else
                  t["vf"][i2][:, 0:HC].rearrange("p (h d) -> p h d", d=64), ["vf%d" % i2],
                  ["v_out"] if cfg is CFG_S else [])

        def attn_A(t, it):
            i2, kr, ncol = it["i2"], it["kr"], it["ncol"]
            zb = (2, 3, 4)[it["i3"]]
            Z = PS[zb]
            zk = "ps%d" % zb
            ef, Lb = t["ef"][i2], t["Lb"][i2]
            multi = len(it["zm"]) > 1
            for zi, (l, r, c0, n) in enumerate(it["zm"]):
                b.mm(Z[0:kr, c0:c0 + n], l, r, it["zkeys"], [zk], start=zi == 0, stop=False, skip=True)
            b.act(ef[:, 0:ncol], Z[:, 0:ncol], AF.Exp, [zk], ["ef%d" % i2])
            b.act(Lb[:, 0:ncol], ef[:, 0:ncol], AF.Ln, ["ef%d" % i2], ["Lb%d" % i2], bias=1.0)
            if it["mask"] is not None:
                b.tt("dve", Lb[:, 0:ncol], Lb[:, 0:ncol], it["mask"], ALU.mult, ["Lb%d" % i2, "cb"], ["Lb%d" % i2])

        def attn_B(t, it):
            i2, kr, ncol = it["i2"], it["kr"], it["ncol"]
            first, last = it["step"] == 0, it["step"] == it["nsteps"] - 1
            ab = (2, 3, 4)[it["i3"]]
            ak = "ps%d" % ab
            Ab, Bb = PS[ab], PS[5]
            ef, Lb, attn, Cc = t["ef"][i2], t["Lb"][i2], t["attn"][i2], t["Cc"]
            zmms = it["zm"]
            multi = len(zmms) > 1
            b.mm(Ab[0:kr, 0:ncol], trineg[0:kr, 0:kr], Lb[0:kr, 0:ncol], ["cb", "Lb%d" % i2], [ak], start=False,
                 stop=True, skip=True)
            if not last:
                b.mm(Bb[:, 0:ncol], ones[0:kr, :], Lb[0:kr, 0:ncol], ["cb", "Lb%d" % i2], ["ps5"])
            if first:
                b.act(attn[:, 0:ncol], Ab[:, 0:ncol], AF.Exp, [ak], ["attn%d" % i2])
            else:
                b.tt("dve", ef[:, 0:ncol], Ab[:, 0:ncol], Cc[:, 0:ncol], ALU.subtract, [ak, "Cc"],
                     ["ef%d" % i2])
                b.act(attn[:, 0:ncol], ef[:, 0:ncol], AF.Exp, ["ef%d" % i2], ["attn%d" % i2])
            if it["mask"] is not None:
                b.tt("dve", attn[:, 0:ncol], attn[:, 0:ncol], it["mask"], ALU.mult, ["attn%d" % i2, "cb"],
                     ["attn%d" % i2])
            if not last:
                if first:
                    b.cp("dve", Cc[:, 0:ncol], Bb[:, 0:ncol], ["ps5"], ["Cc"])
                else:
                    b.tt("dve", Cc[:, 0:ncol], Cc[:, 0:ncol], Bb[:, 0:ncol], ALU.add, ["ps5", "Cc"], ["Cc"])

        def attn_C(t, it):
            i2, kr = it["i2"], it["kr"]
            first, last = it["step"] == 0, it["step"] == it["nsteps"] - 1
            multi = len(it["zm"]) > 1
            attn = t["attn"][i2]
            for ai, (lv, c0, n, o_ap, st_) in enumerate(it["av"]):
                b.mm(o_ap, lv, attn[0:kr, c0:c0 + n], it["vkeys"] + ["attn%d" % i2], [it["okey"]], start=first and st_,
                     stop=last, skip=multi)

        def attn_run(t, items, finish_after):
            n = len(items)
            for i, it in enumerate(items):
                it["i2"] = i % 2
                it["i3"] = i % 3
            attn_A(t, items[0])
            if n > 1:
                attn_A(t, items[1])
            attn_B(t, items[0])
            for i in range(n):
                if i + 2 < n:
                    attn_A(t, items[i + 2])
                if i + 1 < n:
                    attn_B(t, items[i + 1])
                attn_C(t, items[i])
                if i in finish_after:
                    finish_after[i]()

        def attn_finish(t, ncols, sbg_col, out_dram, ob=6):
            O0 = PS[ob]
            GA = PS[0]
            b.cp("act", t["of"][:, 0:ncols], O0[:, 0:ncols], ["ps%d" % ob], ["of"])
            b.act(t["osq"][:, 0:ncols], t["of"][:, 0:ncols], AF.Square, ["of"], ["osq"])
            b.mm(GA[:, 0:ncols], bdones, t["osq"][:, 0:ncols], ["cb", "osq"], ["ps0"])
            b.rstd(t["rso"][:, 0:ncols], GA[:, 0:ncols], 1.0 / 64, RMS_EPS, ["ps0"], ["rso"])
            b.stt("dve", t["onb"][:, 0:ncols], t["of"][:, 0:ncols], sbg_col, t["rso"][:, 0:ncols], ALU.mult, ALU.mult,
                  ["of", "rso", "pp"], ["onb"])
            b.dma("sp", out_dram, t["onb"][:, 0:ncols], ["onb"], [])

        def load_qkg(t, cfg, qkg_d):
            HC = cfg.HC
            b.dma("sp", t["qkg"][:, 0:2 * HC], qkg_d.to_broadcast([128, 2 * HC]), [], ["qkg"])
            b.ts("dve", t["qkg"][:, 0:HC], t["qkg"][:, 0:HC], 0.125, ALU.mult, ["qkg"], ["qkg"])

        FN = ("b0", "b1", "b2", "b3", "b4", "b5", "b6", "b7", "b8", "b9")

        LOCAL = ("b", "AR", "Btb", "Ktb", "Vtb", "sqb", "yob", "Btm", "Ktm", "Vtm", "MTb", "MTk", "Uj", "Lj", "Xj", "Wb",
                 "Ub")

        def rw_alloc(nsets=2):
            t = {}
            t["xrj"] = [A.alloc([128, 520], F32) for _ in range(2)]
            t["hal"] = A.alloc([128, 8, 8], F32)
            t["hal2"] = A.alloc([128, 8, 8], F32)
            t["xsf"] = A.alloc([128, 8, 512], F32)
            t["lora_in"] = A.alloc([128, 512], BF16)
            t["sgb"] = A.alloc([128, 512], BF16)
            t["wa2"] = A.alloc([128, 256], BF16)
            t["g2t"] = A.alloc([128, 256], BF16)
            t["shtmp"] = A.alloc([128, 512], F32)
            t["STf"] = [A.alloc([128, 64], F32) for _ in range(8)]
            t["STb"] = [A.alloc([128, 64], BF16) for _ in range(8)]
            t["c2"] = []
            for _ in range(nsets):
                u = {}
                for n in FN:
                    u[n] = A.alloc([128, 512], F32)
                u["AR"] = A.alloc([128, 8, 2, 64], BF16)
                for n in ("Btb", "Ktb", "Vtb", "sqb", "yob"):
                    u[n] = A.alloc([128, 512], BF16)
                for n in ("Btm", "Ktm", "Vtm"):
                    u[n] = A.alloc([128, 8, 64], BF16)
                u["MTb"] = A.alloc([128, 8, 128], BF16)
                u["MTk"] = A.alloc([128, 8, 128], BF16)
                for n in ("Uj", "Lj", "Xj"):
                    u[n] = [A.alloc([128, 8, 64], BF16) for _ in range(2)]
                u["Wb"] = A.alloc([128, 64], BF16)
                u["Ub"] = A.alloc([128, 64], BF16)
                t["c2"].append(u)
            t.update(t["c2"][0])
            return t

        def stage_rwkv(cfg, t, get_raw, state_of_chunk, mix_dst):
            NT, C, nch, nc2, L = cfg.NT, cfg.C, cfg.nch, cfg.nc2, cfg.L
            nseg, sl = cfg.nseg, cfg.seglen
            xsf = t["xsf"]
            GA, GB, R0 = PS[0], PS[1], PS[7]
            for j in range(cfg.nrch):
                raw, rk = get_raw(j)
                xj = t["xrj"][j % 2][:, 0:nseg * (sl + 1)].rearrange("p (s t) -> p s t", s=nseg)
                kj = "xrj%d" % (j % 2)
                b.cp("act", xj[:, :, 1:sl + 1], raw.rearrange("p (s t) -> p s t", s=nseg), [rk], [kj])
                b.cp("pool", xj[:, :, 0:1], t["hal"][:, j, 0:nseg].unsqueeze(2), ["hal"], [kj])
                b.tt("dve", t["shtmp"][:, 0:NT].rearrange("p (s t) -> p s t", s=nseg), xj[:, :, 0:sl], xj[:, :, 1:sl + 1],
                     ALU.subtract, [kj], ["shtmp"])
                b.stt("dve", xsf[:, j, 0:NT].rearrange("p (s t) -> p s t", s=nseg),
                      t["shtmp"][:, 0:NT].rearrange("p (s t) -> p s t", s=nseg), pp[:, L["mu"] + j:L["mu"] + j + 1],
                      xj[:, :, 1:sl + 1], ALU.mult, ALU.add, ["shtmp", kj, "pp"], ["xs%d" % j])
                b.cp("pool", t["hal2"][:, j, 0:nseg].unsqueeze(2), xj[:, :, sl:sl + 1], [kj], ["hal2"])
                if cfg is CFG_P:
                    b.cp("pool", t["hal"][:, j, 0:1], t["hal2"][:, j, 0:1], ["hal2"], ["hal"])
            jw, jg = 3 * nc2, 3 * nc2 + 1
            b.act(t["lora_in"][0:64, 0:NT], xsf[0:64, jw, 0:NT], AF.Tanh, ["xs%d" % jw], ["lora_in"])
            b.cp("pool", t["lora_in"][64:128, 0:NT], xsf[64:128, jw, 0:NT], ["xs%d" % jw], ["lora_in"])
            b.act(t["sgb"][:, 0:NT], xsf[:, jg, 0:NT], AF.Sigmoid, ["xs%d" % jg], ["sgb"])
            seg = cf[:, CF["seg%d" % C]:CF["seg%d" % C] + NT]
            maskA = cf[0:C, CF["maskA%d" % C]:CF["maskA%d" % C] + 2 * C]
            maskL = cf[0:C, CF["maskL%d" % C]:CF["maskL%d" % C] + C]
            idC = cf[0:C, CF["id%d" % C]:CF["id%d" % C] + C]
            AR, Btb, Ktb, Vtb, sqb = t["AR"], t["Btb"], t["Ktb"], t["Vtb"], t["sqb"]
            BANKS = ((0, 1, 7, 6), (2, 3, 4, 5))

            def c2body(c2, t):
                ia, ib, ir, iy = BANKS[c2]
                GA, GB, R0, PY = PS[ia], PS[ib], PS[ir], PS[iy]
                kA, kB, kR, kY = "ps%d" % ia, "ps%d" % ib, "ps%d" % ir, "ps%d" % iy
                AR, Btb, Ktb, Vtb, sqb = t["AR"], t["Btb"], t["Ktb"], t["Vtb"], t["sqb"]
                jr, jk, jv = c2, nc2 + c2, 2 * nc2 + c2
                cs = slice(c2 * 128, (c2 + 1) * 128)
                col = lambda name: pp[:, L[name] + c2:L[name] + c2 + 1]
                f = {n: t[n][:, 0:NT] for n in FN}
                xs_r, xs_k, xs_v = xsf[:, jr, 0:NT], xsf[:, jk, 0:NT], xsf[:, jv, 0:NT]
                kr_, kk_, kv_ = "xs%d" % jr, "xs%d" % jk, "xs%d" % jv
                P1, P2, P3 = GA[:, 0:NT], GB[:, 0:NT], R0[:, 0:NT]
                sgu, a_, g_, lw, cl, clm, eP, eM, ePx, tmp = (f["b0"], f["b1"], f["b2"], f["b3"], f["b4"], f["b5"],
                                                              f["b6"], f["b7"], f["b8"], f["b9"])
                b.mm(P1, t["wa2"][0:64, cs], t["lora_in"][0:64, 0:NT], ["wa2", "lora_in"], [kA])
                b.mm(P2, t["wa2"][64:128, cs], t["lora_in"][64:128, 0:NT], ["wa2", "lora_in"], [kB])
                b.mm(P3, t["g2t"][:, cs], t["sgb"][:, 0:NT], ["g2t", "sgb"], [kR])
                b.act(sgu, P1, AF.Sigmoid, [kA, "pp"], ["b0"], bias=col("w0"))
                b.act(a_, P2, AF.Sigmoid, [kB, "pp"], ["b1"], bias=col("a0"))
                b.cp("act", g_, P3, [kR], ["b2"])
                yield
                b.ts("pool", lw, sgu, -0.6065306597126334, ALU.mult, ["b0"], ["b3"])
                b.scan(cl, seg, lw, ["cf", "b3"], ["b4"])
                b.tt("pool", clm, cl, lw, ALU.subtract, ["b4", "b3"], ["b5"])
                b.act(eP, cl, AF.Exp, ["b4"], ["b6"])
                b.act(eM, cl, AF.Exp, ["b4"], ["b7"], scale=-1.0)
                b.act(ePx, clm, AF.Exp, ["b5"], ["b8"])
                yield
                kkr = f["b3"]
                b.ts("dve", kkr, xs_k, col("kk"), ALU.mult, [kk_, "pp"], ["b3"])
                b.act(sqb[:, 0:NT], kkr, AF.Square, ["b3"], ["sqb"])
                b.mm(P1, bdones, sqb[:, 0:NT], ["cb", "sqb"], [kA])
                b.ts("dve", tmp, P1, 1e-24, ALU.max, [kA], ["b9"])
                b.rstd(tmp, tmp, 1.0, 0.0, ["b9"], ["b9"])
                kk = f["b4"]
                b.tt("dve", kk, kkr, tmp, ALU.mult, ["b3", "b9"], ["b4"])
                yield
                tt_ = f["b5"]
                b.ts("dve", tt_, a_, col("ka"), ALU.mult, ["b1", "pp", "omka"], ["b5"], s2=omka[:, c2:c2 + 1],
                     op1=ALU.add)
                kp = f["b5"]
                b.tt("dve", kp, xs_k, tt_, ALU.mult, [kk_, "b5"], ["b5"])
                ARv = AR[:, 0:nch, :, 0:C]
                b.stt("dve", ARv[:, :, 0, :], kk.rearrange("p (c t) -> p c t", t=C), -1.0,
                      ePx.rearrange("p (c t) -> p c t", t=C), ALU.mult, ALU.mult, ["b4", "b8"], ["AR"])
                b.tt("pool", ARv[:, :, 1, :], xs_r.rearrange("p (c t) -> p c t", t=C),
                     eP.rearrange("p (c t) -> p c t", t=C), ALU.mult, [kr_, "b6"], ["AR"])
                b.tt("dve", tmp, kk, a_, ALU.mult, ["b4", "b1"], ["b9"])
                b.tt("dve", Btb[:, 0:NT], tmp, eM, ALU.mult, ["b9", "b7"], ["Btb"])
                b.tt("pool", Ktb[:, 0:NT], kp, eM, ALU.mult, ["b5", "b7"], ["Ktb"])
                b.cp("pool", Vtb[:, 0:NT], xs_v, [kv_], ["Vtb"])
                yield
                b.tt("dve", tmp, xs_r, kp, ALU.mult, [kr_, "b5"], ["b9"])
                b.ts("dve", sqb[:, 0:NT], tmp, col("rk"), ALU.mult, ["b9", "pp"], ["sqb"])
                b.mm(P2, bdones, sqb[:, 0:NT], ["cb", "sqb"], [kB])
                bonus = f["b8"]
                b.tt("dve", bonus, P2, xs_v, ALU.mult, [kB, kv_], ["b8"])
                yield
                pbR = bfview(R0)
                PR = slice(0, 128)
                npr = 128
                HO = [slice(hh * 64, hh * 64 + C) for hh in range(2)]
                HR = [slice(hh * 64, hh * 64 + 64) for hh in range(2)]
                for (src, dst, sk, dk, ev) in ((Btb, t["Btm"], "Btb", "Btm", "act"), (Ktb, t["Ktm"], "Ktb", "Ktm", "dve"),
                                               (Vtb, t["Vtm"], "Vtb", "Vtm", "act")):
                    for hh in range(2):
                        for ch in range(nch):
                            b.tr(pbR[HO[hh], ch * 64:(ch + 1) * 64], src[HR[hh], ch * C:(ch + 1) * C],
                                 ident[HR[hh], HR[hh]], [sk, "cb"], [kR])
                    b.cp(ev, dst[PR, 0:nch, :], pbR[PR, 0:nch * 64].rearrange("p (c f) -> p c f", f=64), [kR], [dk])
                    yield
                MTb, MTk, Uj, Lj, Xj = t["MTb"], t["MTk"], t["Uj"], t["Lj"], t["Xj"]
                mA = cf[PR, CF["maskA%d" % C]:CF["maskA%d" % C] + 2 * C]
                mL = cf[PR, CF["maskL%d" % C]:CF["maskL%d" % C] + C]
                mI = cf[PR, CF["id%d" % C]:CF["id%d" % C] + C]
                for g0 in range(0, nch, 4):
                    for hh in range(2):
                        for ch in range(g0, g0 + 4):
                            q4 = ch - g0
                            arf = AR[HR[hh], ch, :, 0:C]
                            b.mm(GA[HO[hh], q4 * 2 * C:(q4 + 1) * 2 * C].rearrange("p (a t) -> p a t", a=2),
                                 Btb[HR[hh], ch * C:(ch + 1) * C], arf, ["Btb", "AR"], [kA])
                            b.mm(GB[HO[hh], q4 * 2 * C:(q4 + 1) * 2 * C].rearrange("p (a t) -> p a t", a=2),
                                 Ktb[HR[hh], ch * C:(ch + 1) * C], arf, ["Ktb", "AR"], [kB])
                    b.tt("dve", MTb[PR, g0:g0 + 4, 0:2 * C], GA[PR, 0:8 * C].rearrange("p (q t) -> p q t", q=4),
                         mA.unsqueeze(1).to_broadcast([npr, 4, 2 * C]), ALU.mult, [kA, "cf"], ["MTb"])
                    b.tt("dve", MTk[PR, g0:g0 + 4, 0:2 * C], GB[PR, 0:8 * C].rearrange("p (q t) -> p q t", q=4),
                         mA.unsqueeze(1).to_broadcast([npr, 4, 2 * C]), ALU.mult, [kB, "cf"], ["MTk"])
                    yield
                for hh in range(2):
                    for ch in range(nch):
                        b.mm(R0[HO[hh], ch * C:(ch + 1) * C], AR[HR[hh], ch, 0, 0:C], Btb[HR[hh], ch * C:(ch + 1) * C],
                             ["AR", "Btb"], [kR])
                b.tt("dve", Lj[0][PR, 0:nch, 0:C], R0[PR, 0:nch * C].rearrange("p (q t) -> p q t", t=C),
                     mL.unsqueeze(1).to_broadcast([npr, nch, C]), ALU.mult, [kR, "cf"], ["Lj0"])
                b.cp("pool", Uj[0][PR, 0:nch, 0:C], MTb[PR, 0:nch, 0:C], ["MTb"], ["Uj0"])
                b.tt("dve", Xj[0][PR, 0:nch, 0:C], MTb[PR, 0:nch, 0:C], mI.unsqueeze(1).to_broadcast([npr, nch, C]),
                     ALU.add, ["MTb", "cf"], ["Xj0"])
                yield
                for lv in range(1, cfg.nlev + 1):
                    pi, ci = (lv - 1) % 2, lv % 2
                    Up, Lp, Un, Ln_, Xp, Xn = Uj[pi], Lj[pi], Uj[ci], Lj[ci], Xj[pi], Xj[ci]
                    pl, pu, px = GA, GB, R0
                    plk, puk, pxk = kA, kB, kR
                    for hh in range(2):
                        for ch in range(nch):
                            b.mm(pl[HO[hh], ch * C:(ch + 1) * C], Up[HO[hh], ch, 0:C], Lp[HO[hh], ch, 0:C],
                                 ["Uj%d" % pi, "Lj%d" % pi], [plk])
                    b.cp("act", Ln_[PR, 0:nch, 0:C], pl[PR, 0:nch * C].rearrange("p (q t) -> p q t", t=C), [plk],
                         ["Lj%d" % ci])
                    yield
                    if lv < cfg.nlev:
                        for hh in range(2):
                            for ch in range(nch):
                                b.mm(pu[HO[hh], ch * C:(ch + 1) * C], Lp[HO[hh], ch, 0:C], Up[HO[hh], ch, 0:C],
                                     ["Uj%d" % pi, "Lj%d" % pi], [puk])
                        b.cp("dve", Un[PR, 0:nch, 0:C], pu[PR, 0:nch * C].rearrange("p (q t) -> p q t", t=C), [puk],
                             ["Uj%d" % ci])
                    for hh in range(2):
                        for ch in range(nch):
                            b.mm(px[HO[hh], ch * C:(ch + 1) * C], Ln_[HO[hh], ch, 0:C], Xp[HO[hh], ch, 0:C],
                                 ["Lj%d" % ci, "Xj%d" % pi], [pxk])
                    b.tt("dve", Xn[PR, 0:nch, 0:C], px[PR, 0:nch * C].rearrange("p (q t) -> p q t", t=C),
                         Xp[PR, 0:nch, 0:C], ALU.add, [pxk, "Xj%d" % pi], ["Xj%d" % ci])
                    yield
                Xf = Xj[cfg.nlev % 2]
                xk = "Xj%d" % (cfg.nlev % 2)
                Wb, Ub, Btm, Ktm, Vtm = t["Wb"], t["Ub"], t["Btm"], t["Ktm"], t["Vtm"]
                for ch in range(nch):
                    si = state_of_chunk(c2, ch)
                    sf, sb_, skf, skb = t["STf"][si], t["STb"][si], "STf%d" % si, "STb%d" % si
                    for hh in range(2):
                        b.mm(GA[HO[hh], 0:64], AR[HR[hh], ch, 0, 0:C], sb_[HR[hh], :], ["AR", skb], [kA], start=True,
                             stop=False)
                        b.mm(GA[HO[hh], 0:64], MTk[HO[hh], ch, 0:C], Vtm[HO[hh], ch, :], ["MTk", "Vtm"], [kA],
                             start=False, stop=True)
                    b.cp("act", Wb[PR, :], GA[PR, 0:64], [kA], ["Wb"])
                    yield
                    for hh in range(2):
                        b.mm(GB[HO[hh], 0:64], Xf[HO[hh], ch, 0:C], Wb[HO[hh], :], [xk, "Wb"], [kB])
                    b.cp("dve", Ub[PR, :], GB[PR, 0:64], [kB], ["Ub"])
                    yield
                    for hh in range(2):
                        yo = PY[HR[hh], ch * C:(ch + 1) * C]
                        b.mm(yo, sb_[HR[hh], :], AR[HR[hh], ch, 1, 0:C], [skb, "AR"], [kY], start=True, stop=False)
                        b.mm(yo, Ub[HO[hh], :], MTb[HO[hh], ch, C:2 * C], ["Ub", "MTb"], [kY], start=False, stop=False)
                        b.mm(yo, Vtm[HO[hh], ch, :], MTk[HO[hh], ch, C:2 * C], ["Vtm", "MTk"], [kY], start=False,
                             stop=True)
                        so = R0[HR[hh], 0:64]
                        b.mm(so, Btm[HO[hh], ch, :], Ub[HO[hh], :], ["Btm", "Ub"], [kR], start=True, stop=False)
                        b.mm(so, Ktm[HO[hh], ch, :], Vtm[HO[hh], ch, :], ["Ktm", "Vtm"], [kR], start=False, stop=True)
                    b.tt("dve", sf, R0[:, 0:64], sf, ALU.add, [kR, skf], [skf])
                    b.ts("dve", sf, sf, t["b6"][:, ch * C + C - 1:ch * C + C], ALU.mult, [skf, "b6"], [skf])
                    b.cp("pool", sb_, sf, [skf], [skb])
                    yield
                y_, yc, rs = f["b0"], f["b1"], f["b7"]
                b.cp("act", y_, PY[:, 0:NT], [kY], ["b0"])
                b.cp("pool", sqb[:, 0:NT], y_, ["b0"], ["sqb"])
                b.mm(P1, bdmean, sqb[:, 0:NT], ["cb", "sqb"], [kA])
                b.tt("dve", yc, y_, P1, ALU.subtract, ["b0", kA], ["b1"])
                yield
                b.act(sqb[:, 0:NT], yc, AF.Square, ["b1"], ["sqb"])
                b.mm(P2, bdmean, sqb[:, 0:NT], ["cb", "sqb"], [kB])
                b.rstd(rs, P2, 1.0, LNX_EPS, [kB], ["b7"])
                b.tt("dve", yc, yc, rs, ALU.mult, ["b1", "b7"], ["b1"])
                b.ts("dve", yc, yc, col("lw"), ALU.mult, ["b1", "pp"], ["b1"], s2=col("lb"), op1=ALU.add)
                b.tt("pool", yc, yc, bonus, ALU.add, ["b1", "b8"], ["b1"])
                b.tt("dve", t["yob"][:, 0:NT], yc, g_, ALU.mult, ["b1", "b2"], ["yob"])
                b.dma("sp", mix_dst(c2), t["yob"][:, 0:NT], ["yob"], [])

            gens = []
            for c2 in range(nc2):
                tc = dict(t)
                tc.update(t["c2"][c2 % len(t["c2"])])
                gens.append((c2, c2body(c2, tc)))
            b.local = LOCAL
            while gens:
                for item in list(gens):
                    c2, g = item
                    b.suffix = "@%d" % c2
                    try:
                        next(g)
                    except StopIteration:
                        gens.remove(item)
            b.suffix = ""

        cfg = CFG_S
        load_params(cfg, "s", pp_s)
        W1 = A.alloc([128, 16, 1024], BF16)
        for kc in range(16):
            b.dma("pool", W1[:, kc, :], w1s[kc * 128:(kc + 1) * 128, :], [], ["W1_%d" % kc])
        t = sb_alloc()
        load_qkg(t, cfg, qkg_s)
        KcT = A.alloc([64, 2, 4, PAST], BF16)
        Vc = A.alloc([128, 4, 16, 128], BF16)
        kTs = A.alloc([64, 2, 256], BF16)
        vnew = A.alloc([32, NS, 128], BF16)
        vsb = A.alloc([128, 2, 128], BF16)
        stage_norm_T(2, xs)
        for m in range(2):
            stage_sb_tok(cfg, t, W1, m, kTs[:, :, m * 128:(m + 1) * 128], vsb[:, m, :], k_s[m * 128:(m + 1) * 128, :],
                         v_s[m * 128:(m + 1) * 128, :])
        for j in range(cfg.nrch):
            bank = PS[j % 2]
            bk = "ps%d" % (j % 2)
            for kc in range(16):
                b.mm(bank[:, 0:256], W1[:, kc, 384 + j * 128:384 + (j + 1) * 128], xnT[:, kc, 0:256], ["W1_%d" % kc, "xnT"], [bk],
                     start=kc == 0, stop=kc == 15)
            b.cp("act", xrs_raw[:, j, :], bank[:, 0:256], [bk], ["xrs_raw"])
        if stop == 1:
            S.emit(ctx)
            return nc
        for bb in range(NS):
            bank = PS[bb // 4]
            b.mm(bank[0:32, (bb % 4) * 128:(bb % 4 + 1) * 128], ident[:, (bb % 4) * 32:(bb % 4 + 1) * 32], vsb[:, bb // 4, :],
                 ["cb", "Vres"], ["ps%d" % (bb // 4)])
        for g in range(2):
            b.cp(("act", "dve")[g], vnew[0:32, 4 * g:4 * g + 4, :], PS[g][0:32, :].rearrange("p (b f) -> p b f", b=4),
                 ["ps%d" % g], ["vnew"])
        masks = cb[:, CB["masks"]:CB["masks"] + 256]
        if stop == 1.2:
            S.emit(ctx)
            return nc
        for grp in range(2):
            for b4 in range(4):
                bb = grp * 4 + b4
                b.dma("pool", KcT[:, :, b4, :], kcT[:, :, bb, :], [], ["KcT%d" % b4])
                b.dma("pool", Vc[:, b4, :, :], vc[bb].rearrange("(kb p) f -> p kb f", p=128), [], ["Vc%d" % b4])
            if stop == 1.4:
                S.emit(ctx)
                return nc
            nsteps = 17
            items = []
            for step in range(nsteps):
                kb = 16 - step
                pairs = [(hh, b4) for hh in range(2) for b4 in range(4)]
                cols = lambda b4: slice((grp * 4 + b4) * 32, (grp * 4 + b4 + 1) * 32)
                if kb == 16:
                    zm = [(kTs[:, hh, cols(b4)], t["qT"][:, hh, cols(b4)], hh * 128 + b4 * 32, 32) for hh, b4 in pairs]
                    av = [(vnew[0:32, grp * 4 + b4, hh * 64:hh * 64 + 64], hh * 128 + b4 * 32, 32,
                           PS[6][hh * 64:hh * 64 + 64, cols(b4)], b4 == 0) for hh, b4 in pairs]
                    items.append(dict(step=step, nsteps=nsteps, kr=32, ncol=256, zm=zm, av=av, mask=masks,
                                      zkeys=["kT", "qT"], vkeys=["vnew"], okey="ps6"))
                else:
                    zm = [(KcT[:, hh, b4, kb * 128:(kb + 1) * 128], t["qT"][:, hh, cols(b4)], hh * 128 + b4 * 32, 32)
                          for hh, b4 in pairs]
                    av = [(Vc[:, b4, kb, hh * 64:hh * 64 + 64], hh * 128 + b4 * 32, 32,
                           PS[6][hh * 64:hh * 64 + 64, cols(b4)], b4 == 0) for hh, b4 in pairs]
                    items.append(dict(step=step, nsteps=nsteps, kr=128, ncol=256, zm=zm, av=av, mask=None,
                                      zkeys=["KcT%d" % q for q in range(4)] + ["qT"],
                                      vkeys=["Vc%d" % q for q in range(4)], okey="ps6"))
            attn_run(t, items, {})
        attn_finish(t, 256, pp[:, cfg.L["sbg"]:cfg.L["sbg"] + 1], mix_s[0:128, :])
        if stop == 2:
            S.emit(ctx)
            return nc
        S.barrier()

        A.reset(rw_mark)
        t = rw_alloc()
        b.dma("pool", t["wa2"][:, 0:128], wa2_s, [], ["wa2"])
        b.dma("pool", t["g2t"][:, 0:128], g2_s, [], ["g2t"])
        b.dma("sp", t["hal"][:, 0:5, :], sh0, [], ["hal"])
        for bb in range(NS):
            b.dma("sp", t["STf"][bb], st0[:, bb, :], [], ["STf%d" % bb])
            b.cp("pool", t["STb"][bb], t["STf"][bb], ["STf%d" % bb], ["STb%d" % bb])
        stage_rwkv(cfg, t, lambda j: (xrs_raw[:, j, :], "xrs_raw"), lambda c2, ch: ch, lambda c2: mix_s[128:256, :])
        b.dma("sp", sh_s, t["hal2"][:, 0:5, :], ["hal2"], [])
        for bb in range(NS):
            b.dma("sp", s_s[:, bb, :], t["STf"][bb], ["STf%d" % bb], [])
        if stop == 3:
            S.emit(ctx)
            return nc
        S.barrier()

        cfg = CFG_P
        A.reset(base_mark)
        b.dma("sp", g1b, g1.to_broadcast([128, D]), [], ["g1b"])
        load_params(cfg, "p", pp_p)
        W1 = A.alloc([128, 16, 768], BF16)
        for kc in range(16):
            b.dma("pool", W1[:, kc, :], w1p_sb[kc * 128:(kc + 1) * 128, :], [], ["W1_%d" % kc])
        t = sb_alloc()
        load_qkg(t, cfg, qkg_p)
        kTr = A.alloc([64, 4, T_P], BF16)
        Vr = A.alloc([128, 32, 256], BF16)
        for s in range(8):
            stage_norm_T(4, xp[s * 512:(s + 1) * 512, :])
            b.dma("sp", xnT_d[s], xnT.rearrange("p a b -> p (a b)"), ["xnT"], ["xnT_d%d" % s])
            for m in range(4):
                t0 = s * 512 + m * 128
                stage_sb_tok(cfg, t, W1, m, kTr[:, :, t0:t0 + 128], Vr[:, t0 // 128, :],
                             k_p[:, t0:t0 + 128, :].rearrange("h t d -> t h d"),
                             v_p[:, t0:t0 + 128, :].rearrange("h t d -> t h d"))
            items = []
            fin = {}
            for c2 in range(2):
                ob = 6 + (c2 % 2)
                for hh in range(2):
                    h = 2 * c2 + hh
                    hr = slice(hh * 64, hh * 64 + 64)
                    nsteps = 4 * s + 4
                    for step in range(nsteps):
                        kb = 4 * s + 3 - step
                        di = kb - 4 * s
                        zm = [(kTr[:, h, kb * 128:(kb + 1) * 128], t["qT"][:, h, 0:512], 0, 512)]
                        av = [(Vr[:, kb, h * 64:(h + 1) * 64], 0, 512, PS[ob][hr, 0:512], True)]
                        mk = cb[:, CB["maskp"] + 512 * di:CB["maskp"] + 512 * (di + 1)] if di >= 0 else None
                        items.append(dict(step=step, nsteps=nsteps, kr=128, ncol=512, zm=zm, av=av, mask=mk,
                                          zkeys=["kT", "qT"], vkeys=["Vres"], okey="ps%d" % ob))
                fin[len(items) - 1] = (lambda c2=c2, ob=ob, s=s: attn_finish(
                    t, 512, pp[:, cfg.L["sbg"] + c2:cfg.L["sbg"] + c2 + 1],
                    mix_p[c2 * 128:(c2 + 1) * 128, s * 512:(s + 1) * 512], ob))
            attn_run(t, items, fin)
        if stop == 4:
            S.emit(ctx)
            return nc
        S.barrier()

        A.reset(rw_mark)
        W1 = A.alloc([128, 16, 1024], BF16)
        for kc in range(16):
            b.dma("pool", W1[:, kc, :], w1p_rw[kc * 128:(kc + 1) * 128, :], [], ["W1_%d" % kc])
        t = rw_alloc()
        b.dma("pool", t["wa2"], wa2_p, [], ["wa2"])
        b.dma("pool", t["g2t"], g2_p, [], ["g2t"])
        b.memset("pool", t["hal"], 0.0, [], ["hal"])
        for c2 in range(2):
            b.memset("pool", t["STf"][c2], 0.0, [], ["STf%d" % c2])
            b.memset("pool", t["STb"][c2], 0.0, [], ["STb%d" % c2])
        for s in range(8):
            b.dma("sp", xnT.rearrange("p a b -> p (a b)"), xnT_d[s], [], ["xnT"])

            def get_raw(j):
                bank = PS[j % 2]
                bk = "ps%d" % (j % 2)
                for kc in range(16):
                    b.mm(bank[:, 0:512], W1[:, kc, j * 128:(j + 1) * 128], xnT[:, kc, 0:512], ["W1_%d" % kc, "xnT"], [bk],
                         start=kc == 0, stop=kc == 15)
                return bank[:, 0:512], bk
            stage_rwkv(cfg, t, get_raw, lambda c2, ch: c2,
                       lambda c2: mix_p[256 + c2 * 128:256 + (c2 + 1) * 128, s * 512:(s + 1) * 512])
        b.dma("sp", sh_p, t["hal2"][:, :, 0], ["hal2"], [], slow=True)
        for c2 in range(2):
            b.dma("sp", s_p[c2], t["STf"][c2], ["STf%d" % c2], [])
        S.emit(ctx)
    return nc


NTK = 1058
NTO = 1056


def build_phase2():
    nc = bass.Bass("TRN2", target_bir_lowering=False)
    dt = lambda name, shape, ty, kind: nc.dram_tensor(name, shape, ty, kind=kind).ap()
    IN, OUT = "ExternalInput", "ExternalOutput"
    x2in = dt("x2in", [NTK, D], F32, IN)
    cat = dt("cat", [D, NTK], BF16, IN)
    wout = dt("wout", [D, D], F32, IN)
    wup = dt("wup", [D, DFF], F32, IN)
    wgate = dt("wgate", [D, DFF], F32, IN)
    wdown = dt("wdown", [DFF, D], F32, IN)
    g2n = dt("g2n", [1, D], F32, IN)
    ppf = dt("ppf", [128, 4 * NFC], F32, IN)
    conv0 = dt("conv0", [128, NFC, 2], F32, IN)
    hscale = dt("hscale", [128, 1], F32, IN)
    identd = dt("identd", [128, 128], F32, IN)
    y = dt("y", [NTO, D], F32, OUT)
    convp = dt("convp", [128, NFC, 2], F32, OUT)
    convs = dt("convs", [128, NFC, 2], F32, OUT)
    x2d = nc.dram_tensor("x2d", [NTK, D], F32).ap()

    with ExitStack() as ctx:
        S = Sched(nc)
        b = B(nc, S)
        arena_t = ctx.enter_context(nc.sbuf_tensor("arena", [128, ARENA_BYTES // 2], BF16))
        A = Arena(arena_t)
        PS = [ctx.enter_context(nc.psum_tensor("ps%d" % i, [128, 512], F32)) for i in range(8)]
        ident = A.alloc([128, 128], BF16)
        pf = A.alloc([128, 4 * NFC], F32)
        c0t = A.alloc([128, NFC, 2], F32)
        hs = A.alloc([128, 1], F32)
        cstp = A.alloc([128, NFC, 2], F32)
        csts = A.alloc([128, NFC, 2], F32)
        st = A.alloc([128, 8], F32)
        xn2T = A.alloc([128, 16, NTK], BF16)
        b.dma("pool", ident, identd, [], ["ident"])
        b.dma("sp", pf, ppf, [], ["pf"])
        b.dma("sp", c0t, conv0, [], ["c0t"])
        b.dma("sp", hs, hscale, [], ["hs"])
        m_persist = A.mark()

        g2b = A.alloc([128, D], F32)
        catT = A.alloc([128, 16, NTK], BF16)
        Wo = A.alloc([128, 16, D], BF16)
        xt = [A.alloc([128, D], F32) for _ in range(2)]
        x2t = [A.alloc([128, D], F32) for _ in range(2)]
        xn = A.alloc([128, D], BF16)
        b.dma("sp", g2b, g2n.to_broadcast([128, D]), [], ["g2b"])
        for kc in range(16):
            b.dma("sp", catT[:, kc, :], cat[kc * 128:(kc + 1) * 128, :], [], ["catT%d" % kc])
            b.dma("pool", Wo[:, kc, :], wout[kc * 128:(kc + 1) * 128, :], [], ["Wo%d" % kc])
        subt = [(m * 128, 128) for m in range(8)] + [(NTK - 128, 128)]
        for m, (r0, nr) in enumerate(subt):
            i2 = m % 2
            kx, k2 = "xt%d" % i2, "x2t%d" % i2
            b.dma("sp", xt[i2][0:nr], x2in[r0:r0 + nr, :], [], [kx])
            for nb in range(4):
                bank = PS[i2 * 4 + nb]
                bk = "ps%d" % (i2 * 4 + nb)
                for kc in range(16):
                    b.mm(bank[0:nr, :], catT[:, kc, r0:r0 + nr], Wo[:, kc, nb * 512:(nb + 1) * 512],
                         ["catT%d" % kc, "Wo%d" % kc], [bk], start=kc == 0, stop=kc == 15)
                b.tt("dve", x2t[i2][0:nr, nb * 512:(nb + 1) * 512], bank[0:nr, :], xt[i2][0:nr, nb * 512:(nb + 1) * 512],
                     ALU.add, [bk, kx], [k2])
            b.dma("sp", x2d[r0:r0 + nr, :], x2t[i2][0:nr], [k2], ["x2d"])
            b.act(xn[0:nr], x2t[i2][0:nr], AF.Square, [k2], ["xn", "ss"], accum=st[0:nr, 0:1])
            b.rstd(st[0:nr, 1:2], st[0:nr, 0:1], 1.0 / D, RMS_EPS, ["ss"], ["rs"])
            b.stt("dve", xn[0:nr], x2t[i2][0:nr], st[0:nr, 1:2], g2b[0:nr], ALU.mult, ALU.mult, [k2, "rs", "g2b"],
                  ["xn"])
            for g in range(4):
                bank = PS[i2 * 4 + g]
                bk = "ps%d" % (i2 * 4 + g)
                pb = bank[:].bitcast(BF16)
                for c in range(4):
                    b.tr(pb[:, c * 128:c * 128 + nr], xn[0:nr, (4 * g + c) * 128:(4 * g + c + 1) * 128],
                         ident[0:nr, 0:nr], ["xn", "ident"], [bk])
                b.cp(("act", "dve")[g % 2], xn2T[:, 4 * g:4 * g + 4, r0:r0 + nr],
                     pb[:, 0:512].rearrange("p (c t) -> p c t", c=4)[:, :, 0:nr], [bk], ["xn2T"])
        S.barrier()

        A.reset(m_persist)
        hT = A.alloc([128, NFC, NTO], BF16)
        m_hT = A.mark()
        Wu = [A.alloc([128, 16, 256], BF16) for _ in range(2)]
        Wg = [A.alloc([128, 16, 256], BF16) for _ in range(2)]
        gtp = [A.alloc([128, NTK + 2], F32) for _ in range(2)]
        ub = [A.alloc([128, NTK], F32) for _ in range(2)]
        acc = [A.alloc([128, NTK], F32) for _ in range(2)]
        groups = [(0, 353), (353, 353), (706, 352)]
        for blk in range(NFC // 2):
            w2i = blk % 2
            ku, kg = "Wu%d" % w2i, "Wg%d" % w2i
            b.dma("pool", Wu[w2i], wup[:, blk * 256:(blk + 1) * 256].rearrange("(kc p) n -> p kc n", p=128), [], [ku])
            b.dma("pool", Wg[w2i], wgate[:, blk * 256:(blk + 1) * 256].rearrange("(kc p) n -> p kc n", p=128), [], [kg])
            for fi in range(2):
                fc = 2 * blk + fi
                f2 = fc % 2
                kgt, kub, kac = "gtp%d" % f2, "ub%d" % f2, "acc%d" % f2
                G, U, AC = gtp[f2], ub[f2], acc[f2]
                for tg, (c0, n) in enumerate(groups):
                    pi = (fc * 3 + tg) % 4
                    UB, GBk = PS[2 * pi], PS[2 * pi + 1]
                    uk, gk = "ps%d" % (2 * pi), "ps%d" % (2 * pi + 1)
                    for kc in range(16):
                        b.mm(UB[:, 0:n], Wu[w2i][:, kc, fi * 128:(fi + 1) * 128], xn2T[:, kc, c0:c0 + n], [ku, "xn2T"],
                             [uk], start=kc == 0, stop=kc == 15)
                    for kc in range(16):
                        b.mm(GBk[:, 0:n], Wg[w2i][:, kc, fi * 128:(fi + 1) * 128], xn2T[:, kc, c0:c0 + n], [kg, "xn2T"],
                             [gk], start=kc == 0, stop=kc == 15)
                    b.cp("dve", U[:, c0:c0 + n], UB[:, 0:n], [uk], [kub])
                    if tg < 2:
                        b.cp("act", G[:, c0:c0 + n], GBk[:, 0:n], [gk], [kgt])
                    else:
                        b.cp("act", G[:, 706:1026], GBk[:, 0:320], [gk], [kgt])
                        b.cp("act", G[:, 1028:1060], GBk[:, 320:352], [gk], [kgt])
                b.ts("pool", G[:, 0:2], G[:, 0:2], hs[:, 0:1], ALU.mult, [kgt, "hs"], [kgt])
                b.cp("pool", G[:, 1026:1028], c0t[:, fc, :], [kgt, "c0t"], [kgt])
                b.cp("pool", cstp[:, fc, :], G[:, 1024:1026], [kgt], ["cstp"])
                b.cp("pool", csts[:, fc, :], G[:, 1058:1060], [kgt], ["csts"])
                wcol = lambda i: pf[:, i * NFC + fc:i * NFC + fc + 1]
                b.ts("dve", AC[:, 0:NTK], G[:, 2:NTK + 2], wcol(2), ALU.mult, [kgt, "pf"], [kac], s2=wcol(3), op1=ALU.add)
                b.stt("dve", AC[:, 0:NTK], G[:, 1:NTK + 1], wcol(1), AC[:, 0:NTK], ALU.mult, ALU.add, [kgt, "pf", kac],
                      [kac])
                b.stt("dve", AC[:, 0:NTK], G[:, 0:NTK], wcol(0), AC[:, 0:NTK], ALU.mult, ALU.add, [kgt, "pf", kac],
                      [kac])
                b.act(AC[:, 0:NTK], AC[:, 0:NTK], AF.Silu, [kac], [kac])
                b.tt("dve", hT[:, fc, 0:1024], AC[:, 0:1024], U[:, 2:1026], ALU.mult, [kac, kub], ["hT%d" % fc])
                b.tt("pool", hT[:, fc, 1024:1056], AC[:, 1026:1058], U[:, 1026:1058], ALU.mult, [kac, kub],
                     ["hT%d" % fc])
        b.dma("sp", convp, cstp, ["cstp"], [])
        b.dma("sp", convs, csts, ["csts"], [])
        S.barrier()

        A.reset(m_hT)
        Wd = [A.alloc([128, NFC, 256], BF16) for _ in range(2)]
        x2s = [A.alloc([128, 256], F32) for _ in range(2)]
        yt = [A.alloc([128, 256], F32) for _ in range(2)]
        subo = [(m * 128, 128) for m in range(8)] + [(NTO - 128, 128)]
        cnt = 0
        for nb in range(8):
            w2i = nb % 2
            kd = "Wd%d" % w2i
            b.dma("pool", Wd[w2i], wdown[:, nb * 256:(nb + 1) * 256].rearrange("(fc p) n -> p fc n", p=128), [], [kd])
            for m, (r0, nr) in enumerate(subo):
                i2 = cnt % 2
                bank = PS[cnt % 8]
                bk = "ps%d" % (cnt % 8)
                cnt += 1
                b.dma("sp", x2s[i2][0:nr], x2d[2 + r0:2 + r0 + nr, nb * 256:(nb + 1) * 256], [], ["x2s%d" % i2])
                for fc in range(NFC):
                    b.mm(bank[0:nr, 0:256], hT[:, fc, r0:r0 + nr], Wd[w2i][:, fc, :], [kd], [bk], start=fc == 0,
                         stop=fc == NFC - 1)
                b.tt("dve", yt[i2][0:nr], bank[0:nr, 0:256], x2s[i2][0:nr], ALU.add, [bk, "x2s%d" % i2], ["yt%d" % i2])
                b.dma("sp", y[r0:r0 + nr, nb * 256:(nb + 1) * 256], yt[i2][0:nr], ["yt%d" % i2], [])
        S.emit(ctx)
    return nc


_CACHE = {}


def _progs():
    if "p1" not in _CACHE:
        _CACHE["p1"] = build_phase1()
        _CACHE["p2"] = build_phase2()
    return _CACHE["p1"], _CACHE["p2"]


def _pp(cfg, base, mu_cols, inp):
    L = cfg.L
    nc2 = cfg.nc2
    pp = np.zeros((128, L["n"]), np.float32)
    mu = inp["mu_shift"][0]
    for j, c0 in enumerate(mu_cols):
        pp[:, L["mu"] + j] = mu[c0:c0 + 128]
    vecs = dict(w0=inp["w0"][0], a0=inp["a0"][0], kk=inp["k_k"][0], ka=inp["k_a"][0], rk=inp["r_k"][0].reshape(-1),
                lw=inp["lnx_w"][0], lb=inp["lnx_b"][0], sbg=inp["sb_out_g"][0].reshape(-1))
    for k, v in vecs.items():
        for c2 in range(nc2):
            pp[:, L[k] + c2] = v[base + c2 * 128:base + (c2 + 1) * 128]
    return pp


def _phase1(inp):
    p1, p2 = _progs()
    cb, cf = make_consts()
    w_in = inp["w_in"][0]
    RW = 3072
    in1 = []
    for c in range(8):
        bq, j = divmod(c, 4)
        pb, sbase = 256 * j, 128 * c
        d = {}
        d["xp"] = np.ascontiguousarray(inp["x_prompt"][bq])
        d["xs"] = np.ascontiguousarray(inp["x_sample"].reshape(NS * T_S, D))
        d["w1p_sb"] = np.ascontiguousarray(np.concatenate([w_in[:, o + pb:o + pb + 256] for o in (0, 1024, 2048)], 1))
        rw_cols_p = [RW + pb, RW + pb + 128, RW + 1024 + pb, RW + 1024 + pb + 128, RW + 2048 + pb, RW + 2048 + pb + 128,
                     RW + 3072, RW + 3200]
        d["w1p_rw"] = np.ascontiguousarray(np.concatenate([w_in[:, o:o + 128] for o in rw_cols_p], 1))
        rw_cols_s = [RW + sbase, RW + 1024 + sbase, RW + 2048 + sbase, RW + 3072, RW + 3200]
        d["w1s"] = np.ascontiguousarray(np.concatenate([w_in[:, o + sbase:o + sbase + 128] for o in (0, 1024, 2048)] +
                                                       [w_in[:, o:o + 128] for o in rw_cols_s], 1))
        kc = inp["cache_sb_k"][0][:, 2 * c:2 * c + 2]
        d["kcT"] = np.ascontiguousarray(kc.transpose(3, 1, 0, 2))
        vcc = inp["cache_sb_v"][0][:, 2 * c:2 * c + 2]
        d["vc"] = np.ascontiguousarray(vcc.transpose(0, 2, 1, 3).reshape(NS, PAST, 128))
        s0 = inp["state_rwkv"][0][:, 2 * c:2 * c + 2]
        d["st0"] = np.ascontiguousarray(s0.transpose(1, 3, 0, 2).reshape(128, NS, 64))
        sh = inp["state_rwkv_shift"][0][:, 0, :]
        d["sh0"] = np.ascontiguousarray(np.stack([sh[:, o - RW:o - RW + 128] for o in rw_cols_s], 1).transpose(2, 1, 0))
        d["g1"] = np.ascontiguousarray(inp["norm1_g"][0][None])
        qg, kg = inp["q_norm_g"][0], inp["k_norm_g"][0]
        d["qkg_p"] = np.concatenate([np.tile(qg, 4), np.tile(kg, 4)])[None].astype(np.float32)
        d["qkg_s"] = np.concatenate([np.tile(qg, 2), np.tile(kg, 2)])[None].astype(np.float32)
        d["pp_p"] = _pp(CFG_P, pb, [o - RW for o in rw_cols_p], inp)
        d["pp_s"] = _pp(CFG_S, sbase, [o - RW for o in rw_cols_s], inp)
        d["wa2_p"] = np.ascontiguousarray(np.concatenate([inp["w2"][0][:, pb:pb + 256], inp["a2"][0][:, pb:pb + 256]], 0))
        d["wa2_s"] = np.ascontiguousarray(np.concatenate([inp["w2"][0][:, sbase:sbase + 128],
                                                          inp["a2"][0][:, sbase:sbase + 128]], 0))
        d["g2_p"] = np.ascontiguousarray(inp["g2"][0][:, pb:pb + 256])
        d["g2_s"] = np.ascontiguousarray(inp["g2"][0][:, sbase:sbase + 128])
        d["cbd"] = cb
        d["cfd"] = cf
        in1.append(d)
    r1 = run_bass_kernel_spmd(p1, in1, core_ids=list(range(8))).results
    _CACHE["r1"] = r1

    f32 = np.float32
    k_prompt = np.zeros((1, 2, 16, T_P, 64), f32)
    v_prompt = np.zeros((1, 2, 16, T_P, 64), f32)
    rwkv_prompt = np.zeros((1, 2, 16, 64, 64), f32)
    shift_prompt = np.zeros((1, 2, 1, 3328), f32)
    k_sample = np.zeros((1, NS, 16, T_S, 64), f32)
    v_sample = np.zeros((1, NS, 16, T_S, 64), f32)
    rwkv_sample = np.zeros((1, NS, 16, 64, 64), f32)
    shift_sample = np.zeros((1, NS, 1, 3328), f32)
    cat_p = [np.zeros((D, T_P), ml_dtypes.bfloat16) for _ in range(2)]
    cat_s = np.zeros((D, NS * T_S), ml_dtypes.bfloat16)
    for c in range(8):
        bq, j = divmod(c, 4)
        r = r1[c]
        k_prompt[0, bq, 4 * j:4 * j + 4] = r["k_p"]
        v_prompt[0, bq, 4 * j:4 * j + 4] = r["v_p"]
        rwkv_prompt[0, bq, 4 * j:4 * j + 4] = r["s_p"].reshape(2, 2, 64, 64).transpose(0, 1, 3, 2).reshape(4, 64, 64)
        shp = r["sh_p"]
        pb = 256 * j
        for jj, o in enumerate([pb, pb + 128, 1024 + pb, 1024 + pb + 128, 2048 + pb, 2048 + pb + 128, 3072, 3200]):
            shift_prompt[0, bq, 0, o:o + 128] = shp[:, jj]
        k_sample[0, :, 2 * c:2 * c + 2] = r["k_s"].reshape(NS, T_S, 2, 64).transpose(0, 2, 1, 3)
        v_sample[0, :, 2 * c:2 * c + 2] = r["v_s"].reshape(NS, T_S, 2, 64).transpose(0, 2, 1, 3)
        rwkv_sample[0, :, 2 * c:2 * c + 2] = r["s_s"].reshape(2, 64, NS, 64).transpose(2, 0, 3, 1)
        shs = r["sh_s"]
        sbase = 128 * c
        for jj, o in enumerate([sbase, 1024 + sbase, 2048 + sbase, 3072, 3200]):
            shift_sample[0, :, 0, o:o + 128] = shs[:, jj, :].T
        cat_p[bq][256 * j:256 * j + 256] = r["mix_p"][0:256]
        cat_p[bq][1024 + 256 * j:1024 + 256 * j + 256] = r["mix_p"][256:512]
        cat_s[128 * c:128 * c + 128] = r["mix_s"][0:128]
        cat_s[1024 + 128 * c:1024 + 128 * c + 128] = r["mix_s"][128:256]

    outs1 = (k_prompt, v_prompt, rwkv_prompt, shift_prompt, k_sample, v_sample, rwkv_sample, shift_sample)
    return outs1, cat_p, cat_s


def _phase2(inp, cat_p, cat_s):
    p1, p2 = _progs()
    f32 = np.float32
    cw, cbias = inp["ffn_conv_w"][0], inp["ffn_conv_b"][0]
    ppf = np.concatenate([cw[i].reshape(NFC, 128).T for i in range(3)] + [cbias.reshape(NFC, 128).T], 1).astype(f32)
    in2 = []
    for c in range(8):
        bq, j = divmod(c, 4)
        d = {}
        x2 = np.zeros((NTK, D), f32)
        ct = np.zeros((D, NTK), ml_dtypes.bfloat16)
        t0 = 1024 * j
        if j > 0:
            x2[0:2] = inp["x_prompt"][bq, t0 - 2:t0]
            ct[:, 0:2] = cat_p[bq][:, t0 - 2:t0]
        x2[2:1026] = inp["x_prompt"][bq, t0:t0 + 1024]
        ct[:, 2:1026] = cat_p[bq][:, t0:t0 + 1024]
        x2[1026:] = inp["x_sample"][c]
        ct[:, 1026:] = cat_s[:, 32 * c:32 * c + 32]
        d["x2in"] = x2
        d["cat"] = ct
        d["wout"] = np.ascontiguousarray(inp["w_out"][0])
        d["wup"] = np.ascontiguousarray(inp["w_ffn_up"][0])
        d["wgate"] = np.ascontiguousarray(inp["w_ffn_gate"][0])
        d["wdown"] = np.ascontiguousarray(inp["w_ffn_down"][0])
        d["g2n"] = np.ascontiguousarray(inp["norm2_g"][0][None])
        d["ppf"] = ppf
        d["conv0"] = np.ascontiguousarray(inp["state_ffn_conv"][0][c].reshape(2, NFC, 128).transpose(2, 1, 0))
        d["hscale"] = np.full((128, 1), 0.0 if j == 0 else 1.0, f32)
        d["identd"] = np.eye(128, dtype=f32)
        in2.append(d)
    r2 = run_bass_kernel_spmd(p2, in2, core_ids=list(range(8))).results
    y_prompt = np.zeros((2, T_P, D), f32)
    y_sample = np.zeros((NS, T_S, D), f32)
    conv_prompt = np.zeros((1, 2, 2, DFF), f32)
    conv_sample = np.zeros((1, NS, 2, DFF), f32)
    for c in range(8):
        bq, j = divmod(c, 4)
        r = r2[c]
        y_prompt[bq, 1024 * j:1024 * j + 1024] = r["y"][0:1024]
        y_sample[c] = r["y"][1024:1056]
        if j == 3:
            conv_prompt[0, bq] = r["convp"].transpose(2, 1, 0).reshape(2, DFF)
        conv_sample[0, c] = r["convs"].transpose(2, 1, 0).reshape(2, DFF)
    return y_prompt, y_sample, conv_prompt, conv_sample


def kernel(**inp):
    inp = {k: np.asarray(v) for k, v in inp.items()}
    (k_prompt, v_prompt, rwkv_prompt, shift_prompt, k_sample, v_sample, rwkv_sample, shift_sample), cat_p, cat_s = \
        _phase1(inp)
    y_prompt, y_sample, conv_prompt, conv_sample = _phase2(inp, cat_p, cat_s)
    return (y_prompt, y_sample, k_prompt, v_prompt, rwkv_prompt, shift_prompt, conv_prompt,
            k_sample, v_sample, rwkv_sample, shift_sample, conv_sample)
```
